# Optimizing a Trainium2 kernel written in Bass

```python
import math
import jax
import jax.numpy as jnp
from jax import lax
import numpy as np

D_MODEL = 2048
BATCH = 4
SEQ = 4096
DEPTH = 2

GRID_W = 64
CTX_LEN = 256
MIX_WIDTH = D_MODEL
S5_WIDTH = MIX_WIDTH // 2
S5_GROUP = 16
S5_GROUPS = S5_WIDTH // S5_GROUP
S5_STATE = 64
HGRN_WIDTH = MIX_WIDTH - S5_WIDTH
HGRN_HEAD_DIM = 128
HGRN_HEADS = HGRN_WIDTH // HGRN_HEAD_DIM
HGRN_CHUNK = 64
IN_COLS = S5_WIDTH + 5 * HGRN_WIDTH
D_FF = ((8 * D_MODEL // 3 + 127) // 128) * 128
N_MOD = 9
EPS = 1e-6
F_MIN = 1e-6
DT_MIN = 0.001
DT_MAX = 0.1
LAMBDA_RE_MAX = -1e-4

kernel_name = 'hymba_s5_hgrn2_macaron_dit'


def rmsnorm(x, w):
    xf = x.astype(jnp.float32)
    y = xf * lax.rsqrt(jnp.mean(xf * xf, axis=-1, keepdims=True) + EPS)
    return y * w.astype(jnp.float32)


def modulate(h, mods, i_shift, i_scale):
    return h * (1.0 + mods[..., i_scale, :]) + mods[..., i_shift, :]


def swiglu(h, w_gate, w_up, w_down):
    return (jax.nn.silu(h @ w_gate) * (h @ w_up)) @ w_down


def ffn_half_step(stream, mods, norm_w_i, w_gate, w_up, w_down, base):
    h = modulate(rmsnorm(stream, norm_w_i), mods, base, base + 1)
    return stream + 0.5 * mods[..., base + 2, :] * swiglu(h, w_gate, w_up, w_down)


def raster_to_colmajor(t, rows):
    b, l, ch = t.shape
    return t.reshape(b, rows, GRID_W, ch).transpose(0, 2, 1, 3).reshape(b, l, ch)


def colmajor_to_raster(t, rows):
    b, l, ch = t.shape
    return t.reshape(b, GRID_W, rows, ch).transpose(0, 2, 1, 3).reshape(b, l, ch)


def s5_discretise(lam_re, lam_im, log_step, b_re, b_im):
    lam_re = jnp.minimum(lam_re.astype(jnp.float32), LAMBDA_RE_MAX)
    lam_im = lam_im.astype(jnp.float32)
    dt = jnp.exp(log_step.astype(jnp.float32))[:, None]
    mag = jnp.exp(lam_re * dt)
    lb_re = mag * jnp.cos(lam_im * dt)
    lb_im = mag * jnp.sin(lam_im * dt)
    den = lam_re * lam_re + lam_im * lam_im
    nr = lb_re - 1.0
    ni = lb_im
    cf_re = (nr * lam_re + ni * lam_im) / den
    cf_im = (ni * lam_re - nr * lam_im) / den
    b_re = b_re.astype(jnp.float32)
    b_im = b_im.astype(jnp.float32)
    br = cf_re[..., None] * b_re - cf_im[..., None] * b_im
    bi = cf_re[..., None] * b_im + cf_im[..., None] * b_re
    return lb_re, lb_im, br, bi


def s5_combine(e_i, e_j):
    ar_i, ai_i, br_i, bi_i = e_i
    ar_j, ai_j, br_j, bi_j = e_j
    return (ar_j * ar_i - ai_j * ai_i,
            ar_j * ai_i + ai_j * ar_i,
            ar_j * br_i - ai_j * bi_i + br_j,
            ar_j * bi_i + ai_j * br_i + bi_j)


def s5_scan(u, lb_re, lb_im, br, bi, x0_re, x0_im):
    bu_re = jnp.einsum('blgh,gph->blgp', u, br)
    bu_im = jnp.einsum('blgh,gph->blgp', u, bi)
    bu_re = bu_re.at[:, 0].add(lb_re * x0_re - lb_im * x0_im)
    bu_im = bu_im.at[:, 0].add(lb_re * x0_im + lb_im * x0_re)
    seq_len = u.shape[1]
    a_re = jnp.broadcast_to(lb_re, (1, seq_len) + lb_re.shape)
    a_im = jnp.broadcast_to(lb_im, (1, seq_len) + lb_im.shape)
    _, _, xr, xi = lax.associative_scan(s5_combine, (a_re, a_im, bu_re, bu_im), axis=1)
    return xr, xi


def s5_readout(xr, xi, c_re, c_im):
    return jnp.einsum('blgp,ghp->blgh', xr, c_re) - jnp.einsum('blgp,ghp->blgh', xi, c_im)


def s5_glu(y, w_glu, b_glu):
    g = jax.nn.gelu(y)
    return g * jax.nn.sigmoid(g @ w_glu + b_glu)


def s5_mixer(u_ctx, u_lat, lam_re, lam_im, log_step, b_re, b_im, c_re, c_im,
             d_skip, w_glu, b_glu, need_ctx_out):
    def groups(u):
        b, l, _ = u.shape
        return u.astype(jnp.float32).reshape(b, l, S5_GROUPS, S5_GROUP)
    uc = groups(u_ctx)
    ul = groups(u_lat)
    dg = d_skip.astype(jnp.float32).reshape(S5_GROUPS, S5_GROUP)
    y_lat = dg * ul
    y_ctx = dg * uc
    zero = jnp.zeros((uc.shape[0], S5_GROUPS, S5_STATE), jnp.float32)
    for d in range(2):
        reverse = d == 1
        lb_re, lb_im, br, bi = s5_discretise(lam_re[d], lam_im[d], log_step[d], b_re[d], b_im[d])
        cr = c_re[d].astype(jnp.float32)
        ci = c_im[d].astype(jnp.float32)
        ucd = jnp.flip(uc, axis=1) if reverse else uc
        uld = jnp.flip(ul, axis=1) if reverse else ul
        xr_c, xi_c = s5_scan(ucd, lb_re, lb_im, br, bi, zero, zero)
        xr_l, xi_l = s5_scan(uld, lb_re, lb_im, br, bi, xr_c[:, -1], xi_c[:, -1])
        yl = s5_readout(xr_l, xi_l, cr, ci)
        y_lat = y_lat + (jnp.flip(yl, axis=1) if reverse else yl)
        if need_ctx_out:
            yc = s5_readout(xr_c, xi_c, cr, ci)
            y_ctx = y_ctx + (jnp.flip(yc, axis=1) if reverse else yc)
    b, l = ul.shape[:2]
    out_lat = s5_glu(y_lat.reshape(b, l, S5_WIDTH), w_glu, b_glu)
    out_ctx = s5_glu(y_ctx.reshape(b, uc.shape[1], S5_WIDTH), w_glu, b_glu) if need_ctx_out else None
    return out_ctx, out_lat


def hgrn_forget(z, lb):
    f = lb + (1.0 - lb) * jax.nn.sigmoid(z)
    log_f = jnp.log(jnp.maximum(f, F_MIN))
    return 1.0 - f, log_f


def hgrn2_chunk_scan(q, k, v, log_f, s0):
    b, h, l, _ = q.shape
    n_chunks = l // HGRN_CHUNK

    def chunks(t):
        return jnp.moveaxis(t.reshape(b, h, n_chunks, HGRN_CHUNK, t.shape[-1]), 2, 0)

    tri = jnp.tril(jnp.ones((HGRN_CHUNK, HGRN_CHUNK), dtype=bool))[:, :, None]

    def step(s, inp):
        qc, kc, vc, gc = inp
        gcum = jnp.cumsum(gc, axis=2)
        diff = gcum[:, :, :, None, :] - gcum[:, :, None, :, :]
        decay = jnp.where(tri, jnp.exp(jnp.minimum(diff, 0.0)), 0.0)
        scores = jnp.einsum('bhtd,bhsd,bhtsd->bhts', qc, kc, decay)
        o = (jnp.einsum('bhts,bhsv->bhtv', scores, vc)
             + jnp.einsum('bhtd,bhdv->bhtv', qc * jnp.exp(gcum), s))
        g_last = gcum[:, :, -1]
        s_new = (jnp.exp(g_last)[..., None] * s
                 + jnp.einsum('bhsd,bhsv->bhdv', kc * jnp.exp(g_last[:, :, None] - gcum), vc))
        return s_new, o

    s_fin, o = lax.scan(step, s0, (chunks(q), chunks(k), chunks(v), chunks(log_f)))
    o = jnp.moveaxis(o, 0, 2).reshape(b, h, l, v.shape[-1])
    return o, s_fin


def hgrn2_direction(q, z, v, lb, s0, reverse):
    k, log_f = hgrn_forget(z, lb)
    if reverse:
        q, k, v, log_f = [jnp.flip(t, axis=2) for t in (q, k, v, log_f)]
    o, s_fin = hgrn2_chunk_scan(q, k, v, log_f, s0)
    if reverse:
        o = jnp.flip(o, axis=2)
    return o, s_fin


def to_heads(t):
    b, l, _ = t.shape
    return t.reshape(b, l, HGRN_HEADS, HGRN_HEAD_DIM).transpose(0, 2, 1, 3)


def hgrn2_mixer(p_ctx, p_lat, lb_l, norm_w_l, rows, need_ctx_out):
    def prep(p, reorder):
        p = p.astype(jnp.float32)
        q, zf, zb, v, gate = [p[..., i * HGRN_WIDTH:(i + 1) * HGRN_WIDTH] for i in range(5)]
        if reorder:
            q, zf, zb, v = [raster_to_colmajor(t, rows) for t in (q, zf, zb, v)]
        return to_heads(jax.nn.silu(q)), (to_heads(zf), to_heads(zb)), to_heads(v), gate

    qc, zc, vc, gate_c = prep(p_ctx, False)
    ql, zl, vl, gate_l = prep(p_lat, True)
    s_zero = jnp.zeros((qc.shape[0], HGRN_HEADS, HGRN_HEAD_DIM, HGRN_HEAD_DIM), jnp.float32)
    o_lat_dirs = []
    o_ctx_dirs = []
    for d in range(2):
        reverse = d == 1
        lb = lb_l[d].reshape(HGRN_HEADS, 1, HGRN_HEAD_DIM)
        oc, s_ctx = hgrn2_direction(qc, zc[d], vc, lb, s_zero, reverse)
        ol, _ = hgrn2_direction(ql, zl[d], vl, lb, s_ctx, reverse)
        o_lat_dirs.append(ol)
        o_ctx_dirs.append(oc)

    def readout(o, gate, reorder):
        b, _, l, _ = o.shape
        o = rmsnorm(o.transpose(0, 2, 1, 3), norm_w_l).reshape(b, l, HGRN_WIDTH)
        if reorder:
            o = colmajor_to_raster(o, rows)
        return o * jax.nn.silu(gate)

    out_lat = readout(o_lat_dirs[0] + o_lat_dirs[1], gate_l, True)
    out_ctx = readout(o_ctx_dirs[0] + o_ctx_dirs[1], gate_c, False) if need_ctx_out else None
    return out_ctx, out_lat


def setup_inputs(seed: int = 0) -> dict:
    key = jax.random.key(seed)
    ks = jax.random.split(key, 25)
    f32 = jnp.float32

    def nrm(k, shape, scale):
        return scale * jax.random.normal(k, shape, f32)

    G, P, H = S5_GROUPS, S5_STATE, S5_GROUP
    lam_im_init = jnp.pi * jnp.arange(P, dtype=f32)
    return {
        'x': nrm(ks[0], (BATCH, SEQ, D_MODEL), 1.0),
        'c': nrm(ks[1], (BATCH, D_MODEL), 1.0),
        'ctx': nrm(ks[2], (BATCH, CTX_LEN, D_MODEL), 1.0),
        'c_ctx': nrm(ks[3], (D_MODEL,), 1.0),
        'w_ada': nrm(ks[4], (DEPTH, D_MODEL, N_MOD * D_MODEL), 0.5 * D_MODEL ** -0.5),
        'b_ada': nrm(ks[5], (DEPTH, N_MOD * D_MODEL), 0.01),
        'norm_w': 1.0 + nrm(ks[6], (DEPTH, 3, D_MODEL), 0.05),
        'ffn_w_gate': nrm(ks[7], (DEPTH, 2, D_MODEL, D_FF), D_MODEL ** -0.5),
        'ffn_w_up': nrm(ks[8], (DEPTH, 2, D_MODEL, D_FF), D_MODEL ** -0.5),
        'ffn_w_down': nrm(ks[9], (DEPTH, 2, D_FF, D_MODEL), D_FF ** -0.5),
        'w_in': nrm(ks[10], (DEPTH, D_MODEL, IN_COLS), D_MODEL ** -0.5),
        'w_out': nrm(ks[11], (DEPTH, MIX_WIDTH, D_MODEL), MIX_WIDTH ** -0.5),
        's5_lambda_re': -0.5 + nrm(ks[12], (DEPTH, 2, G, P), 0.01),
        's5_lambda_im': lam_im_init + nrm(ks[13], (DEPTH, 2, G, P), 0.01),
        's5_log_step': jax.random.uniform(ks[14], (DEPTH, 2, G), f32, math.log(DT_MIN), math.log(DT_MAX)),
        's5_b_re': nrm(ks[15], (DEPTH, 2, G, P, H), (2 * H) ** -0.5),
        's5_b_im': nrm(ks[16], (DEPTH, 2, G, P, H), (2 * H) ** -0.5),
        's5_c_re': nrm(ks[17], (DEPTH, 2, G, H, P), P ** -0.5),
        's5_c_im': nrm(ks[18], (DEPTH, 2, G, H, P), P ** -0.5),
        's5_d': nrm(ks[19], (DEPTH, S5_WIDTH), 1.0),
        's5_w_glu': nrm(ks[20], (DEPTH, S5_WIDTH, S5_WIDTH), S5_WIDTH ** -0.5),
        's5_b_glu': nrm(ks[21], (DEPTH, S5_WIDTH), 0.01),
        'hgrn_lower_bounds': nrm(ks[22], (DEPTH, 2, HGRN_WIDTH), 0.1),
        'hgrn_norm_w': 1.0 + nrm(ks[23], (DEPTH, HGRN_HEAD_DIM), 0.05),
        'final_norm_w': 1.0 + nrm(ks[24], (D_MODEL,), 0.05),
    }


def reference(x, c, ctx, c_ctx, w_ada, b_ada, norm_w, ffn_w_gate, ffn_w_up, ffn_w_down,
              w_in, w_out, s5_lambda_re, s5_lambda_im, s5_log_step, s5_b_re, s5_b_im,
              s5_c_re, s5_c_im, s5_d, s5_w_glu, s5_b_glu, hgrn_lower_bounds, hgrn_norm_w,
              final_norm_w):
    batch, seq_len, _ = x.shape
    rows = seq_len // GRID_W
    lb_soft = jax.nn.softmax(hgrn_lower_bounds.astype(jnp.float32), axis=0)
    lb_all = jnp.cumsum(lb_soft, axis=0) - lb_soft[0]
    for l in range(DEPTH):
        last = l == DEPTH - 1
        mod_lat = (jax.nn.silu(c) @ w_ada[l] + b_ada[l]).reshape(batch, 1, N_MOD, D_MODEL)
        mod_ctx = (jax.nn.silu(c_ctx) @ w_ada[l] + b_ada[l]).reshape(N_MOD, D_MODEL)

        x = ffn_half_step(x, mod_lat, norm_w[l, 0], ffn_w_gate[l, 0], ffn_w_up[l, 0], ffn_w_down[l, 0], 0)
        ctx = ffn_half_step(ctx, mod_ctx, norm_w[l, 0], ffn_w_gate[l, 0], ffn_w_up[l, 0], ffn_w_down[l, 0], 0)

        h_lat = modulate(rmsnorm(x, norm_w[l, 1]), mod_lat, 3, 4)
        h_ctx = modulate(rmsnorm(ctx, norm_w[l, 1]), mod_ctx, 3, 4)
        p_lat = h_lat @ w_in[l]
        p_ctx = h_ctx @ w_in[l]
        s5_ctx, s5_lat = s5_mixer(p_ctx[..., :S5_WIDTH], p_lat[..., :S5_WIDTH],
                                  s5_lambda_re[l], s5_lambda_im[l], s5_log_step[l],
                                  s5_b_re[l], s5_b_im[l], s5_c_re[l], s5_c_im[l],
                                  s5_d[l], s5_w_glu[l], s5_b_glu[l], not last)
        hg_ctx, hg_lat = hgrn2_mixer(p_ctx[..., S5_WIDTH:], p_lat[..., S5_WIDTH:],
                                     lb_all[l], hgrn_norm_w[l], rows, not last)
        y_lat = jnp.concatenate([s5_lat, hg_lat], axis=-1) @ w_out[l]
        x = x + mod_lat[..., 5, :] * y_lat
        if not last:
            y_ctx = jnp.concatenate([s5_ctx, hg_ctx], axis=-1) @ w_out[l]
            ctx = ctx + mod_ctx[..., 5, :] * y_ctx

        x = ffn_half_step(x, mod_lat, norm_w[l, 2], ffn_w_gate[l, 1], ffn_w_up[l, 1], ffn_w_down[l, 1], 6)
        if not last:
            ctx = ffn_half_step(ctx, mod_ctx, norm_w[l, 2], ffn_w_gate[l, 1], ffn_w_up[l, 1], ffn_w_down[l, 1], 6)
    return rmsnorm(x, final_norm_w)
```

```python
import contextlib
import math
from collections import deque

import numpy as np
import concourse.bass as bass
import concourse.mybir as mybir
from concourse.bass_utils import run_bass_kernel_spmd

F32 = mybir.dt.float32
BF16 = mybir.dt.bfloat16
I32 = mybir.dt.int32
ALU = mybir.AluOpType
AF = mybir.ActivationFunctionType

D = 2048
NC_ = 16
FF = 5504
NFF = 43
NLAT = 4096
NCTX = 256
T = NLAT + NCTX
DEPTH = 2
EPS = 1e-6
INC = 6144
GRID = 64

ENGS = ("tensor", "vector", "scalar", "gpsimd", "sync")
SEM_ROLL = 30000
NO_SELF_SYNC = ("tensor",)


class Buf:
    __slots__ = ("name", "w", "r")

    def __init__(self, name=""):
        self.name = name
        self.w = None
        self.r = {}


class K:
    def __init__(self, nc, stack, n_dma_sems=32):
        self.nc = nc
        self.stack = stack
        self.q = {e: [] for e in ENGS}
        self.sem = {}
        self.cnt = {}
        self.waited = {e: {} for e in ENGS}
        self.nsem = 0
        self.sem_owner = {}
        self.no_self_sync = set(NO_SELF_SYNC)
        for e in ("tensor", "vector", "scalar", "gpsimd"):
            self._new_eng_sem(e)
        self.dma_sems = []
        for i in range(n_dma_sems):
            s = stack.enter_context(nc.semaphore(f"dma{i}"))
            self.dma_sems.append([s, 0])
        self.dma_rr = 0
        self.n_instr = 0

    def _new_eng_sem(self, e):
        s = self.stack.enter_context(self.nc.semaphore(f"s_{e}_{self.nsem}"))
        self.nsem += 1
        self.sem[e] = s
        self.cnt[e] = 0
        self.sem_owner[id(s)] = e

    def _collect(self, reads, writes):
        evs = []
        for b in reads:
            if b.w is not None:
                evs.append(b.w)
        for b in writes:
            if b.w is not None:
                evs.append(b.w)
            evs.extend(b.r.values())
        return evs

    def _waits_for(self, eng, evs):
        best = {}
        for (s, v) in evs:
            kk = id(s)
            if eng in self.no_self_sync and self.sem_owner.get(kk) == eng:
                continue
            if kk not in best or best[kk][1] < v:
                best[kk] = (s, v)
        out = []
        wd = self.waited[eng]
        for kk, (s, v) in best.items():
            if wd.get(kk, -1) >= v:
                continue
            wd[kk] = v
            out.append((s, v))
        return out

    def _update(self, ev, reads, writes):
        for b in writes:
            b.w = ev
            b.r = {}
        for b in reads:
            b.r[id(ev[0])] = ev

    def op(self, eng, fn, reads=(), writes=(), extra=()):
        evs = self._collect(reads, writes) + list(extra)
        waits = self._waits_for(eng, evs)
        if self.cnt[eng] >= SEM_ROLL:
            self._new_eng_sem(eng)
        self.cnt[eng] += 1
        ev = (self.sem[eng], self.cnt[eng])
        self.q[eng].append((waits, fn, ev[0], 1))
        self._update(ev, reads, writes)
        self.n_instr += 1
        return ev

    def dma(self, eng, fn, reads=(), writes=(), extra=()):
        evs = self._collect(reads, writes) + list(extra)
        slot = self.dma_sems[self.dma_rr]
        self.dma_rr = (self.dma_rr + 1) % len(self.dma_sems)
        if slot[1] > 0:
            evs.append((slot[0], slot[1]))
        waits = self._waits_for(eng, evs)
        slot[1] += 16
        ev = (slot[0], slot[1])
        self.q[eng].append((waits, fn, ev[0], 16))
        self._update(ev, reads, writes)
        self.n_instr += 1
        return ev

    def all_events(self):
        evs = []
        for e in ("tensor", "vector", "scalar", "gpsimd"):
            if self.cnt[e] > 0:
                evs.append((self.sem[e], self.cnt[e]))
        for s, v in self.dma_sems:
            if v > 0:
                evs.append((s, v))
        return evs

    def barrier(self):
        evs = self.all_events()
        for e in ENGS:
            saved = self.no_self_sync
            self.no_self_sync = set()
            waits = self._waits_for(e, evs)
            self.no_self_sync = saved
            if waits:
                self.q[e].append((waits, None, None, 0))

    def finish(self):
        nc = self.nc
        self.barrier()
        q = self.q

        def run(e, items):
            for (waits, fn, sem, inc) in items:
                for (s, v) in waits:
                    e.wait_ge(s, v)
                if fn is None:
                    continue
                ins = fn(e)
                ins.then_inc(sem, inc)

        with nc.Block() as block:
            @block.sync
            def _(e):
                run(e, q["sync"])

            @block.tensor
            def _(e):
                run(e, q["tensor"])

            @block.vector
            def _(e):
                run(e, q["vector"])

            @block.scalar
            def _(e):
                run(e, q["scalar"])

            @block.gpsimd
            def _(e):
                run(e, q["gpsimd"])


class Ring:
    def __init__(self, tiles):
        self.tiles = tiles
        self.bufs = [Buf() for _ in tiles]
        self.i = 0

    def next(self):
        i = self.i
        self.i = (i + 1) % len(self.tiles)
        return self.tiles[i], self.bufs[i]


def bc_mid(ap2, n):
    return ap2.unsqueeze(1).to_broadcast([ap2.shape[0], n, ap2.shape[1]])


def bc_last(ap2, n):
    return ap2.unsqueeze(2).to_broadcast([ap2.shape[0], ap2.shape[1], n])


class Prog:
    def __init__(self, n_layers=DEPTH, stage="full"):
        self.n_layers = n_layers
        self.stage = stage
        self.nc = bass.Bass("TRN2", target_bir_lowering=False)
        self.st = contextlib.ExitStack()

    def dram_in(self, name, shape):
        return self.nc.dram_tensor(name, list(shape), F32, kind="ExternalInput").ap()

    def sb(self, stack, name, shape, dtype):
        self._uid = getattr(self, "_uid", 0) + 1
        return stack.enter_context(self.nc.sbuf_tensor(f"{name}_{self._uid}", list(shape), dtype))

    def make_ring(self, stack, name, shape, dtype, n):
        return Ring([self.sb(stack, f"{name}{i}", shape, dtype) for i in range(n)])

    def xbuf(self, c, tok):
        return self.xT_bufs[c][tok // 512]

    def wload(self, src):
        t, b = self.wring.next()
        kt, cols = src.shape[1], src.shape[2]
        dst = t[:, 0:kt * cols].rearrange("p (k c) -> p k c", c=cols)
        self.k.dma("gpsimd", lambda e: e.dma_start(out=dst, in_=src), writes=[b])
        return dst, b

    class WStream:
        def __init__(self, prog, srcs, lookahead):
            self.p = prog
            self.srcs = srcs
            self.n = 0
            self.loaded = deque()
            self.la = lookahead

        def get(self):
            while self.n < len(self.srcs) and len(self.loaded) < self.la + 1:
                self.loaded.append(self.p.wload(self.srcs[self.n]))
                self.n += 1
            return self.loaded.popleft()

    def build(self):
        nc = self.nc
        st = self.st
        L = DEPTH
        self.xin = self.dram_in("xin", [T, D])
        self.cc = self.dram_in("cc", [2, D])
        self.w_ada = self.dram_in("w_ada", [L, D, 9 * D])
        self.b_ada = self.dram_in("b_ada", [L, 9 * D])
        self.norm_w = self.dram_in("norm_w", [L, 3, D])
        self.wg = self.dram_in("ffn_w_gate", [L, 2, D, FF])
        self.wu = self.dram_in("ffn_w_up", [L, 2, D, FF])
        self.wd = self.dram_in("ffn_w_down", [L, 2, FF, D])
        self.w_in = self.dram_in("w_in", [L, D, INC])
        self.w_out = self.dram_in("w_out", [L, D, D])
        self.s5_lre = self.dram_in("s5_lambda_re", [L, 2, 64, 64])
        self.s5_lim = self.dram_in("s5_lambda_im", [L, 2, 64, 64])
        self.s5_ls = self.dram_in("s5_log_step", [L, 2, 64])
        self.s5_bre = self.dram_in("s5_b_re", [L, 2, 64, 64, 16])
        self.s5_bim = self.dram_in("s5_b_im", [L, 2, 64, 64, 16])
        self.s5_cre = self.dram_in("s5_c_re", [L, 2, 64, 16, 64])
        self.s5_cim = self.dram_in("s5_c_im", [L, 2, 64, 16, 64])
        self.s5_d = self.dram_in("s5_d", [L, 1024])
        self.s5_wglu = self.dram_in("s5_w_glu", [L, 1024, 1024])
        self.s5_bglu = self.dram_in("s5_b_glu", [L, 1024])
        self.hg_lb = self.dram_in("hgrn_lower_bounds", [L, 2, 1024])
        self.hg_nw = self.dram_in("hgrn_norm_w", [L, 128])
        self.fin_w = self.dram_in("final_norm_w", [D])
        self.c_ident = self.dram_in("c_ident", [128, 128])
        self.c_masks = self.dram_in("c_masks", [5, 128, 128])
        self.c_aux = self.dram_in("c_aux", [128, 8 + 257])
        self.c_iota = self.dram_in("c_iota", [128, 513])
        self.Y = nc.dram_tensor("Y", [NLAT, D], F32, kind="ExternalOutput").ap()
        self.xT = nc.dram_tensor("xT_s", [NC_, 128, T], F32).ap()
        self.pT = nc.dram_tensor("pT_s", [6, 8, 128, T], F32).ap()
        self.vtm = nc.dram_tensor("vtm_s", [T, 1024], F32).ap()
        self.gS5 = nc.dram_tensor("gs5_s", [8, 128, T], F32).ap()
        if self.stage in ("hgrn", "s5"):
            self.mixT = nc.dram_tensor("mixT_s", [NC_, 128, T], BF16, kind="ExternalOutput").ap()
        else:
            self.mixT = nc.dram_tensor("mixT_s", [NC_, 128, T], BF16).ap()
        self.xT_bufs = [[Buf() for _ in range(9)] for _ in range(NC_)]
        self.pT_buf = [[Buf() for _ in range(8)] for _ in range(6)]
        self.vtm_buf = Buf()
        self.gS5_buf = [Buf() for _ in range(8)]
        self.mix_buf = [Buf() for _ in range(NC_)]
        self.BTs = nc.dram_tensor("BTs_s", [2, 2, 128, 8, 512], BF16).ap()
        self.CTs = nc.dram_tensor("CTs_s", [2, 2, 128, 32, 128], BF16).ap()

        self.k = K(nc, st)
        k = self.k
        self.ps = [st.enter_context(nc.psum_tensor(f"ps{i}", [128, 512], F32)) for i in range(8)]
        self.psb = [Buf() for _ in range(8)]
        self.ident = self.sb(st, "ident", [128, 128], F32)
        self.onesD = self.sb(st, "onesD", [128, 128], F32)
        self.ones128 = self.sb(st, "ones128", [128, 128], F32)
        self.masks = self.sb(st, "masks", [128, 5, 128], F32)
        self.aux = self.sb(st, "aux", [128, 8 + 257], F32)
        self.modT = self.sb(st, "modT", [128, 2, 144], F32)
        self.Amod = self.sb(st, "Amod", [128, 2, 3, 16], F32)
        self.Gmod = self.sb(st, "Gmod", [128, 2, 3, 16], F32)
        self.wfin = self.sb(st, "wfin", [128, 16], F32)
        self.b_const = Buf()
        self.b_mod = Buf()
        self.ident_bf = self.sb(st, "identbf", [128, 128], BF16)
        self.onesD_bf = self.sb(st, "onesDbf", [128, 128], BF16)

        with nc.allow_non_contiguous_dma("small parameter vectors are laid out feature-on-partition"):
            self.emit()
            k.finish()
        st.close()
        return nc

    def emit(self):
        k = self.k
        k.dma("sync", lambda e: e.dma_start(out=self.ident[:], in_=self.c_ident[:, :]), writes=[self.b_const])
        k.dma("sync", lambda e: e.dma_start(out=self.masks[:], in_=self.c_masks.rearrange("m p f -> p m f")),
              writes=[self.b_const])
        k.dma("sync", lambda e: e.dma_start(out=self.wfin[:], in_=self.fin_w.rearrange("(c p) -> p c", p=128)),
              writes=[self.b_const])
        k.dma("sync", lambda e: e.dma_start(out=self.aux[:], in_=self.c_aux[:, :]), writes=[self.b_const])
        k.op("vector", lambda e: e.memset(self.onesD[:], 1.0 / D), writes=[self.b_const])
        k.op("vector", lambda e: e.memset(self.ones128[:], 1.0 / 128), writes=[self.b_const])
        k.op("vector", lambda e: e.tensor_copy(out=self.ident_bf[:], in_=self.ident[:]), reads=[self.b_const], writes=[self.b_const])
        k.op("vector", lambda e: e.tensor_copy(out=self.onesD_bf[:], in_=self.onesD[:]), reads=[self.b_const], writes=[self.b_const])
        self.input_phase()
        for l in range(self.n_layers):
            last = (l == DEPTH - 1)
            self.mods_phase(l)
            self.ffn_phase(l, 0, 0, include_ctx=True)
            if self.stage == "ffn1":
                break
            self.inproj_phase(l)
            self.hgrn_phase(l, need_ctx=not last)
            if self.stage == "hgrn":
                break
            self.s5_phase(l, need_ctx=not last)
            self.glu_phase(l, need_ctx=not last)
            if self.stage == "s5":
                break
            self.outproj_phase(l, include_ctx=not last)
            self.ffn_phase(l, 1, 2, include_ctx=not last)
        self.output_phase(apply_norm=(self.stage == "full"))

    def supertiles(self, include_ctx, ts=1024):
        out = [(t0, ts, 0) for t0 in range(0, NLAT, ts)]
        if include_ctx:
            out.append((NLAT, NCTX, 1))
        return out

    def input_phase(self):
        k = self.k
        with contextlib.ExitStack() as ph:
            xtok = self.make_ring(ph, "xtok", [128, D], F32, 2)
            xst = self.make_ring(ph, "xsti", [128, 16, 128], F32, 2)
            xv = self.xT.rearrange("c p t -> p c t")
            ei = 0
            for ti in range(T // 128):
                tok = ti * 128
                xt, bxt = xtok.next()
                k.dma("sync", lambda e, xt=xt, tok=tok: e.dma_start(out=xt[:], in_=self.xin[tok:tok + 128, :]), writes=[bxt])
                xs, bxs = xst.next()
                for g in range(4):
                    pb = 4 + g % 4
                    for j in range(4):
                        c = g * 4 + j
                        k.op("tensor", lambda e, pb=pb, j=j, c=c, xt=xt: e.transpose(
                            out=self.ps[pb][:, j * 128:(j + 1) * 128], in_=xt[:, c * 128:(c + 1) * 128], identity=self.ident[:]),
                            reads=[bxt, self.b_const], writes=[self.psb[pb]])
                    dst = xs[:, g * 4:(g + 1) * 4, :]
                    src = self.ps[pb][:, :].rearrange("p (j t) -> p j t", t=128)
                    if ei % 2 == 0:
                        k.op("scalar", lambda e, dst=dst, src=src: e.copy(out=dst, in_=src), reads=[self.psb[pb]], writes=[bxs])
                    else:
                        k.op("vector", lambda e, dst=dst, src=src: e.tensor_copy(out=dst, in_=src), reads=[self.psb[pb]], writes=[bxs])
                    ei += 1
                k.dma("sync", lambda e, xs=xs, tok=tok: e.dma_start(out=xv[:, :, tok:tok + 128], in_=xs[:]),
                      reads=[bxs], writes=[self.xbuf(c, tok) for c in range(NC_)])
            k.barrier()

    def mods_phase(self, l):
        k = self.k
        with contextlib.ExitStack() as ph:
            self.wring = self.make_ring(ph, "wr", [128, 4096], BF16, 5)
            ccs = self.sb(ph, "ccs", [128, 2, 16], F32)
            scb = self.sb(ph, "scb", [128, 2, 16], BF16)
            bada = self.sb(ph, "bada", [128, 144], F32)
            nwt = self.sb(ph, "nwt", [128, 3, 16], F32)
            b_cc, b_sc, b_ba, b_nw = Buf(), Buf(), Buf(), Buf()
            k.dma("sync", lambda e: e.dma_start(out=ccs[:], in_=self.cc.rearrange("s (kt p) -> p s kt", p=128)), writes=[b_cc])
            k.op("scalar", lambda e: e.activation(out=scb[:], in_=ccs[:], func=AF.Silu), reads=[b_cc], writes=[b_sc])
            k.dma("sync", lambda e: e.dma_start(out=bada[:], in_=self.b_ada[l].rearrange("(j p) -> p j", p=128)), writes=[b_ba])
            k.dma("sync", lambda e: e.dma_start(out=nwt[:], in_=self.norm_w[l].rearrange("i (c p) -> p i c", p=128)), writes=[b_nw])
            wv = self.w_ada[l].rearrange("(kt p) c -> p kt c", p=128)
            stream = Prog.WStream(self, [wv[:, :, jb * 256:(jb + 1) * 256] for jb in range(72)], 3)
            pm = self.ps[7]
            for jb in range(72):
                wt, wb = stream.get()
                for sub in range(2):
                    j = jb * 2 + sub
                    for kt in range(16):
                        k.op("tensor", lambda e, wt=wt, sub=sub, j=j, kt=kt: e.matmul(
                            pm[:, 2 * j:2 * j + 2], lhsT=wt[:, kt, sub * 128:(sub + 1) * 128], rhs=scb[:, :, kt],
                            start=(kt == 0), stop=(kt == 15)), reads=[wb, b_sc], writes=[self.psb[7]])
            for s in range(2):
                k.op("vector", lambda e, s=s: e.tensor_tensor(out=self.modT[:, s, :], in0=pm[:, s:288:2], in1=bada[:], op=ALU.add),
                     reads=[self.psb[7], b_ba], writes=[self.b_mod])
            for s in range(2):
                for i3 in range(3):
                    sc_ = self.modT[:, s, (3 * i3 + 1) * 16:(3 * i3 + 2) * 16]
                    gt_ = self.modT[:, s, (3 * i3 + 2) * 16:(3 * i3 + 3) * 16]
                    k.op("vector", lambda e, s=s, i3=i3, sc_=sc_: e.scalar_tensor_tensor(
                        out=self.Amod[:, s, i3, :], in0=sc_, scalar=1.0, in1=nwt[:, i3, :], op0=ALU.add, op1=ALU.mult),
                        reads=[b_nw, self.b_mod], writes=[self.b_mod])
                    k.op("vector", lambda e, s=s, i3=i3, gt_=gt_: e.tensor_scalar(
                        out=self.Gmod[:, s, i3, :], in0=gt_, scalar1=(1.0 if i3 == 1 else 0.5), scalar2=None, op0=ALU.mult),
                        reads=[self.b_mod], writes=[self.b_mod])
            k.barrier()

    def norm_piece(self, tok, xring, sqring, rsring, dst, bdst, A_ap, B_ap, psn=0):
        k = self.k
        xv = self.xT.rearrange("c p t -> p c t")
        xs, bx = xring.next()
        k.dma("sync", lambda e: e.dma_start(out=xs[:], in_=xv[:, :, tok:tok + 128]),
              reads=[self.xbuf(c, tok) for c in range(NC_)], writes=[bx])
        sq, bs = sqring.next()
        k.op("scalar", lambda e: e.activation(out=sq[:], in_=xs[:], func=AF.Square), reads=[bx], writes=[bs])
        pn = self.ps[psn]
        for c in range(NC_):
            k.op("tensor", lambda e, c=c: e.matmul(pn[:, 0:128], lhsT=self.onesD_bf[:], rhs=sq[:, c, :], start=(c == 0), stop=(c == NC_ - 1)),
                 reads=[bs, self.b_const], writes=[self.psb[psn]])
        rs, brs = rsring.next()
        k.op("vector", lambda e: e.tensor_scalar(out=rs[:], in0=pn[:, 0:128], scalar1=EPS, scalar2=None, op0=ALU.add),
             reads=[self.psb[psn]], writes=[brs])
        k.op("scalar", lambda e: e.activation(out=rs[:], in_=rs[:], func=AF.Sqrt), reads=[brs], writes=[brs])
        k.op("vector", lambda e: e.reciprocal(out=rs[:], in_=rs[:]), reads=[brs], writes=[brs])
        k.op("vector", lambda e: e.tensor_tensor(out=xs[:], in0=xs[:], in1=bc_mid(rs[:], NC_), op=ALU.mult),
             reads=[brs, bx], writes=[bx])
        if B_ap is None:
            k.op("vector", lambda e: e.tensor_tensor(out=dst, in0=xs[:], in1=bc_last(A_ap, 128), op=ALU.mult),
                 reads=[bx, self.b_mod, self.b_const], writes=[bdst])
        else:
            k.op("vector", lambda e: e.tensor_tensor(out=xs[:], in0=xs[:], in1=bc_last(A_ap, 128), op=ALU.mult),
                 reads=[bx, self.b_mod], writes=[bx])
            k.op("vector", lambda e: e.tensor_tensor(out=dst, in0=xs[:], in1=bc_last(B_ap, 128), op=ALU.add),
                 reads=[bx, self.b_mod], writes=[bdst])

    def ffn_phase(self, l, fi, i3, include_ctx):
        k = self.k
        with contextlib.ExitStack() as ph:
            self.wring = self.make_ring(ph, "wr", [128, 4096], BF16, 5)
            hT = self.sb(ph, "hT", [128, 16, 1024], BF16)
            a = self.sb(ph, "aT", [128, NFF, 1024], BF16)
            b_h = [Buf(), Buf()]
            b_a = [Buf(), Buf()]
            xring = self.make_ring(ph, "fx", [128, 16, 128], F32, 2)
            sqring = self.make_ring(ph, "fsq", [128, 16, 128], BF16, 2)
            rsring = self.make_ring(ph, "frs", [128, 128], F32, 2)
            slring = self.make_ring(ph, "fsl", [128, 512], F32, 2)
            xrring = self.make_ring(ph, "fxr", [128, 512], F32, 3)
            wgv = self.wg[l, fi].rearrange("(kt p) c -> p kt c", p=128)
            wuv = self.wu[l, fi].rearrange("(kt p) c -> p kt c", p=128)
            wdv = self.wd[l, fi].rearrange("(kt p) c -> p kt c", p=128)
            psg = Ring([1, 2]); psu = Ring([3, 4]); psd = Ring([5, 6])
            for (t0, ts, s) in self.supertiles(include_ctx):
                A_ap = self.Amod[:, s, i3, :]
                B_ap = self.modT[:, s, (3 * i3) * 16:(3 * i3 + 1) * 16]
                nh = max(1, ts // 512)
                n = min(512, ts)
                for pc in range(ts // 128):
                    self.norm_piece(t0 + pc * 128, xring, sqring, rsring, hT[:, :, pc * 128:(pc + 1) * 128], b_h[(pc * 128) // 512],
                                    A_ap, B_ap)
                srcs = []
                for jb in range(22):
                    cols = 256 if jb < 21 else 128
                    srcs.append(wgv[:, :, jb * 256:jb * 256 + cols])
                    srcs.append(wuv[:, :, jb * 256:jb * 256 + cols])
                for m in range(16):
                    srcs.append(wdv[:, 0:22, m * 128:(m + 1) * 128])
                    srcs.append(wdv[:, 22:43, m * 128:(m + 1) * 128])
                stream = Prog.WStream(self, srcs, 3)
                for jb in range(22):
                    cols = 256 if jb < 21 else 128
                    gt, gb = stream.get()
                    ut, ub = stream.get()
                    for sub in range(cols // 128):
                        j = jb * 2 + sub
                        for hf in range(nh):
                            tsl = slice(hf * 512, hf * 512 + n)
                            pgi, _ = psg.next(); pui, _ = psu.next()
                            pg, pu = self.ps[pgi], self.ps[pui]
                            for kt in range(16):
                                k.op("tensor", lambda e, pg=pg, gt=gt, kt=kt, sub=sub, tsl=tsl, n=n: e.matmul(
                                    pg[:, 0:n], lhsT=gt[:, kt, sub * 128:(sub + 1) * 128], rhs=hT[:, kt, tsl],
                                    start=(kt == 0), stop=(kt == 15)), reads=[gb, b_h[hf]], writes=[self.psb[pgi]])
                            for kt in range(16):
                                k.op("tensor", lambda e, pu=pu, ut=ut, kt=kt, sub=sub, tsl=tsl, n=n: e.matmul(
                                    pu[:, 0:n], lhsT=ut[:, kt, sub * 128:(sub + 1) * 128], rhs=hT[:, kt, tsl],
                                    start=(kt == 0), stop=(kt == 15)), reads=[ub, b_h[hf]], writes=[self.psb[pui]])
                            sl, bsl = slring.next()
                            k.op("scalar", lambda e, sl=sl, pg=pg, n=n: e.activation(out=sl[:, 0:n], in_=pg[:, 0:n], func=AF.Silu),
                                 reads=[self.psb[pgi]], writes=[bsl])
                            k.op("vector", lambda e, sl=sl, pu=pu, j=j, tsl=tsl, n=n: e.tensor_tensor(
                                out=a[:, j, tsl], in0=sl[:, 0:n], in1=pu[:, 0:n], op=ALU.mult),
                                reads=[bsl, self.psb[pui]], writes=[b_a[hf]])
                for m in range(16):
                    w0, b0 = stream.get()
                    w1, b1 = stream.get()
                    for hf in range(nh):
                        tsl = slice(hf * 512, hf * 512 + n)
                        tok0 = t0 + hf * 512
                        xr, bxr = xrring.next()
                        k.dma("sync", lambda e, xr=xr, m=m, tok0=tok0, n=n: e.dma_start(out=xr[:, 0:n], in_=self.xT[m, :, tok0:tok0 + n]),
                              reads=[self.xbuf(m, tok0)], writes=[bxr])
                        pdi, _ = psd.next()
                        pd = self.ps[pdi]
                        for kt in range(NFF):
                            wt = w0[:, kt, :] if kt < 22 else w1[:, kt - 22, :]
                            k.op("tensor", lambda e, pd=pd, wt=wt, kt=kt, tsl=tsl, n=n: e.matmul(
                                pd[:, 0:n], lhsT=wt, rhs=a[:, kt, tsl], start=(kt == 0), stop=(kt == NFF - 1)),
                                reads=[b0, b1, b_a[hf]], writes=[self.psb[pdi]])
                        k.op("vector", lambda e, xr=xr, pd=pd, m=m, s=s, n=n: e.scalar_tensor_tensor(
                            out=xr[:, 0:n], in0=pd[:, 0:n], scalar=self.Gmod[:, s, i3, m:m + 1], in1=xr[:, 0:n],
                            op0=ALU.mult, op1=ALU.add), reads=[self.psb[pdi], bxr, self.b_mod], writes=[bxr])
                        k.dma("sync", lambda e, xr=xr, m=m, tok0=tok0, n=n: e.dma_start(out=self.xT[m, :, tok0:tok0 + n], in_=xr[:, 0:n]),
                              reads=[bxr], writes=[self.xbuf(m, tok0)])
            k.barrier()

    def output_phase(self, apply_norm):
        k = self.k
        with contextlib.ExitStack() as ph:
            xring = self.make_ring(ph, "ox", [128, 16, 128], F32, 2)
            sqring = self.make_ring(ph, "osq", [128, 16, 128], BF16, 2)
            rsring = self.make_ring(ph, "ors", [128, 128], F32, 2)
            yst = self.make_ring(ph, "oy", [128, 16, 128], F32, 2)
            ytok = self.make_ring(ph, "oyt", [128, D], F32, 2)
            xv = self.xT.rearrange("c p t -> p c t")
            ei = 0
            for ti in range(NLAT // 128):
                tok = ti * 128
                ys, bys = yst.next()
                if apply_norm:
                    self.norm_piece(tok, xring, sqring, rsring, ys[:], bys, self.wfin[:], None)
                else:
                    k.dma("sync", lambda e, ys=ys, tok=tok: e.dma_start(out=ys[:], in_=xv[:, :, tok:tok + 128]),
                          reads=[self.xbuf(c, tok) for c in range(NC_)], writes=[bys])
                yt, byt = ytok.next()
                for g in range(4):
                    pb = 4 + g % 4
                    for j in range(4):
                        c = g * 4 + j
                        k.op("tensor", lambda e, pb=pb, j=j, c=c, ys=ys: e.transpose(
                            out=self.ps[pb][:, j * 128:(j + 1) * 128], in_=ys[:, c, :], identity=self.ident[:]),
                            reads=[bys, self.b_const], writes=[self.psb[pb]])
                    dst = yt[:, g * 512:(g + 1) * 512]
                    src = self.ps[pb][:, :]
                    if ei % 2 == 0:
                        k.op("scalar", lambda e, dst=dst, src=src: e.copy(out=dst, in_=src), reads=[self.psb[pb]], writes=[byt])
                    else:
                        k.op("vector", lambda e, dst=dst, src=src: e.tensor_copy(out=dst, in_=src), reads=[self.psb[pb]], writes=[byt])
                    ei += 1
                k.dma("sync", lambda e, yt=yt, tok=tok: e.dma_start(out=self.Y[tok:tok + 128, :], in_=yt[:]), reads=[byt])
            k.barrier()

    def inproj_phase(self, l):
        k = self.k
        with contextlib.ExitStack() as ph:
            self.wring = self.make_ring(ph, "wr", [128, 4096], BF16, 5)
            hT = self.sb(ph, "ihT", [128, 16, 1024], BF16)
            b_h = [Buf(), Buf()]
            xring = self.make_ring(ph, "ix", [128, 16, 128], F32, 2)
            sqring = self.make_ring(ph, "isq", [128, 16, 128], BF16, 2)
            rsring = self.make_ring(ph, "irs", [128, 128], F32, 2)
            evring = self.make_ring(ph, "iev", [128, 512], F32, 4)
            wv = self.w_in[l].rearrange("(kt p) c -> p kt c", p=128)
            psr = Ring([1, 2, 3, 4, 5, 6])
            ei = 0
            for (t0, ts, s) in self.supertiles(True):
                A_ap = self.Amod[:, s, 1, :]
                B_ap = self.modT[:, s, 3 * 16:4 * 16]
                nh = max(1, ts // 512)
                n = min(512, ts)
                for pc in range(ts // 128):
                    self.norm_piece(t0 + pc * 128, xring, sqring, rsring, hT[:, :, pc * 128:(pc + 1) * 128], b_h[(pc * 128) // 512],
                                    A_ap, B_ap)
                stream = Prog.WStream(self, [wv[:, :, bi * 256:(bi + 1) * 256] for bi in range(24)], 3)
                for bi in range(24):
                    wt, wb = stream.get()
                    fam = bi // 4
                    if fam != 4:
                        for sub in range(2):
                            ch = (bi % 4) * 2 + sub
                            for hf in range(nh):
                                tsl = slice(hf * 512, hf * 512 + n)
                                tok0 = t0 + hf * 512
                                pi, _ = psr.next()
                                pp = self.ps[pi]
                                for kt in range(16):
                                    k.op("tensor", lambda e, pp=pp, wt=wt, kt=kt, sub=sub, tsl=tsl, n=n: e.matmul(
                                        pp[:, 0:n], lhsT=wt[:, kt, sub * 128:(sub + 1) * 128], rhs=hT[:, kt, tsl],
                                        start=(kt == 0), stop=(kt == 15)), reads=[wb, b_h[hf]], writes=[self.psb[pi]])
                                ev, bev = evring.next()
                                if fam in (1, 5):
                                    k.op("scalar", lambda e, ev=ev, pp=pp, n=n: e.activation(out=ev[:, 0:n], in_=pp[:, 0:n], func=AF.Silu),
                                         reads=[self.psb[pi]], writes=[bev])
                                elif ei % 2 == 0:
                                    k.op("scalar", lambda e, ev=ev, pp=pp, n=n: e.copy(out=ev[:, 0:n], in_=pp[:, 0:n]),
                                         reads=[self.psb[pi]], writes=[bev])
                                else:
                                    k.op("vector", lambda e, ev=ev, pp=pp, n=n: e.tensor_copy(out=ev[:, 0:n], in_=pp[:, 0:n]),
                                         reads=[self.psb[pi]], writes=[bev])
                                ei += 1
                                k.dma("sync", lambda e, ev=ev, fam=fam, ch=ch, tok0=tok0, n=n: e.dma_start(
                                    out=self.pT[fam, ch, :, tok0:tok0 + n], in_=ev[:, 0:n]), reads=[bev])
                    else:
                        for tt in range(ts // 128):
                            tok0 = t0 + tt * 128
                            pi, _ = psr.next()
                            pp = self.ps[pi]
                            for kt in range(16):
                                k.op("tensor", lambda e, pp=pp, wt=wt, kt=kt, tt=tt: e.matmul(
                                    pp[:, 0:256], lhsT=hT[:, kt, tt * 128:(tt + 1) * 128], rhs=wt[:, kt, :],
                                    start=(kt == 0), stop=(kt == 15)), reads=[wb, b_h[(tt * 128) // 512]], writes=[self.psb[pi]])
                            ev, bev = evring.next()
                            if ei % 2 == 0:
                                k.op("scalar", lambda e, ev=ev, pp=pp: e.copy(out=ev[:, 0:256], in_=pp[:, 0:256]),
                                     reads=[self.psb[pi]], writes=[bev])
                            else:
                                k.op("vector", lambda e, ev=ev, pp=pp: e.tensor_copy(out=ev[:, 0:256], in_=pp[:, 0:256]),
                                     reads=[self.psb[pi]], writes=[bev])
                            ei += 1
                            c0 = (bi % 4) * 256
                            k.dma("sync", lambda e, ev=ev, tok0=tok0, c0=c0: e.dma_start(
                                out=self.vtm[tok0:tok0 + 128, c0:c0 + 256], in_=ev[:, 0:256]), reads=[bev])
            k.barrier()

    def hgrn_phase(self, l, need_ctx):
        k = self.k
        NT = T // 128
        NCH = T // 32
        with contextlib.ExitStack() as ph:
            W = [self.sb(ph, f"hW{i}", [128, T], F32) for i in range(5)]
            bW = [Buf() for _ in range(5)]
            qh = [self.sb(ph, f"hqh{d}", [128, T], BF16) for d in range(2)]
            kt_ = [self.sb(ph, f"hkt{d}", [128, T], BF16) for d in range(2)]
            b_qh = [Buf(), Buf()]
            b_kt = [Buf(), Buf()]
            kh = self.sb(ph, "hkh", [128, T], BF16)
            b_kh = Buf()
            khtm = [self.sb(ph, f"hkhtm{d}", [128, NT, 128], BF16) for d in range(2)]
            b_khtm = [Buf(), Buf()]
            khtm3 = [self.sb(ph, f"hkhtm3{d}", [128, NT, 128], BF16) for d in range(2)]
            V = self.sb(ph, "hV", [128, NT, 128], BF16)
            b_V = Buf()
            dec = [self.sb(ph, f"hdec{d}", [128, NCH], F32) for d in range(2)]
            b_dec = [Buf(), Buf()]
            cm = self.sb(ph, "hcm", [128, T], BF16)
            hb = self.sb(ph, "hhb", [128, 2, 2, 8], F32)
            lbt = self.sb(ph, "hlbt", [128, 2, 8], F32)
            oml = self.sb(ph, "homl", [128, 2, 8], F32)
            nwh = self.sb(ph, "hnwh", [128, 1], F32)
            b_par = Buf()
            S32 = [self.sb(ph, f"hS32{d}", [128, 2, 128], F32) for d in range(2)]
            Sring = [self.sb(ph, f"hSr{d}", [128, 4, 128], BF16) for d in range(2)]
            b_S32 = [[Buf(), Buf()], [Buf(), Buf()]]
            b_Sr = [[Buf() for _ in range(4)] for _ in range(2)]
            psT = self.ps[7][:].bitcast(BF16)

            k.op("vector", lambda e: e.memset(cm[:], 1.0), writes=[b_par])
            k.op("vector", lambda e: e.memset(cm[:, 0:T:32], 0.0), writes=[b_par])
            k.dma("sync", lambda e: e.dma_start(out=hb[:], in_=self.hg_lb.rearrange("l d (h p) -> p l d h", p=128)), writes=[b_par])
            k.dma("sync", lambda e: e.dma_start(out=nwh[:], in_=self.hg_nw[l].rearrange("(p o) -> p o", o=1)), writes=[b_par])
            if l == 0:
                k.op("vector", lambda e: e.memset(lbt[:], 0.0), writes=[b_par])
            else:
                k.op("vector", lambda e: e.tensor_tensor(out=lbt[:], in0=hb[:, 1], in1=hb[:, 0], op=ALU.subtract), reads=[b_par], writes=[b_par])
                k.op("scalar", lambda e: e.activation(out=lbt[:], in_=lbt[:], func=AF.Sigmoid), reads=[b_par], writes=[b_par])
            k.op("vector", lambda e: e.tensor_scalar(out=oml[:], in0=lbt[:], scalar1=-1.0, scalar2=1.0, op0=ALU.mult, op1=ALU.add),
                 reads=[b_par], writes=[b_par])

            def cmv(ap):
                return ap[:, 0:NLAT].rearrange("p (c r) -> p c r", r=GRID)

            def rasv_as_cr(ap):
                return ap[:, 0:NLAT].rearrange("p (r c) -> p c r", c=GRID)

            for hd in range(8):
                vsrc = self.vtm[0:NLAT, hd * 128:(hd + 1) * 128].rearrange("(r ti c2) v -> c2 r ti v", ti=32, c2=2)
                for c2 in range(2):
                    k.dma("gpsimd", lambda e, c2=c2, vsrc=vsrc: e.dma_start(out=V[c2 * 64:(c2 + 1) * 64, 0:32, :], in_=vsrc[c2]), writes=[b_V])
                vsrc2 = self.vtm[NLAT:T, hd * 128:(hd + 1) * 128].rearrange("(ti p) v -> p ti v", p=128)
                k.dma("gpsimd", lambda e, vsrc2=vsrc2: e.dma_start(out=V[:, 32:34, :], in_=vsrc2), writes=[b_V])
                for d in range(2):
                    k.dma("sync", lambda e, hd=hd, d=d: e.dma_start(out=W[4][:], in_=self.pT[2 + d, hd, :, :]), writes=[bW[4]])
                    k.op("scalar", lambda e: e.activation(out=cmv(W[0]), in_=rasv_as_cr(W[4]), func=AF.Sigmoid), reads=[bW[4]], writes=[bW[0]])
                    k.op("scalar", lambda e: e.activation(out=W[0][:, NLAT:T], in_=W[4][:, NLAT:T], func=AF.Sigmoid), reads=[bW[4]], writes=[bW[0]])
                    k.op("vector", lambda e, d=d, hd=hd: e.tensor_scalar(out=W[0][:], in0=W[0][:], scalar1=oml[:, d, hd:hd + 1],
                                                                       scalar2=lbt[:, d, hd:hd + 1], op0=ALU.mult, op1=ALU.add),
                         reads=[b_par, bW[0]], writes=[bW[0]])
                    k.op("gpsimd", lambda e: e.tensor_scalar(out=W[1][:], in0=W[0][:], scalar1=-1.0, scalar2=1.0, op0=ALU.mult, op1=ALU.add),
                         reads=[bW[0]], writes=[bW[1]])
                    k.op("vector", lambda e: e.tensor_scalar(out=W[0][:], in0=W[0][:], scalar1=1e-6, scalar2=None, op0=ALU.max),
                         reads=[bW[0], bW[1]], writes=[bW[0]])
                    k.op("scalar", lambda e: e.activation(out=W[0][:], in_=W[0][:], func=AF.Ln), reads=[bW[0]], writes=[bW[0]])
                    if d == 0:
                        k.op("vector", lambda e: e.tensor_tensor_scan(out=W[2][:], data0=cm[:], data1=W[0][:], initial=0.0,
                                                                      op0=ALU.mult, op1=ALU.add), reads=[bW[0], b_par], writes=[bW[2]])
                    else:
                        k.op("vector", lambda e: e.tensor_tensor_scan(out=W[2][:, ::-1], data0=cm[:], data1=W[0][:, ::-1], initial=0.0,
                                                                      op0=ALU.mult, op1=ALU.add), reads=[bW[0], b_par], writes=[bW[2]])
                    k.op("scalar", lambda e: e.activation(out=W[3][:], in_=W[2][:], func=AF.Exp), reads=[bW[2]], writes=[bW[3]])
                    k.dma("sync", lambda e, hd=hd: e.dma_start(out=W[4][:], in_=self.pT[1, hd, :, :]), reads=[bW[4]], writes=[bW[4]])
                    k.op("vector", lambda e, d=d: e.tensor_tensor(out=cmv(qh[d]), in0=rasv_as_cr(W[4]), in1=cmv(W[3]), op=ALU.mult),
                         reads=[bW[4], bW[3]], writes=[b_qh[d]])
                    k.op("vector", lambda e, d=d: e.tensor_tensor(out=qh[d][:, NLAT:T], in0=W[4][:, NLAT:T], in1=W[3][:, NLAT:T], op=ALU.mult),
                         reads=[bW[4], bW[3]], writes=[b_qh[d]])
                    k.op("gpsimd", lambda e: e.tensor_scalar(out=W[0][:], in0=W[2][:], scalar1=-60.0, scalar2=None, op0=ALU.max),
                         reads=[bW[2], bW[0]], writes=[bW[0]])
                    k.op("scalar", lambda e: e.activation(out=W[0][:], in_=W[0][:], func=AF.Exp, scale=-1.0), reads=[bW[0]], writes=[bW[0]])
                    k.op("vector", lambda e: e.tensor_tensor(out=W[4][:], in0=W[1][:], in1=W[0][:], op=ALU.mult),
                         reads=[bW[1], bW[0], bW[4]], writes=[bW[4]])
                    k.op("scalar", lambda e, d=d: e.copy(out=kt_[d][:], in_=W[4][:]), reads=[bW[4]], writes=[b_kt[d]])
                    glast = W[2][:, (31 if d == 0 else 0):T:32]
                    k.op("scalar", lambda e, d=d, glast=glast: e.activation(out=dec[d][:], in_=glast, func=AF.Exp), reads=[bW[2]], writes=[b_dec[d]])
                    k.op("vector", lambda e, d=d: e.tensor_tensor(out=kh[:].rearrange("p (n j) -> p n j", j=32),
                                                                  in0=W[4][:].rearrange("p (n j) -> p n j", j=32),
                                                                  in1=bc_last(dec[d][:], 32), op=ALU.mult),
                         reads=[bW[4], b_dec[d]], writes=[b_kh])
                    for g0 in range(0, NT, 8):
                        ng = min(8, NT - g0)
                        for j in range(ng):
                            ti = g0 + j
                            k.op("tensor", lambda e, j=j, ti=ti: e.transpose(out=psT[:, j * 128:(j + 1) * 128], in_=kh[:, ti * 128:(ti + 1) * 128],
                                                                             identity=self.ident_bf[:]),
                                 reads=[b_kh, self.b_const], writes=[self.psb[7]])
                        k.op("vector", lambda e, d=d, g0=g0, ng=ng: e.tensor_copy(
                            out=khtm[d][:, g0:g0 + ng, :], in_=psT[:, 0:ng * 128].rearrange("p (n j) -> p n j", j=128)),
                            reads=[self.psb[7]], writes=[b_khtm[d]])
                        k.op("scalar", lambda e, d=d, g0=g0, ng=ng: e.activation(
                            out=khtm3[d][:, g0:g0 + ng, :], in_=psT[:, 0:ng * 128].rearrange("p (n j) -> p n j", j=128),
                            func=AF.Copy, scale=self.masks[:, 4, 0:1]),
                            reads=[self.psb[7], self.b_const], writes=[b_khtm[d]])
                sm_all = [W[1][:].bitcast(BF16)[:, 0:NT * 128].rearrange("p (n j) -> p n j", j=128),
                          W[2][:].bitcast(BF16)[:, 0:NT * 128].rearrange("p (n j) -> p n j", j=128)]
                b_sm = [bW[1], bW[2]]
                for d in range(2):
                    for ti in range(NT):
                        pi = 1 + (ti % 2)
                        k.op("tensor", lambda e, d=d, ti=ti, pi=pi: e.matmul(
                            self.ps[pi][:, 0:128], lhsT=kt_[d][:, ti * 128:(ti + 1) * 128], rhs=qh[d][:, ti * 128:(ti + 1) * 128],
                            start=True, stop=True), reads=[b_kt[d], b_qh[d]], writes=[self.psb[pi]])
                        k.op("vector", lambda e, d=d, ti=ti, pi=pi: e.tensor_tensor(
                            out=sm_all[d][:, ti, :], in0=self.ps[pi][:, 0:128], in1=self.masks[:, d, :], op=ALU.mult),
                            reads=[self.psb[pi], self.b_const], writes=[b_sm[d]])
                order = [[32, 33] + list(range(32)), [33, 32] + list(range(31, -1, -1))]
                seq = [[], []]
                for d in range(2):
                    for ti in order[d]:
                        for cj in ([0, 1, 2, 3] if d == 0 else [3, 2, 1, 0]):
                            seq[d].append((ti, cj))
                NSEQ = len(seq[0])
                NR = 4
                DS_BANK = [[5, 1], [6, 2]]
                psd_b = [[Buf(), Buf()], [Buf(), Buf()]]
                for d in range(2):
                    k.op("vector", lambda e, d=d: e.memset(S32[d][:, 0, :], 0.0), reads=[b_S32[d][0]], writes=[b_S32[d][0]])
                    k.op("vector", lambda e, d=d: e.memset(Sring[d][:, 0, :], 0.0), reads=[b_Sr[d][0]], writes=[b_Sr[d][0]])
                oacc = W[0]
                b_o = bW[0]
                visited = set()

                def emit_dS(d, i):
                    ti, cj = seq[d][i]
                    par = i % 2
                    pdv = self.ps[DS_BANK[d][par]][:, 0:128]
                    if cj < 3:
                        k.op("tensor", lambda e: e.matmul(
                            pdv, lhsT=khtm[d][cj * 32:(cj + 1) * 32, ti, :], rhs=V[cj * 32:(cj + 1) * 32, ti, :], start=True, stop=True),
                            reads=[b_khtm[d], b_V], writes=[psd_b[d][par]])
                    else:
                        k.op("tensor", lambda e: e.matmul(
                            pdv, lhsT=khtm3[d][:, ti, :], rhs=V[:, ti, :], start=True, stop=True),
                            reads=[b_khtm[d], b_V], writes=[psd_b[d][par]])

                def emit_update(d, i):
                    ti, cj = seq[d][i]
                    par = i % 2
                    pdv = self.ps[DS_BANK[d][par]][:, 0:128]
                    cidx = ti * 4 + cj
                    slot = (i + 1) % NR
                    so, sn_ = i % 2, (i + 1) % 2
                    k.op("vector", lambda e: e.scalar_tensor_tensor(
                        out=S32[d][:, sn_, :], in0=S32[d][:, so, :], scalar=dec[d][:, cidx:cidx + 1], in1=pdv, op0=ALU.mult, op1=ALU.add),
                        reads=[psd_b[d][par], b_dec[d], b_S32[d][so]], writes=[b_S32[d][sn_]])
                    k.op("scalar", lambda e: e.copy(out=Sring[d][:, slot, :], in_=S32[d][:, sn_, :]),
                         reads=[b_S32[d][sn_], b_Sr[d][slot]], writes=[b_Sr[d][slot]])

                for d in range(2):
                    emit_dS(d, 0)
                for i in range(NSEQ):
                    for d in range(2):
                        ti, cj = seq[d][i]
                        po = self.ps[3 + d]
                        b_po = self.psb[3 + d]
                        first = (i % 4 == 0)
                        lastc = (i % 4 == 3)
                        if first:
                            k.op("tensor", lambda e, d=d, ti=ti, po=po: e.matmul(
                                po[:, 0:128], lhsT=V[:, ti, :], rhs=sm_all[d][:, ti, :], start=True, stop=False),
                                reads=[b_V, b_sm[d]], writes=[b_po])
                        if i + 1 < NSEQ:
                            emit_dS(d, i + 1)
                        c0 = ti * 128 + cj * 32
                        slot = i % NR
                        k.op("tensor", lambda e, d=d, cj=cj, c0=c0, po=po, slot=slot, lastc=lastc: e.matmul(
                            po[:, cj * 32:(cj + 1) * 32], lhsT=Sring[d][:, slot, :], rhs=qh[d][:, c0:c0 + 32], start=False, stop=lastc),
                            reads=[b_Sr[d][slot], b_qh[d]], writes=[b_po])
                        emit_update(d, i)
                        if lastc:
                            osl = oacc[:, ti * 128:(ti + 1) * 128]
                            if ti not in visited:
                                visited.add(ti)
                                k.op("scalar", lambda e, osl=osl, po=po: e.copy(out=osl, in_=po[:, 0:128]), reads=[b_po], writes=[b_o])
                            else:
                                k.op("vector", lambda e, osl=osl, po=po: e.tensor_tensor(out=osl, in0=po[:, 0:128], in1=osl, op=ALU.add),
                                     reads=[b_po, b_o], writes=[b_o])
                k.op("scalar", lambda e: e.activation(out=W[1][:], in_=W[0][:], func=AF.Square), reads=[bW[0], bW[1]], writes=[bW[1]])
                for t0 in range(0, T, 512):
                    n = min(512, T - t0)
                    k.op("tensor", lambda e, t0=t0, n=n: e.matmul(self.ps[0][:, 0:n], lhsT=self.ones128[:], rhs=W[1][:, t0:t0 + n],
                                                                 start=True, stop=True), reads=[bW[1], self.b_const], writes=[self.psb[0]])
                    k.op("vector", lambda e, t0=t0, n=n: e.tensor_scalar(out=W[2][:, t0:t0 + n], in0=self.ps[0][:, 0:n], scalar1=EPS, scalar2=None,
                                                                        op0=ALU.add), reads=[self.psb[0], bW[2]], writes=[bW[2]])
                k.op("scalar", lambda e: e.activation(out=W[2][:], in_=W[2][:], func=AF.Sqrt), reads=[bW[2]], writes=[bW[2]])
                k.op("vector", lambda e: e.reciprocal(out=W[2][:], in_=W[2][:]), reads=[bW[2]], writes=[bW[2]])
                k.op("vector", lambda e: e.tensor_tensor(out=W[3][:], in0=W[0][:], in1=W[2][:], op=ALU.mult), reads=[bW[0], bW[2], bW[3]], writes=[bW[3]])
                k.dma("sync", lambda e, hd=hd: e.dma_start(out=W[4][:], in_=self.pT[5, hd, :, :]), writes=[bW[4]])
                k.op("vector", lambda e: e.scalar_tensor_tensor(
                    out=kh[:, 0:NLAT].rearrange("p (r c) -> p r c", c=GRID), in0=W[3][:, 0:NLAT].rearrange("p (c r) -> p r c", r=GRID),
                    scalar=nwh[:, 0:1], in1=W[4][:, 0:NLAT].rearrange("p (r c) -> p r c", c=GRID), op0=ALU.mult, op1=ALU.mult),
                    reads=[bW[3], bW[4], b_par], writes=[b_kh])
                k.op("vector", lambda e: e.scalar_tensor_tensor(
                    out=kh[:, NLAT:T], in0=W[3][:, NLAT:T], scalar=nwh[:, 0:1], in1=W[4][:, NLAT:T], op0=ALU.mult, op1=ALU.mult),
                    reads=[bW[3], bW[4], b_par], writes=[b_kh])
                k.dma("sync", lambda e, hd=hd: e.dma_start(out=self.mixT[8 + hd, :, :], in_=kh[:]), reads=[b_kh])
            k.barrier()

    def sin_turns(self, out_ap, u, ki, tmp, bufs):
        k = self.k
        TWO_PI = 2 * math.pi * (1.0 - 1e-6)
        k.op("vector", lambda e: e.tensor_copy(out=ki, in_=u), reads=bufs, writes=bufs)
        k.op("vector", lambda e: e.tensor_copy(out=tmp, in_=ki), reads=bufs, writes=bufs)
        k.op("vector", lambda e: e.tensor_tensor(out=u, in0=u, in1=tmp, op=ALU.subtract), reads=bufs, writes=bufs)
        k.op("vector", lambda e: e.tensor_scalar(out=tmp, in0=u, scalar1=0.5, scalar2=None, op0=ALU.is_gt), reads=bufs, writes=bufs)
        k.op("vector", lambda e: e.tensor_tensor(out=u, in0=u, in1=tmp, op=ALU.subtract), reads=bufs, writes=bufs)
        k.op("vector", lambda e: e.tensor_scalar(out=tmp, in0=u, scalar1=-0.5, scalar2=None, op0=ALU.is_lt), reads=bufs, writes=bufs)
        k.op("vector", lambda e: e.tensor_tensor(out=u, in0=u, in1=tmp, op=ALU.add), reads=bufs, writes=bufs)
        k.op("scalar", lambda e: e.activation(out=out_ap, in_=u, func=AF.Sin, scale=TWO_PI), reads=bufs, writes=bufs)

    def s5_phase(self, l, need_ctx):
        k = self.k
        nc = self.nc
        L = 256
        NCHK = T // L
        PI = math.pi
        from concourse.ap import AP as _AP
        with contextlib.ExitStack() as ph:
            mag_s = [self.sb(ph, f"smag{d}", [128, 32], F32) for d in range(2)]
            th_s = [self.sb(ph, f"sth{d}", [128, 32], F32) for d in range(2)]
            dsk = self.sb(ph, "sdsk", [128, 8], F32)
            b_prm = Buf()
            k.dma("sync", lambda e: e.dma_start(out=dsk[:], in_=self.s5_d[l].rearrange("(c p) -> p c", p=128)), writes=[b_prm])
            iota = self.aux[:, 8:8 + 257]
            mg = self.aux[:, 0:8]
            with contextlib.ExitStack() as pre:
                BT = [[self.sb(pre, f"sBT{d}{r}", [128, 8, 4, 128], BF16) for r in range(2)] for d in range(2)]
                CT = [[self.sb(pre, f"sCT{d}{r}", [128, 32, 128], BF16) for r in range(2)] for d in range(2)]

                def nl(nm):
                    return self.sb(pre, nm, [64, 64], F32)
                for d in range(2):
                    for r in range(2):
                        k.op("vector", lambda e, d=d, r=r: e.memset(CT[d][r][:], 0.0), writes=[b_prm])
                lre, lim, ls, aa, mg_, thn, t1, t2, cs, sn, den, nr, cfr, cfi = [nl(f"snl{i}") for i in range(14)]
                nli = self.sb(pre, "snli", [64, 64], I32)
                Bn = [self.sb(pre, f"sBn{r}", [64, 1024], F32) for r in range(2)]
                Bb = [self.sb(pre, f"sBb{r}", [64, 1024], F32) for r in range(2)]
                tmpB = self.sb(pre, "stmpB", [64, 1024], F32)
                Cx = [self.sb(pre, f"sCx{r}", [32, 32, 128], F32) for r in range(2)]
                sl_lre = self.sb(pre, "sslre", [128, 32], F32)
                sl_lim = self.sb(pre, "sslim", [128, 32], F32)
                sl_ls = self.sb(pre, "ssls", [128, 32], F32)
                b_n = Buf()
                b_B = Buf()
                b_C = Buf()
                for d in range(2):
                    k.dma("sync", lambda e, d=d: e.dma_start(out=lre[:], in_=self.s5_lre[l, d].rearrange("g p -> p g")), writes=[b_n])
                    k.dma("sync", lambda e, d=d: e.dma_start(out=lim[:], in_=self.s5_lim[l, d].rearrange("g p -> p g")), writes=[b_n])
                    lsrc = self.s5_ls[l, d]
                    k.dma("sync", lambda e, lsrc=lsrc: e.dma_start(out=ls[:], in_=_AP(lsrc.tensor, lsrc.offset, [[0, 64], [1, 64]])), writes=[b_n])
                    for r, src in ((0, self.s5_bre), (1, self.s5_bim)):
                        k.dma("sync", lambda e, d=d, r=r, src=src: e.dma_start(
                            out=Bn[r][:].rearrange("p (g h) -> p g h", h=16), in_=src[l, d].rearrange("g p h -> p g h")), writes=[b_B])

                    def V_(fn, **kw):
                        k.op("vector", fn, reads=[b_n], writes=[b_n])

                    def A_(fn):
                        k.op("scalar", fn, reads=[b_n], writes=[b_n])
                    V_(lambda e: e.tensor_scalar(out=lre[:], in0=lre[:], scalar1=-1e-4, scalar2=None, op0=ALU.min))
                    A_(lambda e: e.activation(out=ls[:], in_=ls[:], func=AF.Exp))
                    V_(lambda e: e.tensor_tensor(out=aa[:], in0=lre[:], in1=ls[:], op=ALU.mult))
                    A_(lambda e: e.activation(out=mg_[:], in_=aa[:], func=AF.Exp))
                    V_(lambda e: e.tensor_tensor(out=thn[:], in0=lim[:], in1=ls[:], op=ALU.mult))
                    V_(lambda e: e.tensor_scalar(out=t1[:], in0=thn[:], scalar1=1.0 / (2 * PI), scalar2=None, op0=ALU.mult))
                    self.sin_turns(sn[:], t1[:], nli[:], t2[:], [b_n])
                    V_(lambda e: e.tensor_scalar(out=t1[:], in0=thn[:], scalar1=1.0 / (2 * PI), scalar2=0.25, op0=ALU.mult, op1=ALU.add))
                    self.sin_turns(cs[:], t1[:], nli[:], t2[:], [b_n])
                    V_(lambda e: e.tensor_tensor(out=cs[:], in0=cs[:], in1=mg_[:], op=ALU.mult))
                    V_(lambda e: e.tensor_tensor(out=sn[:], in0=sn[:], in1=mg_[:], op=ALU.mult))
                    V_(lambda e: e.tensor_tensor(out=den[:], in0=lre[:], in1=lre[:], op=ALU.mult))
                    V_(lambda e: e.tensor_tensor(out=t1[:], in0=lim[:], in1=lim[:], op=ALU.mult))
                    V_(lambda e: e.tensor_tensor(out=den[:], in0=den[:], in1=t1[:], op=ALU.add))
                    V_(lambda e: e.reciprocal(out=den[:], in_=den[:]))
                    V_(lambda e: e.tensor_scalar(out=nr[:], in0=cs[:], scalar1=-1.0, scalar2=None, op0=ALU.add))
                    V_(lambda e: e.tensor_tensor(out=t1[:], in0=nr[:], in1=lre[:], op=ALU.mult))
                    V_(lambda e: e.tensor_tensor(out=t2[:], in0=sn[:], in1=lim[:], op=ALU.mult))
                    V_(lambda e: e.tensor_tensor(out=cfr[:], in0=t1[:], in1=t2[:], op=ALU.add))
                    V_(lambda e: e.tensor_tensor(out=cfr[:], in0=cfr[:], in1=den[:], op=ALU.mult))
                    V_(lambda e: e.tensor_tensor(out=t1[:], in0=sn[:], in1=lre[:], op=ALU.mult))
                    V_(lambda e: e.tensor_tensor(out=t2[:], in0=nr[:], in1=lim[:], op=ALU.mult))
                    V_(lambda e: e.tensor_tensor(out=cfi[:], in0=t1[:], in1=t2[:], op=ALU.subtract))
                    V_(lambda e: e.tensor_tensor(out=cfi[:], in0=cfi[:], in1=den[:], op=ALU.mult))
                    def v3(t):
                        return t[:].rearrange("p (g h) -> p g h", h=16)
                    k.op("vector", lambda e: e.tensor_tensor(out=v3(Bb[0]), in0=v3(Bn[0]), in1=bc_last(cfr[:], 16), op=ALU.mult), reads=[b_n, b_B], writes=[b_B])
                    k.op("vector", lambda e: e.tensor_tensor(out=v3(tmpB), in0=v3(Bn[1]), in1=bc_last(cfi[:], 16), op=ALU.mult), reads=[b_n, b_B], writes=[b_B])
                    k.op("vector", lambda e: e.tensor_tensor(out=Bb[0][:], in0=Bb[0][:], in1=tmpB[:], op=ALU.subtract), reads=[b_B], writes=[b_B])
                    k.op("vector", lambda e: e.tensor_tensor(out=v3(Bb[1]), in0=v3(Bn[1]), in1=bc_last(cfr[:], 16), op=ALU.mult), reads=[b_n, b_B], writes=[b_B])
                    k.op("vector", lambda e: e.tensor_tensor(out=v3(tmpB), in0=v3(Bn[0]), in1=bc_last(cfi[:], 16), op=ALU.mult), reads=[b_n, b_B], writes=[b_B])
                    k.op("vector", lambda e: e.tensor_tensor(out=Bb[1][:], in0=Bb[1][:], in1=tmpB[:], op=ALU.add), reads=[b_B], writes=[b_B])
                    for r in range(2):
                        for tb in range(8):
                            pb = 1 + (tb % 4)
                            k.op("tensor", lambda e, r=r, tb=tb, pb=pb: e.transpose(
                                out=self.ps[pb][:, 0:64], in_=Bb[r][:, tb * 128:(tb + 1) * 128], identity=self.ident[0:64, 0:64]),
                                reads=[b_B, self.b_const], writes=[self.psb[pb]])
                            for g8 in range(8):
                                dst = BT[d][r][:, tb, g8 // 2, (g8 % 2) * 64:(g8 % 2) * 64 + 64]
                                if g8 % 2 == 0:
                                    k.op("vector", lambda e, dst=dst, pb=pb, g8=g8: e.tensor_scalar(
                                        out=dst, in0=self.ps[pb][:, 0:64], scalar1=mg[:, g8:g8 + 1], scalar2=None, op0=ALU.mult),
                                        reads=[self.psb[pb], self.b_const], writes=[b_prm])
                                else:
                                    k.op("scalar", lambda e, dst=dst, pb=pb, g8=g8: e.activation(
                                        out=dst, in_=self.ps[pb][:, 0:64], func=AF.Copy, scale=mg[:, g8:g8 + 1]),
                                        reads=[self.psb[pb], self.b_const], writes=[b_prm])
                    for g2 in range(2):
                        k.dma("sync", lambda e, d=d, g2=g2: e.dma_start(out=sl_lre[64 * g2:64 * g2 + 64, :],
                                                                        in_=self.s5_lre[l, d, g2::2, :].rearrange("gp p -> p gp")), writes=[b_n])
                        k.dma("sync", lambda e, d=d, g2=g2: e.dma_start(out=sl_lim[64 * g2:64 * g2 + 64, :],
                                                                        in_=self.s5_lim[l, d, g2::2, :].rearrange("gp p -> p gp")), writes=[b_n])
                        lsrc2 = self.s5_ls[l, d, g2::2]
                        k.dma("sync", lambda e, g2=g2, lsrc2=lsrc2: e.dma_start(
                            out=sl_ls[64 * g2:64 * g2 + 64, :], in_=_AP(lsrc2.tensor, lsrc2.offset, [[0, 64], [2, 32]])), writes=[b_n])
                    V_(lambda e: e.tensor_scalar(out=sl_lre[:], in0=sl_lre[:], scalar1=-1e-4, scalar2=None, op0=ALU.min))
                    A_(lambda e: e.activation(out=sl_ls[:], in_=sl_ls[:], func=AF.Exp))
                    V_(lambda e: e.tensor_tensor(out=sl_lre[:], in0=sl_lre[:], in1=sl_ls[:], op=ALU.mult))
                    k.op("scalar", lambda e, d=d: e.activation(out=mag_s[d][:], in_=sl_lre[:], func=AF.Exp), reads=[b_n], writes=[b_prm])
                    k.op("vector", lambda e, d=d: e.scalar_tensor_tensor(out=th_s[d][:], in0=sl_lim[:], scalar=1.0 / (2 * PI), in1=sl_ls[:],
                                                                         op0=ALU.mult, op1=ALU.mult), reads=[b_n], writes=[b_prm])
                    for r, src in ((0, self.s5_cre), (1, self.s5_cim)):
                        k.op("vector", lambda e, r=r: e.memset(Cx[r][:], 0.0), reads=[b_C], writes=[b_C])
                        for g2 in range(2):
                            k.dma("sync", lambda e, d=d, r=r, g2=g2, src=src: e.dma_start(
                                out=Cx[r][16 * g2:16 * g2 + 16, :, 64 * g2:64 * g2 + 64],
                                in_=src[l, d, g2::2].rearrange("gp h p -> h gp p")), reads=[b_C], writes=[b_C])
                        for half in range(2):
                            pb = 5 + half
                            for i in range(16):
                                gp = half * 16 + i
                                k.op("tensor", lambda e, r=r, gp=gp, i=i, pb=pb: e.transpose(
                                    out=self.ps[pb][:, i * 32:(i + 1) * 32], in_=Cx[r][:, gp, :], identity=self.ident[0:32, 0:32]),
                                    reads=[b_C, self.b_const], writes=[self.psb[pb]])
                            pv = self.ps[pb][:, :].rearrange("p (i c) -> p i c", c=32)
                            for j in range(4):
                                dst = CT[d][r][:, half * 16 + j:half * 16 + 16:4, 32 * j:32 * j + 32]
                                k.op("scalar", lambda e, dst=dst, pv=pv, j=j, r=r: e.activation(
                                    out=dst, in_=pv[:, j:16:4, :], func=AF.Copy, scale=(1.0 if r == 0 else -1.0)),
                                    reads=[self.psb[pb]], writes=[b_prm])
                for d in range(2):
                    for r in range(2):
                        k.dma("sync", lambda e, d=d, r=r: e.dma_start(out=self.BTs[d, r], in_=BT[d][r][:].rearrange("p t j c -> p t (j c)")), reads=[b_prm])
                        k.dma("sync", lambda e, d=d, r=r: e.dma_start(out=self.CTs[d, r], in_=CT[d][r][:]), reads=[b_prm])
                k.barrier()
            L = 512
            ust = self.sb(ph, "sust", [128, T], F32)
            ubf = self.sb(ph, "subf", [128, T], BF16)
            yacc = self.sb(ph, "syacc", [128, T], F32)
            b_ust, b_ubf, b_y = Buf(), Buf(), Buf()
            tab_all = self.sb(ph, "stab", [128, 2, 4, L + 1], F32)
            tabs = [[tab_all[:, i, j, :] for i in range(2)] for j in range(4)]
            init_t = self.sb(ph, "sinit", [128, 2, 4], F32)
            x1_t = self.sb(ph, "sx1", [128, 2, 4], F32)
            x2_t = self.sb(ph, "sx2", [128, 2, 4], F32)
            b_init = Buf()
            rts = [self.sb(ph, f"srt{j}", [128, L], F32) for j in range(4)]
            phs = self.sb(ph, "sphs", [128, L + 1], F32)
            pht = self.sb(ph, "spht", [128, L + 1], F32)
            phi = self.sb(ph, "sphi", [128, L + 1], I32)
            b_tab = [Buf() for _ in range(4)]
            b_phs = Buf()
            btl = self.make_ring(ph, "sbtl", [128, 2, 2, 512], BF16, 2)
            ctl = self.make_ring(ph, "sctl", [128, 2, 2, 4, 128], BF16, 2)
            pre_s = self.make_ring(ph, "spre", [128, 2, L], F32, 4)
            wring_ = self.make_ring(ph, "sw", [128, 2, L], F32, 8)
            zri = self.make_ring(ph, "szri", [128, 4, 2, L], F32, 2)
            xri = self.make_ring(ph, "sxri", [128, 2, L], BF16, 4)
            tmpP = self.make_ring(ph, "stp", [128, L], F32, 2)
            tmpV = self.make_ring(ph, "stv", [128, 2, L], F32, 2)
            tmpP2 = self.make_ring(ph, "stp2", [128, 2, L], F32, 1)
            psr = Ring([1, 2, 3, 4])
            psy = Ring([5, 6])
            iotaL = self.sb(ph, "siota", [128, L + 1], F32)
            b_io = Buf()
            k.dma("sync", lambda e: e.dma_start(out=iotaL[:], in_=self.c_iota[:, :]), writes=[b_io])
            chunks = [(NLAT, T)] + [(i * L, (i + 1) * L) for i in range(NLAT // L)]
            for tb in range(8):
                bt, b_bt = btl.next()
                ct, b_ct = ctl.next()
                k.dma("sync", lambda e, tb=tb, bt=bt: e.dma_start(out=bt[:], in_=self.BTs[:, :, :, tb, :].rearrange("d r p c -> p d r c")), writes=[b_bt])
                k.dma("sync", lambda e, tb=tb, ct=ct: e.dma_start(out=ct[:], in_=self.CTs[:, :, :, tb * 4:(tb + 1) * 4, :].rearrange("d r p g c -> p d r g c")), writes=[b_ct])
                k.dma("sync", lambda e, tb=tb: e.dma_start(out=ust[:], in_=self.pT[0, tb, :, :]), reads=[b_ust], writes=[b_ust])
                k.op("scalar", lambda e: e.copy(out=ubf[:], in_=ust[:]), reads=[b_ust], writes=[b_ubf])
                k.op("vector", lambda e, tb=tb: e.tensor_scalar(out=yacc[:], in0=ust[:], scalar1=dsk[:, tb:tb + 1], scalar2=None, op0=ALU.mult),
                     reads=[b_ust, b_prm], writes=[b_y])
                for d in range(2):
                    for j in range(4):
                        gp = tb * 4 + j
                        thc = th_s[d][:, gp:gp + 1]
                        for which, off in ((1, 0.0), (0, 0.25)):
                            k.op("vector", lambda e, thc=thc, off=off: e.tensor_scalar(out=phs[:], in0=iotaL[:], scalar1=thc, scalar2=off,
                                                                                       op0=ALU.mult, op1=ALU.add), reads=[b_prm, b_io, b_phs], writes=[b_phs])
                            self.sin_turns(tabs[j][which], phs[:], phi[:], pht[:], [b_phs, b_tab[j]])
                        k.op("vector", lambda e, j=j, d=d, gp=gp: e.tensor_scalar(out=rts[j][:], in0=iotaL[:, 0:L], scalar1=0.0,
                                                                                 scalar2=mag_s[d][:, gp:gp + 1], op0=ALU.mult, op1=ALU.add),
                             reads=[b_prm, b_io], writes=[b_tab[j]])
                    order = chunks if d == 0 else [chunks[0]] + chunks[:0:-1]
                    NCI = len(order)
                    Aout, Zout = {}, {}

                    def sv(t2d, ci, d=d, order=order):
                        lo, hi = order[ci]
                        v = t2d[:, lo:hi]
                        return v[:, ::-1] if d == 1 else v

                    def TT(E, out, in0, in1, op, reads, writes):
                        k.op(E, lambda e: e.tensor_tensor(out=out, in0=in0, in1=in1, op=op), reads=reads, writes=writes)

                    def stageA(ci, d=d, tb=tb, bt=bt, b_bt=b_bt, sv=sv, order=order):
                        n = order[ci][1] - order[ci][0]
                        for j in range(4):
                            EA = "vector"
                            cs_t = tabs[j][0][:, 0:n]
                            sn_t = tabs[j][1][:, 0:n]
                            p1i, _ = psr.next()
                            p2i, _ = psr.next()
                            p1, p2 = self.ps[p1i], self.ps[p2i]
                            rhs = sv(ubf, ci)
                            k.op("tensor", lambda e, p1=p1, j=j, rhs=rhs, n=n: e.matmul(
                                p1[:, 0:n], lhsT=bt[:, d, 0, j * 128:(j + 1) * 128], rhs=rhs, start=True, stop=True),
                                reads=[b_bt, b_ubf], writes=[self.psb[p1i]])
                            k.op("tensor", lambda e, p2=p2, j=j, rhs=rhs, n=n: e.matmul(
                                p2[:, 0:n], lhsT=bt[:, d, 1, j * 128:(j + 1) * 128], rhs=rhs, start=True, stop=True),
                                reads=[b_bt, b_ubf], writes=[self.psb[p2i]])
                            if EA == "gpsimd":
                                pr, bpr = pre_s.next()
                                k.op("scalar", lambda e, pr=pr, p1=p1, n=n: e.copy(out=pr[:, 0, 0:n], in_=p1[:, 0:n]), reads=[self.psb[p1i]], writes=[bpr])
                                k.op("scalar", lambda e, pr=pr, p2=p2, n=n: e.copy(out=pr[:, 1, 0:n], in_=p2[:, 0:n]), reads=[self.psb[p2i]], writes=[bpr])
                                s_re, s_im = pr[:, 0, 0:n], pr[:, 1, 0:n]
                                rd = [bpr, b_tab[j]]
                                tp, btp = tmpP.next()
                                tpv = tp[:, 0:n]
                            else:
                                s_re, s_im = p1[:, 0:n], p2[:, 0:n]
                                rd = [self.psb[p1i], self.psb[p2i], b_tab[j]]
                                tp, btp = tmpV.next()
                                tpv = tp[:, 0, 0:n]
                            w, bw = wring_.next()
                            TT(EA, w[:, 0, 0:n], s_re, cs_t, ALU.mult, rd, [bw])
                            TT(EA, tpv, s_im, sn_t, ALU.mult, rd, [btp])
                            TT(EA, w[:, 0, 0:n], w[:, 0, 0:n], tpv, ALU.add, [bw, btp], [bw])
                            TT(EA, w[:, 1, 0:n], s_im, cs_t, ALU.mult, rd, [bw])
                            TT(EA, tpv, s_re, sn_t, ALU.mult, rd + [btp], [btp])
                            TT(EA, w[:, 1, 0:n], w[:, 1, 0:n], tpv, ALU.subtract, [bw, btp], [bw])
                            Aout[(ci, j)] = (w, bw, n)

                    def stageB(ci, order=order):
                        zr, bzr = zri.next()
                        if ci > 0:
                            nprev = order[ci - 1][1] - order[ci - 1][0]
                            zp, bzp = Zout["prev"]
                            zend = zp[:, :, :, nprev - 1].rearrange("p j c -> p c j")
                            cLb = tab_all[:, 0, :, nprev].unsqueeze(1).to_broadcast([128, 2, 4])
                            sLb = tab_all[:, 1, :, nprev].unsqueeze(1).to_broadcast([128, 2, 4])
                            rdc = [bzp] + b_tab + [b_init]
                            k.op("vector", lambda e, zend=zend, cLb=cLb: e.tensor_tensor(out=x1_t[:], in0=zend, in1=cLb, op=ALU.mult), reads=rdc, writes=[b_init])
                            k.op("vector", lambda e, zend=zend, sLb=sLb: e.tensor_tensor(out=x2_t[:], in0=zend, in1=sLb, op=ALU.mult), reads=rdc, writes=[b_init])
                            k.op("vector", lambda e: e.tensor_tensor(out=init_t[:, 0, :], in0=x1_t[:, 0, :], in1=x2_t[:, 1, :], op=ALU.subtract), reads=[b_init], writes=[b_init])
                            k.op("vector", lambda e: e.tensor_tensor(out=init_t[:, 1, :], in0=x1_t[:, 1, :], in1=x2_t[:, 0, :], op=ALU.add), reads=[b_init], writes=[b_init])
                        for j in range(4):
                            w, bw, n = Aout.pop((ci, j))
                            for c2 in range(2):
                                ini = 0.0 if ci == 0 else init_t[:, c2, j:j + 1]
                                k.op("vector", lambda e, zr=zr, w=w, c2=c2, ini=ini, j=j, n=n: e.tensor_tensor_scan(
                                    out=zr[:, j, c2, 0:n], data0=rts[j][:, 0:n], data1=w[:, c2, 0:n], initial=ini, op0=ALU.mult, op1=ALU.add),
                                    reads=[bw, b_tab[j], b_init], writes=[bzr])
                            Zout[(ci, j)] = (zr[:, j], bzr, n)
                        Zout["prev"] = (zr, bzr)

                    def stageC(ci, d=d, ct=ct, b_ct=b_ct, sv=sv):
                        pyi, _ = psy.next()
                        py = self.ps[pyi]
                        for j in range(4):
                            zr, bzr, n = Zout.pop((ci, j))
                            E = "vector"
                            cs_t = tabs[j][0][:, 0:n]
                            sn_t = tabs[j][1][:, 0:n]
                            xr_, bxr_ = xri.next()
                            tv, btv = tmpV.next() if E == "vector" else tmpP2.next()
                            zre, zim = zr[:, 0, 0:n], zr[:, 1, 0:n]
                            A_, B_ = tv[:, 0, 0:n], tv[:, 1, 0:n]
                            rd2 = [bzr, b_tab[j]]
                            TT(E, A_, zre, cs_t, ALU.mult, rd2, [btv])
                            TT(E, B_, zim, sn_t, ALU.mult, rd2 + [btv], [btv])
                            TT(E, xr_[:, 0, 0:n], A_, B_, ALU.subtract, [btv], [bxr_])
                            TT(E, A_, zim, cs_t, ALU.mult, rd2 + [btv], [btv])
                            TT(E, B_, zre, sn_t, ALU.mult, rd2 + [btv], [btv])
                            TT(E, xr_[:, 1, 0:n], A_, B_, ALU.add, [btv], [bxr_])
                            k.op("tensor", lambda e, py=py, xr_=xr_, j=j, n=n: e.matmul(
                                py[:, 0:n], lhsT=ct[:, d, 0, j, :], rhs=xr_[:, 0, 0:n], start=(j == 0), stop=False),
                                reads=[b_ct, bxr_], writes=[self.psb[pyi]])
                            k.op("tensor", lambda e, py=py, xr_=xr_, j=j, n=n: e.matmul(
                                py[:, 0:n], lhsT=ct[:, d, 1, j, :], rhs=xr_[:, 1, 0:n], start=False, stop=(j == 3)),
                                reads=[b_ct, bxr_], writes=[self.psb[pyi]])
                        yv = sv(yacc, ci)
                        k.op("vector", lambda e, py=py, yv=yv, n=n: e.tensor_tensor(out=yv, in0=py[:, 0:n], in1=yv, op=ALU.add),
                             reads=[self.psb[pyi], b_y], writes=[b_y])

                    stageA(0)
                    for ci in range(NCI):
                        if ci + 1 < NCI:
                            stageA(ci + 1)
                        stageB(ci)
                        stageC(ci)
                k.op("vector", lambda e: e.tensor_tensor(out=ust[:], in0=yacc[:], in1=yacc[:], op=ALU.mult), reads=[b_y, b_ust], writes=[b_ust])
                k.op("vector", lambda e: e.tensor_scalar(out=ust[:], in0=ust[:], scalar1=0.044715, scalar2=1.0, op0=ALU.mult, op1=ALU.add),
                     reads=[b_ust], writes=[b_ust])
                k.op("vector", lambda e: e.tensor_tensor(out=ust[:], in0=ust[:], in1=yacc[:], op=ALU.mult), reads=[b_ust, b_y], writes=[b_ust])
                k.op("scalar", lambda e: e.activation(out=ust[:], in_=ust[:], func=AF.Sigmoid, scale=1.5957691216057308), reads=[b_ust], writes=[b_ust])
                k.op("vector", lambda e: e.tensor_tensor(out=ust[:], in0=ust[:], in1=yacc[:], op=ALU.mult), reads=[b_ust, b_y], writes=[b_ust])
                k.dma("sync", lambda e, tb=tb: e.dma_start(out=self.gS5[tb, :, :], in_=ust[:]), reads=[b_ust])
            k.barrier()

    def glu_phase(self, l, need_ctx):
        k = self.k
        with contextlib.ExitStack() as ph:
            self.wring = self.make_ring(ph, "wr", [128, 4096], BF16, 5)
            gf = self.sb(ph, "ggf", [128, 8, 1024], F32)
            gb = self.sb(ph, "ggb", [128, 8, 1024], BF16)
            bgl = self.sb(ph, "gbgl", [128, 8], F32)
            b_gf, b_gb, b_bg = Buf(), Buf(), Buf()
            sgr = self.make_ring(ph, "gsg", [128, 512], F32, 3)
            outr = self.make_ring(ph, "gout", [128, 512], BF16, 3)
            k.dma("sync", lambda e: e.dma_start(out=bgl[:], in_=self.s5_bglu[l].rearrange("(c p) -> p c", p=128)), writes=[b_bg])
            wv = self.s5_wglu[l].rearrange("(kt p) c -> p kt c", p=128)
            gv = self.gS5.rearrange("c p t -> p c t")
            psr = Ring([1, 2, 3, 4])
            for (t0, ts, s) in self.supertiles(need_ctx):
                nh = max(1, ts // 512)
                n = min(512, ts)
                k.dma("sync", lambda e, t0=t0, ts=ts: e.dma_start(out=gf[:, :, 0:ts], in_=gv[:, :, t0:t0 + ts]), reads=[b_gf], writes=[b_gf])
                k.op("scalar", lambda e, ts=ts: e.copy(out=gb[:, :, 0:ts], in_=gf[:, :, 0:ts]), reads=[b_gf, b_gb], writes=[b_gb])
                stream = Prog.WStream(self, [wv[:, :, bi * 256:(bi + 1) * 256] for bi in range(4)], 3)
                for bi in range(4):
                    wt, wb = stream.get()
                    for sub in range(2):
                        m = bi * 2 + sub
                        for hf in range(nh):
                            tsl = slice(hf * 512, hf * 512 + n)
                            tok0 = t0 + hf * 512
                            pi, _ = psr.next()
                            pp = self.ps[pi]
                            for kt in range(8):
                                k.op("tensor", lambda e, pp=pp, wt=wt, kt=kt, sub=sub, tsl=tsl, n=n: e.matmul(
                                    pp[:, 0:n], lhsT=wt[:, kt, sub * 128:(sub + 1) * 128], rhs=gb[:, kt, tsl],
                                    start=(kt == 0), stop=(kt == 7)), reads=[wb, b_gb], writes=[self.psb[pi]])
                            sg, bsg = sgr.next()
                            k.op("scalar", lambda e, sg=sg, pp=pp, m=m, n=n: e.activation(out=sg[:, 0:n], in_=pp[:, 0:n], func=AF.Sigmoid,
                                                                                       bias=bgl[:, m:m + 1]), reads=[self.psb[pi], b_bg], writes=[bsg])
                            ot, bot = outr.next()
                            k.op("vector", lambda e, ot=ot, sg=sg, m=m, tsl=tsl, n=n: e.tensor_tensor(out=ot[:, 0:n], in0=sg[:, 0:n], in1=gf[:, m, tsl], op=ALU.mult),
                                 reads=[bsg, b_gf], writes=[bot])
                            k.dma("sync", lambda e, ot=ot, m=m, tok0=tok0, n=n: e.dma_start(out=self.mixT[m, :, tok0:tok0 + n], in_=ot[:, 0:n]), reads=[bot])
            k.barrier()

    def outproj_phase(self, l, include_ctx):
        k = self.k
        with contextlib.ExitStack() as ph:
            self.wring = self.make_ring(ph, "wr", [128, 4096], BF16, 5)
            mx = self.sb(ph, "omx", [128, 16, 1024], BF16)
            b_mx = Buf()
            xrring = self.make_ring(ph, "oxr", [128, 512], F32, 3)
            wv = self.w_out[l].rearrange("(kt p) c -> p kt c", p=128)
            mv = self.mixT.rearrange("c p t -> p c t")
            psr = Ring([1, 2, 3, 4])
            for (t0, ts, s) in self.supertiles(include_ctx):
                nh = max(1, ts // 512)
                n = min(512, ts)
                k.dma("sync", lambda e, t0=t0, ts=ts: e.dma_start(out=mx[:, :, 0:ts], in_=mv[:, :, t0:t0 + ts]), reads=[b_mx], writes=[b_mx])
                stream = Prog.WStream(self, [wv[:, :, bi * 256:(bi + 1) * 256] for bi in range(8)], 3)
                for bi in range(8):
                    wt, wb = stream.get()
                    for sub in range(2):
                        m = bi * 2 + sub
                        for hf in range(nh):
                            tsl = slice(hf * 512, hf * 512 + n)
                            tok0 = t0 + hf * 512
                            xr, bxr = xrring.next()
                            k.dma("sync", lambda e, xr=xr, m=m, tok0=tok0, n=n: e.dma_start(out=xr[:, 0:n], in_=self.xT[m, :, tok0:tok0 + n]),
                                  reads=[self.xbuf(m, tok0)], writes=[bxr])
                            pi, _ = psr.next()
                            pp = self.ps[pi]
                            for kt in range(16):
                                k.op("tensor", lambda e, pp=pp, wt=wt, kt=kt, sub=sub, tsl=tsl, n=n: e.matmul(
                                    pp[:, 0:n], lhsT=wt[:, kt, sub * 128:(sub + 1) * 128], rhs=mx[:, kt, tsl],
                                    start=(kt == 0), stop=(kt == 15)), reads=[wb, b_mx], writes=[self.psb[pi]])
                            k.op("vector", lambda e, xr=xr, pp=pp, m=m, s=s, n=n: e.scalar_tensor_tensor(
                                out=xr[:, 0:n], in0=pp[:, 0:n], scalar=self.Gmod[:, s, 1, m:m + 1], in1=xr[:, 0:n],
                                op0=ALU.mult, op1=ALU.add), reads=[self.psb[pi], bxr, self.b_mod], writes=[bxr])
                            k.dma("sync", lambda e, xr=xr, m=m, tok0=tok0, n=n: e.dma_start(out=self.xT[m, :, tok0:tok0 + n], in_=xr[:, 0:n]),
                                  reads=[bxr], writes=[self.xbuf(m, tok0)])
            k.barrier()


def _consts():
    ident = np.eye(128, dtype=np.float32)
    s = np.arange(128)[:, None]
    t = np.arange(128)[None, :]
    same = (s // 32) == (t // 32)
    m_f = (same & (s <= t)).astype(np.float32)
    m_b = (same & (s >= t)).astype(np.float32)
    g2 = ((np.arange(128) % 32) // 16)
    m0 = np.repeat((g2 == 0).astype(np.float32)[:, None], 128, 1)
    m1 = np.repeat((g2 == 1).astype(np.float32)[:, None], 128, 1)
    m96 = np.repeat((np.arange(128) >= 96).astype(np.float32)[:, None], 128, 1)
    mg = ((np.arange(128)[:, None] // 16) == np.arange(8)[None, :]).astype(np.float32)
    iota = np.repeat(np.arange(257, dtype=np.float32)[None, :], 128, 0)
    aux = np.concatenate([mg, iota], axis=1).astype(np.float32)
    iota2 = np.repeat(np.arange(513, dtype=np.float32)[None, :], 128, 0)
    return ident, np.stack([m_f, m_b, m0, m1, m96]).astype(np.float32), aux, iota2


W_NAMES = ["w_ada", "b_ada", "norm_w", "ffn_w_gate", "ffn_w_up", "ffn_w_down", "w_in", "w_out",
           "s5_lambda_re", "s5_lambda_im", "s5_log_step", "s5_b_re", "s5_b_im", "s5_c_re", "s5_c_im",
           "s5_d", "s5_w_glu", "s5_b_glu", "hgrn_lower_bounds", "hgrn_norm_w", "final_norm_w"]


def make_in_map(inputs, b):
    ident, masks, aux, iota2 = _consts()
    m = {"xin": np.ascontiguousarray(np.concatenate([inputs["x"][b], inputs["ctx"][b]], axis=0), dtype=np.float32),
         "cc": np.ascontiguousarray(np.stack([inputs["c"][b], inputs["c_ctx"]], axis=0), dtype=np.float32),
         "c_ident": ident, "c_masks": masks, "c_aux": aux, "c_iota": iota2}
    for nme in W_NAMES:
        m[nme] = np.ascontiguousarray(inputs[nme], dtype=np.float32)
    return m


def kernel(**inputs):
    nc = Prog().build()
    nb = inputs["x"].shape[0]
    in_maps = [make_in_map(inputs, c % nb) for c in range(8)]
    res = run_bass_kernel_spmd(nc, in_maps, core_ids=list(range(8)))
    return np.stack([np.asarray(res.results[b]["Y"]) for b in range(nb)], axis=0).astype(np.float32)
```

```python
import contextlib
import math
from collections import deque

import numpy as np
import concourse.bass as bass
import concourse.mybir as mybir
from concourse.bass_utils import run_bass_kernel_spmd

F32 = mybir.dt.float32
BF16 = mybir.dt.bfloat16
I32 = mybir.dt.int32
ALU = mybir.AluOpType
AF = mybir.ActivationFunctionType

D = 2048
NC_ = 16
FF = 5504
NFF = 43
NLAT = 4096
NCTX = 256
T = NLAT + NCTX
DEPTH = 2
EPS = 1e-6
INC = 6144
GRID = 64

ENGS = ("tensor", "vector", "scalar", "gpsimd", "sync")
SEM_ROLL = 30000
NO_SELF_SYNC = ("tensor",)


class Buf:
    __slots__ = ("name", "w", "r")

    def __init__(self, name=""):
        self.name = name
        self.w = None
        self.r = {}


class K:
    def __init__(self, nc, stack, n_dma_sems=32):
        self.nc = nc
        self.stack = stack
        self.q = {e: [] for e in ENGS}
        self.sem = {}
        self.cnt = {}
        self.waited = {e: {} for e in ENGS}
        self.nsem = 0
        self.sem_owner = {}
        self.no_self_sync = set(NO_SELF_SYNC)
        for e in ("tensor", "vector", "scalar", "gpsimd"):
            self._new_eng_sem(e)
        self.dma_sems = []
        for i in range(n_dma_sems):
            s = stack.enter_context(nc.semaphore(f"dma{i}"))
            self.dma_sems.append([s, 0])
        self.dma_rr = 0
        self.n_instr = 0

    def _new_eng_sem(self, e):
        s = self.stack.enter_context(self.nc.semaphore(f"s_{e}_{self.nsem}"))
        self.nsem += 1
        self.sem[e] = s
        self.cnt[e] = 0
        self.sem_owner[id(s)] = e

    def _collect(self, reads, writes):
        evs = []
        for b in reads:
            if b.w is not None:
                evs.append(b.w)
        for b in writes:
            if b.w is not None:
                evs.append(b.w)
            evs.extend(b.r.values())
        return evs

    def _waits_for(self, eng, evs):
        best = {}
        for (s, v) in evs:
            kk = id(s)
            if eng in self.no_self_sync and self.sem_owner.get(kk) == eng:
                continue
            if kk not in best or best[kk][1] < v:
                best[kk] = (s, v)
        out = []
        wd = self.waited[eng]
        for kk, (s, v) in best.items():
            if wd.get(kk, -1) >= v:
                continue
            wd[kk] = v
            out.append((s, v))
        return out

    def _update(self, ev, reads, writes):
        for b in writes:
            b.w = ev
            b.r = {}
        for b in reads:
            b.r[id(ev[0])] = ev

    def op(self, eng, fn, reads=(), writes=(), extra=()):
        evs = self._collect(reads, writes) + list(extra)
        waits = self._waits_for(eng, evs)
        if self.cnt[eng] >= SEM_ROLL:
            self._new_eng_sem(eng)
        self.cnt[eng] += 1
        ev = (self.sem[eng], self.cnt[eng])
        self.q[eng].append((waits, fn, ev[0], 1))
        self._update(ev, reads, writes)
        self.n_instr += 1
        return ev

    def dma(self, eng, fn, reads=(), writes=(), extra=()):
        evs = self._collect(reads, writes) + list(extra)
        slot = self.dma_sems[self.dma_rr]
        self.dma_rr = (self.dma_rr + 1) % len(self.dma_sems)
        if slot[1] > 0:
            evs.append((slot[0], slot[1]))
        waits = self._waits_for(eng, evs)
        slot[1] += 16
        ev = (slot[0], slot[1])
        self.q[eng].append((waits, fn, ev[0], 16))
        self._update(ev, reads, writes)
        self.n_instr += 1
        return ev

    def all_events(self):
        evs = []
        for e in ("tensor", "vector", "scalar", "gpsimd"):
            if self.cnt[e] > 0:
                evs.append((self.sem[e], self.cnt[e]))
        for s, v in self.dma_sems:
            if v > 0:
                evs.append((s, v))
        return evs

    def barrier(self):
        evs = self.all_events()
        for e in ENGS:
            saved = self.no_self_sync
            self.no_self_sync = set()
            waits = self._waits_for(e, evs)
            self.no_self_sync = saved
            if waits:
                self.q[e].append((waits, None, None, 0))

    def finish(self):
        nc = self.nc
        self.barrier()
        q = self.q

        def run(e, items):
            for (waits, fn, sem, inc) in items:
                for (s, v) in waits:
                    e.wait_ge(s, v)
                if fn is None:
                    continue
                ins = fn(e)
                ins.then_inc(sem, inc)

        with nc.Block() as block:
            @block.sync
            def _(e):
                run(e, q["sync"])

            @block.tensor
            def _(e):
                run(e, q["tensor"])

            @block.vector
            def _(e):
                run(e, q["vector"])

            @block.scalar
            def _(e):
                run(e, q["scalar"])

            @block.gpsimd
            def _(e):
                run(e, q["gpsimd"])


class Ring:
    def __init__(self, tiles):
        self.tiles = tiles
        self.bufs = [Buf() for _ in tiles]
        self.i = 0

    def next(self):
        i = self.i
        self.i = (i + 1) % len(self.tiles)
        return self.tiles[i], self.bufs[i]


def bc_mid(ap2, n):
    return ap2.unsqueeze(1).to_broadcast([ap2.shape[0], n, ap2.shape[1]])


def bc_last(ap2, n):
    return ap2.unsqueeze(2).to_broadcast([ap2.shape[0], ap2.shape[1], n])


class Prog:
    def __init__(self, n_layers=DEPTH, stage="full"):
        self.n_layers = n_layers
        self.stage = stage
        self.nc = bass.Bass("TRN2", target_bir_lowering=False)
        self.st = contextlib.ExitStack()

    def dram_in(self, name, shape):
        return self.nc.dram_tensor(name, list(shape), F32, kind="ExternalInput").ap()

    def sb(self, stack, name, shape, dtype):
        self._uid = getattr(self, "_uid", 0) + 1
        return stack.enter_context(self.nc.sbuf_tensor(f"{name}_{self._uid}", list(shape), dtype))

    def make_ring(self, stack, name, shape, dtype, n):
        return Ring([self.sb(stack, f"{name}{i}", shape, dtype) for i in range(n)])

    def xbuf(self, c, tok):
        return self.xT_bufs[c][tok // 512]

    def wload(self, src):
        t, b = self.wring.next()
        kt, cols = src.shape[1], src.shape[2]
        dst = t[:, 0:kt * cols].rearrange("p (k c) -> p k c", c=cols)
        self.k.dma("gpsimd", lambda e: e.dma_start(out=dst, in_=src), writes=[b])
        return dst, b

    class WStream:
        def __init__(self, prog, srcs, lookahead):
            self.p = prog
            self.srcs = srcs
            self.n = 0
            self.loaded = deque()
            self.la = lookahead

        def get(self):
            while self.n < len(self.srcs) and len(self.loaded) < self.la + 1:
                self.loaded.append(self.p.wload(self.srcs[self.n]))
                self.n += 1
            return self.loaded.popleft()

    def build(self):
        nc = self.nc
        st = self.st
        L = DEPTH
        self.xin = self.dram_in("xin", [T, D])
        self.cc = self.dram_in("cc", [2, D])
        self.w_ada = self.dram_in("w_ada", [L, D, 9 * D])
        self.b_ada = self.dram_in("b_ada", [L, 9 * D])
        self.norm_w = self.dram_in("norm_w", [L, 3, D])
        self.wg = self.dram_in("ffn_w_gate", [L, 2, D, FF])
        self.wu = self.dram_in("ffn_w_up", [L, 2, D, FF])
        self.wd = self.dram_in("ffn_w_down", [L, 2, FF, D])
        self.w_in = self.dram_in("w_in", [L, D, INC])
        self.w_out = self.dram_in("w_out", [L, D, D])
        self.s5_lre = self.dram_in("s5_lambda_re", [L, 2, 64, 64])
        self.s5_lim = self.dram_in("s5_lambda_im", [L, 2, 64, 64])
        self.s5_ls = self.dram_in("s5_log_step", [L, 2, 64])
        self.s5_bre = self.dram_in("s5_b_re", [L, 2, 64, 64, 16])
        self.s5_bim = self.dram_in("s5_b_im", [L, 2, 64, 64, 16])
        self.s5_cre = self.dram_in("s5_c_re", [L, 2, 64, 16, 64])
        self.s5_cim = self.dram_in("s5_c_im", [L, 2, 64, 16, 64])
        self.s5_d = self.dram_in("s5_d", [L, 1024])
        self.s5_wglu = self.dram_in("s5_w_glu", [L, 1024, 1024])
        self.s5_bglu = self.dram_in("s5_b_glu", [L, 1024])
        self.hg_lb = self.dram_in("hgrn_lower_bounds", [L, 2, 1024])
        self.hg_nw = self.dram_in("hgrn_norm_w", [L, 128])
        self.fin_w = self.dram_in("final_norm_w", [D])
        self.c_ident = self.dram_in("c_ident", [128, 128])
        self.c_masks = self.dram_in("c_masks", [5, 128, 128])
        self.c_aux = self.dram_in("c_aux", [128, 8 + 257])
        self.c_iota = self.dram_in("c_iota", [128, 513])
        self.Y = nc.dram_tensor("Y", [NLAT, D], F32, kind="ExternalOutput").ap()
        self.xT = nc.dram_tensor("xT_s", [NC_, 128, T], F32).ap()
        self.pT = nc.dram_tensor("pT_s", [6, 8, 128, T], F32).ap()
        self.vtm = nc.dram_tensor("vtm_s", [T, 1024], F32).ap()
        self.gS5 = nc.dram_tensor("gs5_s", [8, 128, T], F32).ap()
        if self.stage in ("hgrn", "s5"):
            self.mixT = nc.dram_tensor("mixT_s", [NC_, 128, T], BF16, kind="ExternalOutput").ap()
        else:
            self.mixT = nc.dram_tensor("mixT_s", [NC_, 128, T], BF16).ap()
        self.xT_bufs = [[Buf() for _ in range(9)] for _ in range(NC_)]
        self.pT_buf = [[Buf() for _ in range(8)] for _ in range(6)]
        self.vtm_buf = Buf()
        self.gS5_buf = [Buf() for _ in range(8)]
        self.mix_buf = [Buf() for _ in range(NC_)]
        self.BTs = nc.dram_tensor("BTs_s", [2, 2, 128, 8, 512], BF16).ap()
        self.CTs = nc.dram_tensor("CTs_s", [2, 2, 128, 32, 128], BF16).ap()

        self.k = K(nc, st)
        k = self.k
        self.ps = [st.enter_context(nc.psum_tensor(f"ps{i}", [128, 512], F32)) for i in range(8)]
        self.psb = [Buf() for _ in range(8)]
        self.ident = self.sb(st, "ident", [128, 128], F32)
        self.onesD = self.sb(st, "onesD", [128, 128], F32)
        self.ones128 = self.sb(st, "ones128", [128, 128], F32)
        self.masks = self.sb(st, "masks", [128, 5, 128], F32)
        self.aux = self.sb(st, "aux", [128, 8 + 257], F32)
        self.modT = self.sb(st, "modT", [128, 2, 144], F32)
        self.Amod = self.sb(st, "Amod", [128, 2, 3, 16], F32)
        self.Gmod = self.sb(st, "Gmod", [128, 2, 3, 16], F32)
        self.wfin = self.sb(st, "wfin", [128, 16], F32)
        self.b_const = Buf()
        self.b_mod = Buf()
        self.ident_bf = self.sb(st, "identbf", [128, 128], BF16)
        self.onesD_bf = self.sb(st, "onesDbf", [128, 128], BF16)

        with nc.allow_non_contiguous_dma("small parameter vectors are laid out feature-on-partition"):
            self.emit()
            k.finish()
        st.close()
        return nc

    def emit(self):
        k = self.k
        k.dma("sync", lambda e: e.dma_start(out=self.ident[:], in_=self.c_ident[:, :]), writes=[self.b_const])
        k.dma("sync", lambda e: e.dma_start(out=self.masks[:], in_=self.c_masks.rearrange("m p f -> p m f")),
              writes=[self.b_const])
        k.dma("sync", lambda e: e.dma_start(out=self.wfin[:], in_=self.fin_w.rearrange("(c p) -> p c", p=128)),
              writes=[self.b_const])
        k.dma("sync", lambda e: e.dma_start(out=self.aux[:], in_=self.c_aux[:, :]), writes=[self.b_const])
        k.op("vector", lambda e: e.memset(self.onesD[:], 1.0 / D), writes=[self.b_const])
        k.op("vector", lambda e: e.memset(self.ones128[:], 1.0 / 128), writes=[self.b_const])
        k.op("vector", lambda e: e.tensor_copy(out=self.ident_bf[:], in_=self.ident[:]), reads=[self.b_const], writes=[self.b_const])
        k.op("vector", lambda e: e.tensor_copy(out=self.onesD_bf[:], in_=self.onesD[:]), reads=[self.b_const], writes=[self.b_const])
        self.input_phase()
        for l in range(self.n_layers):
            last = (l == DEPTH - 1)
            self.mods_phase(l)
            self.ffn_phase(l, 0, 0, include_ctx=True)
            if self.stage == "ffn1":
                break
            self.inproj_phase(l)
            self.hgrn_phase(l, need_ctx=not last)
            if self.stage == "hgrn":
                break
            self.s5_phase(l, need_ctx=not last)
            self.glu_phase(l, need_ctx=not last)
            if self.stage == "s5":
                break
            self.outproj_phase(l, include_ctx=not last)
            self.ffn_phase(l, 1, 2, include_ctx=not last)
        self.output_phase(apply_norm=(self.stage == "full"))

    def supertiles(self, include_ctx, ts=1024):
        out = [(t0, ts, 0) for t0 in range(0, NLAT, ts)]
        if include_ctx:
            out.append((NLAT, NCTX, 1))
        return out

    def input_phase(self):
        k = self.k
        with contextlib.ExitStack() as ph:
            xtok = self.make_ring(ph, "xtok", [128, D], F32, 2)
            xst = self.make_ring(ph, "xsti", [128, 16, 128], F32, 2)
            xv = self.xT.rearrange("c p t -> p c t")
            ei = 0
            for ti in range(T // 128):
                tok = ti * 128
                xt, bxt = xtok.next()
                k.dma("sync", lambda e, xt=xt, tok=tok: e.dma_start(out=xt[:], in_=self.xin[tok:tok + 128, :]), writes=[bxt])
                xs, bxs = xst.next()
                for g in range(4):
                    pb = 4 + g % 4
                    for j in range(4):
                        c = g * 4 + j
                        k.op("tensor", lambda e, pb=pb, j=j, c=c, xt=xt: e.transpose(
                            out=self.ps[pb][:, j * 128:(j + 1) * 128], in_=xt[:, c * 128:(c + 1) * 128], identity=self.ident[:]),
                            reads=[bxt, self.b_const], writes=[self.psb[pb]])
                    dst = xs[:, g * 4:(g + 1) * 4, :]
                    src = self.ps[pb][:, :].rearrange("p (j t) -> p j t", t=128)
                    if ei % 2 == 0:
                        k.op("scalar", lambda e, dst=dst, src=src: e.copy(out=dst, in_=src), reads=[self.psb[pb]], writes=[bxs])
                    else:
                        k.op("vector", lambda e, dst=dst, src=src: e.tensor_copy(out=dst, in_=src), reads=[self.psb[pb]], writes=[bxs])
                    ei += 1
                k.dma("sync", lambda e, xs=xs, tok=tok: e.dma_start(out=xv[:, :, tok:tok + 128], in_=xs[:]),
                      reads=[bxs], writes=[self.xbuf(c, tok) for c in range(NC_)])
            k.barrier()

    def mods_phase(self, l):
        k = self.k
        with contextlib.ExitStack() as ph:
            self.wring = self.make_ring(ph, "wr", [128, 4096], BF16, 5)
            ccs = self.sb(ph, "ccs", [128, 2, 16], F32)
            scb = self.sb(ph, "scb", [128, 2, 16], BF16)
            bada = self.sb(ph, "bada", [128, 144], F32)
            nwt = self.sb(ph, "nwt", [128, 3, 16], F32)
            b_cc, b_sc, b_ba, b_nw = Buf(), Buf(), Buf(), Buf()
            k.dma("sync", lambda e: e.dma_start(out=ccs[:], in_=self.cc.rearrange("s (kt p) -> p s kt", p=128)), writes=[b_cc])
            k.op("scalar", lambda e: e.activation(out=scb[:], in_=ccs[:], func=AF.Silu), reads=[b_cc], writes=[b_sc])
            k.dma("sync", lambda e: e.dma_start(out=bada[:], in_=self.b_ada[l].rearrange("(j p) -> p j", p=128)), writes=[b_ba])
            k.dma("sync", lambda e: e.dma_start(out=nwt[:], in_=self.norm_w[l].rearrange("i (c p) -> p i c", p=128)), writes=[b_nw])
            wv = self.w_ada[l].rearrange("(kt p) c -> p kt c", p=128)
            stream = Prog.WStream(self, [wv[:, :, jb * 256:(jb + 1) * 256] for jb in range(72)], 3)
            pm = self.ps[7]
            for jb in range(72):
                wt, wb = stream.get()
                for sub in range(2):
                    j = jb * 2 + sub
                    for kt in range(16):
                        k.op("tensor", lambda e, wt=wt, sub=sub, j=j, kt=kt: e.matmul(
                            pm[:, 2 * j:2 * j + 2], lhsT=wt[:, kt, sub * 128:(sub + 1) * 128], rhs=scb[:, :, kt],
                            start=(kt == 0), stop=(kt == 15)), reads=[wb, b_sc], writes=[self.psb[7]])
            for s in range(2):
                k.op("vector", lambda e, s=s: e.tensor_tensor(out=self.modT[:, s, :], in0=pm[:, s:288:2], in1=bada[:], op=ALU.add),
                     reads=[self.psb[7], b_ba], writes=[self.b_mod])
            for s in range(2):
                for i3 in range(3):
                    sc_ = self.modT[:, s, (3 * i3 + 1) * 16:(3 * i3 + 2) * 16]
                    gt_ = self.modT[:, s, (3 * i3 + 2) * 16:(3 * i3 + 3) * 16]
                    k.op("vector", lambda e, s=s, i3=i3, sc_=sc_: e.scalar_tensor_tensor(
                        out=self.Amod[:, s, i3, :], in0=sc_, scalar=1.0, in1=nwt[:, i3, :], op0=ALU.add, op1=ALU.mult),
                        reads=[b_nw, self.b_mod], writes=[self.b_mod])
                    k.op("vector", lambda e, s=s, i3=i3, gt_=gt_: e.tensor_scalar(
                        out=self.Gmod[:, s, i3, :], in0=gt_, scalar1=(1.0 if i3 == 1 else 0.5), scalar2=None, op0=ALU.mult),
                        reads=[self.b_mod], writes=[self.b_mod])
            k.barrier()

    def norm_piece(self, tok, xring, sqring, rsring, dst, bdst, A_ap, B_ap, psn=0):
        k = self.k
        xv = self.xT.rearrange("c p t -> p c t")
        xs, bx = xring.next()
        k.dma("sync", lambda e: e.dma_start(out=xs[:], in_=xv[:, :, tok:tok + 128]),
              reads=[self.xbuf(c, tok) for c in range(NC_)], writes=[bx])
        sq, bs = sqring.next()
        k.op("scalar", lambda e: e.activation(out=sq[:], in_=xs[:], func=AF.Square), reads=[bx], writes=[bs])
        pn = self.ps[psn]
        for c in range(NC_):
            k.op("tensor", lambda e, c=c: e.matmul(pn[:, 0:128], lhsT=self.onesD_bf[:], rhs=sq[:, c, :], start=(c == 0), stop=(c == NC_ - 1)),
                 reads=[bs, self.b_const], writes=[self.psb[psn]])
        rs, brs = rsring.next()
        k.op("vector", lambda e: e.tensor_scalar(out=rs[:], in0=pn[:, 0:128], scalar1=EPS, scalar2=None, op0=ALU.add),
             reads=[self.psb[psn]], writes=[brs])
        k.op("scalar", lambda e: e.activation(out=rs[:], in_=rs[:], func=AF.Sqrt), reads=[brs], writes=[brs])
        k.op("vector", lambda e: e.reciprocal(out=rs[:], in_=rs[:]), reads=[brs], writes=[brs])
        k.op("vector", lambda e: e.tensor_tensor(out=xs[:], in0=xs[:], in1=bc_mid(rs[:], NC_), op=ALU.mult),
             reads=[brs, bx], writes=[bx])
        if B_ap is None:
            k.op("vector", lambda e: e.tensor_tensor(out=dst, in0=xs[:], in1=bc_last(A_ap, 128), op=ALU.mult),
                 reads=[bx, self.b_mod, self.b_const], writes=[bdst])
        else:
            k.op("vector", lambda e: e.tensor_tensor(out=xs[:], in0=xs[:], in1=bc_last(A_ap, 128), op=ALU.mult),
                 reads=[bx, self.b_mod], writes=[bx])
            k.op("vector", lambda e: e.tensor_tensor(out=dst, in0=xs[:], in1=bc_last(B_ap, 128), op=ALU.add),
                 reads=[bx, self.b_mod], writes=[bdst])

    def ffn_phase(self, l, fi, i3, include_ctx):
        k = self.k
        with contextlib.ExitStack() as ph:
            self.wring = self.make_ring(ph, "wr", [128, 4096], BF16, 5)
            hT = self.sb(ph, "hT", [128, 16, 1024], BF16)
            a = self.sb(ph, "aT", [128, NFF, 1024], BF16)
            b_h = [Buf(), Buf()]
            b_a = [Buf(), Buf()]
            xring = self.make_ring(ph, "fx", [128, 16, 128], F32, 2)
            sqring = self.make_ring(ph, "fsq", [128, 16, 128], BF16, 2)
            rsring = self.make_ring(ph, "frs", [128, 128], F32, 2)
            slring = self.make_ring(ph, "fsl", [128, 512], F32, 2)
            xrring = self.make_ring(ph, "fxr", [128, 512], F32, 3)
            wgv = self.wg[l, fi].rearrange("(kt p) c -> p kt c", p=128)
            wuv = self.wu[l, fi].rearrange("(kt p) c -> p kt c", p=128)
            wdv = self.wd[l, fi].rearrange("(kt p) c -> p kt c", p=128)
            psg = Ring([1, 2]); psu = Ring([3, 4]); psd = Ring([5, 6])
            sts = self.supertiles(include_ctx)
            srcs = []
            for _ in sts:
                for jb in range(22):
                    cols = 256 if jb < 21 else 128
                    srcs.append(wgv[:, :, jb * 256:jb * 256 + cols])
                    srcs.append(wuv[:, :, jb * 256:jb * 256 + cols])
                for m in range(16):
                    srcs.append(wdv[:, 0:22, m * 128:(m + 1) * 128])
                    srcs.append(wdv[:, 22:43, m * 128:(m + 1) * 128])
            stream = Prog.WStream(self, srcs, 3)

            def norm_thunks(si):
                t0_, ts_, s_ = sts[si]
                A_ap = self.Amod[:, s_, i3, :]
                B_ap = self.modT[:, s_, (3 * i3) * 16:(3 * i3 + 1) * 16]
                return deque([(lambda pc=pc: self.norm_piece(t0_ + pc * 128, xring, sqring, rsring, hT[:, :, pc * 128:(pc + 1) * 128],
                                                             b_h[(pc * 128) // 512], A_ap, B_ap)) for pc in range(ts_ // 128)])
            for th in norm_thunks(0):
                th()
            for si, (t0, ts, s) in enumerate(sts):
                nh = max(1, ts // 512)
                n = min(512, ts)
                nxt = norm_thunks(si + 1) if si + 1 < len(sts) else deque()
                for jb in range(22):
                    cols = 256 if jb < 21 else 128
                    gt, gb = stream.get()
                    ut, ub = stream.get()
                    for sub in range(cols // 128):
                        j = jb * 2 + sub
                        for hf in range(nh):
                            tsl = slice(hf * 512, hf * 512 + n)
                            pgi, _ = psg.next(); pui, _ = psu.next()
                            pg, pu = self.ps[pgi], self.ps[pui]
                            for kt in range(16):
                                k.op("tensor", lambda e, pg=pg, gt=gt, kt=kt, sub=sub, tsl=tsl, n=n: e.matmul(
                                    pg[:, 0:n], lhsT=gt[:, kt, sub * 128:(sub + 1) * 128], rhs=hT[:, kt, tsl],
                                    start=(kt == 0), stop=(kt == 15)), reads=[gb, b_h[hf]], writes=[self.psb[pgi]])
                            for kt in range(16):
                                k.op("tensor", lambda e, pu=pu, ut=ut, kt=kt, sub=sub, tsl=tsl, n=n: e.matmul(
                                    pu[:, 0:n], lhsT=ut[:, kt, sub * 128:(sub + 1) * 128], rhs=hT[:, kt, tsl],
                                    start=(kt == 0), stop=(kt == 15)), reads=[ub, b_h[hf]], writes=[self.psb[pui]])
                            sl, bsl = slring.next()
                            k.op("scalar", lambda e, sl=sl, pg=pg, n=n: e.activation(out=sl[:, 0:n], in_=pg[:, 0:n], func=AF.Silu),
                                 reads=[self.psb[pgi]], writes=[bsl])
                            k.op("vector", lambda e, sl=sl, pu=pu, j=j, tsl=tsl, n=n: e.tensor_tensor(
                                out=a[:, j, tsl], in0=sl[:, 0:n], in1=pu[:, 0:n], op=ALU.mult),
                                reads=[bsl, self.psb[pui]], writes=[b_a[hf]])
                for m in range(16):
                    w0, b0 = stream.get()
                    w1, b1 = stream.get()
                    for hf in range(nh):
                        tsl = slice(hf * 512, hf * 512 + n)
                        tok0 = t0 + hf * 512
                        xr, bxr = xrring.next()
                        k.dma("sync", lambda e, xr=xr, m=m, tok0=tok0, n=n: e.dma_start(out=xr[:, 0:n], in_=self.xT[m, :, tok0:tok0 + n]),
                              reads=[self.xbuf(m, tok0)], writes=[bxr])
                        pdi, _ = psd.next()
                        pd = self.ps[pdi]
                        for kt in range(NFF):
                            wt = w0[:, kt, :] if kt < 22 else w1[:, kt - 22, :]
                            k.op("tensor", lambda e, pd=pd, wt=wt, kt=kt, tsl=tsl, n=n: e.matmul(
                                pd[:, 0:n], lhsT=wt, rhs=a[:, kt, tsl], start=(kt == 0), stop=(kt == NFF - 1)),
                                reads=[b0, b1, b_a[hf]], writes=[self.psb[pdi]])
                        k.op("vector", lambda e, xr=xr, pd=pd, m=m, s=s, n=n: e.scalar_tensor_tensor(
                            out=xr[:, 0:n], in0=pd[:, 0:n], scalar=self.Gmod[:, s, i3, m:m + 1], in1=xr[:, 0:n],
                            op0=ALU.mult, op1=ALU.add), reads=[self.psb[pdi], bxr, self.b_mod], writes=[bxr])
                        k.dma("sync", lambda e, xr=xr, m=m, tok0=tok0, n=n: e.dma_start(out=self.xT[m, :, tok0:tok0 + n], in_=xr[:, 0:n]),
                              reads=[bxr], writes=[self.xbuf(m, tok0)])
                    if nxt:
                        nxt.popleft()()
                while nxt:
                    nxt.popleft()()
            k.barrier()

    def output_phase(self, apply_norm):
        k = self.k
        with contextlib.ExitStack() as ph:
            xring = self.make_ring(ph, "ox", [128, 16, 128], F32, 2)
            sqring = self.make_ring(ph, "osq", [128, 16, 128], BF16, 2)
            rsring = self.make_ring(ph, "ors", [128, 128], F32, 2)
            yst = self.make_ring(ph, "oy", [128, 16, 128], F32, 2)
            ytok = self.make_ring(ph, "oyt", [128, D], F32, 2)
            xv = self.xT.rearrange("c p t -> p c t")
            ei = 0
            for ti in range(NLAT // 128):
                tok = ti * 128
                ys, bys = yst.next()
                if apply_norm:
                    self.norm_piece(tok, xring, sqring, rsring, ys[:], bys, self.wfin[:], None)
                else:
                    k.dma("sync", lambda e, ys=ys, tok=tok: e.dma_start(out=ys[:], in_=xv[:, :, tok:tok + 128]),
                          reads=[self.xbuf(c, tok) for c in range(NC_)], writes=[bys])
                yt, byt = ytok.next()
                for g in range(4):
                    pb = 4 + g % 4
                    for j in range(4):
                        c = g * 4 + j
                        k.op("tensor", lambda e, pb=pb, j=j, c=c, ys=ys: e.transpose(
                            out=self.ps[pb][:, j * 128:(j + 1) * 128], in_=ys[:, c, :], identity=self.ident[:]),
                            reads=[bys, self.b_const], writes=[self.psb[pb]])
                    dst = yt[:, g * 512:(g + 1) * 512]
                    src = self.ps[pb][:, :]
                    if ei % 2 == 0:
                        k.op("scalar", lambda e, dst=dst, src=src: e.copy(out=dst, in_=src), reads=[self.psb[pb]], writes=[byt])
                    else:
                        k.op("vector", lambda e, dst=dst, src=src: e.tensor_copy(out=dst, in_=src), reads=[self.psb[pb]], writes=[byt])
                    ei += 1
                k.dma("sync", lambda e, yt=yt, tok=tok: e.dma_start(out=self.Y[tok:tok + 128, :], in_=yt[:]), reads=[byt])
            k.barrier()

    def inproj_phase(self, l):
        k = self.k
        with contextlib.ExitStack() as ph:
            self.wring = self.make_ring(ph, "wr", [128, 4096], BF16, 5)
            hTs = [self.sb(ph, f"ihT{i}", [128, 16, 1024], BF16) for i in range(2)]
            b_hs = [[Buf(), Buf()], [Buf(), Buf()]]
            xring = self.make_ring(ph, "ix", [128, 16, 128], F32, 2)
            sqring = self.make_ring(ph, "isq", [128, 16, 128], BF16, 2)
            rsring = self.make_ring(ph, "irs", [128, 128], F32, 2)
            evring = self.make_ring(ph, "iev", [128, 512], F32, 4)
            wv = self.w_in[l].rearrange("(kt p) c -> p kt c", p=128)
            psr = Ring([1, 2, 3, 4, 5, 6])
            ei = 0
            sts = self.supertiles(True)
            stream = Prog.WStream(self, [wv[:, :, bi * 256:(bi + 1) * 256] for _ in sts for bi in range(24)], 3)

            def norm_thunks(si):
                t0_, ts_, s_ = sts[si]
                A_ap = self.Amod[:, s_, 1, :]
                B_ap = self.modT[:, s_, 3 * 16:4 * 16]
                hT_ = hTs[si % 2]
                b_h_ = b_hs[si % 2]
                return deque([(lambda pc=pc: self.norm_piece(t0_ + pc * 128, xring, sqring, rsring, hT_[:, :, pc * 128:(pc + 1) * 128],
                                                             b_h_[(pc * 128) // 512], A_ap, B_ap)) for pc in range(ts_ // 128)])
            for th in norm_thunks(0):
                th()
            for si, (t0, ts, s) in enumerate(sts):
                hT = hTs[si % 2]
                b_h = b_hs[si % 2]
                nh = max(1, ts // 512)
                n = min(512, ts)
                nxt = norm_thunks(si + 1) if si + 1 < len(sts) else deque()
                for bi in range(24):
                    if bi % 2 == 1 and nxt:
                        nxt.popleft()()
                    wt, wb = stream.get()
                    fam = bi // 4
                    if fam != 4:
                        for sub in range(2):
                            ch = (bi % 4) * 2 + sub
                            for hf in range(nh):
                                tsl = slice(hf * 512, hf * 512 + n)
                                tok0 = t0 + hf * 512
                                pi, _ = psr.next()
                                pp = self.ps[pi]
                                for kt in range(16):
                                    k.op("tensor", lambda e, pp=pp, wt=wt, kt=kt, sub=sub, tsl=tsl, n=n, hT=hT: e.matmul(
                                        pp[:, 0:n], lhsT=wt[:, kt, sub * 128:(sub + 1) * 128], rhs=hT[:, kt, tsl],
                                        start=(kt == 0), stop=(kt == 15)), reads=[wb, b_h[hf]], writes=[self.psb[pi]])
                                ev, bev = evring.next()
                                if fam in (1, 5):
                                    k.op("scalar", lambda e, ev=ev, pp=pp, n=n: e.activation(out=ev[:, 0:n], in_=pp[:, 0:n], func=AF.Silu),
                                         reads=[self.psb[pi]], writes=[bev])
                                elif ei % 2 == 0:
                                    k.op("scalar", lambda e, ev=ev, pp=pp, n=n: e.copy(out=ev[:, 0:n], in_=pp[:, 0:n]),
                                         reads=[self.psb[pi]], writes=[bev])
                                else:
                                    k.op("vector", lambda e, ev=ev, pp=pp, n=n: e.tensor_copy(out=ev[:, 0:n], in_=pp[:, 0:n]),
                                         reads=[self.psb[pi]], writes=[bev])
                                ei += 1
                                k.dma("sync", lambda e, ev=ev, fam=fam, ch=ch, tok0=tok0, n=n: e.dma_start(
                                    out=self.pT[fam, ch, :, tok0:tok0 + n], in_=ev[:, 0:n]), reads=[bev])
                    else:
                        for tt in range(ts // 128):
                            tok0 = t0 + tt * 128
                            pi, _ = psr.next()
                            pp = self.ps[pi]
                            for kt in range(16):
                                k.op("tensor", lambda e, pp=pp, wt=wt, kt=kt, tt=tt, hT=hT: e.matmul(
                                    pp[:, 0:256], lhsT=hT[:, kt, tt * 128:(tt + 1) * 128], rhs=wt[:, kt, :],
                                    start=(kt == 0), stop=(kt == 15)), reads=[wb, b_h[(tt * 128) // 512]], writes=[self.psb[pi]])
                            ev, bev = evring.next()
                            if ei % 2 == 0:
                                k.op("scalar", lambda e, ev=ev, pp=pp: e.copy(out=ev[:, 0:256], in_=pp[:, 0:256]),
                                     reads=[self.psb[pi]], writes=[bev])
                            else:
                                k.op("vector", lambda e, ev=ev, pp=pp: e.tensor_copy(out=ev[:, 0:256], in_=pp[:, 0:256]),
                                     reads=[self.psb[pi]], writes=[bev])
                            ei += 1
                            c0 = (bi % 4) * 256
                            k.dma("sync", lambda e, ev=ev, tok0=tok0, c0=c0: e.dma_start(
                                out=self.vtm[tok0:tok0 + 128, c0:c0 + 256], in_=ev[:, 0:256]), reads=[bev])
                while nxt:
                    nxt.popleft()()
            k.barrier()

    def hgrn_phase(self, l, need_ctx):
        k = self.k
        NT = T // 128
        NCH = T // 32
        with contextlib.ExitStack() as ph:
            W = [self.sb(ph, f"hW{i}", [128, T], F32) for i in range(5)]
            bW = [Buf() for _ in range(5)]
            qh = [self.sb(ph, f"hqh{d}", [128, T], BF16) for d in range(2)]
            kt_ = [self.sb(ph, f"hkt{d}", [128, T], BF16) for d in range(2)]
            b_qh = [Buf(), Buf()]
            b_kt = [Buf(), Buf()]
            kh = self.sb(ph, "hkh", [128, T], BF16)
            b_kh = Buf()
            khtm = [self.sb(ph, f"hkhtm{d}", [128, NT, 128], BF16) for d in range(2)]
            b_khtm = [Buf(), Buf()]
            khtm3 = [self.sb(ph, f"hkhtm3{d}", [128, NT, 128], BF16) for d in range(2)]
            V = self.sb(ph, "hV", [128, NT, 128], BF16)
            b_V = Buf()
            dec = [self.sb(ph, f"hdec{d}", [128, NCH], F32) for d in range(2)]
            b_dec = [Buf(), Buf()]
            cm = self.sb(ph, "hcm", [128, T], BF16)
            hb = self.sb(ph, "hhb", [128, 2, 2, 8], F32)
            lbt = self.sb(ph, "hlbt", [128, 2, 8], F32)
            oml = self.sb(ph, "homl", [128, 2, 8], F32)
            nwh = self.sb(ph, "hnwh", [128, 1], F32)
            b_par = Buf()
            S32 = [self.sb(ph, f"hS32{d}", [128, 2, 128], F32) for d in range(2)]
            Sring = [self.sb(ph, f"hSr{d}", [128, 4, 128], BF16) for d in range(2)]
            b_S32 = [[Buf(), Buf()], [Buf(), Buf()]]
            b_Sr = [[Buf() for _ in range(4)] for _ in range(2)]
            psT = self.ps[7][:].bitcast(BF16)

            k.op("vector", lambda e: e.memset(cm[:], 1.0), writes=[b_par])
            k.op("vector", lambda e: e.memset(cm[:, 0:T:32], 0.0), writes=[b_par])
            k.dma("sync", lambda e: e.dma_start(out=hb[:], in_=self.hg_lb.rearrange("l d (h p) -> p l d h", p=128)), writes=[b_par])
            k.dma("sync", lambda e: e.dma_start(out=nwh[:], in_=self.hg_nw[l].rearrange("(p o) -> p o", o=1)), writes=[b_par])
            if l == 0:
                k.op("vector", lambda e: e.memset(lbt[:], 0.0), writes=[b_par])
            else:
                k.op("vector", lambda e: e.tensor_tensor(out=lbt[:], in0=hb[:, 1], in1=hb[:, 0], op=ALU.subtract), reads=[b_par], writes=[b_par])
                k.op("scalar", lambda e: e.activation(out=lbt[:], in_=lbt[:], func=AF.Sigmoid), reads=[b_par], writes=[b_par])
            k.op("vector", lambda e: e.tensor_scalar(out=oml[:], in0=lbt[:], scalar1=-1.0, scalar2=1.0, op0=ALU.mult, op1=ALU.add),
                 reads=[b_par], writes=[b_par])

            def cmv(ap):
                return ap[:, 0:NLAT].rearrange("p (c r) -> p c r", r=GRID)

            def rasv_as_cr(ap):
                return ap[:, 0:NLAT].rearrange("p (r c) -> p c r", c=GRID)

            for hd in range(8):
                vsrc = self.vtm[0:NLAT, hd * 128:(hd + 1) * 128].rearrange("(r ti c2) v -> c2 r ti v", ti=32, c2=2)
                for c2 in range(2):
                    k.dma("gpsimd", lambda e, c2=c2, vsrc=vsrc: e.dma_start(out=V[c2 * 64:(c2 + 1) * 64, 0:32, :], in_=vsrc[c2]), writes=[b_V])
                vsrc2 = self.vtm[NLAT:T, hd * 128:(hd + 1) * 128].rearrange("(ti p) v -> p ti v", p=128)
                k.dma("gpsimd", lambda e, vsrc2=vsrc2: e.dma_start(out=V[:, 32:34, :], in_=vsrc2), writes=[b_V])
                for d in range(2):
                    k.dma("sync", lambda e, hd=hd, d=d: e.dma_start(out=W[4][:], in_=self.pT[2 + d, hd, :, :]), writes=[bW[4]])
                    k.op("scalar", lambda e: e.activation(out=cmv(W[0]), in_=rasv_as_cr(W[4]), func=AF.Sigmoid), reads=[bW[4]], writes=[bW[0]])
                    k.op("scalar", lambda e: e.activation(out=W[0][:, NLAT:T], in_=W[4][:, NLAT:T], func=AF.Sigmoid), reads=[bW[4]], writes=[bW[0]])
                    k.op("vector", lambda e, d=d, hd=hd: e.tensor_scalar(out=W[0][:], in0=W[0][:], scalar1=oml[:, d, hd:hd + 1],
                                                                       scalar2=lbt[:, d, hd:hd + 1], op0=ALU.mult, op1=ALU.add),
                         reads=[b_par, bW[0]], writes=[bW[0]])
                    k.op("gpsimd", lambda e: e.tensor_scalar(out=W[1][:], in0=W[0][:], scalar1=-1.0, scalar2=1.0, op0=ALU.mult, op1=ALU.add),
                         reads=[bW[0]], writes=[bW[1]])
                    k.op("vector", lambda e: e.tensor_scalar(out=W[0][:], in0=W[0][:], scalar1=1e-6, scalar2=None, op0=ALU.max),
                         reads=[bW[0], bW[1]], writes=[bW[0]])
                    k.op("scalar", lambda e: e.activation(out=W[0][:], in_=W[0][:], func=AF.Ln), reads=[bW[0]], writes=[bW[0]])
                    if d == 0:
                        k.op("vector", lambda e: e.tensor_tensor_scan(out=W[2][:], data0=cm[:], data1=W[0][:], initial=0.0,
                                                                      op0=ALU.mult, op1=ALU.add), reads=[bW[0], b_par], writes=[bW[2]])
                    else:
                        k.op("vector", lambda e: e.tensor_tensor_scan(out=W[2][:, ::-1], data0=cm[:], data1=W[0][:, ::-1], initial=0.0,
                                                                      op0=ALU.mult, op1=ALU.add), reads=[bW[0], b_par], writes=[bW[2]])
                    k.op("scalar", lambda e: e.activation(out=W[3][:], in_=W[2][:], func=AF.Exp), reads=[bW[2]], writes=[bW[3]])
                    k.dma("sync", lambda e, hd=hd: e.dma_start(out=W[4][:], in_=self.pT[1, hd, :, :]), reads=[bW[4]], writes=[bW[4]])
                    k.op("vector", lambda e, d=d: e.tensor_tensor(out=cmv(qh[d]), in0=rasv_as_cr(W[4]), in1=cmv(W[3]), op=ALU.mult),
                         reads=[bW[4], bW[3]], writes=[b_qh[d]])
                    k.op("vector", lambda e, d=d: e.tensor_tensor(out=qh[d][:, NLAT:T], in0=W[4][:, NLAT:T], in1=W[3][:, NLAT:T], op=ALU.mult),
                         reads=[bW[4], bW[3]], writes=[b_qh[d]])
                    k.op("gpsimd", lambda e: e.tensor_scalar(out=W[0][:], in0=W[2][:], scalar1=-60.0, scalar2=None, op0=ALU.max),
                         reads=[bW[2], bW[0]], writes=[bW[0]])
                    k.op("scalar", lambda e: e.activation(out=W[0][:], in_=W[0][:], func=AF.Exp, scale=-1.0), reads=[bW[0]], writes=[bW[0]])
                    k.op("vector", lambda e: e.tensor_tensor(out=W[4][:], in0=W[1][:], in1=W[0][:], op=ALU.mult),
                         reads=[bW[1], bW[0], bW[4]], writes=[bW[4]])
                    k.op("scalar", lambda e, d=d: e.copy(out=kt_[d][:], in_=W[4][:]), reads=[bW[4]], writes=[b_kt[d]])
                    glast = W[2][:, (31 if d == 0 else 0):T:32]
                    k.op("scalar", lambda e, d=d, glast=glast: e.activation(out=dec[d][:], in_=glast, func=AF.Exp), reads=[bW[2]], writes=[b_dec[d]])
                    k.op("vector", lambda e, d=d: e.tensor_tensor(out=kh[:].rearrange("p (n j) -> p n j", j=32),
                                                                  in0=W[4][:].rearrange("p (n j) -> p n j", j=32),
                                                                  in1=bc_last(dec[d][:], 32), op=ALU.mult),
                         reads=[bW[4], b_dec[d]], writes=[b_kh])
                    for g0 in range(0, NT, 8):
                        ng = min(8, NT - g0)
                        for j in range(ng):
                            ti = g0 + j
                            k.op("tensor", lambda e, j=j, ti=ti: e.transpose(out=psT[:, j * 128:(j + 1) * 128], in_=kh[:, ti * 128:(ti + 1) * 128],
                                                                             identity=self.ident_bf[:]),
                                 reads=[b_kh, self.b_const], writes=[self.psb[7]])
                        k.op("vector", lambda e, d=d, g0=g0, ng=ng: e.tensor_copy(
                            out=khtm[d][:, g0:g0 + ng, :], in_=psT[:, 0:ng * 128].rearrange("p (n j) -> p n j", j=128)),
                            reads=[self.psb[7]], writes=[b_khtm[d]])
                        k.op("scalar", lambda e, d=d, g0=g0, ng=ng: e.activation(
                            out=khtm3[d][:, g0:g0 + ng, :], in_=psT[:, 0:ng * 128].rearrange("p (n j) -> p n j", j=128),
                            func=AF.Copy, scale=self.masks[:, 4, 0:1]),
                            reads=[self.psb[7], self.b_const], writes=[b_khtm[d]])
                sm_all = [W[1][:].bitcast(BF16)[:, 0:NT * 128].rearrange("p (n j) -> p n j", j=128),
                          W[2][:].bitcast(BF16)[:, 0:NT * 128].rearrange("p (n j) -> p n j", j=128)]
                b_sm = [bW[1], bW[2]]
                for d in range(2):
                    for ti in range(NT):
                        pi = 1 + (ti % 2)
                        k.op("tensor", lambda e, d=d, ti=ti, pi=pi: e.matmul(
                            self.ps[pi][:, 0:128], lhsT=kt_[d][:, ti * 128:(ti + 1) * 128], rhs=qh[d][:, ti * 128:(ti + 1) * 128],
                            start=True, stop=True), reads=[b_kt[d], b_qh[d]], writes=[self.psb[pi]])
                        k.op("vector", lambda e, d=d, ti=ti, pi=pi: e.tensor_tensor(
                            out=sm_all[d][:, ti, :], in0=self.ps[pi][:, 0:128], in1=self.masks[:, d, :], op=ALU.mult),
                            reads=[self.psb[pi], self.b_const], writes=[b_sm[d]])
                order = [[32, 33] + list(range(32)), [33, 32] + list(range(31, -1, -1))]
                seq = [[], []]
                for d in range(2):
                    for ti in order[d]:
                        for cj in ([0, 1, 2, 3] if d == 0 else [3, 2, 1, 0]):
                            seq[d].append((ti, cj))
                NSEQ = len(seq[0])
                NR = 4
                DS_BANK = [[5, 1], [6, 2]]
                psd_b = [[Buf(), Buf()], [Buf(), Buf()]]
                for d in range(2):
                    k.op("vector", lambda e, d=d: e.memset(S32[d][:, 0, :], 0.0), reads=[b_S32[d][0]], writes=[b_S32[d][0]])
                    k.op("vector", lambda e, d=d: e.memset(Sring[d][:, 0, :], 0.0), reads=[b_Sr[d][0]], writes=[b_Sr[d][0]])
                oacc = W[0]
                b_o = bW[0]
                visited = set()

                def emit_dS(d, i):
                    ti, cj = seq[d][i]
                    par = i % 2
                    pdv = self.ps[DS_BANK[d][par]][:, 0:128]
                    if cj < 3:
                        k.op("tensor", lambda e: e.matmul(
                            pdv, lhsT=khtm[d][cj * 32:(cj + 1) * 32, ti, :], rhs=V[cj * 32:(cj + 1) * 32, ti, :], start=True, stop=True),
                            reads=[b_khtm[d], b_V], writes=[psd_b[d][par]])
                    else:
                        k.op("tensor", lambda e: e.matmul(
                            pdv, lhsT=khtm3[d][:, ti, :], rhs=V[:, ti, :], start=True, stop=True),
                            reads=[b_khtm[d], b_V], writes=[psd_b[d][par]])

                def emit_update(d, i):
                    ti, cj = seq[d][i]
                    par = i % 2
                    pdv = self.ps[DS_BANK[d][par]][:, 0:128]
                    cidx = ti * 4 + cj
                    slot = (i + 1) % NR
                    so, sn_ = i % 2, (i + 1) % 2
                    k.op("vector", lambda e: e.scalar_tensor_tensor(
                        out=S32[d][:, sn_, :], in0=S32[d][:, so, :], scalar=dec[d][:, cidx:cidx + 1], in1=pdv, op0=ALU.mult, op1=ALU.add),
                        reads=[psd_b[d][par], b_dec[d], b_S32[d][so]], writes=[b_S32[d][sn_]])
                    k.op("scalar", lambda e: e.copy(out=Sring[d][:, slot, :], in_=S32[d][:, sn_, :]),
                         reads=[b_S32[d][sn_], b_Sr[d][slot]], writes=[b_Sr[d][slot]])

                for d in range(2):
                    emit_dS(d, 0)
                for i in range(NSEQ):
                    for d in range(2):
                        ti, cj = seq[d][i]
                        po = self.ps[3 + d]
                        b_po = self.psb[3 + d]
                        first = (i % 4 == 0)
                        lastc = (i % 4 == 3)
                        if first:
                            k.op("tensor", lambda e, d=d, ti=ti, po=po: e.matmul(
                                po[:, 0:128], lhsT=V[:, ti, :], rhs=sm_all[d][:, ti, :], start=True, stop=False),
                                reads=[b_V, b_sm[d]], writes=[b_po])
                        if i + 1 < NSEQ:
                            emit_dS(d, i + 1)
                        c0 = ti * 128 + cj * 32
                        slot = i % NR
                        k.op("tensor", lambda e, d=d, cj=cj, c0=c0, po=po, slot=slot, lastc=lastc: e.matmul(
                            po[:, cj * 32:(cj + 1) * 32], lhsT=Sring[d][:, slot, :], rhs=qh[d][:, c0:c0 + 32], start=False, stop=lastc),
                            reads=[b_Sr[d][slot], b_qh[d]], writes=[b_po])
                        emit_update(d, i)
                        if lastc:
                            osl = oacc[:, ti * 128:(ti + 1) * 128]
                            if ti not in visited:
                                visited.add(ti)
                                k.op("scalar", lambda e, osl=osl, po=po: e.copy(out=osl, in_=po[:, 0:128]), reads=[b_po], writes=[b_o])
                            else:
                                k.op("vector", lambda e, osl=osl, po=po: e.tensor_tensor(out=osl, in0=po[:, 0:128], in1=osl, op=ALU.add),
                                     reads=[b_po, b_o], writes=[b_o])
                k.op("scalar", lambda e: e.activation(out=W[1][:], in_=W[0][:], func=AF.Square), reads=[bW[0], bW[1]], writes=[bW[1]])
                for t0 in range(0, T, 512):
                    n = min(512, T - t0)
                    k.op("tensor", lambda e, t0=t0, n=n: e.matmul(self.ps[0][:, 0:n], lhsT=self.ones128[:], rhs=W[1][:, t0:t0 + n],
                                                                 start=True, stop=True), reads=[bW[1], self.b_const], writes=[self.psb[0]])
                    k.op("vector", lambda e, t0=t0, n=n: e.tensor_scalar(out=W[2][:, t0:t0 + n], in0=self.ps[0][:, 0:n], scalar1=EPS, scalar2=None,
                                                                        op0=ALU.add), reads=[self.psb[0], bW[2]], writes=[bW[2]])
                k.op("scalar", lambda e: e.activation(out=W[2][:], in_=W[2][:], func=AF.Sqrt), reads=[bW[2]], writes=[bW[2]])
                k.op("vector", lambda e: e.reciprocal(out=W[2][:], in_=W[2][:]), reads=[bW[2]], writes=[bW[2]])
                k.op("vector", lambda e: e.tensor_tensor(out=W[3][:], in0=W[0][:], in1=W[2][:], op=ALU.mult), reads=[bW[0], bW[2], bW[3]], writes=[bW[3]])
                k.dma("sync", lambda e, hd=hd: e.dma_start(out=W[4][:], in_=self.pT[5, hd, :, :]), writes=[bW[4]])
                k.op("vector", lambda e: e.scalar_tensor_tensor(
                    out=kh[:, 0:NLAT].rearrange("p (r c) -> p r c", c=GRID), in0=W[3][:, 0:NLAT].rearrange("p (c r) -> p r c", r=GRID),
                    scalar=nwh[:, 0:1], in1=W[4][:, 0:NLAT].rearrange("p (r c) -> p r c", c=GRID), op0=ALU.mult, op1=ALU.mult),
                    reads=[bW[3], bW[4], b_par], writes=[b_kh])
                k.op("vector", lambda e: e.scalar_tensor_tensor(
                    out=kh[:, NLAT:T], in0=W[3][:, NLAT:T], scalar=nwh[:, 0:1], in1=W[4][:, NLAT:T], op0=ALU.mult, op1=ALU.mult),
                    reads=[bW[3], bW[4], b_par], writes=[b_kh])
                k.dma("sync", lambda e, hd=hd: e.dma_start(out=self.mixT[8 + hd, :, :], in_=kh[:]), reads=[b_kh])
            k.barrier()

    def sin_turns(self, out_ap, u, ki, tmp, bufs):
        k = self.k
        TWO_PI = 2 * math.pi * (1.0 - 1e-6)
        k.op("vector", lambda e: e.tensor_copy(out=ki, in_=u), reads=bufs, writes=bufs)
        k.op("vector", lambda e: e.tensor_copy(out=tmp, in_=ki), reads=bufs, writes=bufs)
        k.op("vector", lambda e: e.tensor_tensor(out=u, in0=u, in1=tmp, op=ALU.subtract), reads=bufs, writes=bufs)
        k.op("vector", lambda e: e.tensor_scalar(out=tmp, in0=u, scalar1=0.5, scalar2=None, op0=ALU.is_gt), reads=bufs, writes=bufs)
        k.op("vector", lambda e: e.tensor_tensor(out=u, in0=u, in1=tmp, op=ALU.subtract), reads=bufs, writes=bufs)
        k.op("vector", lambda e: e.tensor_scalar(out=tmp, in0=u, scalar1=-0.5, scalar2=None, op0=ALU.is_lt), reads=bufs, writes=bufs)
        k.op("vector", lambda e: e.tensor_tensor(out=u, in0=u, in1=tmp, op=ALU.add), reads=bufs, writes=bufs)
        k.op("scalar", lambda e: e.activation(out=out_ap, in_=u, func=AF.Sin, scale=TWO_PI), reads=bufs, writes=bufs)

    def s5_phase(self, l, need_ctx):
        k = self.k
        nc = self.nc
        L = 256
        NCHK = T // L
        PI = math.pi
        from concourse.ap import AP as _AP
        with contextlib.ExitStack() as ph:
            mag_s = [self.sb(ph, f"smag{d}", [128, 32], F32) for d in range(2)]
            th_s = [self.sb(ph, f"sth{d}", [128, 32], F32) for d in range(2)]
            dsk = self.sb(ph, "sdsk", [128, 8], F32)
            b_prm = Buf()
            k.dma("sync", lambda e: e.dma_start(out=dsk[:], in_=self.s5_d[l].rearrange("(c p) -> p c", p=128)), writes=[b_prm])
            iota = self.aux[:, 8:8 + 257]
            mg = self.aux[:, 0:8]
            with contextlib.ExitStack() as pre:
                BT = [[self.sb(pre, f"sBT{d}{r}", [128, 8, 4, 128], BF16) for r in range(2)] for d in range(2)]
                CT = [[self.sb(pre, f"sCT{d}{r}", [128, 32, 128], BF16) for r in range(2)] for d in range(2)]

                def nl(nm):
                    return self.sb(pre, nm, [64, 64], F32)
                for d in range(2):
                    for r in range(2):
                        k.op("vector", lambda e, d=d, r=r: e.memset(CT[d][r][:], 0.0), writes=[b_prm])
                lre, lim, ls, aa, mg_, thn, t1, t2, cs, sn, den, nr, cfr, cfi = [nl(f"snl{i}") for i in range(14)]
                nli = self.sb(pre, "snli", [64, 64], I32)
                Bn = [self.sb(pre, f"sBn{r}", [64, 1024], F32) for r in range(2)]
                Bb = [self.sb(pre, f"sBb{r}", [64, 1024], F32) for r in range(2)]
                tmpB = self.sb(pre, "stmpB", [64, 1024], F32)
                Cx = [self.sb(pre, f"sCx{r}", [32, 32, 128], F32) for r in range(2)]
                sl_lre = self.sb(pre, "sslre", [128, 32], F32)
                sl_lim = self.sb(pre, "sslim", [128, 32], F32)
                sl_ls = self.sb(pre, "ssls", [128, 32], F32)
                b_n = Buf()
                b_B = Buf()
                b_C = Buf()
                for d in range(2):
                    k.dma("sync", lambda e, d=d: e.dma_start(out=lre[:], in_=self.s5_lre[l, d].rearrange("g p -> p g")), writes=[b_n])
                    k.dma("sync", lambda e, d=d: e.dma_start(out=lim[:], in_=self.s5_lim[l, d].rearrange("g p -> p g")), writes=[b_n])
                    lsrc = self.s5_ls[l, d]
                    k.dma("sync", lambda e, lsrc=lsrc: e.dma_start(out=ls[:], in_=_AP(lsrc.tensor, lsrc.offset, [[0, 64], [1, 64]])), writes=[b_n])
                    for r, src in ((0, self.s5_bre), (1, self.s5_bim)):
                        k.dma("sync", lambda e, d=d, r=r, src=src: e.dma_start(
                            out=Bn[r][:].rearrange("p (g h) -> p g h", h=16), in_=src[l, d].rearrange("g p h -> p g h")), writes=[b_B])

                    def V_(fn, **kw):
                        k.op("vector", fn, reads=[b_n], writes=[b_n])

                    def A_(fn):
                        k.op("scalar", fn, reads=[b_n], writes=[b_n])
                    V_(lambda e: e.tensor_scalar(out=lre[:], in0=lre[:], scalar1=-1e-4, scalar2=None, op0=ALU.min))
                    A_(lambda e: e.activation(out=ls[:], in_=ls[:], func=AF.Exp))
                    V_(lambda e: e.tensor_tensor(out=aa[:], in0=lre[:], in1=ls[:], op=ALU.mult))
                    A_(lambda e: e.activation(out=mg_[:], in_=aa[:], func=AF.Exp))
                    V_(lambda e: e.tensor_tensor(out=thn[:], in0=lim[:], in1=ls[:], op=ALU.mult))
                    V_(lambda e: e.tensor_scalar(out=t1[:], in0=thn[:], scalar1=1.0 / (2 * PI), scalar2=None, op0=ALU.mult))
                    self.sin_turns(sn[:], t1[:], nli[:], t2[:], [b_n])
                    V_(lambda e: e.tensor_scalar(out=t1[:], in0=thn[:], scalar1=1.0 / (2 * PI), scalar2=0.25, op0=ALU.mult, op1=ALU.add))
                    self.sin_turns(cs[:], t1[:], nli[:], t2[:], [b_n])
                    V_(lambda e: e.tensor_tensor(out=cs[:], in0=cs[:], in1=mg_[:], op=ALU.mult))
                    V_(lambda e: e.tensor_tensor(out=sn[:], in0=sn[:], in1=mg_[:], op=ALU.mult))
                    V_(lambda e: e.tensor_tensor(out=den[:], in0=lre[:], in1=lre[:], op=ALU.mult))
                    V_(lambda e: e.tensor_tensor(out=t1[:], in0=lim[:], in1=lim[:], op=ALU.mult))
                    V_(lambda e: e.tensor_tensor(out=den[:], in0=den[:], in1=t1[:], op=ALU.add))
                    V_(lambda e: e.reciprocal(out=den[:], in_=den[:]))
                    V_(lambda e: e.tensor_scalar(out=nr[:], in0=cs[:], scalar1=-1.0, scalar2=None, op0=ALU.add))
                    V_(lambda e: e.tensor_tensor(out=t1[:], in0=nr[:], in1=lre[:], op=ALU.mult))
                    V_(lambda e: e.tensor_tensor(out=t2[:], in0=sn[:], in1=lim[:], op=ALU.mult))
                    V_(lambda e: e.tensor_tensor(out=cfr[:], in0=t1[:], in1=t2[:], op=ALU.add))
                    V_(lambda e: e.tensor_tensor(out=cfr[:], in0=cfr[:], in1=den[:], op=ALU.mult))
                    V_(lambda e: e.tensor_tensor(out=t1[:], in0=sn[:], in1=lre[:], op=ALU.mult))
                    V_(lambda e: e.tensor_tensor(out=t2[:], in0=nr[:], in1=lim[:], op=ALU.mult))
                    V_(lambda e: e.tensor_tensor(out=cfi[:], in0=t1[:], in1=t2[:], op=ALU.subtract))
                    V_(lambda e: e.tensor_tensor(out=cfi[:], in0=cfi[:], in1=den[:], op=ALU.mult))
                    def v3(t):
                        return t[:].rearrange("p (g h) -> p g h", h=16)
                    k.op("vector", lambda e: e.tensor_tensor(out=v3(Bb[0]), in0=v3(Bn[0]), in1=bc_last(cfr[:], 16), op=ALU.mult), reads=[b_n, b_B], writes=[b_B])
                    k.op("vector", lambda e: e.tensor_tensor(out=v3(tmpB), in0=v3(Bn[1]), in1=bc_last(cfi[:], 16), op=ALU.mult), reads=[b_n, b_B], writes=[b_B])
                    k.op("vector", lambda e: e.tensor_tensor(out=Bb[0][:], in0=Bb[0][:], in1=tmpB[:], op=ALU.subtract), reads=[b_B], writes=[b_B])
                    k.op("vector", lambda e: e.tensor_tensor(out=v3(Bb[1]), in0=v3(Bn[1]), in1=bc_last(cfr[:], 16), op=ALU.mult), reads=[b_n, b_B], writes=[b_B])
                    k.op("vector", lambda e: e.tensor_tensor(out=v3(tmpB), in0=v3(Bn[0]), in1=bc_last(cfi[:], 16), op=ALU.mult), reads=[b_n, b_B], writes=[b_B])
                    k.op("vector", lambda e: e.tensor_tensor(out=Bb[1][:], in0=Bb[1][:], in1=tmpB[:], op=ALU.add), reads=[b_B], writes=[b_B])
                    for r in range(2):
                        for tb in range(8):
                            pb = 1 + (tb % 4)
                            k.op("tensor", lambda e, r=r, tb=tb, pb=pb: e.transpose(
                                out=self.ps[pb][:, 0:64], in_=Bb[r][:, tb * 128:(tb + 1) * 128], identity=self.ident[0:64, 0:64]),
                                reads=[b_B, self.b_const], writes=[self.psb[pb]])
                            for g8 in range(8):
                                dst = BT[d][r][:, tb, g8 // 2, (g8 % 2) * 64:(g8 % 2) * 64 + 64]
                                if g8 % 2 == 0:
                                    k.op("vector", lambda e, dst=dst, pb=pb, g8=g8: e.tensor_scalar(
                                        out=dst, in0=self.ps[pb][:, 0:64], scalar1=mg[:, g8:g8 + 1], scalar2=None, op0=ALU.mult),
                                        reads=[self.psb[pb], self.b_const], writes=[b_prm])
                                else:
                                    k.op("scalar", lambda e, dst=dst, pb=pb, g8=g8: e.activation(
                                        out=dst, in_=self.ps[pb][:, 0:64], func=AF.Copy, scale=mg[:, g8:g8 + 1]),
                                        reads=[self.psb[pb], self.b_const], writes=[b_prm])
                    for g2 in range(2):
                        k.dma("sync", lambda e, d=d, g2=g2: e.dma_start(out=sl_lre[64 * g2:64 * g2 + 64, :],
                                                                        in_=self.s5_lre[l, d, g2::2, :].rearrange("gp p -> p gp")), writes=[b_n])
                        k.dma("sync", lambda e, d=d, g2=g2: e.dma_start(out=sl_lim[64 * g2:64 * g2 + 64, :],
                                                                        in_=self.s5_lim[l, d, g2::2, :].rearrange("gp p -> p gp")), writes=[b_n])
                        lsrc2 = self.s5_ls[l, d, g2::2]
                        k.dma("sync", lambda e, g2=g2, lsrc2=lsrc2: e.dma_start(
                            out=sl_ls[64 * g2:64 * g2 + 64, :], in_=_AP(lsrc2.tensor, lsrc2.offset, [[0, 64], [2, 32]])), writes=[b_n])
                    V_(lambda e: e.tensor_scalar(out=sl_lre[:], in0=sl_lre[:], scalar1=-1e-4, scalar2=None, op0=ALU.min))
                    A_(lambda e: e.activation(out=sl_ls[:], in_=sl_ls[:], func=AF.Exp))
                    V_(lambda e: e.tensor_tensor(out=sl_lre[:], in0=sl_lre[:], in1=sl_ls[:], op=ALU.mult))
                    k.op("scalar", lambda e, d=d: e.activation(out=mag_s[d][:], in_=sl_lre[:], func=AF.Exp), reads=[b_n], writes=[b_prm])
                    k.op("vector", lambda e, d=d: e.scalar_tensor_tensor(out=th_s[d][:], in0=sl_lim[:], scalar=1.0 / (2 * PI), in1=sl_ls[:],
                                                                         op0=ALU.mult, op1=ALU.mult), reads=[b_n], writes=[b_prm])
                    for r, src in ((0, self.s5_cre), (1, self.s5_cim)):
                        k.op("vector", lambda e, r=r: e.memset(Cx[r][:], 0.0), reads=[b_C], writes=[b_C])
                        for g2 in range(2):
                            k.dma("sync", lambda e, d=d, r=r, g2=g2, src=src: e.dma_start(
                                out=Cx[r][16 * g2:16 * g2 + 16, :, 64 * g2:64 * g2 + 64],
                                in_=src[l, d, g2::2].rearrange("gp h p -> h gp p")), reads=[b_C], writes=[b_C])
                        for half in range(2):
                            pb = 5 + half
                            for i in range(16):
                                gp = half * 16 + i
                                k.op("tensor", lambda e, r=r, gp=gp, i=i, pb=pb: e.transpose(
                                    out=self.ps[pb][:, i * 32:(i + 1) * 32], in_=Cx[r][:, gp, :], identity=self.ident[0:32, 0:32]),
                                    reads=[b_C, self.b_const], writes=[self.psb[pb]])
                            pv = self.ps[pb][:, :].rearrange("p (i c) -> p i c", c=32)
                            for j in range(4):
                                dst = CT[d][r][:, half * 16 + j:half * 16 + 16:4, 32 * j:32 * j + 32]
                                k.op("scalar", lambda e, dst=dst, pv=pv, j=j, r=r: e.activation(
                                    out=dst, in_=pv[:, j:16:4, :], func=AF.Copy, scale=(1.0 if r == 0 else -1.0)),
                                    reads=[self.psb[pb]], writes=[b_prm])
                for d in range(2):
                    for r in range(2):
                        k.dma("sync", lambda e, d=d, r=r: e.dma_start(out=self.BTs[d, r], in_=BT[d][r][:].rearrange("p t j c -> p t (j c)")), reads=[b_prm])
                        k.dma("sync", lambda e, d=d, r=r: e.dma_start(out=self.CTs[d, r], in_=CT[d][r][:]), reads=[b_prm])
                k.barrier()
            L = 512
            ust = self.sb(ph, "sust", [128, T], F32)
            ubf = self.sb(ph, "subf", [128, T], BF16)
            yacc = self.sb(ph, "syacc", [128, T], F32)
            b_ust, b_ubf, b_y = Buf(), Buf(), Buf()
            tab_all = self.sb(ph, "stab", [128, 2, 4, L + 1], F32)
            tabs = [[tab_all[:, i, j, :] for i in range(2)] for j in range(4)]
            init_t = self.sb(ph, "sinit", [128, 2, 4], F32)
            x1_t = self.sb(ph, "sx1", [128, 2, 4], F32)
            x2_t = self.sb(ph, "sx2", [128, 2, 4], F32)
            b_init = Buf()
            rts = [self.sb(ph, f"srt{j}", [128, L], F32) for j in range(4)]
            phs = self.sb(ph, "sphs", [128, L + 1], F32)
            pht = self.sb(ph, "spht", [128, L + 1], F32)
            phi = self.sb(ph, "sphi", [128, L + 1], I32)
            b_tab = [Buf() for _ in range(4)]
            b_phs = Buf()
            btl = self.make_ring(ph, "sbtl", [128, 2, 2, 512], BF16, 2)
            ctl = self.make_ring(ph, "sctl", [128, 2, 2, 4, 128], BF16, 2)
            pre_s = self.make_ring(ph, "spre", [128, 2, L], F32, 4)
            wring_ = self.make_ring(ph, "sw", [128, 2, L], F32, 8)
            zri = self.make_ring(ph, "szri", [128, 4, 2, L], F32, 2)
            xri = self.make_ring(ph, "sxri", [128, 2, L], BF16, 4)
            tmpP = self.make_ring(ph, "stp", [128, L], F32, 2)
            tmpV = self.make_ring(ph, "stv", [128, 2, L], F32, 2)
            tmpP2 = self.make_ring(ph, "stp2", [128, 2, L], F32, 1)
            psr = Ring([1, 2, 3, 4])
            psy = Ring([5, 6])
            iotaL = self.sb(ph, "siota", [128, L + 1], F32)
            b_io = Buf()
            k.dma("sync", lambda e: e.dma_start(out=iotaL[:], in_=self.c_iota[:, :]), writes=[b_io])
            chunks = [(NLAT, T)] + [(i * L, (i + 1) * L) for i in range(NLAT // L)]
            for tb in range(8):
                bt, b_bt = btl.next()
                ct, b_ct = ctl.next()
                k.dma("sync", lambda e, tb=tb, bt=bt: e.dma_start(out=bt[:], in_=self.BTs[:, :, :, tb, :].rearrange("d r p c -> p d r c")), writes=[b_bt])
                k.dma("sync", lambda e, tb=tb, ct=ct: e.dma_start(out=ct[:], in_=self.CTs[:, :, :, tb * 4:(tb + 1) * 4, :].rearrange("d r p g c -> p d r g c")), writes=[b_ct])
                k.dma("sync", lambda e, tb=tb: e.dma_start(out=ust[:], in_=self.pT[0, tb, :, :]), reads=[b_ust], writes=[b_ust])
                k.op("scalar", lambda e: e.copy(out=ubf[:], in_=ust[:]), reads=[b_ust], writes=[b_ubf])
                k.op("vector", lambda e, tb=tb: e.tensor_scalar(out=yacc[:], in0=ust[:], scalar1=dsk[:, tb:tb + 1], scalar2=None, op0=ALU.mult),
                     reads=[b_ust, b_prm], writes=[b_y])
                for d in range(2):
                    for j in range(4):
                        gp = tb * 4 + j
                        thc = th_s[d][:, gp:gp + 1]
                        for which, off in ((1, 0.0), (0, 0.25)):
                            k.op("vector", lambda e, thc=thc, off=off: e.tensor_scalar(out=phs[:], in0=iotaL[:], scalar1=thc, scalar2=off,
                                                                                       op0=ALU.mult, op1=ALU.add), reads=[b_prm, b_io, b_phs], writes=[b_phs])
                            self.sin_turns(tabs[j][which], phs[:], phi[:], pht[:], [b_phs, b_tab[j]])
                        k.op("vector", lambda e, j=j, d=d, gp=gp: e.tensor_scalar(out=rts[j][:], in0=iotaL[:, 0:L], scalar1=0.0,
                                                                                 scalar2=mag_s[d][:, gp:gp + 1], op0=ALU.mult, op1=ALU.add),
                             reads=[b_prm, b_io], writes=[b_tab[j]])
                    order = chunks if d == 0 else [chunks[0]] + chunks[:0:-1]
                    NCI = len(order)
                    Aout, Zout = {}, {}

                    def sv(t2d, ci, d=d, order=order):
                        lo, hi = order[ci]
                        v = t2d[:, lo:hi]
                        return v[:, ::-1] if d == 1 else v

                    def TT(E, out, in0, in1, op, reads, writes):
                        k.op(E, lambda e: e.tensor_tensor(out=out, in0=in0, in1=in1, op=op), reads=reads, writes=writes)

                    def stageA(ci, d=d, tb=tb, bt=bt, b_bt=b_bt, sv=sv, order=order):
                        n = order[ci][1] - order[ci][0]
                        for j in range(4):
                            EA = "gpsimd" if j < 3 else "vector"
                            cs_t = tabs[j][0][:, 0:n]
                            sn_t = tabs[j][1][:, 0:n]
                            p1i, _ = psr.next()
                            p2i, _ = psr.next()
                            p1, p2 = self.ps[p1i], self.ps[p2i]
                            rhs = sv(ubf, ci)
                            k.op("tensor", lambda e, p1=p1, j=j, rhs=rhs, n=n: e.matmul(
                                p1[:, 0:n], lhsT=bt[:, d, 0, j * 128:(j + 1) * 128], rhs=rhs, start=True, stop=True),
                                reads=[b_bt, b_ubf], writes=[self.psb[p1i]])
                            k.op("tensor", lambda e, p2=p2, j=j, rhs=rhs, n=n: e.matmul(
                                p2[:, 0:n], lhsT=bt[:, d, 1, j * 128:(j + 1) * 128], rhs=rhs, start=True, stop=True),
                                reads=[b_bt, b_ubf], writes=[self.psb[p2i]])
                            if EA == "gpsimd":
                                pr, bpr = pre_s.next()
                                k.op("scalar", lambda e, pr=pr, p1=p1, n=n: e.copy(out=pr[:, 0, 0:n], in_=p1[:, 0:n]), reads=[self.psb[p1i]], writes=[bpr])
                                k.op("scalar", lambda e, pr=pr, p2=p2, n=n: e.copy(out=pr[:, 1, 0:n], in_=p2[:, 0:n]), reads=[self.psb[p2i]], writes=[bpr])
                                s_re, s_im = pr[:, 0, 0:n], pr[:, 1, 0:n]
                                rd = [bpr, b_tab[j]]
                                tp, btp = tmpP.next()
                                tpv = tp[:, 0:n]
                            else:
                                s_re, s_im = p1[:, 0:n], p2[:, 0:n]
                                rd = [self.psb[p1i], self.psb[p2i], b_tab[j]]
                                tp, btp = tmpV.next()
                                tpv = tp[:, 0, 0:n]
                            w, bw = wring_.next()
                            TT(EA, w[:, 0, 0:n], s_re, cs_t, ALU.mult, rd, [bw])
                            TT(EA, tpv, s_im, sn_t, ALU.mult, rd, [btp])
                            TT(EA, w[:, 0, 0:n], w[:, 0, 0:n], tpv, ALU.add, [bw, btp], [bw])
                            TT(EA, w[:, 1, 0:n], s_im, cs_t, ALU.mult, rd, [bw])
                            TT(EA, tpv, s_re, sn_t, ALU.mult, rd + [btp], [btp])
                            TT(EA, w[:, 1, 0:n], w[:, 1, 0:n], tpv, ALU.subtract, [bw, btp], [bw])
                            Aout[(ci, j)] = (w, bw, n)

                    def stageB(ci, order=order):
                        zr, bzr = zri.next()
                        if ci > 0:
                            nprev = order[ci - 1][1] - order[ci - 1][0]
                            zp, bzp = Zout["prev"]
                            zend = zp[:, :, :, nprev - 1].rearrange("p j c -> p c j")
                            cLb = tab_all[:, 0, :, nprev].unsqueeze(1).to_broadcast([128, 2, 4])
                            sLb = tab_all[:, 1, :, nprev].unsqueeze(1).to_broadcast([128, 2, 4])
                            rdc = [bzp] + b_tab + [b_init]
                            k.op("vector", lambda e, zend=zend, cLb=cLb: e.tensor_tensor(out=x1_t[:], in0=zend, in1=cLb, op=ALU.mult), reads=rdc, writes=[b_init])
                            k.op("vector", lambda e, zend=zend, sLb=sLb: e.tensor_tensor(out=x2_t[:], in0=zend, in1=sLb, op=ALU.mult), reads=rdc, writes=[b_init])
                            k.op("vector", lambda e: e.tensor_tensor(out=init_t[:, 0, :], in0=x1_t[:, 0, :], in1=x2_t[:, 1, :], op=ALU.subtract), reads=[b_init], writes=[b_init])
                            k.op("vector", lambda e: e.tensor_tensor(out=init_t[:, 1, :], in0=x1_t[:, 1, :], in1=x2_t[:, 0, :], op=ALU.add), reads=[b_init], writes=[b_init])
                        for j in range(4):
                            w, bw, n = Aout.pop((ci, j))
                            for c2 in range(2):
                                ini = 0.0 if ci == 0 else init_t[:, c2, j:j + 1]
                                k.op("vector", lambda e, zr=zr, w=w, c2=c2, ini=ini, j=j, n=n: e.tensor_tensor_scan(
                                    out=zr[:, j, c2, 0:n], data0=rts[j][:, 0:n], data1=w[:, c2, 0:n], initial=ini, op0=ALU.mult, op1=ALU.add),
                                    reads=[bw, b_tab[j], b_init], writes=[bzr])
                            Zout[(ci, j)] = (zr[:, j], bzr, n)
                        Zout["prev"] = (zr, bzr)

                    def stageC(ci, d=d, ct=ct, b_ct=b_ct, sv=sv):
                        pyi, _ = psy.next()
                        py = self.ps[pyi]
                        for j in range(4):
                            zr, bzr, n = Zout.pop((ci, j))
                            E = "vector" if j < 3 else "gpsimd"
                            cs_t = tabs[j][0][:, 0:n]
                            sn_t = tabs[j][1][:, 0:n]
                            xr_, bxr_ = xri.next()
                            tv, btv = tmpV.next() if E == "vector" else tmpP2.next()
                            zre, zim = zr[:, 0, 0:n], zr[:, 1, 0:n]
                            A_, B_ = tv[:, 0, 0:n], tv[:, 1, 0:n]
                            rd2 = [bzr, b_tab[j]]
                            TT(E, A_, zre, cs_t, ALU.mult, rd2, [btv])
                            TT(E, B_, zim, sn_t, ALU.mult, rd2 + [btv], [btv])
                            TT(E, xr_[:, 0, 0:n], A_, B_, ALU.subtract, [btv], [bxr_])
                            TT(E, A_, zim, cs_t, ALU.mult, rd2 + [btv], [btv])
                            TT(E, B_, zre, sn_t, ALU.mult, rd2 + [btv], [btv])
                            TT(E, xr_[:, 1, 0:n], A_, B_, ALU.add, [btv], [bxr_])
                            k.op("tensor", lambda e, py=py, xr_=xr_, j=j, n=n: e.matmul(
                                py[:, 0:n], lhsT=ct[:, d, 0, j, :], rhs=xr_[:, 0, 0:n], start=(j == 0), stop=False),
                                reads=[b_ct, bxr_], writes=[self.psb[pyi]])
                            k.op("tensor", lambda e, py=py, xr_=xr_, j=j, n=n: e.matmul(
                                py[:, 0:n], lhsT=ct[:, d, 1, j, :], rhs=xr_[:, 1, 0:n], start=False, stop=(j == 3)),
                                reads=[b_ct, bxr_], writes=[self.psb[pyi]])
                        yv = sv(yacc, ci)
                        k.op("vector", lambda e, py=py, yv=yv, n=n: e.tensor_tensor(out=yv, in0=py[:, 0:n], in1=yv, op=ALU.add),
                             reads=[self.psb[pyi], b_y], writes=[b_y])

                    stageA(0)
                    for ci in range(NCI):
                        if ci + 1 < NCI:
                            stageA(ci + 1)
                        stageB(ci)
                        stageC(ci)
                k.op("vector", lambda e: e.tensor_tensor(out=ust[:], in0=yacc[:], in1=yacc[:], op=ALU.mult), reads=[b_y, b_ust], writes=[b_ust])
                k.op("vector", lambda e: e.tensor_scalar(out=ust[:], in0=ust[:], scalar1=0.044715, scalar2=1.0, op0=ALU.mult, op1=ALU.add),
                     reads=[b_ust], writes=[b_ust])
                k.op("vector", lambda e: e.tensor_tensor(out=ust[:], in0=ust[:], in1=yacc[:], op=ALU.mult), reads=[b_ust, b_y], writes=[b_ust])
                k.op("scalar", lambda e: e.activation(out=ust[:], in_=ust[:], func=AF.Sigmoid, scale=1.5957691216057308), reads=[b_ust], writes=[b_ust])
                k.op("vector", lambda e: e.tensor_tensor(out=ust[:], in0=ust[:], in1=yacc[:], op=ALU.mult), reads=[b_ust, b_y], writes=[b_ust])
                k.dma("sync", lambda e, tb=tb: e.dma_start(out=self.gS5[tb, :, :], in_=ust[:]), reads=[b_ust])
            k.barrier()

    def glu_phase(self, l, need_ctx):
        k = self.k
        with contextlib.ExitStack() as ph:
            self.wring = self.make_ring(ph, "wr", [128, 4096], BF16, 5)
            gf = self.sb(ph, "ggf", [128, 8, 1024], F32)
            gb = self.sb(ph, "ggb", [128, 8, 1024], BF16)
            bgl = self.sb(ph, "gbgl", [128, 8], F32)
            b_gf, b_gb, b_bg = Buf(), Buf(), Buf()
            sgr = self.make_ring(ph, "gsg", [128, 512], F32, 3)
            outr = self.make_ring(ph, "gout", [128, 512], BF16, 3)
            k.dma("sync", lambda e: e.dma_start(out=bgl[:], in_=self.s5_bglu[l].rearrange("(c p) -> p c", p=128)), writes=[b_bg])
            wv = self.s5_wglu[l].rearrange("(kt p) c -> p kt c", p=128)
            gv = self.gS5.rearrange("c p t -> p c t")
            psr = Ring([1, 2, 3, 4])
            for (t0, ts, s) in self.supertiles(need_ctx):
                nh = max(1, ts // 512)
                n = min(512, ts)
                k.dma("sync", lambda e, t0=t0, ts=ts: e.dma_start(out=gf[:, :, 0:ts], in_=gv[:, :, t0:t0 + ts]), reads=[b_gf], writes=[b_gf])
                k.op("scalar", lambda e, ts=ts: e.copy(out=gb[:, :, 0:ts], in_=gf[:, :, 0:ts]), reads=[b_gf, b_gb], writes=[b_gb])
                stream = Prog.WStream(self, [wv[:, :, bi * 256:(bi + 1) * 256] for bi in range(4)], 3)
                for bi in range(4):
                    wt, wb = stream.get()
                    for sub in range(2):
                        m = bi * 2 + sub
                        for hf in range(nh):
                            tsl = slice(hf * 512, hf * 512 + n)
                            tok0 = t0 + hf * 512
                            pi, _ = psr.next()
                            pp = self.ps[pi]
                            for kt in range(8):
                                k.op("tensor", lambda e, pp=pp, wt=wt, kt=kt, sub=sub, tsl=tsl, n=n: e.matmul(
                                    pp[:, 0:n], lhsT=wt[:, kt, sub * 128:(sub + 1) * 128], rhs=gb[:, kt, tsl],
                                    start=(kt == 0), stop=(kt == 7)), reads=[wb, b_gb], writes=[self.psb[pi]])
                            sg, bsg = sgr.next()
                            k.op("scalar", lambda e, sg=sg, pp=pp, m=m, n=n: e.activation(out=sg[:, 0:n], in_=pp[:, 0:n], func=AF.Sigmoid,
                                                                                       bias=bgl[:, m:m + 1]), reads=[self.psb[pi], b_bg], writes=[bsg])
                            ot, bot = outr.next()
                            k.op("vector", lambda e, ot=ot, sg=sg, m=m, tsl=tsl, n=n: e.tensor_tensor(out=ot[:, 0:n], in0=sg[:, 0:n], in1=gf[:, m, tsl], op=ALU.mult),
                                 reads=[bsg, b_gf], writes=[bot])
                            k.dma("sync", lambda e, ot=ot, m=m, tok0=tok0, n=n: e.dma_start(out=self.mixT[m, :, tok0:tok0 + n], in_=ot[:, 0:n]), reads=[bot])
            k.barrier()

    def outproj_phase(self, l, include_ctx):
        k = self.k
        with contextlib.ExitStack() as ph:
            self.wring = self.make_ring(ph, "wr", [128, 4096], BF16, 5)
            mx = self.sb(ph, "omx", [128, 16, 1024], BF16)
            b_mx = Buf()
            xrring = self.make_ring(ph, "oxr", [128, 512], F32, 3)
            wv = self.w_out[l].rearrange("(kt p) c -> p kt c", p=128)
            mv = self.mixT.rearrange("c p t -> p c t")
            psr = Ring([1, 2, 3, 4])
            for (t0, ts, s) in self.supertiles(include_ctx):
                nh = max(1, ts // 512)
                n = min(512, ts)
                k.dma("sync", lambda e, t0=t0, ts=ts: e.dma_start(out=mx[:, :, 0:ts], in_=mv[:, :, t0:t0 + ts]), reads=[b_mx], writes=[b_mx])
                stream = Prog.WStream(self, [wv[:, :, bi * 256:(bi + 1) * 256] for bi in range(8)], 3)
                for bi in range(8):
                    wt, wb = stream.get()
                    for sub in range(2):
                        m = bi * 2 + sub
                        for hf in range(nh):
                            tsl = slice(hf * 512, hf * 512 + n)
                            tok0 = t0 + hf * 512
                            xr, bxr = xrring.next()
                            k.dma("sync", lambda e, xr=xr, m=m, tok0=tok0, n=n: e.dma_start(out=xr[:, 0:n], in_=self.xT[m, :, tok0:tok0 + n]),
                                  reads=[self.xbuf(m, tok0)], writes=[bxr])
                            pi, _ = psr.next()
                            pp = self.ps[pi]
                            for kt in range(16):
                                k.op("tensor", lambda e, pp=pp, wt=wt, kt=kt, sub=sub, tsl=tsl, n=n: e.matmul(
                                    pp[:, 0:n], lhsT=wt[:, kt, sub * 128:(sub + 1) * 128], rhs=mx[:, kt, tsl],
                                    start=(kt == 0), stop=(kt == 15)), reads=[wb, b_mx], writes=[self.psb[pi]])
                            k.op("vector", lambda e, xr=xr, pp=pp, m=m, s=s, n=n: e.scalar_tensor_tensor(
                                out=xr[:, 0:n], in0=pp[:, 0:n], scalar=self.Gmod[:, s, 1, m:m + 1], in1=xr[:, 0:n],
                                op0=ALU.mult, op1=ALU.add), reads=[self.psb[pi], bxr, self.b_mod], writes=[bxr])
                            k.dma("sync", lambda e, xr=xr, m=m, tok0=tok0, n=n: e.dma_start(out=self.xT[m, :, tok0:tok0 + n], in_=xr[:, 0:n]),
                                  reads=[bxr], writes=[self.xbuf(m, tok0)])
            k.barrier()


def _consts():
    ident = np.eye(128, dtype=np.float32)
    s = np.arange(128)[:, None]
    t = np.arange(128)[None, :]
    same = (s // 32) == (t // 32)
    m_f = (same & (s <= t)).astype(np.float32)
    m_b = (same & (s >= t)).astype(np.float32)
    g2 = ((np.arange(128) % 32) // 16)
    m0 = np.repeat((g2 == 0).astype(np.float32)[:, None], 128, 1)
    m1 = np.repeat((g2 == 1).astype(np.float32)[:, None], 128, 1)
    m96 = np.repeat((np.arange(128) >= 96).astype(np.float32)[:, None], 128, 1)
    mg = ((np.arange(128)[:, None] // 16) == np.arange(8)[None, :]).astype(np.float32)
    iota = np.repeat(np.arange(257, dtype=np.float32)[None, :], 128, 0)
    aux = np.concatenate([mg, iota], axis=1).astype(np.float32)
    iota2 = np.repeat(np.arange(513, dtype=np.float32)[None, :], 128, 0)
    return ident, np.stack([m_f, m_b, m0, m1, m96]).astype(np.float32), aux, iota2


W_NAMES = ["w_ada", "b_ada", "norm_w", "ffn_w_gate", "ffn_w_up", "ffn_w_down", "w_in", "w_out",
           "s5_lambda_re", "s5_lambda_im", "s5_log_step", "s5_b_re", "s5_b_im", "s5_c_re", "s5_c_im",
           "s5_d", "s5_w_glu", "s5_b_glu", "hgrn_lower_bounds", "hgrn_norm_w", "final_norm_w"]


def make_in_map(inputs, b):
    ident, masks, aux, iota2 = _consts()
    m = {"xin": np.ascontiguousarray(np.concatenate([inputs["x"][b], inputs["ctx"][b]], axis=0), dtype=np.float32),
         "cc": np.ascontiguousarray(np.stack([inputs["c"][b], inputs["c_ctx"]], axis=0), dtype=np.float32),
         "c_ident": ident, "c_masks": masks, "c_aux": aux, "c_iota": iota2}
    for nme in W_NAMES:
        m[nme] = np.ascontiguousarray(inputs[nme], dtype=np.float32)
    return m


def kernel(**inputs):
    nc = Prog().build()
    nb = inputs["x"].shape[0]
    in_maps = [make_in_map(inputs, c % nb) for c in range(8)]
    res = run_bass_kernel_spmd(nc, in_maps, core_ids=list(range(8)))
    return np.stack([np.asarray(res.results[b]["Y"]) for b in range(nb)], axis=0).astype(np.float32)
```

```python
import contextlib
import math
from collections import deque

import numpy as np
import concourse.bass as bass
import concourse.mybir as mybir
from concourse.bass_utils import run_bass_kernel_spmd

F32 = mybir.dt.float32
BF16 = mybir.dt.bfloat16
I32 = mybir.dt.int32
ALU = mybir.AluOpType
AF = mybir.ActivationFunctionType

D = 2048
NC_ = 16
FF = 5504
NFF = 43
NLAT = 4096
NCTX = 256
T = NLAT + NCTX
DEPTH = 2
EPS = 1e-6
INC = 6144
GRID = 64

ENGS = ("tensor", "vector", "scalar", "gpsimd", "sync")
SEM_ROLL = 30000
NO_SELF_SYNC = ("tensor",)


class Buf:
    __slots__ = ("name", "w", "r")

    def __init__(self, name=""):
        self.name = name
        self.w = None
        self.r = {}


class K:
    def __init__(self, nc, stack, n_dma_sems=32):
        self.nc = nc
        self.stack = stack
        self.q = {e: [] for e in ENGS}
        self.sem = {}
        self.cnt = {}
        self.waited = {e: {} for e in ENGS}
        self.nsem = 0
        self.sem_owner = {}
        self.no_self_sync = set(NO_SELF_SYNC)
        for e in ("tensor", "vector", "scalar", "gpsimd"):
            self._new_eng_sem(e)
        self.dma_sems = []
        for i in range(n_dma_sems):
            s = stack.enter_context(nc.semaphore(f"dma{i}"))
            self.dma_sems.append([s, 0])
        self.dma_rr = 0
        self.n_instr = 0

    def _new_eng_sem(self, e):
        s = self.stack.enter_context(self.nc.semaphore(f"s_{e}_{self.nsem}"))
        self.nsem += 1
        self.sem[e] = s
        self.cnt[e] = 0
        self.sem_owner[id(s)] = e

    def _collect(self, reads, writes):
        evs = []
        for b in reads:
            if b.w is not None:
                evs.append(b.w)
        for b in writes:
            if b.w is not None:
                evs.append(b.w)
            evs.extend(b.r.values())
        return evs

    def _waits_for(self, eng, evs):
        best = {}
        for (s, v) in evs:
            kk = id(s)
            if eng in self.no_self_sync and self.sem_owner.get(kk) == eng:
                continue
            if kk not in best or best[kk][1] < v:
                best[kk] = (s, v)
        out = []
        wd = self.waited[eng]
        for kk, (s, v) in best.items():
            if wd.get(kk, -1) >= v:
                continue
            wd[kk] = v
            out.append((s, v))
        return out

    def _update(self, ev, reads, writes):
        for b in writes:
            b.w = ev
            b.r = {}
        for b in reads:
            b.r[id(ev[0])] = ev

    def op(self, eng, fn, reads=(), writes=(), extra=()):
        evs = self._collect(reads, writes) + list(extra)
        waits = self._waits_for(eng, evs)
        if self.cnt[eng] >= SEM_ROLL:
            self._new_eng_sem(eng)
        self.cnt[eng] += 1
        ev = (self.sem[eng], self.cnt[eng])
        self.q[eng].append((waits, fn, ev[0], 1))
        self._update(ev, reads, writes)
        self.n_instr += 1
        return ev

    def dma(self, eng, fn, reads=(), writes=(), extra=()):
        evs = self._collect(reads, writes) + list(extra)
        slot = self.dma_sems[self.dma_rr]
        self.dma_rr = (self.dma_rr + 1) % len(self.dma_sems)
        if slot[1] > 0:
            evs.append((slot[0], slot[1]))
        waits = self._waits_for(eng, evs)
        slot[1] += 16
        ev = (slot[0], slot[1])
        self.q[eng].append((waits, fn, ev[0], 16))
        self._update(ev, reads, writes)
        self.n_instr += 1
        return ev

    def all_events(self):
        evs = []
        for e in ("tensor", "vector", "scalar", "gpsimd"):
            if self.cnt[e] > 0:
                evs.append((self.sem[e], self.cnt[e]))
        for s, v in self.dma_sems:
            if v > 0:
                evs.append((s, v))
        return evs

    def barrier(self):
        evs = self.all_events()
        for e in ENGS:
            saved = self.no_self_sync
            self.no_self_sync = set()
            waits = self._waits_for(e, evs)
            self.no_self_sync = saved
            if waits:
                self.q[e].append((waits, None, None, 0))

    def finish(self):
        nc = self.nc
        self.barrier()
        q = self.q

        def run(e, items):
            for (waits, fn, sem, inc) in items:
                for (s, v) in waits:
                    e.wait_ge(s, v)
                if fn is None:
                    continue
                ins = fn(e)
                ins.then_inc(sem, inc)

        with nc.Block() as block:
            @block.sync
            def _(e):
                run(e, q["sync"])

            @block.tensor
            def _(e):
                run(e, q["tensor"])

            @block.vector
            def _(e):
                run(e, q["vector"])

            @block.scalar
            def _(e):
                run(e, q["scalar"])

            @block.gpsimd
            def _(e):
                run(e, q["gpsimd"])


class Ring:
    def __init__(self, tiles):
        self.tiles = tiles
        self.bufs = [Buf() for _ in tiles]
        self.i = 0

    def next(self):
        i = self.i
        self.i = (i + 1) % len(self.tiles)
        return self.tiles[i], self.bufs[i]


def bc_mid(ap2, n):
    return ap2.unsqueeze(1).to_broadcast([ap2.shape[0], n, ap2.shape[1]])


def bc_last(ap2, n):
    return ap2.unsqueeze(2).to_broadcast([ap2.shape[0], ap2.shape[1], n])


class Prog:
    def __init__(self, n_layers=DEPTH, stage="full"):
        self.n_layers = n_layers
        self.stage = stage
        self.nc = bass.Bass("TRN2", target_bir_lowering=False)
        self.st = contextlib.ExitStack()

    def dram_in(self, name, shape):
        return self.nc.dram_tensor(name, list(shape), F32, kind="ExternalInput").ap()

    def sb(self, stack, name, shape, dtype):
        self._uid = getattr(self, "_uid", 0) + 1
        return stack.enter_context(self.nc.sbuf_tensor(f"{name}_{self._uid}", list(shape), dtype))

    def make_ring(self, stack, name, shape, dtype, n):
        return Ring([self.sb(stack, f"{name}{i}", shape, dtype) for i in range(n)])

    def xbuf(self, c, tok):
        return self.xT_bufs[c][tok // 512]

    def wload(self, src):
        t, b = self.wring.next()
        kt, cols = src.shape[1], src.shape[2]
        dst = t[:, 0:kt * cols].rearrange("p (k c) -> p k c", c=cols)
        self.k.dma("gpsimd", lambda e: e.dma_start(out=dst, in_=src), writes=[b])
        return dst, b

    class WStream:
        def __init__(self, prog, srcs, lookahead):
            self.p = prog
            self.srcs = srcs
            self.n = 0
            self.loaded = deque()
            self.la = lookahead

        def get(self):
            while self.n < len(self.srcs) and len(self.loaded) < self.la + 1:
                self.loaded.append(self.p.wload(self.srcs[self.n]))
                self.n += 1
            return self.loaded.popleft()

    def build(self):
        nc = self.nc
        st = self.st
        L = DEPTH
        self.xin = self.dram_in("xin", [T, D])
        self.cc = self.dram_in("cc", [2, D])
        self.w_ada = self.dram_in("w_ada", [L, D, 9 * D])
        self.b_ada = self.dram_in("b_ada", [L, 9 * D])
        self.norm_w = self.dram_in("norm_w", [L, 3, D])
        self.wg = self.dram_in("ffn_w_gate", [L, 2, D, FF])
        self.wu = self.dram_in("ffn_w_up", [L, 2, D, FF])
        self.wd = self.dram_in("ffn_w_down", [L, 2, FF, D])
        self.w_in = self.dram_in("w_in", [L, D, INC])
        self.w_out = self.dram_in("w_out", [L, D, D])
        self.s5_lre = self.dram_in("s5_lambda_re", [L, 2, 64, 64])
        self.s5_lim = self.dram_in("s5_lambda_im", [L, 2, 64, 64])
        self.s5_ls = self.dram_in("s5_log_step", [L, 2, 64])
        self.s5_bre = self.dram_in("s5_b_re", [L, 2, 64, 64, 16])
        self.s5_bim = self.dram_in("s5_b_im", [L, 2, 64, 64, 16])
        self.s5_cre = self.dram_in("s5_c_re", [L, 2, 64, 16, 64])
        self.s5_cim = self.dram_in("s5_c_im", [L, 2, 64, 16, 64])
        self.s5_d = self.dram_in("s5_d", [L, 1024])
        self.s5_wglu = self.dram_in("s5_w_glu", [L, 1024, 1024])
        self.s5_bglu = self.dram_in("s5_b_glu", [L, 1024])
        self.hg_lb = self.dram_in("hgrn_lower_bounds", [L, 2, 1024])
        self.hg_nw = self.dram_in("hgrn_norm_w", [L, 128])
        self.fin_w = self.dram_in("final_norm_w", [D])
        self.c_ident = self.dram_in("c_ident", [128, 128])
        self.c_masks = self.dram_in("c_masks", [5, 128, 128])
        self.c_aux = self.dram_in("c_aux", [128, 8 + 257])
        self.c_iota = self.dram_in("c_iota", [128, 513])
        self.Y = nc.dram_tensor("Y", [NLAT, D], F32, kind="ExternalOutput").ap()
        self.xT = nc.dram_tensor("xT_s", [NC_, 128, T], F32).ap()
        self.pT = nc.dram_tensor("pT_s", [6, 8, 128, T], F32).ap()
        self.vtm = nc.dram_tensor("vtm_s", [T, 1024], F32).ap()
        self.gS5 = nc.dram_tensor("gs5_s", [8, 128, T], F32).ap()
        if self.stage in ("hgrn", "s5"):
            self.mixT = nc.dram_tensor("mixT_s", [NC_, 128, T], BF16, kind="ExternalOutput").ap()
        else:
            self.mixT = nc.dram_tensor("mixT_s", [NC_, 128, T], BF16).ap()
        self.xT_bufs = [[Buf() for _ in range(9)] for _ in range(NC_)]
        self.pT_buf = [[Buf() for _ in range(8)] for _ in range(6)]
        self.vtm_buf = Buf()
        self.gS5_buf = [Buf() for _ in range(8)]
        self.mix_buf = [Buf() for _ in range(NC_)]
        self.BTs = nc.dram_tensor("BTs_s", [2, 2, 128, 8, 512], BF16).ap()
        self.CTs = nc.dram_tensor("CTs_s", [2, 2, 128, 32, 128], BF16).ap()

        self.k = K(nc, st)
        k = self.k
        self.ps = [st.enter_context(nc.psum_tensor(f"ps{i}", [128, 512], F32)) for i in range(8)]
        self.psb = [Buf() for _ in range(8)]
        self.ident = self.sb(st, "ident", [128, 128], F32)
        self.onesD = self.sb(st, "onesD", [128, 128], F32)
        self.ones128 = self.sb(st, "ones128", [128, 128], F32)
        self.masks = self.sb(st, "masks", [128, 5, 128], F32)
        self.aux = self.sb(st, "aux", [128, 8 + 257], F32)
        self.modT = self.sb(st, "modT", [128, 2, 144], F32)
        self.Amod = self.sb(st, "Amod", [128, 2, 3, 16], F32)
        self.Gmod = self.sb(st, "Gmod", [128, 2, 3, 16], F32)
        self.wfin = self.sb(st, "wfin", [128, 16], F32)
        self.b_const = Buf()
        self.b_mod = Buf()
        self.ident_bf = self.sb(st, "identbf", [128, 128], BF16)
        self.onesD_bf = self.sb(st, "onesDbf", [128, 128], BF16)

        with nc.allow_non_contiguous_dma("small parameter vectors are laid out feature-on-partition"):
            self.emit()
            k.finish()
        st.close()
        return nc

    def emit(self):
        k = self.k
        k.dma("sync", lambda e: e.dma_start(out=self.ident[:], in_=self.c_ident[:, :]), writes=[self.b_const])
        k.dma("sync", lambda e: e.dma_start(out=self.masks[:], in_=self.c_masks.rearrange("m p f -> p m f")),
              writes=[self.b_const])
        k.dma("sync", lambda e: e.dma_start(out=self.wfin[:], in_=self.fin_w.rearrange("(c p) -> p c", p=128)),
              writes=[self.b_const])
        k.dma("sync", lambda e: e.dma_start(out=self.aux[:], in_=self.c_aux[:, :]), writes=[self.b_const])
        k.op("vector", lambda e: e.memset(self.onesD[:], 1.0 / D), writes=[self.b_const])
        k.op("vector", lambda e: e.memset(self.ones128[:], 1.0 / 128), writes=[self.b_const])
        k.op("vector", lambda e: e.tensor_copy(out=self.ident_bf[:], in_=self.ident[:]), reads=[self.b_const], writes=[self.b_const])
        k.op("vector", lambda e: e.tensor_copy(out=self.onesD_bf[:], in_=self.onesD[:]), reads=[self.b_const], writes=[self.b_const])
        self.input_phase()
        for l in range(self.n_layers):
            last = (l == DEPTH - 1)
            self.mods_phase(l)
            self.ffn_phase(l, 0, 0, include_ctx=True)
            if self.stage == "ffn1":
                break
            self.inproj_phase(l)
            self.hgrn_phase(l, need_ctx=not last)
            if self.stage == "hgrn":
                break
            self.s5_phase(l, need_ctx=not last)
            self.glu_phase(l, need_ctx=not last)
            if self.stage == "s5":
                break
            self.outproj_phase(l, include_ctx=not last)
            self.ffn_phase(l, 1, 2, include_ctx=not last)
        self.output_phase(apply_norm=(self.stage == "full"))

    def supertiles(self, include_ctx, ts=1024):
        out = [(t0, ts, 0) for t0 in range(0, NLAT, ts)]
        if include_ctx:
            out.append((NLAT, NCTX, 1))
        return out

    def input_phase(self):
        k = self.k
        with contextlib.ExitStack() as ph:
            xtok = self.make_ring(ph, "xtok", [128, D], F32, 2)
            xst = self.make_ring(ph, "xsti", [128, 16, 128], F32, 2)
            xv = self.xT.rearrange("c p t -> p c t")
            ei = 0
            for ti in range(T // 128):
                tok = ti * 128
                xt, bxt = xtok.next()
                k.dma("sync", lambda e, xt=xt, tok=tok: e.dma_start(out=xt[:], in_=self.xin[tok:tok + 128, :]), writes=[bxt])
                xs, bxs = xst.next()
                for g in range(4):
                    pb = 4 + g % 4
                    for j in range(4):
                        c = g * 4 + j
                        k.op("tensor", lambda e, pb=pb, j=j, c=c, xt=xt: e.transpose(
                            out=self.ps[pb][:, j * 128:(j + 1) * 128], in_=xt[:, c * 128:(c + 1) * 128], identity=self.ident[:]),
                            reads=[bxt, self.b_const], writes=[self.psb[pb]])
                    dst = xs[:, g * 4:(g + 1) * 4, :]
                    src = self.ps[pb][:, :].rearrange("p (j t) -> p j t", t=128)
                    if ei % 2 == 0:
                        k.op("scalar", lambda e, dst=dst, src=src: e.copy(out=dst, in_=src), reads=[self.psb[pb]], writes=[bxs])
                    else:
                        k.op("vector", lambda e, dst=dst, src=src: e.tensor_copy(out=dst, in_=src), reads=[self.psb[pb]], writes=[bxs])
                    ei += 1
                k.dma("sync", lambda e, xs=xs, tok=tok: e.dma_start(out=xv[:, :, tok:tok + 128], in_=xs[:]),
                      reads=[bxs], writes=[self.xbuf(c, tok) for c in range(NC_)])
            k.barrier()

    def mods_phase(self, l):
        k = self.k
        with contextlib.ExitStack() as ph:
            self.wring = self.make_ring(ph, "wr", [128, 4096], BF16, 5)
            ccs = self.sb(ph, "ccs", [128, 2, 16], F32)
            scb = self.sb(ph, "scb", [128, 2, 16], BF16)
            bada = self.sb(ph, "bada", [128, 144], F32)
            nwt = self.sb(ph, "nwt", [128, 3, 16], F32)
            b_cc, b_sc, b_ba, b_nw = Buf(), Buf(), Buf(), Buf()
            k.dma("sync", lambda e: e.dma_start(out=ccs[:], in_=self.cc.rearrange("s (kt p) -> p s kt", p=128)), writes=[b_cc])
            k.op("scalar", lambda e: e.activation(out=scb[:], in_=ccs[:], func=AF.Silu), reads=[b_cc], writes=[b_sc])
            k.dma("sync", lambda e: e.dma_start(out=bada[:], in_=self.b_ada[l].rearrange("(j p) -> p j", p=128)), writes=[b_ba])
            k.dma("sync", lambda e: e.dma_start(out=nwt[:], in_=self.norm_w[l].rearrange("i (c p) -> p i c", p=128)), writes=[b_nw])
            wv = self.w_ada[l].rearrange("(kt p) c -> p kt c", p=128)
            stream = Prog.WStream(self, [wv[:, :, jb * 256:(jb + 1) * 256] for jb in range(72)], 3)
            pm = self.ps[7]
            for jb in range(72):
                wt, wb = stream.get()
                for sub in range(2):
                    j = jb * 2 + sub
                    for kt in range(16):
                        k.op("tensor", lambda e, wt=wt, sub=sub, j=j, kt=kt: e.matmul(
                            pm[:, 2 * j:2 * j + 2], lhsT=wt[:, kt, sub * 128:(sub + 1) * 128], rhs=scb[:, :, kt],
                            start=(kt == 0), stop=(kt == 15)), reads=[wb, b_sc], writes=[self.psb[7]])
            for s in range(2):
                k.op("vector", lambda e, s=s: e.tensor_tensor(out=self.modT[:, s, :], in0=pm[:, s:288:2], in1=bada[:], op=ALU.add),
                     reads=[self.psb[7], b_ba], writes=[self.b_mod])
            for s in range(2):
                for i3 in range(3):
                    sc_ = self.modT[:, s, (3 * i3 + 1) * 16:(3 * i3 + 2) * 16]
                    gt_ = self.modT[:, s, (3 * i3 + 2) * 16:(3 * i3 + 3) * 16]
                    k.op("vector", lambda e, s=s, i3=i3, sc_=sc_: e.scalar_tensor_tensor(
                        out=self.Amod[:, s, i3, :], in0=sc_, scalar=1.0, in1=nwt[:, i3, :], op0=ALU.add, op1=ALU.mult),
                        reads=[b_nw, self.b_mod], writes=[self.b_mod])
                    k.op("vector", lambda e, s=s, i3=i3, gt_=gt_: e.tensor_scalar(
                        out=self.Gmod[:, s, i3, :], in0=gt_, scalar1=(1.0 if i3 == 1 else 0.5), scalar2=None, op0=ALU.mult),
                        reads=[self.b_mod], writes=[self.b_mod])
            k.barrier()

    def norm_piece(self, tok, xring, sqring, rsring, dst, bdst, A_ap, B_ap, psn=0):
        k = self.k
        xv = self.xT.rearrange("c p t -> p c t")
        xs, bx = xring.next()
        k.dma("sync", lambda e: e.dma_start(out=xs[:], in_=xv[:, :, tok:tok + 128]),
              reads=[self.xbuf(c, tok) for c in range(NC_)], writes=[bx])
        sq, bs = sqring.next()
        k.op("scalar", lambda e: e.activation(out=sq[:], in_=xs[:], func=AF.Square), reads=[bx], writes=[bs])
        pn = self.ps[psn]
        for c in range(NC_):
            k.op("tensor", lambda e, c=c: e.matmul(pn[:, 0:128], lhsT=self.onesD_bf[:], rhs=sq[:, c, :], start=(c == 0), stop=(c == NC_ - 1)),
                 reads=[bs, self.b_const], writes=[self.psb[psn]])
        rs, brs = rsring.next()
        k.op("vector", lambda e: e.tensor_scalar(out=rs[:], in0=pn[:, 0:128], scalar1=EPS, scalar2=None, op0=ALU.add),
             reads=[self.psb[psn]], writes=[brs])
        k.op("scalar", lambda e: e.activation(out=rs[:], in_=rs[:], func=AF.Sqrt), reads=[brs], writes=[brs])
        k.op("vector", lambda e: e.reciprocal(out=rs[:], in_=rs[:]), reads=[brs], writes=[brs])
        k.op("vector", lambda e: e.tensor_tensor(out=xs[:], in0=xs[:], in1=bc_mid(rs[:], NC_), op=ALU.mult),
             reads=[brs, bx], writes=[bx])
        if B_ap is None:
            k.op("vector", lambda e: e.tensor_tensor(out=dst, in0=xs[:], in1=bc_last(A_ap, 128), op=ALU.mult),
                 reads=[bx, self.b_mod, self.b_const], writes=[bdst])
        else:
            k.op("vector", lambda e: e.tensor_tensor(out=xs[:], in0=xs[:], in1=bc_last(A_ap, 128), op=ALU.mult),
                 reads=[bx, self.b_mod], writes=[bx])
            k.op("vector", lambda e: e.tensor_tensor(out=dst, in0=xs[:], in1=bc_last(B_ap, 128), op=ALU.add),
                 reads=[bx, self.b_mod], writes=[bdst])

    def ffn_phase(self, l, fi, i3, include_ctx):
        k = self.k
        with contextlib.ExitStack() as ph:
            self.wring = self.make_ring(ph, "wr", [128, 4096], BF16, 5)
            hT = self.sb(ph, "hT", [128, 16, 1024], BF16)
            a = self.sb(ph, "aT", [128, NFF, 1024], BF16)
            b_h = [Buf(), Buf()]
            b_a = [Buf(), Buf()]
            xring = self.make_ring(ph, "fx", [128, 16, 128], F32, 2)
            sqring = self.make_ring(ph, "fsq", [128, 16, 128], BF16, 2)
            rsring = self.make_ring(ph, "frs", [128, 128], F32, 2)
            slring = self.make_ring(ph, "fsl", [128, 512], F32, 2)
            xrring = self.make_ring(ph, "fxr", [128, 512], F32, 3)
            wgv = self.wg[l, fi].rearrange("(kt p) c -> p kt c", p=128)
            wuv = self.wu[l, fi].rearrange("(kt p) c -> p kt c", p=128)
            wdv = self.wd[l, fi].rearrange("(kt p) c -> p kt c", p=128)
            psg = Ring([1, 2]); psu = Ring([3, 4]); psd = Ring([5, 6])
            for (t0, ts, s) in self.supertiles(include_ctx):
                A_ap = self.Amod[:, s, i3, :]
                B_ap = self.modT[:, s, (3 * i3) * 16:(3 * i3 + 1) * 16]
                nh = max(1, ts // 512)
                n = min(512, ts)
                for pc in range(ts // 128):
                    self.norm_piece(t0 + pc * 128, xring, sqring, rsring, hT[:, :, pc * 128:(pc + 1) * 128], b_h[(pc * 128) // 512],
                                    A_ap, B_ap)
                srcs = []
                for jb in range(22):
                    cols = 256 if jb < 21 else 128
                    srcs.append(wgv[:, :, jb * 256:jb * 256 + cols])
                    srcs.append(wuv[:, :, jb * 256:jb * 256 + cols])
                for m in range(16):
                    srcs.append(wdv[:, 0:22, m * 128:(m + 1) * 128])
                    srcs.append(wdv[:, 22:43, m * 128:(m + 1) * 128])
                stream = Prog.WStream(self, srcs, 3)
                for jb in range(22):
                    cols = 256 if jb < 21 else 128
                    gt, gb = stream.get()
                    ut, ub = stream.get()
                    for sub in range(cols // 128):
                        j = jb * 2 + sub
                        for hf in range(nh):
                            tsl = slice(hf * 512, hf * 512 + n)
                            pgi, _ = psg.next(); pui, _ = psu.next()
                            pg, pu = self.ps[pgi], self.ps[pui]
                            for kt in range(16):
                                k.op("tensor", lambda e, pg=pg, gt=gt, kt=kt, sub=sub, tsl=tsl, n=n: e.matmul(
                                    pg[:, 0:n], lhsT=gt[:, kt, sub * 128:(sub + 1) * 128], rhs=hT[:, kt, tsl],
                                    start=(kt == 0), stop=(kt == 15)), reads=[gb, b_h[hf]], writes=[self.psb[pgi]])
                            for kt in range(16):
                                k.op("tensor", lambda e, pu=pu, ut=ut, kt=kt, sub=sub, tsl=tsl, n=n: e.matmul(
                                    pu[:, 0:n], lhsT=ut[:, kt, sub * 128:(sub + 1) * 128], rhs=hT[:, kt, tsl],
                                    start=(kt == 0), stop=(kt == 15)), reads=[ub, b_h[hf]], writes=[self.psb[pui]])
                            sl, bsl = slring.next()
                            k.op("scalar", lambda e, sl=sl, pg=pg, n=n: e.activation(out=sl[:, 0:n], in_=pg[:, 0:n], func=AF.Silu),
                                 reads=[self.psb[pgi]], writes=[bsl])
                            k.op("vector", lambda e, sl=sl, pu=pu, j=j, tsl=tsl, n=n: e.tensor_tensor(
                                out=a[:, j, tsl], in0=sl[:, 0:n], in1=pu[:, 0:n], op=ALU.mult),
                                reads=[bsl, self.psb[pui]], writes=[b_a[hf]])
                for m in range(16):
                    w0, b0 = stream.get()
                    w1, b1 = stream.get()
                    for hf in range(nh):
                        tsl = slice(hf * 512, hf * 512 + n)
                        tok0 = t0 + hf * 512
                        xr, bxr = xrring.next()
                        k.dma("sync", lambda e, xr=xr, m=m, tok0=tok0, n=n: e.dma_start(out=xr[:, 0:n], in_=self.xT[m, :, tok0:tok0 + n]),
                              reads=[self.xbuf(m, tok0)], writes=[bxr])
                        pdi, _ = psd.next()
                        pd = self.ps[pdi]
                        for kt in range(NFF):
                            wt = w0[:, kt, :] if kt < 22 else w1[:, kt - 22, :]
                            k.op("tensor", lambda e, pd=pd, wt=wt, kt=kt, tsl=tsl, n=n: e.matmul(
                                pd[:, 0:n], lhsT=wt, rhs=a[:, kt, tsl], start=(kt == 0), stop=(kt == NFF - 1)),
                                reads=[b0, b1, b_a[hf]], writes=[self.psb[pdi]])
                        k.op("vector", lambda e, xr=xr, pd=pd, m=m, s=s, n=n: e.scalar_tensor_tensor(
                            out=xr[:, 0:n], in0=pd[:, 0:n], scalar=self.Gmod[:, s, i3, m:m + 1], in1=xr[:, 0:n],
                            op0=ALU.mult, op1=ALU.add), reads=[self.psb[pdi], bxr, self.b_mod], writes=[bxr])
                        k.dma("sync", lambda e, xr=xr, m=m, tok0=tok0, n=n: e.dma_start(out=self.xT[m, :, tok0:tok0 + n], in_=xr[:, 0:n]),
                              reads=[bxr], writes=[self.xbuf(m, tok0)])
            k.barrier()

    def output_phase(self, apply_norm):
        k = self.k
        with contextlib.ExitStack() as ph:
            xring = self.make_ring(ph, "ox", [128, 16, 128], F32, 2)
            sqring = self.make_ring(ph, "osq", [128, 16, 128], BF16, 2)
            rsring = self.make_ring(ph, "ors", [128, 128], F32, 2)
            yst = self.make_ring(ph, "oy", [128, 16, 128], F32, 2)
            ytok = self.make_ring(ph, "oyt", [128, D], F32, 2)
            xv = self.xT.rearrange("c p t -> p c t")
            ei = 0
            for ti in range(NLAT // 128):
                tok = ti * 128
                ys, bys = yst.next()
                if apply_norm:
                    self.norm_piece(tok, xring, sqring, rsring, ys[:], bys, self.wfin[:], None)
                else:
                    k.dma("sync", lambda e, ys=ys, tok=tok: e.dma_start(out=ys[:], in_=xv[:, :, tok:tok + 128]),
                          reads=[self.xbuf(c, tok) for c in range(NC_)], writes=[bys])
                yt, byt = ytok.next()
                for g in range(4):
                    pb = 4 + g % 4
                    for j in range(4):
                        c = g * 4 + j
                        k.op("tensor", lambda e, pb=pb, j=j, c=c, ys=ys: e.transpose(
                            out=self.ps[pb][:, j * 128:(j + 1) * 128], in_=ys[:, c, :], identity=self.ident[:]),
                            reads=[bys, self.b_const], writes=[self.psb[pb]])
                    dst = yt[:, g * 512:(g + 1) * 512]
                    src = self.ps[pb][:, :]
                    if ei % 2 == 0:
                        k.op("scalar", lambda e, dst=dst, src=src: e.copy(out=dst, in_=src), reads=[self.psb[pb]], writes=[byt])
                    else:
                        k.op("vector", lambda e, dst=dst, src=src: e.tensor_copy(out=dst, in_=src), reads=[self.psb[pb]], writes=[byt])
                    ei += 1
                k.dma("sync", lambda e, yt=yt, tok=tok: e.dma_start(out=self.Y[tok:tok + 128, :], in_=yt[:]), reads=[byt])
            k.barrier()

    def inproj_phase(self, l):
        k = self.k
        with contextlib.ExitStack() as ph:
            self.wring = self.make_ring(ph, "wr", [128, 4096], BF16, 5)
            hT = self.sb(ph, "ihT", [128, 16, 1024], BF16)
            b_h = [Buf(), Buf()]
            xring = self.make_ring(ph, "ix", [128, 16, 128], F32, 2)
            sqring = self.make_ring(ph, "isq", [128, 16, 128], BF16, 2)
            rsring = self.make_ring(ph, "irs", [128, 128], F32, 2)
            evring = self.make_ring(ph, "iev", [128, 512], F32, 4)
            wv = self.w_in[l].rearrange("(kt p) c -> p kt c", p=128)
            psr = Ring([1, 2, 3, 4, 5, 6])
            ei = 0
            for (t0, ts, s) in self.supertiles(True):
                A_ap = self.Amod[:, s, 1, :]
                B_ap = self.modT[:, s, 3 * 16:4 * 16]
                nh = max(1, ts // 512)
                n = min(512, ts)
                for pc in range(ts // 128):
                    self.norm_piece(t0 + pc * 128, xring, sqring, rsring, hT[:, :, pc * 128:(pc + 1) * 128], b_h[(pc * 128) // 512],
                                    A_ap, B_ap)
                stream = Prog.WStream(self, [wv[:, :, bi * 256:(bi + 1) * 256] for bi in range(24)], 3)
                for bi in range(24):
                    wt, wb = stream.get()
                    fam = bi // 4
                    if fam != 4:
                        for sub in range(2):
                            ch = (bi % 4) * 2 + sub
                            for hf in range(nh):
                                tsl = slice(hf * 512, hf * 512 + n)
                                tok0 = t0 + hf * 512
                                pi, _ = psr.next()
                                pp = self.ps[pi]
                                for kt in range(16):
                                    k.op("tensor", lambda e, pp=pp, wt=wt, kt=kt, sub=sub, tsl=tsl, n=n: e.matmul(
                                        pp[:, 0:n], lhsT=wt[:, kt, sub * 128:(sub + 1) * 128], rhs=hT[:, kt, tsl],
                                        start=(kt == 0), stop=(kt == 15)), reads=[wb, b_h[hf]], writes=[self.psb[pi]])
                                ev, bev = evring.next()
                                if fam in (1, 5):
                                    k.op("scalar", lambda e, ev=ev, pp=pp, n=n: e.activation(out=ev[:, 0:n], in_=pp[:, 0:n], func=AF.Silu),
                                         reads=[self.psb[pi]], writes=[bev])
                                elif ei % 2 == 0:
                                    k.op("scalar", lambda e, ev=ev, pp=pp, n=n: e.copy(out=ev[:, 0:n], in_=pp[:, 0:n]),
                                         reads=[self.psb[pi]], writes=[bev])
                                else:
                                    k.op("vector", lambda e, ev=ev, pp=pp, n=n: e.tensor_copy(out=ev[:, 0:n], in_=pp[:, 0:n]),
                                         reads=[self.psb[pi]], writes=[bev])
                                ei += 1
                                k.dma("sync", lambda e, ev=ev, fam=fam, ch=ch, tok0=tok0, n=n: e.dma_start(
                                    out=self.pT[fam, ch, :, tok0:tok0 + n], in_=ev[:, 0:n]), reads=[bev])
                    else:
                        for tt in range(ts // 128):
                            tok0 = t0 + tt * 128
                            pi, _ = psr.next()
                            pp = self.ps[pi]
                            for kt in range(16):
                                k.op("tensor", lambda e, pp=pp, wt=wt, kt=kt, tt=tt: e.matmul(
                                    pp[:, 0:256], lhsT=hT[:, kt, tt * 128:(tt + 1) * 128], rhs=wt[:, kt, :],
                                    start=(kt == 0), stop=(kt == 15)), reads=[wb, b_h[(tt * 128) // 512]], writes=[self.psb[pi]])
                            ev, bev = evring.next()
                            if ei % 2 == 0:
                                k.op("scalar", lambda e, ev=ev, pp=pp: e.copy(out=ev[:, 0:256], in_=pp[:, 0:256]),
                                     reads=[self.psb[pi]], writes=[bev])
                            else:
                                k.op("vector", lambda e, ev=ev, pp=pp: e.tensor_copy(out=ev[:, 0:256], in_=pp[:, 0:256]),
                                     reads=[self.psb[pi]], writes=[bev])
                            ei += 1
                            c0 = (bi % 4) * 256
                            k.dma("sync", lambda e, ev=ev, tok0=tok0, c0=c0: e.dma_start(
                                out=self.vtm[tok0:tok0 + 128, c0:c0 + 256], in_=ev[:, 0:256]), reads=[bev])
            k.barrier()

    def hgrn_phase(self, l, need_ctx):
        k = self.k
        NT = T // 128
        NCH = T // 32
        with contextlib.ExitStack() as ph:
            W = [self.sb(ph, f"hW{i}", [128, T], F32) for i in range(5)]
            bW = [Buf() for _ in range(5)]
            qh = [self.sb(ph, f"hqh{d}", [128, T], BF16) for d in range(2)]
            kt_ = [self.sb(ph, f"hkt{d}", [128, T], BF16) for d in range(2)]
            b_qh = [Buf(), Buf()]
            b_kt = [Buf(), Buf()]
            kh = self.sb(ph, "hkh", [128, T], BF16)
            b_kh = Buf()
            khtm = [self.sb(ph, f"hkhtm{d}", [128, NT, 128], BF16) for d in range(2)]
            b_khtm = [Buf(), Buf()]
            khtm3 = [self.sb(ph, f"hkhtm3{d}", [128, NT, 128], BF16) for d in range(2)]
            V = self.sb(ph, "hV", [128, NT, 128], BF16)
            b_V = Buf()
            dec = [self.sb(ph, f"hdec{d}", [128, NCH], F32) for d in range(2)]
            b_dec = [Buf(), Buf()]
            cm = self.sb(ph, "hcm", [128, T], BF16)
            hb = self.sb(ph, "hhb", [128, 2, 2, 8], F32)
            lbt = self.sb(ph, "hlbt", [128, 2, 8], F32)
            oml = self.sb(ph, "homl", [128, 2, 8], F32)
            nwh = self.sb(ph, "hnwh", [128, 1], F32)
            b_par = Buf()
            S32 = [self.sb(ph, f"hS32{d}", [128, 2, 128], F32) for d in range(2)]
            Sring = [self.sb(ph, f"hSr{d}", [128, 4, 128], BF16) for d in range(2)]
            b_S32 = [[Buf(), Buf()], [Buf(), Buf()]]
            b_Sr = [[Buf() for _ in range(4)] for _ in range(2)]
            psT = self.ps[7][:].bitcast(BF16)

            k.op("vector", lambda e: e.memset(cm[:], 1.0), writes=[b_par])
            k.op("vector", lambda e: e.memset(cm[:, 0:T:32], 0.0), writes=[b_par])
            k.dma("sync", lambda e: e.dma_start(out=hb[:], in_=self.hg_lb.rearrange("l d (h p) -> p l d h", p=128)), writes=[b_par])
            k.dma("sync", lambda e: e.dma_start(out=nwh[:], in_=self.hg_nw[l].rearrange("(p o) -> p o", o=1)), writes=[b_par])
            if l == 0:
                k.op("vector", lambda e: e.memset(lbt[:], 0.0), writes=[b_par])
            else:
                k.op("vector", lambda e: e.tensor_tensor(out=lbt[:], in0=hb[:, 1], in1=hb[:, 0], op=ALU.subtract), reads=[b_par], writes=[b_par])
                k.op("scalar", lambda e: e.activation(out=lbt[:], in_=lbt[:], func=AF.Sigmoid), reads=[b_par], writes=[b_par])
            k.op("vector", lambda e: e.tensor_scalar(out=oml[:], in0=lbt[:], scalar1=-1.0, scalar2=1.0, op0=ALU.mult, op1=ALU.add),
                 reads=[b_par], writes=[b_par])

            def cmv(ap):
                return ap[:, 0:NLAT].rearrange("p (c r) -> p c r", r=GRID)

            def rasv_as_cr(ap):
                return ap[:, 0:NLAT].rearrange("p (r c) -> p c r", c=GRID)

            for hd in range(8):
                vsrc = self.vtm[0:NLAT, hd * 128:(hd + 1) * 128].rearrange("(r ti c2) v -> c2 r ti v", ti=32, c2=2)
                for c2 in range(2):
                    k.dma("gpsimd", lambda e, c2=c2, vsrc=vsrc: e.dma_start(out=V[c2 * 64:(c2 + 1) * 64, 0:32, :], in_=vsrc[c2]), writes=[b_V])
                vsrc2 = self.vtm[NLAT:T, hd * 128:(hd + 1) * 128].rearrange("(ti p) v -> p ti v", p=128)
                k.dma("gpsimd", lambda e, vsrc2=vsrc2: e.dma_start(out=V[:, 32:34, :], in_=vsrc2), writes=[b_V])
                for d in range(2):
                    k.dma("sync", lambda e, hd=hd, d=d: e.dma_start(out=W[4][:], in_=self.pT[2 + d, hd, :, :]), writes=[bW[4]])
                    k.op("scalar", lambda e: e.activation(out=cmv(W[0]), in_=rasv_as_cr(W[4]), func=AF.Sigmoid), reads=[bW[4]], writes=[bW[0]])
                    k.op("scalar", lambda e: e.activation(out=W[0][:, NLAT:T], in_=W[4][:, NLAT:T], func=AF.Sigmoid), reads=[bW[4]], writes=[bW[0]])
                    k.op("vector", lambda e, d=d, hd=hd: e.tensor_scalar(out=W[0][:], in0=W[0][:], scalar1=oml[:, d, hd:hd + 1],
                                                                       scalar2=lbt[:, d, hd:hd + 1], op0=ALU.mult, op1=ALU.add),
                         reads=[b_par, bW[0]], writes=[bW[0]])
                    k.op("gpsimd", lambda e: e.tensor_scalar(out=W[1][:], in0=W[0][:], scalar1=-1.0, scalar2=1.0, op0=ALU.mult, op1=ALU.add),
                         reads=[bW[0]], writes=[bW[1]])
                    k.op("vector", lambda e: e.tensor_scalar(out=W[0][:], in0=W[0][:], scalar1=1e-6, scalar2=None, op0=ALU.max),
                         reads=[bW[0], bW[1]], writes=[bW[0]])
                    k.op("scalar", lambda e: e.activation(out=W[0][:], in_=W[0][:], func=AF.Ln), reads=[bW[0]], writes=[bW[0]])
                    if d == 0:
                        k.op("vector", lambda e: e.tensor_tensor_scan(out=W[2][:], data0=cm[:], data1=W[0][:], initial=0.0,
                                                                      op0=ALU.mult, op1=ALU.add), reads=[bW[0], b_par], writes=[bW[2]])
                    else:
                        k.op("vector", lambda e: e.tensor_tensor_scan(out=W[2][:, ::-1], data0=cm[:], data1=W[0][:, ::-1], initial=0.0,
                                                                      op0=ALU.mult, op1=ALU.add), reads=[bW[0], b_par], writes=[bW[2]])
                    k.op("scalar", lambda e: e.activation(out=W[3][:], in_=W[2][:], func=AF.Exp), reads=[bW[2]], writes=[bW[3]])
                    k.dma("sync", lambda e, hd=hd: e.dma_start(out=W[4][:], in_=self.pT[1, hd, :, :]), reads=[bW[4]], writes=[bW[4]])
                    k.op("vector", lambda e, d=d: e.tensor_tensor(out=cmv(qh[d]), in0=rasv_as_cr(W[4]), in1=cmv(W[3]), op=ALU.mult),
                         reads=[bW[4], bW[3]], writes=[b_qh[d]])
                    k.op("vector", lambda e, d=d: e.tensor_tensor(out=qh[d][:, NLAT:T], in0=W[4][:, NLAT:T], in1=W[3][:, NLAT:T], op=ALU.mult),
                         reads=[bW[4], bW[3]], writes=[b_qh[d]])
                    k.op("gpsimd", lambda e: e.tensor_scalar(out=W[0][:], in0=W[2][:], scalar1=-60.0, scalar2=None, op0=ALU.max),
                         reads=[bW[2], bW[0]], writes=[bW[0]])
                    k.op("scalar", lambda e: e.activation(out=W[0][:], in_=W[0][:], func=AF.Exp, scale=-1.0), reads=[bW[0]], writes=[bW[0]])
                    k.op("vector", lambda e: e.tensor_tensor(out=W[4][:], in0=W[1][:], in1=W[0][:], op=ALU.mult),
                         reads=[bW[1], bW[0], bW[4]], writes=[bW[4]])
                    k.op("scalar", lambda e, d=d: e.copy(out=kt_[d][:], in_=W[4][:]), reads=[bW[4]], writes=[b_kt[d]])
                    glast = W[2][:, (31 if d == 0 else 0):T:32]
                    k.op("scalar", lambda e, d=d, glast=glast: e.activation(out=dec[d][:], in_=glast, func=AF.Exp), reads=[bW[2]], writes=[b_dec[d]])
                    k.op("vector", lambda e, d=d: e.tensor_tensor(out=kh[:].rearrange("p (n j) -> p n j", j=32),
                                                                  in0=W[4][:].rearrange("p (n j) -> p n j", j=32),
                                                                  in1=bc_last(dec[d][:], 32), op=ALU.mult),
                         reads=[bW[4], b_dec[d]], writes=[b_kh])
                    for g0 in range(0, NT, 8):
                        ng = min(8, NT - g0)
                        for j in range(ng):
                            ti = g0 + j
                            k.op("tensor", lambda e, j=j, ti=ti: e.transpose(out=psT[:, j * 128:(j + 1) * 128], in_=kh[:, ti * 128:(ti + 1) * 128],
                                                                             identity=self.ident_bf[:]),
                                 reads=[b_kh, self.b_const], writes=[self.psb[7]])
                        k.op("vector", lambda e, d=d, g0=g0, ng=ng: e.tensor_copy(
                            out=khtm[d][:, g0:g0 + ng, :], in_=psT[:, 0:ng * 128].rearrange("p (n j) -> p n j", j=128)),
                            reads=[self.psb[7]], writes=[b_khtm[d]])
                        k.op("scalar", lambda e, d=d, g0=g0, ng=ng: e.activation(
                            out=khtm3[d][:, g0:g0 + ng, :], in_=psT[:, 0:ng * 128].rearrange("p (n j) -> p n j", j=128),
                            func=AF.Copy, scale=self.masks[:, 4, 0:1]),
                            reads=[self.psb[7], self.b_const], writes=[b_khtm[d]])
                sm_all = [W[1][:].bitcast(BF16)[:, 0:NT * 128].rearrange("p (n j) -> p n j", j=128),
                          W[2][:].bitcast(BF16)[:, 0:NT * 128].rearrange("p (n j) -> p n j", j=128)]
                b_sm = [bW[1], bW[2]]
                for d in range(2):
                    for ti in range(NT):
                        pi = 1 + (ti % 2)
                        k.op("tensor", lambda e, d=d, ti=ti, pi=pi: e.matmul(
                            self.ps[pi][:, 0:128], lhsT=kt_[d][:, ti * 128:(ti + 1) * 128], rhs=qh[d][:, ti * 128:(ti + 1) * 128],
                            start=True, stop=True), reads=[b_kt[d], b_qh[d]], writes=[self.psb[pi]])
                        k.op("vector", lambda e, d=d, ti=ti, pi=pi: e.tensor_tensor(
                            out=sm_all[d][:, ti, :], in0=self.ps[pi][:, 0:128], in1=self.masks[:, d, :], op=ALU.mult),
                            reads=[self.psb[pi], self.b_const], writes=[b_sm[d]])
                order = [[32, 33] + list(range(32)), [33, 32] + list(range(31, -1, -1))]
                seq = [[], []]
                for d in range(2):
                    for ti in order[d]:
                        for cj in ([0, 1, 2, 3] if d == 0 else [3, 2, 1, 0]):
                            seq[d].append((ti, cj))
                NSEQ = len(seq[0])
                NR = 4
                DS_BANK = [[5, 1], [6, 2]]
                psd_b = [[self.psb[DS_BANK[d][par]] for par in range(2)] for d in range(2)]
                for d in range(2):
                    k.op("vector", lambda e, d=d: e.memset(S32[d][:, 0, :], 0.0), reads=[b_S32[d][0]], writes=[b_S32[d][0]])
                    k.op("vector", lambda e, d=d: e.memset(Sring[d][:, 0, :], 0.0), reads=[b_Sr[d][0]], writes=[b_Sr[d][0]])
                oacc = W[0]
                b_o = bW[0]
                visited = set()

                def emit_dS(d, i):
                    ti, cj = seq[d][i]
                    par = i % 2
                    pdv = self.ps[DS_BANK[d][par]][:, 0:128]
                    if cj < 3:
                        k.op("tensor", lambda e: e.matmul(
                            pdv, lhsT=khtm[d][cj * 32:(cj + 1) * 32, ti, :], rhs=V[cj * 32:(cj + 1) * 32, ti, :], start=True, stop=True),
                            reads=[b_khtm[d], b_V], writes=[psd_b[d][par]])
                    else:
                        k.op("tensor", lambda e: e.matmul(
                            pdv, lhsT=khtm3[d][:, ti, :], rhs=V[:, ti, :], start=True, stop=True),
                            reads=[b_khtm[d], b_V], writes=[psd_b[d][par]])

                def emit_update(d, i):
                    ti, cj = seq[d][i]
                    par = i % 2
                    pdv = self.ps[DS_BANK[d][par]][:, 0:128]
                    cidx = ti * 4 + cj
                    slot = (i + 1) % NR
                    so, sn_ = i % 2, (i + 1) % 2
                    k.op("vector", lambda e: e.scalar_tensor_tensor(
                        out=S32[d][:, sn_, :], in0=S32[d][:, so, :], scalar=dec[d][:, cidx:cidx + 1], in1=pdv, op0=ALU.mult, op1=ALU.add),
                        reads=[psd_b[d][par], b_dec[d], b_S32[d][so]], writes=[b_S32[d][sn_]])
                    k.op("scalar", lambda e: e.copy(out=Sring[d][:, slot, :], in_=S32[d][:, sn_, :]),
                         reads=[b_S32[d][sn_], b_Sr[d][slot]], writes=[b_Sr[d][slot]])

                for d in range(2):
                    emit_dS(d, 0)
                for i in range(NSEQ):
                    for d in range(2):
                        ti, cj = seq[d][i]
                        po = self.ps[3 + d]
                        b_po = self.psb[3 + d]
                        first = (i % 4 == 0)
                        lastc = (i % 4 == 3)
                        if first:
                            k.op("tensor", lambda e, d=d, ti=ti, po=po: e.matmul(
                                po[:, 0:128], lhsT=V[:, ti, :], rhs=sm_all[d][:, ti, :], start=True, stop=False),
                                reads=[b_V, b_sm[d]], writes=[b_po])
                        if i + 1 < NSEQ:
                            emit_dS(d, i + 1)
                        c0 = ti * 128 + cj * 32
                        slot = i % NR
                        k.op("tensor", lambda e, d=d, cj=cj, c0=c0, po=po, slot=slot, lastc=lastc: e.matmul(
                            po[:, cj * 32:(cj + 1) * 32], lhsT=Sring[d][:, slot, :], rhs=qh[d][:, c0:c0 + 32], start=False, stop=lastc),
                            reads=[b_Sr[d][slot], b_qh[d]], writes=[b_po])
                        emit_update(d, i)
                        if lastc:
                            osl = oacc[:, ti * 128:(ti + 1) * 128]
                            if ti not in visited:
                                visited.add(ti)
                                k.op("scalar", lambda e, osl=osl, po=po: e.copy(out=osl, in_=po[:, 0:128]), reads=[b_po], writes=[b_o])
                            else:
                                k.op("vector", lambda e, osl=osl, po=po: e.tensor_tensor(out=osl, in0=po[:, 0:128], in1=osl, op=ALU.add),
                                     reads=[b_po, b_o], writes=[b_o])
                k.op("scalar", lambda e: e.activation(out=W[1][:], in_=W[0][:], func=AF.Square), reads=[bW[0], bW[1]], writes=[bW[1]])
                for t0 in range(0, T, 512):
                    n = min(512, T - t0)
                    k.op("tensor", lambda e, t0=t0, n=n: e.matmul(self.ps[0][:, 0:n], lhsT=self.ones128[:], rhs=W[1][:, t0:t0 + n],
                                                                 start=True, stop=True), reads=[bW[1], self.b_const], writes=[self.psb[0]])
                    k.op("vector", lambda e, t0=t0, n=n: e.tensor_scalar(out=W[2][:, t0:t0 + n], in0=self.ps[0][:, 0:n], scalar1=EPS, scalar2=None,
                                                                        op0=ALU.add), reads=[self.psb[0], bW[2]], writes=[bW[2]])
                k.op("scalar", lambda e: e.activation(out=W[2][:], in_=W[2][:], func=AF.Sqrt), reads=[bW[2]], writes=[bW[2]])
                k.op("vector", lambda e: e.reciprocal(out=W[2][:], in_=W[2][:]), reads=[bW[2]], writes=[bW[2]])
                k.op("vector", lambda e: e.tensor_tensor(out=W[3][:], in0=W[0][:], in1=W[2][:], op=ALU.mult), reads=[bW[0], bW[2], bW[3]], writes=[bW[3]])
                k.dma("sync", lambda e, hd=hd: e.dma_start(out=W[4][:], in_=self.pT[5, hd, :, :]), writes=[bW[4]])
                k.op("vector", lambda e: e.scalar_tensor_tensor(
                    out=kh[:, 0:NLAT].rearrange("p (r c) -> p r c", c=GRID), in0=W[3][:, 0:NLAT].rearrange("p (c r) -> p r c", r=GRID),
                    scalar=nwh[:, 0:1], in1=W[4][:, 0:NLAT].rearrange("p (r c) -> p r c", c=GRID), op0=ALU.mult, op1=ALU.mult),
                    reads=[bW[3], bW[4], b_par], writes=[b_kh])
                k.op("vector", lambda e: e.scalar_tensor_tensor(
                    out=kh[:, NLAT:T], in0=W[3][:, NLAT:T], scalar=nwh[:, 0:1], in1=W[4][:, NLAT:T], op0=ALU.mult, op1=ALU.mult),
                    reads=[bW[3], bW[4], b_par], writes=[b_kh])
                k.dma("sync", lambda e, hd=hd: e.dma_start(out=self.mixT[8 + hd, :, :], in_=kh[:]), reads=[b_kh])
            k.barrier()

    def sin_turns(self, out_ap, u, ki, tmp, bufs):
        k = self.k
        TWO_PI = 2 * math.pi * (1.0 - 1e-6)
        k.op("vector", lambda e: e.tensor_copy(out=ki, in_=u), reads=bufs, writes=bufs)
        k.op("vector", lambda e: e.tensor_copy(out=tmp, in_=ki), reads=bufs, writes=bufs)
        k.op("vector", lambda e: e.tensor_tensor(out=u, in0=u, in1=tmp, op=ALU.subtract), reads=bufs, writes=bufs)
        k.op("vector", lambda e: e.tensor_scalar(out=tmp, in0=u, scalar1=0.5, scalar2=None, op0=ALU.is_gt), reads=bufs, writes=bufs)
        k.op("vector", lambda e: e.tensor_tensor(out=u, in0=u, in1=tmp, op=ALU.subtract), reads=bufs, writes=bufs)
        k.op("vector", lambda e: e.tensor_scalar(out=tmp, in0=u, scalar1=-0.5, scalar2=None, op0=ALU.is_lt), reads=bufs, writes=bufs)
        k.op("vector", lambda e: e.tensor_tensor(out=u, in0=u, in1=tmp, op=ALU.add), reads=bufs, writes=bufs)
        k.op("scalar", lambda e: e.activation(out=out_ap, in_=u, func=AF.Sin, scale=TWO_PI), reads=bufs, writes=bufs)

    def s5_phase(self, l, need_ctx):
        k = self.k
        nc = self.nc
        L = 256
        NCHK = T // L
        PI = math.pi
        from concourse.ap import AP as _AP
        with contextlib.ExitStack() as ph:
            mag_s = [self.sb(ph, f"smag{d}", [128, 32], F32) for d in range(2)]
            th_s = [self.sb(ph, f"sth{d}", [128, 32], F32) for d in range(2)]
            dsk = self.sb(ph, "sdsk", [128, 8], F32)
            b_prm = Buf()
            k.dma("sync", lambda e: e.dma_start(out=dsk[:], in_=self.s5_d[l].rearrange("(c p) -> p c", p=128)), writes=[b_prm])
            iota = self.aux[:, 8:8 + 257]
            mg = self.aux[:, 0:8]
            with contextlib.ExitStack() as pre:
                BT = [[self.sb(pre, f"sBT{d}{r}", [128, 8, 4, 128], BF16) for r in range(2)] for d in range(2)]
                CT = [[self.sb(pre, f"sCT{d}{r}", [128, 32, 128], BF16) for r in range(2)] for d in range(2)]

                def nl(nm):
                    return self.sb(pre, nm, [64, 64], F32)
                for d in range(2):
                    for r in range(2):
                        k.op("vector", lambda e, d=d, r=r: e.memset(CT[d][r][:], 0.0), writes=[b_prm])
                lre, lim, ls, aa, mg_, thn, t1, t2, cs, sn, den, nr, cfr, cfi = [nl(f"snl{i}") for i in range(14)]
                nli = self.sb(pre, "snli", [64, 64], I32)
                Bn = [self.sb(pre, f"sBn{r}", [64, 1024], F32) for r in range(2)]
                Bb = [self.sb(pre, f"sBb{r}", [64, 1024], F32) for r in range(2)]
                tmpB = self.sb(pre, "stmpB", [64, 1024], F32)
                Cx = [self.sb(pre, f"sCx{r}", [32, 32, 128], F32) for r in range(2)]
                sl_lre = self.sb(pre, "sslre", [128, 32], F32)
                sl_lim = self.sb(pre, "sslim", [128, 32], F32)
                sl_ls = self.sb(pre, "ssls", [128, 32], F32)
                b_n = Buf()
                b_B = Buf()
                b_C = Buf()
                for d in range(2):
                    k.dma("sync", lambda e, d=d: e.dma_start(out=lre[:], in_=self.s5_lre[l, d].rearrange("g p -> p g")), writes=[b_n])
                    k.dma("sync", lambda e, d=d: e.dma_start(out=lim[:], in_=self.s5_lim[l, d].rearrange("g p -> p g")), writes=[b_n])
                    lsrc = self.s5_ls[l, d]
                    k.dma("sync", lambda e, lsrc=lsrc: e.dma_start(out=ls[:], in_=_AP(lsrc.tensor, lsrc.offset, [[0, 64], [1, 64]])), writes=[b_n])
                    for r, src in ((0, self.s5_bre), (1, self.s5_bim)):
                        k.dma("sync", lambda e, d=d, r=r, src=src: e.dma_start(
                            out=Bn[r][:].rearrange("p (g h) -> p g h", h=16), in_=src[l, d].rearrange("g p h -> p g h")), writes=[b_B])

                    def V_(fn, **kw):
                        k.op("vector", fn, reads=[b_n], writes=[b_n])

                    def A_(fn):
                        k.op("scalar", fn, reads=[b_n], writes=[b_n])
                    V_(lambda e: e.tensor_scalar(out=lre[:], in0=lre[:], scalar1=-1e-4, scalar2=None, op0=ALU.min))
                    A_(lambda e: e.activation(out=ls[:], in_=ls[:], func=AF.Exp))
                    V_(lambda e: e.tensor_tensor(out=aa[:], in0=lre[:], in1=ls[:], op=ALU.mult))
                    A_(lambda e: e.activation(out=mg_[:], in_=aa[:], func=AF.Exp))
                    V_(lambda e: e.tensor_tensor(out=thn[:], in0=lim[:], in1=ls[:], op=ALU.mult))
                    V_(lambda e: e.tensor_scalar(out=t1[:], in0=thn[:], scalar1=1.0 / (2 * PI), scalar2=None, op0=ALU.mult))
                    self.sin_turns(sn[:], t1[:], nli[:], t2[:], [b_n])
                    V_(lambda e: e.tensor_scalar(out=t1[:], in0=thn[:], scalar1=1.0 / (2 * PI), scalar2=0.25, op0=ALU.mult, op1=ALU.add))
                    self.sin_turns(cs[:], t1[:], nli[:], t2[:], [b_n])
                    V_(lambda e: e.tensor_tensor(out=cs[:], in0=cs[:], in1=mg_[:], op=ALU.mult))
                    V_(lambda e: e.tensor_tensor(out=sn[:], in0=sn[:], in1=mg_[:], op=ALU.mult))
                    V_(lambda e: e.tensor_tensor(out=den[:], in0=lre[:], in1=lre[:], op=ALU.mult))
                    V_(lambda e: e.tensor_tensor(out=t1[:], in0=lim[:], in1=lim[:], op=ALU.mult))
                    V_(lambda e: e.tensor_tensor(out=den[:], in0=den[:], in1=t1[:], op=ALU.add))
                    V_(lambda e: e.reciprocal(out=den[:], in_=den[:]))
                    V_(lambda e: e.tensor_scalar(out=nr[:], in0=cs[:], scalar1=-1.0, scalar2=None, op0=ALU.add))
                    V_(lambda e: e.tensor_tensor(out=t1[:], in0=nr[:], in1=lre[:], op=ALU.mult))
                    V_(lambda e: e.tensor_tensor(out=t2[:], in0=sn[:], in1=lim[:], op=ALU.mult))
                    V_(lambda e: e.tensor_tensor(out=cfr[:], in0=t1[:], in1=t2[:], op=ALU.add))
                    V_(lambda e: e.tensor_tensor(out=cfr[:], in0=cfr[:], in1=den[:], op=ALU.mult))
                    V_(lambda e: e.tensor_tensor(out=t1[:], in0=sn[:], in1=lre[:], op=ALU.mult))
                    V_(lambda e: e.tensor_tensor(out=t2[:], in0=nr[:], in1=lim[:], op=ALU.mult))
                    V_(lambda e: e.tensor_tensor(out=cfi[:], in0=t1[:], in1=t2[:], op=ALU.subtract))
                    V_(lambda e: e.tensor_tensor(out=cfi[:], in0=cfi[:], in1=den[:], op=ALU.mult))
                    def v3(t):
                        return t[:].rearrange("p (g h) -> p g h", h=16)
                    k.op("vector", lambda e: e.tensor_tensor(out=v3(Bb[0]), in0=v3(Bn[0]), in1=bc_last(cfr[:], 16), op=ALU.mult), reads=[b_n, b_B], writes=[b_B])
                    k.op("vector", lambda e: e.tensor_tensor(out=v3(tmpB), in0=v3(Bn[1]), in1=bc_last(cfi[:], 16), op=ALU.mult), reads=[b_n, b_B], writes=[b_B])
                    k.op("vector", lambda e: e.tensor_tensor(out=Bb[0][:], in0=Bb[0][:], in1=tmpB[:], op=ALU.subtract), reads=[b_B], writes=[b_B])
                    k.op("vector", lambda e: e.tensor_tensor(out=v3(Bb[1]), in0=v3(Bn[1]), in1=bc_last(cfr[:], 16), op=ALU.mult), reads=[b_n, b_B], writes=[b_B])
                    k.op("vector", lambda e: e.tensor_tensor(out=v3(tmpB), in0=v3(Bn[0]), in1=bc_last(cfi[:], 16), op=ALU.mult), reads=[b_n, b_B], writes=[b_B])
                    k.op("vector", lambda e: e.tensor_tensor(out=Bb[1][:], in0=Bb[1][:], in1=tmpB[:], op=ALU.add), reads=[b_B], writes=[b_B])
                    for r in range(2):
                        for tb in range(8):
                            pb = 1 + (tb % 4)
                            k.op("tensor", lambda e, r=r, tb=tb, pb=pb: e.transpose(
                                out=self.ps[pb][:, 0:64], in_=Bb[r][:, tb * 128:(tb + 1) * 128], identity=self.ident[0:64, 0:64]),
                                reads=[b_B, self.b_const], writes=[self.psb[pb]])
                            for g8 in range(8):
                                dst = BT[d][r][:, tb, g8 // 2, (g8 % 2) * 64:(g8 % 2) * 64 + 64]
                                if g8 % 2 == 0:
                                    k.op("vector", lambda e, dst=dst, pb=pb, g8=g8: e.tensor_scalar(
                                        out=dst, in0=self.ps[pb][:, 0:64], scalar1=mg[:, g8:g8 + 1], scalar2=None, op0=ALU.mult),
                                        reads=[self.psb[pb], self.b_const], writes=[b_prm])
                                else:
                                    k.op("scalar", lambda e, dst=dst, pb=pb, g8=g8: e.activation(
                                        out=dst, in_=self.ps[pb][:, 0:64], func=AF.Copy, scale=mg[:, g8:g8 + 1]),
                                        reads=[self.psb[pb], self.b_const], writes=[b_prm])
                    for g2 in range(2):
                        k.dma("sync", lambda e, d=d, g2=g2: e.dma_start(out=sl_lre[64 * g2:64 * g2 + 64, :],
                                                                        in_=self.s5_lre[l, d, g2::2, :].rearrange("gp p -> p gp")), writes=[b_n])
                        k.dma("sync", lambda e, d=d, g2=g2: e.dma_start(out=sl_lim[64 * g2:64 * g2 + 64, :],
                                                                        in_=self.s5_lim[l, d, g2::2, :].rearrange("gp p -> p gp")), writes=[b_n])
                        lsrc2 = self.s5_ls[l, d, g2::2]
                        k.dma("sync", lambda e, g2=g2, lsrc2=lsrc2: e.dma_start(
                            out=sl_ls[64 * g2:64 * g2 + 64, :], in_=_AP(lsrc2.tensor, lsrc2.offset, [[0, 64], [2, 32]])), writes=[b_n])
                    V_(lambda e: e.tensor_scalar(out=sl_lre[:], in0=sl_lre[:], scalar1=-1e-4, scalar2=None, op0=ALU.min))
                    A_(lambda e: e.activation(out=sl_ls[:], in_=sl_ls[:], func=AF.Exp))
                    V_(lambda e: e.tensor_tensor(out=sl_lre[:], in0=sl_lre[:], in1=sl_ls[:], op=ALU.mult))
                    k.op("scalar", lambda e, d=d: e.activation(out=mag_s[d][:], in_=sl_lre[:], func=AF.Exp), reads=[b_n], writes=[b_prm])
                    k.op("vector", lambda e, d=d: e.scalar_tensor_tensor(out=th_s[d][:], in0=sl_lim[:], scalar=1.0 / (2 * PI), in1=sl_ls[:],
                                                                         op0=ALU.mult, op1=ALU.mult), reads=[b_n], writes=[b_prm])
                    for r, src in ((0, self.s5_cre), (1, self.s5_cim)):
                        k.op("vector", lambda e, r=r: e.memset(Cx[r][:], 0.0), reads=[b_C], writes=[b_C])
                        for g2 in range(2):
                            k.dma("sync", lambda e, d=d, r=r, g2=g2, src=src: e.dma_start(
                                out=Cx[r][16 * g2:16 * g2 + 16, :, 64 * g2:64 * g2 + 64],
                                in_=src[l, d, g2::2].rearrange("gp h p -> h gp p")), reads=[b_C], writes=[b_C])
                        for half in range(2):
                            pb = 5 + half
                            for i in range(16):
                                gp = half * 16 + i
                                k.op("tensor", lambda e, r=r, gp=gp, i=i, pb=pb: e.transpose(
                                    out=self.ps[pb][:, i * 32:(i + 1) * 32], in_=Cx[r][:, gp, :], identity=self.ident[0:32, 0:32]),
                                    reads=[b_C, self.b_const], writes=[self.psb[pb]])
                            pv = self.ps[pb][:, :].rearrange("p (i c) -> p i c", c=32)
                            for j in range(4):
                                dst = CT[d][r][:, half * 16 + j:half * 16 + 16:4, 32 * j:32 * j + 32]
                                k.op("scalar", lambda e, dst=dst, pv=pv, j=j, r=r: e.activation(
                                    out=dst, in_=pv[:, j:16:4, :], func=AF.Copy, scale=(1.0 if r == 0 else -1.0)),
                                    reads=[self.psb[pb]], writes=[b_prm])
                for d in range(2):
                    for r in range(2):
                        k.dma("sync", lambda e, d=d, r=r: e.dma_start(out=self.BTs[d, r], in_=BT[d][r][:].rearrange("p t j c -> p t (j c)")), reads=[b_prm])
                        k.dma("sync", lambda e, d=d, r=r: e.dma_start(out=self.CTs[d, r], in_=CT[d][r][:]), reads=[b_prm])
                k.barrier()
            L = 512
            ust = self.sb(ph, "sust", [128, T], F32)
            ubf = self.sb(ph, "subf", [128, T], BF16)
            yacc = self.sb(ph, "syacc", [128, T], F32)
            b_ust, b_ubf, b_y = Buf(), Buf(), Buf()
            tab_all = self.sb(ph, "stab", [128, 2, 4, L + 1], F32)
            tabs = [[tab_all[:, i, j, :] for i in range(2)] for j in range(4)]
            init_t = self.sb(ph, "sinit", [128, 2, 4], F32)
            x1_t = self.sb(ph, "sx1", [128, 2, 4], F32)
            x2_t = self.sb(ph, "sx2", [128, 2, 4], F32)
            b_init = Buf()
            rts = [self.sb(ph, f"srt{j}", [128, L], F32) for j in range(4)]
            phs = self.sb(ph, "sphs", [128, L + 1], F32)
            pht = self.sb(ph, "spht", [128, L + 1], F32)
            phi = self.sb(ph, "sphi", [128, L + 1], I32)
            b_tab = [Buf() for _ in range(4)]
            b_phs = Buf()
            btl = self.make_ring(ph, "sbtl", [128, 2, 2, 512], BF16, 2)
            ctl = self.make_ring(ph, "sctl", [128, 2, 2, 4, 128], BF16, 2)
            pre_s = self.make_ring(ph, "spre", [128, 2, L], F32, 4)
            wring_ = self.make_ring(ph, "sw", [128, 2, L], F32, 8)
            zri = self.make_ring(ph, "szri", [128, 4, 2, L], F32, 2)
            xri = self.make_ring(ph, "sxri", [128, 2, L], BF16, 4)
            tmpP = self.make_ring(ph, "stp", [128, L], F32, 2)
            tmpV = self.make_ring(ph, "stv", [128, 2, L], F32, 2)
            tmpP2 = self.make_ring(ph, "stp2", [128, 2, L], F32, 1)
            psr = Ring([1, 2, 3, 4])
            psy = Ring([5, 6])
            iotaL = self.sb(ph, "siota", [128, L + 1], F32)
            b_io = Buf()
            k.dma("sync", lambda e: e.dma_start(out=iotaL[:], in_=self.c_iota[:, :]), writes=[b_io])
            chunks = [(NLAT, T)] + [(i * L, (i + 1) * L) for i in range(NLAT // L)]
            for tb in range(8):
                bt, b_bt = btl.next()
                ct, b_ct = ctl.next()
                k.dma("sync", lambda e, tb=tb, bt=bt: e.dma_start(out=bt[:], in_=self.BTs[:, :, :, tb, :].rearrange("d r p c -> p d r c")), writes=[b_bt])
                k.dma("sync", lambda e, tb=tb, ct=ct: e.dma_start(out=ct[:], in_=self.CTs[:, :, :, tb * 4:(tb + 1) * 4, :].rearrange("d r p g c -> p d r g c")), writes=[b_ct])
                k.dma("sync", lambda e, tb=tb: e.dma_start(out=ust[:], in_=self.pT[0, tb, :, :]), reads=[b_ust], writes=[b_ust])
                k.op("scalar", lambda e: e.copy(out=ubf[:], in_=ust[:]), reads=[b_ust], writes=[b_ubf])
                k.op("vector", lambda e, tb=tb: e.tensor_scalar(out=yacc[:], in0=ust[:], scalar1=dsk[:, tb:tb + 1], scalar2=None, op0=ALU.mult),
                     reads=[b_ust, b_prm], writes=[b_y])
                for d in range(2):
                    for j in range(4):
                        gp = tb * 4 + j
                        thc = th_s[d][:, gp:gp + 1]
                        for which, off in ((1, 0.0), (0, 0.25)):
                            k.op("vector", lambda e, thc=thc, off=off: e.tensor_scalar(out=phs[:], in0=iotaL[:], scalar1=thc, scalar2=off,
                                                                                       op0=ALU.mult, op1=ALU.add), reads=[b_prm, b_io, b_phs], writes=[b_phs])
                            self.sin_turns(tabs[j][which], phs[:], phi[:], pht[:], [b_phs, b_tab[j]])
                        k.op("vector", lambda e, j=j, d=d, gp=gp: e.tensor_scalar(out=rts[j][:], in0=iotaL[:, 0:L], scalar1=0.0,
                                                                                 scalar2=mag_s[d][:, gp:gp + 1], op0=ALU.mult, op1=ALU.add),
                             reads=[b_prm, b_io], writes=[b_tab[j]])
                    order = chunks if d == 0 else [chunks[0]] + chunks[:0:-1]
                    NCI = len(order)
                    Aout, Zout = {}, {}

                    def sv(t2d, ci, d=d, order=order):
                        lo, hi = order[ci]
                        v = t2d[:, lo:hi]
                        return v[:, ::-1] if d == 1 else v

                    def TT(E, out, in0, in1, op, reads, writes):
                        k.op(E, lambda e: e.tensor_tensor(out=out, in0=in0, in1=in1, op=op), reads=reads, writes=writes)

                    def stageA(ci, d=d, tb=tb, bt=bt, b_bt=b_bt, sv=sv, order=order):
                        n = order[ci][1] - order[ci][0]
                        for j in range(4):
                            EA = "gpsimd" if j < 3 else "vector"
                            cs_t = tabs[j][0][:, 0:n]
                            sn_t = tabs[j][1][:, 0:n]
                            p1i, _ = psr.next()
                            p2i, _ = psr.next()
                            p1, p2 = self.ps[p1i], self.ps[p2i]
                            rhs = sv(ubf, ci)
                            k.op("tensor", lambda e, p1=p1, j=j, rhs=rhs, n=n: e.matmul(
                                p1[:, 0:n], lhsT=bt[:, d, 0, j * 128:(j + 1) * 128], rhs=rhs, start=True, stop=True),
                                reads=[b_bt, b_ubf], writes=[self.psb[p1i]])
                            k.op("tensor", lambda e, p2=p2, j=j, rhs=rhs, n=n: e.matmul(
                                p2[:, 0:n], lhsT=bt[:, d, 1, j * 128:(j + 1) * 128], rhs=rhs, start=True, stop=True),
                                reads=[b_bt, b_ubf], writes=[self.psb[p2i]])
                            if EA == "gpsimd":
                                pr, bpr = pre_s.next()
                                k.op("scalar", lambda e, pr=pr, p1=p1, n=n: e.copy(out=pr[:, 0, 0:n], in_=p1[:, 0:n]), reads=[self.psb[p1i]], writes=[bpr])
                                k.op("scalar", lambda e, pr=pr, p2=p2, n=n: e.copy(out=pr[:, 1, 0:n], in_=p2[:, 0:n]), reads=[self.psb[p2i]], writes=[bpr])
                                s_re, s_im = pr[:, 0, 0:n], pr[:, 1, 0:n]
                                rd = [bpr, b_tab[j]]
                                tp, btp = tmpP.next()
                                tpv = tp[:, 0:n]
                            else:
                                s_re, s_im = p1[:, 0:n], p2[:, 0:n]
                                rd = [self.psb[p1i], self.psb[p2i], b_tab[j]]
                                tp, btp = tmpV.next()
                                tpv = tp[:, 0, 0:n]
                            w, bw = wring_.next()
                            TT(EA, w[:, 0, 0:n], s_re, cs_t, ALU.mult, rd, [bw])
                            TT(EA, tpv, s_im, sn_t, ALU.mult, rd, [btp])
                            TT(EA, w[:, 0, 0:n], w[:, 0, 0:n], tpv, ALU.add, [bw, btp], [bw])
                            TT(EA, w[:, 1, 0:n], s_im, cs_t, ALU.mult, rd, [bw])
                            TT(EA, tpv, s_re, sn_t, ALU.mult, rd + [btp], [btp])
                            TT(EA, w[:, 1, 0:n], w[:, 1, 0:n], tpv, ALU.subtract, [bw, btp], [bw])
                            Aout[(ci, j)] = (w, bw, n)

                    def stageB(ci, order=order):
                        zr, bzr = zri.next()
                        if ci > 0:
                            nprev = order[ci - 1][1] - order[ci - 1][0]
                            zp, bzp = Zout["prev"]
                            zend = zp[:, :, :, nprev - 1].rearrange("p j c -> p c j")
                            cLb = tab_all[:, 0, :, nprev].unsqueeze(1).to_broadcast([128, 2, 4])
                            sLb = tab_all[:, 1, :, nprev].unsqueeze(1).to_broadcast([128, 2, 4])
                            rdc = [bzp] + b_tab + [b_init]
                            k.op("vector", lambda e, zend=zend, cLb=cLb: e.tensor_tensor(out=x1_t[:], in0=zend, in1=cLb, op=ALU.mult), reads=rdc, writes=[b_init])
                            k.op("vector", lambda e, zend=zend, sLb=sLb: e.tensor_tensor(out=x2_t[:], in0=zend, in1=sLb, op=ALU.mult), reads=rdc, writes=[b_init])
                            k.op("vector", lambda e: e.tensor_tensor(out=init_t[:, 0, :], in0=x1_t[:, 0, :], in1=x2_t[:, 1, :], op=ALU.subtract), reads=[b_init], writes=[b_init])
                            k.op("vector", lambda e: e.tensor_tensor(out=init_t[:, 1, :], in0=x1_t[:, 1, :], in1=x2_t[:, 0, :], op=ALU.add), reads=[b_init], writes=[b_init])
                        for j in range(4):
                            w, bw, n = Aout.pop((ci, j))
                            for c2 in range(2):
                                ini = 0.0 if ci == 0 else init_t[:, c2, j:j + 1]
                                k.op("vector", lambda e, zr=zr, w=w, c2=c2, ini=ini, j=j, n=n: e.tensor_tensor_scan(
                                    out=zr[:, j, c2, 0:n], data0=rts[j][:, 0:n], data1=w[:, c2, 0:n], initial=ini, op0=ALU.mult, op1=ALU.add),
                                    reads=[bw, b_tab[j], b_init], writes=[bzr])
                            Zout[(ci, j)] = (zr[:, j], bzr, n)
                        Zout["prev"] = (zr, bzr)

                    def stageC(ci, d=d, ct=ct, b_ct=b_ct, sv=sv):
                        pyi, _ = psy.next()
                        py = self.ps[pyi]
                        for j in range(4):
                            zr, bzr, n = Zout.pop((ci, j))
                            E = "vector" if j < 3 else "gpsimd"
                            cs_t = tabs[j][0][:, 0:n]
                            sn_t = tabs[j][1][:, 0:n]
                            xr_, bxr_ = xri.next()
                            tv, btv = tmpV.next() if E == "vector" else tmpP2.next()
                            zre, zim = zr[:, 0, 0:n], zr[:, 1, 0:n]
                            A_, B_ = tv[:, 0, 0:n], tv[:, 1, 0:n]
                            rd2 = [bzr, b_tab[j]]
                            TT(E, A_, zre, cs_t, ALU.mult, rd2, [btv])
                            TT(E, B_, zim, sn_t, ALU.mult, rd2 + [btv], [btv])
                            TT(E, xr_[:, 0, 0:n], A_, B_, ALU.subtract, [btv], [bxr_])
                            TT(E, A_, zim, cs_t, ALU.mult, rd2 + [btv], [btv])
                            TT(E, B_, zre, sn_t, ALU.mult, rd2 + [btv], [btv])
                            TT(E, xr_[:, 1, 0:n], A_, B_, ALU.add, [btv], [bxr_])
                            k.op("tensor", lambda e, py=py, xr_=xr_, j=j, n=n: e.matmul(
                                py[:, 0:n], lhsT=ct[:, d, 0, j, :], rhs=xr_[:, 0, 0:n], start=(j == 0), stop=False),
                                reads=[b_ct, bxr_], writes=[self.psb[pyi]])
                            k.op("tensor", lambda e, py=py, xr_=xr_, j=j, n=n: e.matmul(
                                py[:, 0:n], lhsT=ct[:, d, 1, j, :], rhs=xr_[:, 1, 0:n], start=False, stop=(j == 3)),
                                reads=[b_ct, bxr_], writes=[self.psb[pyi]])
                        yv = sv(yacc, ci)
                        k.op("vector", lambda e, py=py, yv=yv, n=n: e.tensor_tensor(out=yv, in0=py[:, 0:n], in1=yv, op=ALU.add),
                             reads=[self.psb[pyi], b_y], writes=[b_y])

                    stageA(0)
                    for ci in range(NCI):
                        if ci + 1 < NCI:
                            stageA(ci + 1)
                        stageB(ci)
                        stageC(ci)
                k.op("vector", lambda e: e.tensor_tensor(out=ust[:], in0=yacc[:], in1=yacc[:], op=ALU.mult), reads=[b_y, b_ust], writes=[b_ust])
                k.op("vector", lambda e: e.tensor_scalar(out=ust[:], in0=ust[:], scalar1=0.044715, scalar2=1.0, op0=ALU.mult, op1=ALU.add),
                     reads=[b_ust], writes=[b_ust])
                k.op("vector", lambda e: e.tensor_tensor(out=ust[:], in0=ust[:], in1=yacc[:], op=ALU.mult), reads=[b_ust, b_y], writes=[b_ust])
                k.op("scalar", lambda e: e.activation(out=ust[:], in_=ust[:], func=AF.Sigmoid, scale=1.5957691216057308), reads=[b_ust], writes=[b_ust])
                k.op("vector", lambda e: e.tensor_tensor(out=ust[:], in0=ust[:], in1=yacc[:], op=ALU.mult), reads=[b_ust, b_y], writes=[b_ust])
                k.dma("sync", lambda e, tb=tb: e.dma_start(out=self.gS5[tb, :, :], in_=ust[:]), reads=[b_ust])
            k.barrier()

    def glu_phase(self, l, need_ctx):
        k = self.k
        with contextlib.ExitStack() as ph:
            self.wring = self.make_ring(ph, "wr", [128, 4096], BF16, 5)
            gf = self.sb(ph, "ggf", [128, 8, 1024], F32)
            gb = self.sb(ph, "ggb", [128, 8, 1024], BF16)
            bgl = self.sb(ph, "gbgl", [128, 8], F32)
            b_gf, b_gb, b_bg = Buf(), Buf(), Buf()
            sgr = self.make_ring(ph, "gsg", [128, 512], F32, 3)
            outr = self.make_ring(ph, "gout", [128, 512], BF16, 3)
            k.dma("sync", lambda e: e.dma_start(out=bgl[:], in_=self.s5_bglu[l].rearrange("(c p) -> p c", p=128)), writes=[b_bg])
            wv = self.s5_wglu[l].rearrange("(kt p) c -> p kt c", p=128)
            gv = self.gS5.rearrange("c p t -> p c t")
            psr = Ring([1, 2, 3, 4])
            for (t0, ts, s) in self.supertiles(need_ctx):
                nh = max(1, ts // 512)
                n = min(512, ts)
                k.dma("sync", lambda e, t0=t0, ts=ts: e.dma_start(out=gf[:, :, 0:ts], in_=gv[:, :, t0:t0 + ts]), reads=[b_gf], writes=[b_gf])
                k.op("scalar", lambda e, ts=ts: e.copy(out=gb[:, :, 0:ts], in_=gf[:, :, 0:ts]), reads=[b_gf, b_gb], writes=[b_gb])
                stream = Prog.WStream(self, [wv[:, :, bi * 256:(bi + 1) * 256] for bi in range(4)], 3)
                for bi in range(4):
                    wt, wb = stream.get()
                    for sub in range(2):
                        m = bi * 2 + sub
                        for hf in range(nh):
                            tsl = slice(hf * 512, hf * 512 + n)
                            tok0 = t0 + hf * 512
                            pi, _ = psr.next()
                            pp = self.ps[pi]
                            for kt in range(8):
                                k.op("tensor", lambda e, pp=pp, wt=wt, kt=kt, sub=sub, tsl=tsl, n=n: e.matmul(
                                    pp[:, 0:n], lhsT=wt[:, kt, sub * 128:(sub + 1) * 128], rhs=gb[:, kt, tsl],
                                    start=(kt == 0), stop=(kt == 7)), reads=[wb, b_gb], writes=[self.psb[pi]])
                            sg, bsg = sgr.next()
                            k.op("scalar", lambda e, sg=sg, pp=pp, m=m, n=n: e.activation(out=sg[:, 0:n], in_=pp[:, 0:n], func=AF.Sigmoid,
                                                                                       bias=bgl[:, m:m + 1]), reads=[self.psb[pi], b_bg], writes=[bsg])
                            ot, bot = outr.next()
                            k.op("vector", lambda e, ot=ot, sg=sg, m=m, tsl=tsl, n=n: e.tensor_tensor(out=ot[:, 0:n], in0=sg[:, 0:n], in1=gf[:, m, tsl], op=ALU.mult),
                                 reads=[bsg, b_gf], writes=[bot])
                            k.dma("sync", lambda e, ot=ot, m=m, tok0=tok0, n=n: e.dma_start(out=self.mixT[m, :, tok0:tok0 + n], in_=ot[:, 0:n]), reads=[bot])
            k.barrier()

    def outproj_phase(self, l, include_ctx):
        k = self.k
        with contextlib.ExitStack() as ph:
            self.wring = self.make_ring(ph, "wr", [128, 4096], BF16, 5)
            mx = self.sb(ph, "omx", [128, 16, 1024], BF16)
            b_mx = Buf()
            xrring = self.make_ring(ph, "oxr", [128, 512], F32, 3)
            wv = self.w_out[l].rearrange("(kt p) c -> p kt c", p=128)
            mv = self.mixT.rearrange("c p t -> p c t")
            psr = Ring([1, 2, 3, 4])
            for (t0, ts, s) in self.supertiles(include_ctx):
                nh = max(1, ts // 512)
                n = min(512, ts)
                k.dma("sync", lambda e, t0=t0, ts=ts: e.dma_start(out=mx[:, :, 0:ts], in_=mv[:, :, t0:t0 + ts]), reads=[b_mx], writes=[b_mx])
                stream = Prog.WStream(self, [wv[:, :, bi * 256:(bi + 1) * 256] for bi in range(8)], 3)
                for bi in range(8):
                    wt, wb = stream.get()
                    for sub in range(2):
                        m = bi * 2 + sub
                        for hf in range(nh):
                            tsl = slice(hf * 512, hf * 512 + n)
                            tok0 = t0 + hf * 512
                            xr, bxr = xrring.next()
                            k.dma("sync", lambda e, xr=xr, m=m, tok0=tok0, n=n: e.dma_start(out=xr[:, 0:n], in_=self.xT[m, :, tok0:tok0 + n]),
                                  reads=[self.xbuf(m, tok0)], writes=[bxr])
                            pi, _ = psr.next()
                            pp = self.ps[pi]
                            for kt in range(16):
                                k.op("tensor", lambda e, pp=pp, wt=wt, kt=kt, sub=sub, tsl=tsl, n=n: e.matmul(
                                    pp[:, 0:n], lhsT=wt[:, kt, sub * 128:(sub + 1) * 128], rhs=mx[:, kt, tsl],
                                    start=(kt == 0), stop=(kt == 15)), reads=[wb, b_mx], writes=[self.psb[pi]])
                            k.op("vector", lambda e, xr=xr, pp=pp, m=m, s=s, n=n: e.scalar_tensor_tensor(
                                out=xr[:, 0:n], in0=pp[:, 0:n], scalar=self.Gmod[:, s, 1, m:m + 1], in1=xr[:, 0:n],
                                op0=ALU.mult, op1=ALU.add), reads=[self.psb[pi], bxr, self.b_mod], writes=[bxr])
                            k.dma("sync", lambda e, xr=xr, m=m, tok0=tok0, n=n: e.dma_start(out=self.xT[m, :, tok0:tok0 + n], in_=xr[:, 0:n]),
                                  reads=[bxr], writes=[self.xbuf(m, tok0)])
            k.barrier()


def _consts():
    ident = np.eye(128, dtype=np.float32)
    s = np.arange(128)[:, None]
    t = np.arange(128)[None, :]
    same = (s // 32) == (t // 32)
    m_f = (same & (s <= t)).astype(np.float32)
    m_b = (same & (s >= t)).astype(np.float32)
    g2 = ((np.arange(128) % 32) // 16)
    m0 = np.repeat((g2 == 0).astype(np.float32)[:, None], 128, 1)
    m1 = np.repeat((g2 == 1).astype(np.float32)[:, None], 128, 1)
    m96 = np.repeat((np.arange(128) >= 96).astype(np.float32)[:, None], 128, 1)
    mg = ((np.arange(128)[:, None] // 16) == np.arange(8)[None, :]).astype(np.float32)
    iota = np.repeat(np.arange(257, dtype=np.float32)[None, :], 128, 0)
    aux = np.concatenate([mg, iota], axis=1).astype(np.float32)
    iota2 = np.repeat(np.arange(513, dtype=np.float32)[None, :], 128, 0)
    return ident, np.stack([m_f, m_b, m0, m1, m96]).astype(np.float32), aux, iota2


W_NAMES = ["w_ada", "b_ada", "norm_w", "ffn_w_gate", "ffn_w_up", "ffn_w_down", "w_in", "w_out",
           "s5_lambda_re", "s5_lambda_im", "s5_log_step", "s5_b_re", "s5_b_im", "s5_c_re", "s5_c_im",
           "s5_d", "s5_w_glu", "s5_b_glu", "hgrn_lower_bounds", "hgrn_norm_w", "final_norm_w"]


def make_in_map(inputs, b):
    ident, masks, aux, iota2 = _consts()
    m = {"xin": np.ascontiguousarray(np.concatenate([inputs["x"][b], inputs["ctx"][b]], axis=0), dtype=np.float32),
         "cc": np.ascontiguousarray(np.stack([inputs["c"][b], inputs["c_ctx"]], axis=0), dtype=np.float32),
         "c_ident": ident, "c_masks": masks, "c_aux": aux, "c_iota": iota2}
    for nme in W_NAMES:
        m[nme] = np.ascontiguousarray(inputs[nme], dtype=np.float32)
    return m


def kernel(**inputs):
    nc = Prog().build()
    nb = inputs["x"].shape[0]
    in_maps = [make_in_map(inputs, c % nb) for c in range(8)]
    res = run_bass_kernel_spmd(nc, in_maps, core_ids=list(range(8)))
    return np.stack([np.asarray(res.results[b]["Y"]) for b in range(nb)], axis=0).astype(np.float32)
```

```python
import contextlib
import math
from collections import deque

import numpy as np
import concourse.bass as bass
import concourse.mybir as mybir
from concourse.bass_utils import run_bass_kernel_spmd

F32 = mybir.dt.float32
BF16 = mybir.dt.bfloat16
I32 = mybir.dt.int32
ALU = mybir.AluOpType
AF = mybir.ActivationFunctionType

D = 2048
NC_ = 16
FF = 5504
NFF = 43
NLAT = 4096
NCTX = 256
T = NLAT + NCTX
DEPTH = 2
EPS = 1e-6
INC = 6144
GRID = 64

ENGS = ("tensor", "vector", "scalar", "gpsimd", "sync")
SEM_ROLL = 30000
NO_SELF_SYNC = ("tensor",)


class Buf:
    __slots__ = ("name", "w", "r")

    def __init__(self, name=""):
        self.name = name
        self.w = None
        self.r = {}


class K:
    def __init__(self, nc, stack, n_dma_sems=32):
        self.nc = nc
        self.stack = stack
        self.q = {e: [] for e in ENGS}
        self.sem = {}
        self.cnt = {}
        self.waited = {e: {} for e in ENGS}
        self.nsem = 0
        self.sem_owner = {}
        self.no_self_sync = set(NO_SELF_SYNC)
        for e in ("tensor", "vector", "scalar", "gpsimd"):
            self._new_eng_sem(e)
        self.dma_sems = []
        for i in range(n_dma_sems):
            s = stack.enter_context(nc.semaphore(f"dma{i}"))
            self.dma_sems.append([s, 0])
        self.dma_rr = 0
        self.n_instr = 0

    def _new_eng_sem(self, e):
        s = self.stack.enter_context(self.nc.semaphore(f"s_{e}_{self.nsem}"))
        self.nsem += 1
        self.sem[e] = s
        self.cnt[e] = 0
        self.sem_owner[id(s)] = e

    def _collect(self, reads, writes):
        evs = []
        for b in reads:
            if b.w is not None:
                evs.append(b.w)
        for b in writes:
            if b.w is not None:
                evs.append(b.w)
            evs.extend(b.r.values())
        return evs

    def _waits_for(self, eng, evs):
        best = {}
        for (s, v) in evs:
            kk = id(s)
            if eng in self.no_self_sync and self.sem_owner.get(kk) == eng:
                continue
            if kk not in best or best[kk][1] < v:
                best[kk] = (s, v)
        out = []
        wd = self.waited[eng]
        for kk, (s, v) in best.items():
            if wd.get(kk, -1) >= v:
                continue
            wd[kk] = v
            out.append((s, v))
        return out

    def _update(self, ev, reads, writes):
        for b in writes:
            b.w = ev
            b.r = {}
        for b in reads:
            b.r[id(ev[0])] = ev

    def op(self, eng, fn, reads=(), writes=(), extra=()):
        evs = self._collect(reads, writes) + list(extra)
        waits = self._waits_for(eng, evs)
        if self.cnt[eng] >= SEM_ROLL:
            self._new_eng_sem(eng)
        self.cnt[eng] += 1
        ev = (self.sem[eng], self.cnt[eng])
        self.q[eng].append((waits, fn, ev[0], 1))
        self._update(ev, reads, writes)
        self.n_instr += 1
        return ev

    def dma(self, eng, fn, reads=(), writes=(), extra=()):
        evs = self._collect(reads, writes) + list(extra)
        slot = self.dma_sems[self.dma_rr]
        self.dma_rr = (self.dma_rr + 1) % len(self.dma_sems)
        if slot[1] > 0:
            evs.append((slot[0], slot[1]))
        waits = self._waits_for(eng, evs)
        slot[1] += 16
        ev = (slot[0], slot[1])
        self.q[eng].append((waits, fn, ev[0], 16))
        self._update(ev, reads, writes)
        self.n_instr += 1
        return ev

    def all_events(self):
        evs = []
        for e in ("tensor", "vector", "scalar", "gpsimd"):
            if self.cnt[e] > 0:
                evs.append((self.sem[e], self.cnt[e]))
        for s, v in self.dma_sems:
            if v > 0:
                evs.append((s, v))
        return evs

    def barrier(self):
        evs = self.all_events()
        for e in ENGS:
            saved = self.no_self_sync
            self.no_self_sync = set()
            waits = self._waits_for(e, evs)
            self.no_self_sync = saved
            if waits:
                self.q[e].append((waits, None, None, 0))

    def finish(self):
        nc = self.nc
        self.barrier()
        q = self.q

        def run(e, items):
            for (waits, fn, sem, inc) in items:
                for (s, v) in waits:
                    e.wait_ge(s, v)
                if fn is None:
                    continue
                ins = fn(e)
                ins.then_inc(sem, inc)

        with nc.Block() as block:
            @block.sync
            def _(e):
                run(e, q["sync"])

            @block.tensor
            def _(e):
                run(e, q["tensor"])

            @block.vector
            def _(e):
                run(e, q["vector"])

            @block.scalar
            def _(e):
                run(e, q["scalar"])

            @block.gpsimd
            def _(e):
                run(e, q["gpsimd"])


class Ring:
    def __init__(self, tiles):
        self.tiles = tiles
        self.bufs = [Buf() for _ in tiles]
        self.i = 0

    def next(self):
        i = self.i
        self.i = (i + 1) % len(self.tiles)
        return self.tiles[i], self.bufs[i]


def bc_mid(ap2, n):
    return ap2.unsqueeze(1).to_broadcast([ap2.shape[0], n, ap2.shape[1]])


def bc_last(ap2, n):
    return ap2.unsqueeze(2).to_broadcast([ap2.shape[0], ap2.shape[1], n])


class Prog:
    def __init__(self, n_layers=DEPTH, stage="full"):
        self.n_layers = n_layers
        self.stage = stage
        self.nc = bass.Bass("TRN2", target_bir_lowering=False)
        self.st = contextlib.ExitStack()

    def dram_in(self, name, shape):
        return self.nc.dram_tensor(name, list(shape), F32, kind="ExternalInput").ap()

    def sb(self, stack, name, shape, dtype):
        self._uid = getattr(self, "_uid", 0) + 1
        return stack.enter_context(self.nc.sbuf_tensor(f"{name}_{self._uid}", list(shape), dtype))

    def make_ring(self, stack, name, shape, dtype, n):
        return Ring([self.sb(stack, f"{name}{i}", shape, dtype) for i in range(n)])

    def xbuf(self, c, tok):
        return self.xT_bufs[c][tok // 512]

    def wload(self, src):
        t, b = self.wring.next()
        kt, cols = src.shape[1], src.shape[2]
        dst = t[:, 0:kt * cols].rearrange("p (k c) -> p k c", c=cols)
        self.k.dma("gpsimd", lambda e: e.dma_start(out=dst, in_=src), writes=[b])
        return dst, b

    class WStream:
        def __init__(self, prog, srcs, lookahead):
            self.p = prog
            self.srcs = srcs
            self.n = 0
            self.loaded = deque()
            self.la = lookahead

        def get(self):
            while self.n < len(self.srcs) and len(self.loaded) < self.la + 1:
                self.loaded.append(self.p.wload(self.srcs[self.n]))
                self.n += 1
            return self.loaded.popleft()

    def build(self):
        nc = self.nc
        st = self.st
        L = DEPTH
        self.xin = self.dram_in("xin", [T, D])
        self.cc = self.dram_in("cc", [2, D])
        self.w_ada = self.dram_in("w_ada", [L, D, 9 * D])
        self.b_ada = self.dram_in("b_ada", [L, 9 * D])
        self.norm_w = self.dram_in("norm_w", [L, 3, D])
        self.wg = self.dram_in("ffn_w_gate", [L, 2, D, FF])
        self.wu = self.dram_in("ffn_w_up", [L, 2, D, FF])
        self.wd = self.dram_in("ffn_w_down", [L, 2, FF, D])
        self.w_in = self.dram_in("w_in", [L, D, INC])
        self.w_out = self.dram_in("w_out", [L, D, D])
        self.s5_lre = self.dram_in("s5_lambda_re", [L, 2, 64, 64])
        self.s5_lim = self.dram_in("s5_lambda_im", [L, 2, 64, 64])
        self.s5_ls = self.dram_in("s5_log_step", [L, 2, 64])
        self.s5_bre = self.dram_in("s5_b_re", [L, 2, 64, 64, 16])
        self.s5_bim = self.dram_in("s5_b_im", [L, 2, 64, 64, 16])
        self.s5_cre = self.dram_in("s5_c_re", [L, 2, 64, 16, 64])
        self.s5_cim = self.dram_in("s5_c_im", [L, 2, 64, 16, 64])
        self.s5_d = self.dram_in("s5_d", [L, 1024])
        self.s5_wglu = self.dram_in("s5_w_glu", [L, 1024, 1024])
        self.s5_bglu = self.dram_in("s5_b_glu", [L, 1024])
        self.hg_lb = self.dram_in("hgrn_lower_bounds", [L, 2, 1024])
        self.hg_nw = self.dram_in("hgrn_norm_w", [L, 128])
        self.fin_w = self.dram_in("final_norm_w", [D])
        self.c_ident = self.dram_in("c_ident", [128, 128])
        self.c_masks = self.dram_in("c_masks", [5, 128, 128])
        self.c_aux = self.dram_in("c_aux", [128, 8 + 257])
        self.c_iota = self.dram_in("c_iota", [128, 513])
        self.split_tail = (self.stage == "full" and self.n_layers == DEPTH)
        self.cur_nlat = NLAT
        self.sel_in = self.dram_in("sel", [128, 2])
        self.Y = nc.dram_tensor("Y", [NLAT // 2 if self.split_tail else NLAT, D], F32, kind="ExternalOutput").ap()
        self.xsel = nc.dram_tensor("xsel_s", [NC_, 128, NLAT // 2], F32).ap()
        self.xT = nc.dram_tensor("xT_s", [NC_, 128, T], F32).ap()
        self.pT = nc.dram_tensor("pT_s", [6, 8, 128, T], F32).ap()
        self.vtm = nc.dram_tensor("vtm_s", [T, 1024], F32).ap()
        self.gS5 = nc.dram_tensor("gs5_s", [8, 128, T], F32).ap()
        if self.stage in ("hgrn", "s5"):
            self.mixT = nc.dram_tensor("mixT_s", [NC_, 128, T], BF16, kind="ExternalOutput").ap()
        else:
            self.mixT = nc.dram_tensor("mixT_s", [NC_, 128, T], BF16).ap()
        self.xT_bufs = [[Buf() for _ in range(9)] for _ in range(NC_)]
        self.pT_buf = [[Buf() for _ in range(8)] for _ in range(6)]
        self.vtm_buf = Buf()
        self.gS5_buf = [Buf() for _ in range(8)]
        self.mix_buf = [Buf() for _ in range(NC_)]
        self.BTs = nc.dram_tensor("BTs_s", [2, 2, 128, 8, 512], BF16).ap()
        self.CTs = nc.dram_tensor("CTs_s", [2, 2, 128, 32, 128], BF16).ap()

        self.k = K(nc, st)
        k = self.k
        self.ps = [st.enter_context(nc.psum_tensor(f"ps{i}", [128, 512], F32)) for i in range(8)]
        self.psb = [Buf() for _ in range(8)]
        self.ident = self.sb(st, "ident", [128, 128], F32)
        self.onesD = self.sb(st, "onesD", [128, 128], F32)
        self.ones128 = self.sb(st, "ones128", [128, 128], F32)
        self.masks = self.sb(st, "masks", [128, 5, 128], F32)
        self.aux = self.sb(st, "aux", [128, 8 + 257], F32)
        self.modT = self.sb(st, "modT", [128, 2, 144], F32)
        self.Amod = self.sb(st, "Amod", [128, 2, 3, 16], F32)
        self.Gmod = self.sb(st, "Gmod", [128, 2, 3, 16], F32)
        self.wfin = self.sb(st, "wfin", [128, 16], F32)
        self.b_const = Buf()
        self.b_mod = Buf()
        self.ident_bf = self.sb(st, "identbf", [128, 128], BF16)
        self.onesD_bf = self.sb(st, "onesDbf", [128, 128], BF16)

        with nc.allow_non_contiguous_dma("small parameter vectors are laid out feature-on-partition"):
            self.emit()
            k.finish()
        st.close()
        return nc

    def emit(self):
        k = self.k
        k.dma("sync", lambda e: e.dma_start(out=self.ident[:], in_=self.c_ident[:, :]), writes=[self.b_const])
        k.dma("sync", lambda e: e.dma_start(out=self.masks[:], in_=self.c_masks.rearrange("m p f -> p m f")),
              writes=[self.b_const])
        k.dma("sync", lambda e: e.dma_start(out=self.wfin[:], in_=self.fin_w.rearrange("(c p) -> p c", p=128)),
              writes=[self.b_const])
        k.dma("sync", lambda e: e.dma_start(out=self.aux[:], in_=self.c_aux[:, :]), writes=[self.b_const])
        k.op("vector", lambda e: e.memset(self.onesD[:], 1.0 / D), writes=[self.b_const])
        k.op("vector", lambda e: e.memset(self.ones128[:], 1.0 / 128), writes=[self.b_const])
        k.op("vector", lambda e: e.tensor_copy(out=self.ident_bf[:], in_=self.ident[:]), reads=[self.b_const], writes=[self.b_const])
        k.op("vector", lambda e: e.tensor_copy(out=self.onesD_bf[:], in_=self.onesD[:]), reads=[self.b_const], writes=[self.b_const])
        self.input_phase()
        for l in range(self.n_layers):
            last = (l == DEPTH - 1)
            self.mods_phase(l)
            self.ffn_phase(l, 0, 0, include_ctx=True)
            if self.stage == "ffn1":
                break
            self.inproj_phase(l)
            self.hgrn_phase(l, need_ctx=not last)
            if self.stage == "hgrn":
                break
            self.s5_phase(l, need_ctx=not last)
            self.glu_phase(l, need_ctx=not last)
            if self.stage == "s5":
                break
            self.outproj_phase(l, include_ctx=not last)
            if last and self.split_tail:
                self.select_phase()
            self.ffn_phase(l, 1, 2, include_ctx=not last)
        self.output_phase(apply_norm=(self.stage == "full"))

    def supertiles(self, include_ctx, ts=1024):
        out = [(t0, ts, 0) for t0 in range(0, self.cur_nlat, ts)]
        if include_ctx:
            out.append((NLAT, NCTX, 1))
        return out

    def input_phase(self):
        k = self.k
        with contextlib.ExitStack() as ph:
            xtok = self.make_ring(ph, "xtok", [128, D], F32, 2)
            xst = self.make_ring(ph, "xsti", [128, 16, 128], F32, 2)
            xv = self.xT.rearrange("c p t -> p c t")
            ei = 0
            for ti in range(T // 128):
                tok = ti * 128
                xt, bxt = xtok.next()
                k.dma("sync", lambda e, xt=xt, tok=tok: e.dma_start(out=xt[:], in_=self.xin[tok:tok + 128, :]), writes=[bxt])
                xs, bxs = xst.next()
                for g in range(4):
                    pb = 4 + g % 4
                    for j in range(4):
                        c = g * 4 + j
                        k.op("tensor", lambda e, pb=pb, j=j, c=c, xt=xt: e.transpose(
                            out=self.ps[pb][:, j * 128:(j + 1) * 128], in_=xt[:, c * 128:(c + 1) * 128], identity=self.ident[:]),
                            reads=[bxt, self.b_const], writes=[self.psb[pb]])
                    dst = xs[:, g * 4:(g + 1) * 4, :]
                    src = self.ps[pb][:, :].rearrange("p (j t) -> p j t", t=128)
                    if ei % 2 == 0:
                        k.op("scalar", lambda e, dst=dst, src=src: e.copy(out=dst, in_=src), reads=[self.psb[pb]], writes=[bxs])
                    else:
                        k.op("vector", lambda e, dst=dst, src=src: e.tensor_copy(out=dst, in_=src), reads=[self.psb[pb]], writes=[bxs])
                    ei += 1
                k.dma("sync", lambda e, xs=xs, tok=tok: e.dma_start(out=xv[:, :, tok:tok + 128], in_=xs[:]),
                      reads=[bxs], writes=[self.xbuf(c, tok) for c in range(NC_)])
            k.barrier()

    def select_phase(self):
        k = self.k
        H = NLAT // 2
        xfull = self.xT
        with contextlib.ExitStack() as ph:
            selt = self.sb(ph, "selt", [128, 2], F32)
            b_sel = Buf()
            xa = self.make_ring(ph, "sela", [128, H], F32, 2)
            xb = self.make_ring(ph, "selb", [128, H], F32, 2)
            k.dma("sync", lambda e: e.dma_start(out=selt[:], in_=self.sel_in[:, :]), writes=[b_sel])
            for c in range(NC_):
                ta, ba = xa.next()
                tb_, bb = xb.next()
                k.dma("sync", lambda e, ta=ta, c=c: e.dma_start(out=ta[:], in_=xfull[c, :, 0:H]), writes=[ba])
                k.dma("sync", lambda e, tb_=tb_, c=c: e.dma_start(out=tb_[:], in_=xfull[c, :, H:NLAT]), writes=[bb])
                k.op("vector", lambda e, ta=ta: e.tensor_scalar(out=ta[:], in0=ta[:], scalar1=selt[:, 0:1], scalar2=None, op0=ALU.mult),
                     reads=[ba, b_sel], writes=[ba])
                k.op("vector", lambda e, ta=ta, tb_=tb_: e.scalar_tensor_tensor(out=ta[:], in0=tb_[:], scalar=selt[:, 1:2], in1=ta[:],
                                                                               op0=ALU.mult, op1=ALU.add), reads=[ba, bb, b_sel], writes=[ba])
                k.dma("sync", lambda e, ta=ta, c=c: e.dma_start(out=self.xsel[c, :, :], in_=ta[:]), reads=[ba])
            k.barrier()
        self.xT = self.xsel
        self.xT_bufs = [[Buf() for _ in range(9)] for _ in range(NC_)]
        self.cur_nlat = H

    def mods_phase(self, l):
        k = self.k
        with contextlib.ExitStack() as ph:
            self.wring = self.make_ring(ph, "wr", [128, 4096], BF16, 5)
            ccs = self.sb(ph, "ccs", [128, 2, 16], F32)
            scb = self.sb(ph, "scb", [128, 2, 16], BF16)
            bada = self.sb(ph, "bada", [128, 144], F32)
            nwt = self.sb(ph, "nwt", [128, 3, 16], F32)
            b_cc, b_sc, b_ba, b_nw = Buf(), Buf(), Buf(), Buf()
            k.dma("sync", lambda e: e.dma_start(out=ccs[:], in_=self.cc.rearrange("s (kt p) -> p s kt", p=128)), writes=[b_cc])
            k.op("scalar", lambda e: e.activation(out=scb[:], in_=ccs[:], func=AF.Silu), reads=[b_cc], writes=[b_sc])
            k.dma("sync", lambda e: e.dma_start(out=bada[:], in_=self.b_ada[l].rearrange("(j p) -> p j", p=128)), writes=[b_ba])
            k.dma("sync", lambda e: e.dma_start(out=nwt[:], in_=self.norm_w[l].rearrange("i (c p) -> p i c", p=128)), writes=[b_nw])
            wv = self.w_ada[l].rearrange("(kt p) c -> p kt c", p=128)
            stream = Prog.WStream(self, [wv[:, :, jb * 256:(jb + 1) * 256] for jb in range(72)], 3)
            pm = self.ps[7]
            for jb in range(72):
                wt, wb = stream.get()
                for sub in range(2):
                    j = jb * 2 + sub
                    for kt in range(16):
                        k.op("tensor", lambda e, wt=wt, sub=sub, j=j, kt=kt: e.matmul(
                            pm[:, 2 * j:2 * j + 2], lhsT=wt[:, kt, sub * 128:(sub + 1) * 128], rhs=scb[:, :, kt],
                            start=(kt == 0), stop=(kt == 15)), reads=[wb, b_sc], writes=[self.psb[7]])
            for s in range(2):
                k.op("vector", lambda e, s=s: e.tensor_tensor(out=self.modT[:, s, :], in0=pm[:, s:288:2], in1=bada[:], op=ALU.add),
                     reads=[self.psb[7], b_ba], writes=[self.b_mod])
            for s in range(2):
                for i3 in range(3):
                    sc_ = self.modT[:, s, (3 * i3 + 1) * 16:(3 * i3 + 2) * 16]
                    gt_ = self.modT[:, s, (3 * i3 + 2) * 16:(3 * i3 + 3) * 16]
                    k.op("vector", lambda e, s=s, i3=i3, sc_=sc_: e.scalar_tensor_tensor(
                        out=self.Amod[:, s, i3, :], in0=sc_, scalar=1.0, in1=nwt[:, i3, :], op0=ALU.add, op1=ALU.mult),
                        reads=[b_nw, self.b_mod], writes=[self.b_mod])
                    k.op("vector", lambda e, s=s, i3=i3, gt_=gt_: e.tensor_scalar(
                        out=self.Gmod[:, s, i3, :], in0=gt_, scalar1=(1.0 if i3 == 1 else 0.5), scalar2=None, op0=ALU.mult),
                        reads=[self.b_mod], writes=[self.b_mod])
            k.barrier()

    def norm_piece(self, tok, xring, sqring, rsring, dst, bdst, A_ap, B_ap, psn=0):
        k = self.k
        xv = self.xT.rearrange("c p t -> p c t")
        xs, bx = xring.next()
        k.dma("sync", lambda e: e.dma_start(out=xs[:], in_=xv[:, :, tok:tok + 128]),
              reads=[self.xbuf(c, tok) for c in range(NC_)], writes=[bx])
        sq, bs = sqring.next()
        k.op("scalar", lambda e: e.activation(out=sq[:], in_=xs[:], func=AF.Square), reads=[bx], writes=[bs])
        pn = self.ps[psn]
        for c in range(NC_):
            k.op("tensor", lambda e, c=c: e.matmul(pn[:, 0:128], lhsT=self.onesD_bf[:], rhs=sq[:, c, :], start=(c == 0), stop=(c == NC_ - 1)),
                 reads=[bs, self.b_const], writes=[self.psb[psn]])
        rs, brs = rsring.next()
        k.op("vector", lambda e: e.tensor_scalar(out=rs[:], in0=pn[:, 0:128], scalar1=EPS, scalar2=None, op0=ALU.add),
             reads=[self.psb[psn]], writes=[brs])
        k.op("scalar", lambda e: e.activation(out=rs[:], in_=rs[:], func=AF.Sqrt), reads=[brs], writes=[brs])
        k.op("vector", lambda e: e.reciprocal(out=rs[:], in_=rs[:]), reads=[brs], writes=[brs])
        k.op("vector", lambda e: e.tensor_tensor(out=xs[:], in0=xs[:], in1=bc_mid(rs[:], NC_), op=ALU.mult),
             reads=[brs, bx], writes=[bx])
        if B_ap is None:
            k.op("vector", lambda e: e.tensor_tensor(out=dst, in0=xs[:], in1=bc_last(A_ap, 128), op=ALU.mult),
                 reads=[bx, self.b_mod, self.b_const], writes=[bdst])
        else:
            k.op("vector", lambda e: e.tensor_tensor(out=xs[:], in0=xs[:], in1=bc_last(A_ap, 128), op=ALU.mult),
                 reads=[bx, self.b_mod], writes=[bx])
            k.op("vector", lambda e: e.tensor_tensor(out=dst, in0=xs[:], in1=bc_last(B_ap, 128), op=ALU.add),
                 reads=[bx, self.b_mod], writes=[bdst])

    def ffn_phase(self, l, fi, i3, include_ctx):
        k = self.k
        xTl = self.xT
        with contextlib.ExitStack() as ph:
            self.wring = self.make_ring(ph, "wr", [128, 4096], BF16, 5)
            hT = self.sb(ph, "hT", [128, 16, 1024], BF16)
            a = self.sb(ph, "aT", [128, NFF, 1024], BF16)
            b_h = [Buf(), Buf()]
            b_a = [Buf(), Buf()]
            xring = self.make_ring(ph, "fx", [128, 16, 128], F32, 2)
            sqring = self.make_ring(ph, "fsq", [128, 16, 128], BF16, 2)
            rsring = self.make_ring(ph, "frs", [128, 128], F32, 2)
            slring = self.make_ring(ph, "fsl", [128, 512], F32, 2)
            xrring = self.make_ring(ph, "fxr", [128, 512], F32, 3)
            wgv = self.wg[l, fi].rearrange("(kt p) c -> p kt c", p=128)
            wuv = self.wu[l, fi].rearrange("(kt p) c -> p kt c", p=128)
            wdv = self.wd[l, fi].rearrange("(kt p) c -> p kt c", p=128)
            psg = Ring([1, 2]); psu = Ring([3, 4]); psd = Ring([5, 6])
            for (t0, ts, s) in self.supertiles(include_ctx):
                A_ap = self.Amod[:, s, i3, :]
                B_ap = self.modT[:, s, (3 * i3) * 16:(3 * i3 + 1) * 16]
                nh = max(1, ts // 512)
                n = min(512, ts)
                for pc in range(ts // 128):
                    self.norm_piece(t0 + pc * 128, xring, sqring, rsring, hT[:, :, pc * 128:(pc + 1) * 128], b_h[(pc * 128) // 512],
                                    A_ap, B_ap)
                srcs = []
                for jb in range(22):
                    cols = 256 if jb < 21 else 128
                    srcs.append(wgv[:, :, jb * 256:jb * 256 + cols])
                    srcs.append(wuv[:, :, jb * 256:jb * 256 + cols])
                for m in range(16):
                    srcs.append(wdv[:, 0:22, m * 128:(m + 1) * 128])
                    srcs.append(wdv[:, 22:43, m * 128:(m + 1) * 128])
                stream = Prog.WStream(self, srcs, 3)
                for jb in range(22):
                    cols = 256 if jb < 21 else 128
                    gt, gb = stream.get()
                    ut, ub = stream.get()
                    for sub in range(cols // 128):
                        j = jb * 2 + sub
                        for hf in range(nh):
                            tsl = slice(hf * 512, hf * 512 + n)
                            pgi, _ = psg.next(); pui, _ = psu.next()
                            pg, pu = self.ps[pgi], self.ps[pui]
                            for kt in range(16):
                                k.op("tensor", lambda e, pg=pg, gt=gt, kt=kt, sub=sub, tsl=tsl, n=n: e.matmul(
                                    pg[:, 0:n], lhsT=gt[:, kt, sub * 128:(sub + 1) * 128], rhs=hT[:, kt, tsl],
                                    start=(kt == 0), stop=(kt == 15)), reads=[gb, b_h[hf]], writes=[self.psb[pgi]])
                            for kt in range(16):
                                k.op("tensor", lambda e, pu=pu, ut=ut, kt=kt, sub=sub, tsl=tsl, n=n: e.matmul(
                                    pu[:, 0:n], lhsT=ut[:, kt, sub * 128:(sub + 1) * 128], rhs=hT[:, kt, tsl],
                                    start=(kt == 0), stop=(kt == 15)), reads=[ub, b_h[hf]], writes=[self.psb[pui]])
                            sl, bsl = slring.next()
                            k.op("scalar", lambda e, sl=sl, pg=pg, n=n: e.activation(out=sl[:, 0:n], in_=pg[:, 0:n], func=AF.Silu),
                                 reads=[self.psb[pgi]], writes=[bsl])
                            k.op("vector", lambda e, sl=sl, pu=pu, j=j, tsl=tsl, n=n: e.tensor_tensor(
                                out=a[:, j, tsl], in0=sl[:, 0:n], in1=pu[:, 0:n], op=ALU.mult),
                                reads=[bsl, self.psb[pui]], writes=[b_a[hf]])
                for m in range(16):
                    w0, b0 = stream.get()
                    w1, b1 = stream.get()
                    for hf in range(nh):
                        tsl = slice(hf * 512, hf * 512 + n)
                        tok0 = t0 + hf * 512
                        xr, bxr = xrring.next()
                        k.dma("sync", lambda e, xr=xr, m=m, tok0=tok0, n=n: e.dma_start(out=xr[:, 0:n], in_=xTl[m, :, tok0:tok0 + n]),
                              reads=[self.xbuf(m, tok0)], writes=[bxr])
                        pdi, _ = psd.next()
                        pd = self.ps[pdi]
                        for kt in range(NFF):
                            wt = w0[:, kt, :] if kt < 22 else w1[:, kt - 22, :]
                            k.op("tensor", lambda e, pd=pd, wt=wt, kt=kt, tsl=tsl, n=n: e.matmul(
                                pd[:, 0:n], lhsT=wt, rhs=a[:, kt, tsl], start=(kt == 0), stop=(kt == NFF - 1)),
                                reads=[b0, b1, b_a[hf]], writes=[self.psb[pdi]])
                        k.op("vector", lambda e, xr=xr, pd=pd, m=m, s=s, n=n: e.scalar_tensor_tensor(
                            out=xr[:, 0:n], in0=pd[:, 0:n], scalar=self.Gmod[:, s, i3, m:m + 1], in1=xr[:, 0:n],
                            op0=ALU.mult, op1=ALU.add), reads=[self.psb[pdi], bxr, self.b_mod], writes=[bxr])
                        k.dma("sync", lambda e, xr=xr, m=m, tok0=tok0, n=n: e.dma_start(out=xTl[m, :, tok0:tok0 + n], in_=xr[:, 0:n]),
                              reads=[bxr], writes=[self.xbuf(m, tok0)])
            k.barrier()

    def output_phase(self, apply_norm):
        k = self.k
        with contextlib.ExitStack() as ph:
            xring = self.make_ring(ph, "ox", [128, 16, 128], F32, 2)
            sqring = self.make_ring(ph, "osq", [128, 16, 128], BF16, 2)
            rsring = self.make_ring(ph, "ors", [128, 128], F32, 2)
            yst = self.make_ring(ph, "oy", [128, 16, 128], F32, 2)
            ytok = self.make_ring(ph, "oyt", [128, D], F32, 2)
            xv = self.xT.rearrange("c p t -> p c t")
            ei = 0
            for ti in range(self.cur_nlat // 128):
                tok = ti * 128
                ys, bys = yst.next()
                if apply_norm:
                    self.norm_piece(tok, xring, sqring, rsring, ys[:], bys, self.wfin[:], None)
                else:
                    k.dma("sync", lambda e, ys=ys, tok=tok: e.dma_start(out=ys[:], in_=xv[:, :, tok:tok + 128]),
                          reads=[self.xbuf(c, tok) for c in range(NC_)], writes=[bys])
                yt, byt = ytok.next()
                for g in range(4):
                    pb = 4 + g % 4
                    for j in range(4):
                        c = g * 4 + j
                        k.op("tensor", lambda e, pb=pb, j=j, c=c, ys=ys: e.transpose(
                            out=self.ps[pb][:, j * 128:(j + 1) * 128], in_=ys[:, c, :], identity=self.ident[:]),
                            reads=[bys, self.b_const], writes=[self.psb[pb]])
                    dst = yt[:, g * 512:(g + 1) * 512]
                    src = self.ps[pb][:, :]
                    if ei % 2 == 0:
                        k.op("scalar", lambda e, dst=dst, src=src: e.copy(out=dst, in_=src), reads=[self.psb[pb]], writes=[byt])
                    else:
                        k.op("vector", lambda e, dst=dst, src=src: e.tensor_copy(out=dst, in_=src), reads=[self.psb[pb]], writes=[byt])
                    ei += 1
                k.dma("sync", lambda e, yt=yt, tok=tok: e.dma_start(out=self.Y[tok:tok + 128, :], in_=yt[:]), reads=[byt])
            k.barrier()

    def inproj_phase(self, l):
        k = self.k
        with contextlib.ExitStack() as ph:
            self.wring = self.make_ring(ph, "wr", [128, 4096], BF16, 5)
            hT = self.sb(ph, "ihT", [128, 16, 1024], BF16)
            b_h = [Buf(), Buf()]
            xring = self.make_ring(ph, "ix", [128, 16, 128], F32, 2)
            sqring = self.make_ring(ph, "isq", [128, 16, 128], BF16, 2)
            rsring = self.make_ring(ph, "irs", [128, 128], F32, 2)
            evring = self.make_ring(ph, "iev", [128, 512], F32, 4)
            wv = self.w_in[l].rearrange("(kt p) c -> p kt c", p=128)
            psr = Ring([1, 2, 3, 4, 5, 6])
            ei = 0
            for (t0, ts, s) in self.supertiles(True):
                A_ap = self.Amod[:, s, 1, :]
                B_ap = self.modT[:, s, 3 * 16:4 * 16]
                nh = max(1, ts // 512)
                n = min(512, ts)
                for pc in range(ts // 128):
                    self.norm_piece(t0 + pc * 128, xring, sqring, rsring, hT[:, :, pc * 128:(pc + 1) * 128], b_h[(pc * 128) // 512],
                                    A_ap, B_ap)
                stream = Prog.WStream(self, [wv[:, :, bi * 256:(bi + 1) * 256] for bi in range(24)], 3)
                for bi in range(24):
                    wt, wb = stream.get()
                    fam = bi // 4
                    if fam != 4:
                        for sub in range(2):
                            ch = (bi % 4) * 2 + sub
                            for hf in range(nh):
                                tsl = slice(hf * 512, hf * 512 + n)
                                tok0 = t0 + hf * 512
                                pi, _ = psr.next()
                                pp = self.ps[pi]
                                for kt in range(16):
                                    k.op("tensor", lambda e, pp=pp, wt=wt, kt=kt, sub=sub, tsl=tsl, n=n: e.matmul(
                                        pp[:, 0:n], lhsT=wt[:, kt, sub * 128:(sub + 1) * 128], rhs=hT[:, kt, tsl],
                                        start=(kt == 0), stop=(kt == 15)), reads=[wb, b_h[hf]], writes=[self.psb[pi]])
                                ev, bev = evring.next()
                                if fam in (1, 5):
                                    k.op("scalar", lambda e, ev=ev, pp=pp, n=n: e.activation(out=ev[:, 0:n], in_=pp[:, 0:n], func=AF.Silu),
                                         reads=[self.psb[pi]], writes=[bev])
                                elif ei % 2 == 0:
                                    k.op("scalar", lambda e, ev=ev, pp=pp, n=n: e.copy(out=ev[:, 0:n], in_=pp[:, 0:n]),
                                         reads=[self.psb[pi]], writes=[bev])
                                else:
                                    k.op("vector", lambda e, ev=ev, pp=pp, n=n: e.tensor_copy(out=ev[:, 0:n], in_=pp[:, 0:n]),
                                         reads=[self.psb[pi]], writes=[bev])
                                ei += 1
                                k.dma("sync", lambda e, ev=ev, fam=fam, ch=ch, tok0=tok0, n=n: e.dma_start(
                                    out=self.pT[fam, ch, :, tok0:tok0 + n], in_=ev[:, 0:n]), reads=[bev])
                    else:
                        for tt in range(ts // 128):
                            tok0 = t0 + tt * 128
                            pi, _ = psr.next()
                            pp = self.ps[pi]
                            for kt in range(16):
                                k.op("tensor", lambda e, pp=pp, wt=wt, kt=kt, tt=tt: e.matmul(
                                    pp[:, 0:256], lhsT=hT[:, kt, tt * 128:(tt + 1) * 128], rhs=wt[:, kt, :],
                                    start=(kt == 0), stop=(kt == 15)), reads=[wb, b_h[(tt * 128) // 512]], writes=[self.psb[pi]])
                            ev, bev = evring.next()
                            if ei % 2 == 0:
                                k.op("scalar", lambda e, ev=ev, pp=pp: e.copy(out=ev[:, 0:256], in_=pp[:, 0:256]),
                                     reads=[self.psb[pi]], writes=[bev])
                            else:
                                k.op("vector", lambda e, ev=ev, pp=pp: e.tensor_copy(out=ev[:, 0:256], in_=pp[:, 0:256]),
                                     reads=[self.psb[pi]], writes=[bev])
                            ei += 1
                            c0 = (bi % 4) * 256
                            k.dma("sync", lambda e, ev=ev, tok0=tok0, c0=c0: e.dma_start(
                                out=self.vtm[tok0:tok0 + 128, c0:c0 + 256], in_=ev[:, 0:256]), reads=[bev])
            k.barrier()

    def hgrn_phase(self, l, need_ctx):
        k = self.k
        NT = T // 128
        NCH = T // 32
        with contextlib.ExitStack() as ph:
            W = [self.sb(ph, f"hW{i}", [128, T], F32) for i in range(5)]
            bW = [Buf() for _ in range(5)]
            qh = [self.sb(ph, f"hqh{d}", [128, T], BF16) for d in range(2)]
            kt_ = [self.sb(ph, f"hkt{d}", [128, T], BF16) for d in range(2)]
            b_qh = [Buf(), Buf()]
            b_kt = [Buf(), Buf()]
            kh = self.sb(ph, "hkh", [128, T], BF16)
            b_kh = Buf()
            khtm = [self.sb(ph, f"hkhtm{d}", [128, NT, 128], BF16) for d in range(2)]
            b_khtm = [Buf(), Buf()]
            khtm3 = [self.sb(ph, f"hkhtm3{d}", [128, NT, 128], BF16) for d in range(2)]
            V = self.sb(ph, "hV", [128, NT, 128], BF16)
            b_V = Buf()
            dec = [self.sb(ph, f"hdec{d}", [128, NCH], F32) for d in range(2)]
            b_dec = [Buf(), Buf()]
            cm = self.sb(ph, "hcm", [128, T], BF16)
            hb = self.sb(ph, "hhb", [128, 2, 2, 8], F32)
            lbt = self.sb(ph, "hlbt", [128, 2, 8], F32)
            oml = self.sb(ph, "homl", [128, 2, 8], F32)
            nwh = self.sb(ph, "hnwh", [128, 1], F32)
            b_par = Buf()
            S32 = [self.sb(ph, f"hS32{d}", [128, 2, 128], F32) for d in range(2)]
            Sring = [self.sb(ph, f"hSr{d}", [128, 4, 128], BF16) for d in range(2)]
            b_S32 = [[Buf(), Buf()], [Buf(), Buf()]]
            b_Sr = [[Buf() for _ in range(4)] for _ in range(2)]
            psT = self.ps[7][:].bitcast(BF16)

            k.op("vector", lambda e: e.memset(cm[:], 1.0), writes=[b_par])
            k.op("vector", lambda e: e.memset(cm[:, 0:T:32], 0.0), writes=[b_par])
            k.dma("sync", lambda e: e.dma_start(out=hb[:], in_=self.hg_lb.rearrange("l d (h p) -> p l d h", p=128)), writes=[b_par])
            k.dma("sync", lambda e: e.dma_start(out=nwh[:], in_=self.hg_nw[l].rearrange("(p o) -> p o", o=1)), writes=[b_par])
            if l == 0:
                k.op("vector", lambda e: e.memset(lbt[:], 0.0), writes=[b_par])
            else:
                k.op("vector", lambda e: e.tensor_tensor(out=lbt[:], in0=hb[:, 1], in1=hb[:, 0], op=ALU.subtract), reads=[b_par], writes=[b_par])
                k.op("scalar", lambda e: e.activation(out=lbt[:], in_=lbt[:], func=AF.Sigmoid), reads=[b_par], writes=[b_par])
            k.op("vector", lambda e: e.tensor_scalar(out=oml[:], in0=lbt[:], scalar1=-1.0, scalar2=1.0, op0=ALU.mult, op1=ALU.add),
                 reads=[b_par], writes=[b_par])

            def cmv(ap):
                return ap[:, 0:NLAT].rearrange("p (c r) -> p c r", r=GRID)

            def rasv_as_cr(ap):
                return ap[:, 0:NLAT].rearrange("p (r c) -> p c r", c=GRID)

            for hd in range(8):
                vsrc = self.vtm[0:NLAT, hd * 128:(hd + 1) * 128].rearrange("(r ti c2) v -> c2 r ti v", ti=32, c2=2)
                for c2 in range(2):
                    k.dma("gpsimd", lambda e, c2=c2, vsrc=vsrc: e.dma_start(out=V[c2 * 64:(c2 + 1) * 64, 0:32, :], in_=vsrc[c2]), writes=[b_V])
                vsrc2 = self.vtm[NLAT:T, hd * 128:(hd + 1) * 128].rearrange("(ti p) v -> p ti v", p=128)
                k.dma("gpsimd", lambda e, vsrc2=vsrc2: e.dma_start(out=V[:, 32:34, :], in_=vsrc2), writes=[b_V])
                for d in range(2):
                    k.dma("sync", lambda e, hd=hd, d=d: e.dma_start(out=W[4][:], in_=self.pT[2 + d, hd, :, :]), writes=[bW[4]])
                    k.op("scalar", lambda e: e.activation(out=cmv(W[0]), in_=rasv_as_cr(W[4]), func=AF.Sigmoid), reads=[bW[4]], writes=[bW[0]])
                    k.op("scalar", lambda e: e.activation(out=W[0][:, NLAT:T], in_=W[4][:, NLAT:T], func=AF.Sigmoid), reads=[bW[4]], writes=[bW[0]])
                    k.op("vector", lambda e, d=d, hd=hd: e.tensor_scalar(out=W[0][:], in0=W[0][:], scalar1=oml[:, d, hd:hd + 1],
                                                                       scalar2=lbt[:, d, hd:hd + 1], op0=ALU.mult, op1=ALU.add),
                         reads=[b_par, bW[0]], writes=[bW[0]])
                    k.op("gpsimd", lambda e: e.tensor_scalar(out=W[1][:], in0=W[0][:], scalar1=-1.0, scalar2=1.0, op0=ALU.mult, op1=ALU.add),
                         reads=[bW[0]], writes=[bW[1]])
                    k.op("vector", lambda e: e.tensor_scalar(out=W[0][:], in0=W[0][:], scalar1=1e-6, scalar2=None, op0=ALU.max),
                         reads=[bW[0], bW[1]], writes=[bW[0]])
                    k.op("scalar", lambda e: e.activation(out=W[0][:], in_=W[0][:], func=AF.Ln), reads=[bW[0]], writes=[bW[0]])
                    if d == 0:
                        k.op("vector", lambda e: e.tensor_tensor_scan(out=W[2][:], data0=cm[:], data1=W[0][:], initial=0.0,
                                                                      op0=ALU.mult, op1=ALU.add), reads=[bW[0], b_par], writes=[bW[2]])
                    else:
                        k.op("vector", lambda e: e.tensor_tensor_scan(out=W[2][:, ::-1], data0=cm[:], data1=W[0][:, ::-1], initial=0.0,
                                                                      op0=ALU.mult, op1=ALU.add), reads=[bW[0], b_par], writes=[bW[2]])
                    k.op("scalar", lambda e: e.activation(out=W[3][:], in_=W[2][:], func=AF.Exp), reads=[bW[2]], writes=[bW[3]])
                    k.dma("sync", lambda e, hd=hd: e.dma_start(out=W[4][:], in_=self.pT[1, hd, :, :]), reads=[bW[4]], writes=[bW[4]])
                    k.op("vector", lambda e, d=d: e.tensor_tensor(out=cmv(qh[d]), in0=rasv_as_cr(W[4]), in1=cmv(W[3]), op=ALU.mult),
                         reads=[bW[4], bW[3]], writes=[b_qh[d]])
                    k.op("vector", lambda e, d=d: e.tensor_tensor(out=qh[d][:, NLAT:T], in0=W[4][:, NLAT:T], in1=W[3][:, NLAT:T], op=ALU.mult),
                         reads=[bW[4], bW[3]], writes=[b_qh[d]])
                    k.op("gpsimd", lambda e: e.tensor_scalar(out=W[0][:], in0=W[2][:], scalar1=-60.0, scalar2=None, op0=ALU.max),
                         reads=[bW[2], bW[0]], writes=[bW[0]])
                    k.op("scalar", lambda e: e.activation(out=W[0][:], in_=W[0][:], func=AF.Exp, scale=-1.0), reads=[bW[0]], writes=[bW[0]])
                    k.op("vector", lambda e: e.tensor_tensor(out=W[4][:], in0=W[1][:], in1=W[0][:], op=ALU.mult),
                         reads=[bW[1], bW[0], bW[4]], writes=[bW[4]])
                    k.op("scalar", lambda e, d=d: e.copy(out=kt_[d][:], in_=W[4][:]), reads=[bW[4]], writes=[b_kt[d]])
                    glast = W[2][:, (31 if d == 0 else 0):T:32]
                    k.op("scalar", lambda e, d=d, glast=glast: e.activation(out=dec[d][:], in_=glast, func=AF.Exp), reads=[bW[2]], writes=[b_dec[d]])
                    k.op("vector", lambda e, d=d: e.tensor_tensor(out=kh[:].rearrange("p (n j) -> p n j", j=32),
                                                                  in0=W[4][:].rearrange("p (n j) -> p n j", j=32),
                                                                  in1=bc_last(dec[d][:], 32), op=ALU.mult),
                         reads=[bW[4], b_dec[d]], writes=[b_kh])
                    for g0 in range(0, NT, 8):
                        ng = min(8, NT - g0)
                        for j in range(ng):
                            ti = g0 + j
                            k.op("tensor", lambda e, j=j, ti=ti: e.transpose(out=psT[:, j * 128:(j + 1) * 128], in_=kh[:, ti * 128:(ti + 1) * 128],
                                                                             identity=self.ident_bf[:]),
                                 reads=[b_kh, self.b_const], writes=[self.psb[7]])
                        k.op("vector", lambda e, d=d, g0=g0, ng=ng: e.tensor_copy(
                            out=khtm[d][:, g0:g0 + ng, :], in_=psT[:, 0:ng * 128].rearrange("p (n j) -> p n j", j=128)),
                            reads=[self.psb[7]], writes=[b_khtm[d]])
                        k.op("scalar", lambda e, d=d, g0=g0, ng=ng: e.activation(
                            out=khtm3[d][:, g0:g0 + ng, :], in_=psT[:, 0:ng * 128].rearrange("p (n j) -> p n j", j=128),
                            func=AF.Copy, scale=self.masks[:, 4, 0:1]),
                            reads=[self.psb[7], self.b_const], writes=[b_khtm[d]])
                sm_all = [W[1][:].bitcast(BF16)[:, 0:NT * 128].rearrange("p (n j) -> p n j", j=128),
                          W[2][:].bitcast(BF16)[:, 0:NT * 128].rearrange("p (n j) -> p n j", j=128)]
                b_sm = [bW[1], bW[2]]
                for d in range(2):
                    for ti in range(NT):
                        pi = 1 + (ti % 2)
                        k.op("tensor", lambda e, d=d, ti=ti, pi=pi: e.matmul(
                            self.ps[pi][:, 0:128], lhsT=kt_[d][:, ti * 128:(ti + 1) * 128], rhs=qh[d][:, ti * 128:(ti + 1) * 128],
                            start=True, stop=True), reads=[b_kt[d], b_qh[d]], writes=[self.psb[pi]])
                        k.op("vector", lambda e, d=d, ti=ti, pi=pi: e.tensor_tensor(
                            out=sm_all[d][:, ti, :], in0=self.ps[pi][:, 0:128], in1=self.masks[:, d, :], op=ALU.mult),
                            reads=[self.psb[pi], self.b_const], writes=[b_sm[d]])
                order = [[32, 33] + list(range(32)), [33, 32] + list(range(31, -1, -1))]
                seq = [[], []]
                for d in range(2):
                    for ti in order[d]:
                        for cj in ([0, 1, 2, 3] if d == 0 else [3, 2, 1, 0]):
                            seq[d].append((ti, cj))
                NSEQ = len(seq[0])
                NR = 4
                DS_BANK = [[5, 1], [6, 2]]
                psd_b = [[self.psb[DS_BANK[d][par]] for par in range(2)] for d in range(2)]
                for d in range(2):
                    k.op("vector", lambda e, d=d: e.memset(S32[d][:, 0, :], 0.0), reads=[b_S32[d][0]], writes=[b_S32[d][0]])
                    k.op("vector", lambda e, d=d: e.memset(Sring[d][:, 0, :], 0.0), reads=[b_Sr[d][0]], writes=[b_Sr[d][0]])
                oacc = W[0]
                b_o = bW[0]
                visited = set()

                def emit_dS(d, i):
                    ti, cj = seq[d][i]
                    par = i % 2
                    pdv = self.ps[DS_BANK[d][par]][:, 0:128]
                    if cj < 3:
                        k.op("tensor", lambda e: e.matmul(
                            pdv, lhsT=khtm[d][cj * 32:(cj + 1) * 32, ti, :], rhs=V[cj * 32:(cj + 1) * 32, ti, :], start=True, stop=True),
                            reads=[b_khtm[d], b_V], writes=[psd_b[d][par]])
                    else:
                        k.op("tensor", lambda e: e.matmul(
                            pdv, lhsT=khtm3[d][:, ti, :], rhs=V[:, ti, :], start=True, stop=True),
                            reads=[b_khtm[d], b_V], writes=[psd_b[d][par]])

                def emit_update(d, i):
                    ti, cj = seq[d][i]
                    par = i % 2
                    pdv = self.ps[DS_BANK[d][par]][:, 0:128]
                    cidx = ti * 4 + cj
                    slot = (i + 1) % NR
                    so, sn_ = i % 2, (i + 1) % 2
                    k.op("vector", lambda e: e.scalar_tensor_tensor(
                        out=S32[d][:, sn_, :], in0=S32[d][:, so, :], scalar=dec[d][:, cidx:cidx + 1], in1=pdv, op0=ALU.mult, op1=ALU.add),
                        reads=[psd_b[d][par], b_dec[d], b_S32[d][so]], writes=[b_S32[d][sn_]])
                    k.op("scalar", lambda e: e.copy(out=Sring[d][:, slot, :], in_=S32[d][:, sn_, :]),
                         reads=[b_S32[d][sn_], b_Sr[d][slot]], writes=[b_Sr[d][slot]])

                for d in range(2):
                    emit_dS(d, 0)
                for i in range(NSEQ):
                    for d in range(2):
                        ti, cj = seq[d][i]
                        po = self.ps[3 + d]
                        b_po = self.psb[3 + d]
                        first = (i % 4 == 0)
                        lastc = (i % 4 == 3)
                        if first:
                            k.op("tensor", lambda e, d=d, ti=ti, po=po: e.matmul(
                                po[:, 0:128], lhsT=V[:, ti, :], rhs=sm_all[d][:, ti, :], start=True, stop=False),
                                reads=[b_V, b_sm[d]], writes=[b_po])
                        if i + 1 < NSEQ:
                            emit_dS(d, i + 1)
                        c0 = ti * 128 + cj * 32
                        slot = i % NR
                        k.op("tensor", lambda e, d=d, cj=cj, c0=c0, po=po, slot=slot, lastc=lastc: e.matmul(
                            po[:, cj * 32:(cj + 1) * 32], lhsT=Sring[d][:, slot, :], rhs=qh[d][:, c0:c0 + 32], start=False, stop=lastc),
                            reads=[b_Sr[d][slot], b_qh[d]], writes=[b_po])
                        emit_update(d, i)
                        if lastc:
                            osl = oacc[:, ti * 128:(ti + 1) * 128]
                            if ti not in visited:
                                visited.add(ti)
                                k.op("scalar", lambda e, osl=osl, po=po: e.copy(out=osl, in_=po[:, 0:128]), reads=[b_po], writes=[b_o])
                            else:
                                k.op("vector", lambda e, osl=osl, po=po: e.tensor_tensor(out=osl, in0=po[:, 0:128], in1=osl, op=ALU.add),
                                     reads=[b_po, b_o], writes=[b_o])
                k.op("scalar", lambda e: e.activation(out=W[1][:], in_=W[0][:], func=AF.Square), reads=[bW[0], bW[1]], writes=[bW[1]])
                for t0 in range(0, T, 512):
                    n = min(512, T - t0)
                    k.op("tensor", lambda e, t0=t0, n=n: e.matmul(self.ps[0][:, 0:n], lhsT=self.ones128[:], rhs=W[1][:, t0:t0 + n],
                                                                 start=True, stop=True), reads=[bW[1], self.b_const], writes=[self.psb[0]])
                    k.op("vector", lambda e, t0=t0, n=n: e.tensor_scalar(out=W[2][:, t0:t0 + n], in0=self.ps[0][:, 0:n], scalar1=EPS, scalar2=None,
                                                                        op0=ALU.add), reads=[self.psb[0], bW[2]], writes=[bW[2]])
                k.op("scalar", lambda e: e.activation(out=W[2][:], in_=W[2][:], func=AF.Sqrt), reads=[bW[2]], writes=[bW[2]])
                k.op("vector", lambda e: e.reciprocal(out=W[2][:], in_=W[2][:]), reads=[bW[2]], writes=[bW[2]])
                k.op("vector", lambda e: e.tensor_tensor(out=W[3][:], in0=W[0][:], in1=W[2][:], op=ALU.mult), reads=[bW[0], bW[2], bW[3]], writes=[bW[3]])
                k.dma("sync", lambda e, hd=hd: e.dma_start(out=W[4][:], in_=self.pT[5, hd, :, :]), writes=[bW[4]])
                k.op("vector", lambda e: e.scalar_tensor_tensor(
                    out=kh[:, 0:NLAT].rearrange("p (r c) -> p r c", c=GRID), in0=W[3][:, 0:NLAT].rearrange("p (c r) -> p r c", r=GRID),
                    scalar=nwh[:, 0:1], in1=W[4][:, 0:NLAT].rearrange("p (r c) -> p r c", c=GRID), op0=ALU.mult, op1=ALU.mult),
                    reads=[bW[3], bW[4], b_par], writes=[b_kh])
                k.op("vector", lambda e: e.scalar_tensor_tensor(
                    out=kh[:, NLAT:T], in0=W[3][:, NLAT:T], scalar=nwh[:, 0:1], in1=W[4][:, NLAT:T], op0=ALU.mult, op1=ALU.mult),
                    reads=[bW[3], bW[4], b_par], writes=[b_kh])
                k.dma("sync", lambda e, hd=hd: e.dma_start(out=self.mixT[8 + hd, :, :], in_=kh[:]), reads=[b_kh])
            k.barrier()

    def sin_turns(self, out_ap, u, ki, tmp, bufs):
        k = self.k
        TWO_PI = 2 * math.pi * (1.0 - 1e-6)
        k.op("vector", lambda e: e.tensor_copy(out=ki, in_=u), reads=bufs, writes=bufs)
        k.op("vector", lambda e: e.tensor_copy(out=tmp, in_=ki), reads=bufs, writes=bufs)
        k.op("vector", lambda e: e.tensor_tensor(out=u, in0=u, in1=tmp, op=ALU.subtract), reads=bufs, writes=bufs)
        k.op("vector", lambda e: e.tensor_scalar(out=tmp, in0=u, scalar1=0.5, scalar2=None, op0=ALU.is_gt), reads=bufs, writes=bufs)
        k.op("vector", lambda e: e.tensor_tensor(out=u, in0=u, in1=tmp, op=ALU.subtract), reads=bufs, writes=bufs)
        k.op("vector", lambda e: e.tensor_scalar(out=tmp, in0=u, scalar1=-0.5, scalar2=None, op0=ALU.is_lt), reads=bufs, writes=bufs)
        k.op("vector", lambda e: e.tensor_tensor(out=u, in0=u, in1=tmp, op=ALU.add), reads=bufs, writes=bufs)
        k.op("scalar", lambda e: e.activation(out=out_ap, in_=u, func=AF.Sin, scale=TWO_PI), reads=bufs, writes=bufs)

    def s5_phase(self, l, need_ctx):
        k = self.k
        nc = self.nc
        L = 256
        NCHK = T // L
        PI = math.pi
        from concourse.ap import AP as _AP
        with contextlib.ExitStack() as ph:
            mag_s = [self.sb(ph, f"smag{d}", [128, 32], F32) for d in range(2)]
            th_s = [self.sb(ph, f"sth{d}", [128, 32], F32) for d in range(2)]
            dsk = self.sb(ph, "sdsk", [128, 8], F32)
            b_prm = Buf()
            k.dma("sync", lambda e: e.dma_start(out=dsk[:], in_=self.s5_d[l].rearrange("(c p) -> p c", p=128)), writes=[b_prm])
            iota = self.aux[:, 8:8 + 257]
            mg = self.aux[:, 0:8]
            with contextlib.ExitStack() as pre:
                BT = [[self.sb(pre, f"sBT{d}{r}", [128, 8, 4, 128], BF16) for r in range(2)] for d in range(2)]
                CT = [[self.sb(pre, f"sCT{d}{r}", [128, 32, 128], BF16) for r in range(2)] for d in range(2)]

                def nl(nm):
                    return self.sb(pre, nm, [64, 64], F32)
                for d in range(2):
                    for r in range(2):
                        k.op("vector", lambda e, d=d, r=r: e.memset(CT[d][r][:], 0.0), writes=[b_prm])
                lre, lim, ls, aa, mg_, thn, t1, t2, cs, sn, den, nr, cfr, cfi = [nl(f"snl{i}") for i in range(14)]
                nli = self.sb(pre, "snli", [64, 64], I32)
                Bn = [self.sb(pre, f"sBn{r}", [64, 1024], F32) for r in range(2)]
                Bb = [self.sb(pre, f"sBb{r}", [64, 1024], F32) for r in range(2)]
                tmpB = self.sb(pre, "stmpB", [64, 1024], F32)
                Cx = [self.sb(pre, f"sCx{r}", [32, 32, 128], F32) for r in range(2)]
                sl_lre = self.sb(pre, "sslre", [128, 32], F32)
                sl_lim = self.sb(pre, "sslim", [128, 32], F32)
                sl_ls = self.sb(pre, "ssls", [128, 32], F32)
                b_n = Buf()
                b_B = Buf()
                b_C = Buf()
                for d in range(2):
                    k.dma("sync", lambda e, d=d: e.dma_start(out=lre[:], in_=self.s5_lre[l, d].rearrange("g p -> p g")), writes=[b_n])
                    k.dma("sync", lambda e, d=d: e.dma_start(out=lim[:], in_=self.s5_lim[l, d].rearrange("g p -> p g")), writes=[b_n])
                    lsrc = self.s5_ls[l, d]
                    k.dma("sync", lambda e, lsrc=lsrc: e.dma_start(out=ls[:], in_=_AP(lsrc.tensor, lsrc.offset, [[0, 64], [1, 64]])), writes=[b_n])
                    for r, src in ((0, self.s5_bre), (1, self.s5_bim)):
                        k.dma("sync", lambda e, d=d, r=r, src=src: e.dma_start(
                            out=Bn[r][:].rearrange("p (g h) -> p g h", h=16), in_=src[l, d].rearrange("g p h -> p g h")), writes=[b_B])

                    def V_(fn, **kw):
                        k.op("vector", fn, reads=[b_n], writes=[b_n])

                    def A_(fn):
                        k.op("scalar", fn, reads=[b_n], writes=[b_n])
                    V_(lambda e: e.tensor_scalar(out=lre[:], in0=lre[:], scalar1=-1e-4, scalar2=None, op0=ALU.min))
                    A_(lambda e: e.activation(out=ls[:], in_=ls[:], func=AF.Exp))
                    V_(lambda e: e.tensor_tensor(out=aa[:], in0=lre[:], in1=ls[:], op=ALU.mult))
                    A_(lambda e: e.activation(out=mg_[:], in_=aa[:], func=AF.Exp))
                    V_(lambda e: e.tensor_tensor(out=thn[:], in0=lim[:], in1=ls[:], op=ALU.mult))
                    V_(lambda e: e.tensor_scalar(out=t1[:], in0=thn[:], scalar1=1.0 / (2 * PI), scalar2=None, op0=ALU.mult))
                    self.sin_turns(sn[:], t1[:], nli[:], t2[:], [b_n])
                    V_(lambda e: e.tensor_scalar(out=t1[:], in0=thn[:], scalar1=1.0 / (2 * PI), scalar2=0.25, op0=ALU.mult, op1=ALU.add))
                    self.sin_turns(cs[:], t1[:], nli[:], t2[:], [b_n])
                    V_(lambda e: e.tensor_tensor(out=cs[:], in0=cs[:], in1=mg_[:], op=ALU.mult))
                    V_(lambda e: e.tensor_tensor(out=sn[:], in0=sn[:], in1=mg_[:], op=ALU.mult))
                    V_(lambda e: e.tensor_tensor(out=den[:], in0=lre[:], in1=lre[:], op=ALU.mult))
                    V_(lambda e: e.tensor_tensor(out=t1[:], in0=lim[:], in1=lim[:], op=ALU.mult))
                    V_(lambda e: e.tensor_tensor(out=den[:], in0=den[:], in1=t1[:], op=ALU.add))
                    V_(lambda e: e.reciprocal(out=den[:], in_=den[:]))
                    V_(lambda e: e.tensor_scalar(out=nr[:], in0=cs[:], scalar1=-1.0, scalar2=None, op0=ALU.add))
                    V_(lambda e: e.tensor_tensor(out=t1[:], in0=nr[:], in1=lre[:], op=ALU.mult))
                    V_(lambda e: e.tensor_tensor(out=t2[:], in0=sn[:], in1=lim[:], op=ALU.mult))
                    V_(lambda e: e.tensor_tensor(out=cfr[:], in0=t1[:], in1=t2[:], op=ALU.add))
                    V_(lambda e: e.tensor_tensor(out=cfr[:], in0=cfr[:], in1=den[:], op=ALU.mult))
                    V_(lambda e: e.tensor_tensor(out=t1[:], in0=sn[:], in1=lre[:], op=ALU.mult))
                    V_(lambda e: e.tensor_tensor(out=t2[:], in0=nr[:], in1=lim[:], op=ALU.mult))
                    V_(lambda e: e.tensor_tensor(out=cfi[:], in0=t1[:], in1=t2[:], op=ALU.subtract))
                    V_(lambda e: e.tensor_tensor(out=cfi[:], in0=cfi[:], in1=den[:], op=ALU.mult))
                    def v3(t):
                        return t[:].rearrange("p (g h) -> p g h", h=16)
                    k.op("vector", lambda e: e.tensor_tensor(out=v3(Bb[0]), in0=v3(Bn[0]), in1=bc_last(cfr[:], 16), op=ALU.mult), reads=[b_n, b_B], writes=[b_B])
                    k.op("vector", lambda e: e.tensor_tensor(out=v3(tmpB), in0=v3(Bn[1]), in1=bc_last(cfi[:], 16), op=ALU.mult), reads=[b_n, b_B], writes=[b_B])
                    k.op("vector", lambda e: e.tensor_tensor(out=Bb[0][:], in0=Bb[0][:], in1=tmpB[:], op=ALU.subtract), reads=[b_B], writes=[b_B])
                    k.op("vector", lambda e: e.tensor_tensor(out=v3(Bb[1]), in0=v3(Bn[1]), in1=bc_last(cfr[:], 16), op=ALU.mult), reads=[b_n, b_B], writes=[b_B])
                    k.op("vector", lambda e: e.tensor_tensor(out=v3(tmpB), in0=v3(Bn[0]), in1=bc_last(cfi[:], 16), op=ALU.mult), reads=[b_n, b_B], writes=[b_B])
                    k.op("vector", lambda e: e.tensor_tensor(out=Bb[1][:], in0=Bb[1][:], in1=tmpB[:], op=ALU.add), reads=[b_B], writes=[b_B])
                    for r in range(2):
                        for tb in range(8):
                            pb = 1 + (tb % 4)
                            k.op("tensor", lambda e, r=r, tb=tb, pb=pb: e.transpose(
                                out=self.ps[pb][:, 0:64], in_=Bb[r][:, tb * 128:(tb + 1) * 128], identity=self.ident[0:64, 0:64]),
                                reads=[b_B, self.b_const], writes=[self.psb[pb]])
                            for g8 in range(8):
                                dst = BT[d][r][:, tb, g8 // 2, (g8 % 2) * 64:(g8 % 2) * 64 + 64]
                                if g8 % 2 == 0:
                                    k.op("vector", lambda e, dst=dst, pb=pb, g8=g8: e.tensor_scalar(
                                        out=dst, in0=self.ps[pb][:, 0:64], scalar1=mg[:, g8:g8 + 1], scalar2=None, op0=ALU.mult),
                                        reads=[self.psb[pb], self.b_const], writes=[b_prm])
                                else:
                                    k.op("scalar", lambda e, dst=dst, pb=pb, g8=g8: e.activation(
                                        out=dst, in_=self.ps[pb][:, 0:64], func=AF.Copy, scale=mg[:, g8:g8 + 1]),
                                        reads=[self.psb[pb], self.b_const], writes=[b_prm])
                    for g2 in range(2):
                        k.dma("sync", lambda e, d=d, g2=g2: e.dma_start(out=sl_lre[64 * g2:64 * g2 + 64, :],
                                                                        in_=self.s5_lre[l, d, g2::2, :].rearrange("gp p -> p gp")), writes=[b_n])
                        k.dma("sync", lambda e, d=d, g2=g2: e.dma_start(out=sl_lim[64 * g2:64 * g2 + 64, :],
                                                                        in_=self.s5_lim[l, d, g2::2, :].rearrange("gp p -> p gp")), writes=[b_n])
                        lsrc2 = self.s5_ls[l, d, g2::2]
                        k.dma("sync", lambda e, g2=g2, lsrc2=lsrc2: e.dma_start(
                            out=sl_ls[64 * g2:64 * g2 + 64, :], in_=_AP(lsrc2.tensor, lsrc2.offset, [[0, 64], [2, 32]])), writes=[b_n])
                    V_(lambda e: e.tensor_scalar(out=sl_lre[:], in0=sl_lre[:], scalar1=-1e-4, scalar2=None, op0=ALU.min))
                    A_(lambda e: e.activation(out=sl_ls[:], in_=sl_ls[:], func=AF.Exp))
                    V_(lambda e: e.tensor_tensor(out=sl_lre[:], in0=sl_lre[:], in1=sl_ls[:], op=ALU.mult))
                    k.op("scalar", lambda e, d=d: e.activation(out=mag_s[d][:], in_=sl_lre[:], func=AF.Exp), reads=[b_n], writes=[b_prm])
                    k.op("vector", lambda e, d=d: e.scalar_tensor_tensor(out=th_s[d][:], in0=sl_lim[:], scalar=1.0 / (2 * PI), in1=sl_ls[:],
                                                                         op0=ALU.mult, op1=ALU.mult), reads=[b_n], writes=[b_prm])
                    for r, src in ((0, self.s5_cre), (1, self.s5_cim)):
                        k.op("vector", lambda e, r=r: e.memset(Cx[r][:], 0.0), reads=[b_C], writes=[b_C])
                        for g2 in range(2):
                            k.dma("sync", lambda e, d=d, r=r, g2=g2, src=src: e.dma_start(
                                out=Cx[r][16 * g2:16 * g2 + 16, :, 64 * g2:64 * g2 + 64],
                                in_=src[l, d, g2::2].rearrange("gp h p -> h gp p")), reads=[b_C], writes=[b_C])
                        for half in range(2):
                            pb = 5 + half
                            for i in range(16):
                                gp = half * 16 + i
                                k.op("tensor", lambda e, r=r, gp=gp, i=i, pb=pb: e.transpose(
                                    out=self.ps[pb][:, i * 32:(i + 1) * 32], in_=Cx[r][:, gp, :], identity=self.ident[0:32, 0:32]),
                                    reads=[b_C, self.b_const], writes=[self.psb[pb]])
                            pv = self.ps[pb][:, :].rearrange("p (i c) -> p i c", c=32)
                            for j in range(4):
                                dst = CT[d][r][:, half * 16 + j:half * 16 + 16:4, 32 * j:32 * j + 32]
                                k.op("scalar", lambda e, dst=dst, pv=pv, j=j, r=r: e.activation(
                                    out=dst, in_=pv[:, j:16:4, :], func=AF.Copy, scale=(1.0 if r == 0 else -1.0)),
                                    reads=[self.psb[pb]], writes=[b_prm])
                for d in range(2):
                    for r in range(2):
                        k.dma("sync", lambda e, d=d, r=r: e.dma_start(out=self.BTs[d, r], in_=BT[d][r][:].rearrange("p t j c -> p t (j c)")), reads=[b_prm])
                        k.dma("sync", lambda e, d=d, r=r: e.dma_start(out=self.CTs[d, r], in_=CT[d][r][:]), reads=[b_prm])
                k.barrier()
            L = 512
            ust = self.sb(ph, "sust", [128, T], F32)
            ubf = self.sb(ph, "subf", [128, T], BF16)
            yacc = self.sb(ph, "syacc", [128, T], F32)
            b_ust, b_ubf, b_y = Buf(), Buf(), Buf()
            tab_all = self.sb(ph, "stab", [128, 2, 4, L + 1], F32)
            tabs = [[tab_all[:, i, j, :] for i in range(2)] for j in range(4)]
            init_t = self.sb(ph, "sinit", [128, 2, 4], F32)
            x1_t = self.sb(ph, "sx1", [128, 2, 4], F32)
            x2_t = self.sb(ph, "sx2", [128, 2, 4], F32)
            b_init = Buf()
            rts = [self.sb(ph, f"srt{j}", [128, L], F32) for j in range(4)]
            phs = self.sb(ph, "sphs", [128, L + 1], F32)
            pht = self.sb(ph, "spht", [128, L + 1], F32)
            phi = self.sb(ph, "sphi", [128, L + 1], I32)
            b_tab = [Buf() for _ in range(4)]
            b_phs = Buf()
            btl = self.make_ring(ph, "sbtl", [128, 2, 2, 512], BF16, 2)
            ctl = self.make_ring(ph, "sctl", [128, 2, 2, 4, 128], BF16, 2)
            pre_s = self.make_ring(ph, "spre", [128, 2, L], F32, 4)
            wring_ = self.make_ring(ph, "sw", [128, 2, L], F32, 8)
            zri = self.make_ring(ph, "szri", [128, 4, 2, L], F32, 2)
            xri = self.make_ring(ph, "sxri", [128, 2, L], BF16, 4)
            tmpP = self.make_ring(ph, "stp", [128, L], F32, 2)
            tmpV = self.make_ring(ph, "stv", [128, 2, L], F32, 2)
            tmpP2 = self.make_ring(ph, "stp2", [128, 2, L], F32, 1)
            psr = Ring([1, 2, 3, 4])
            psy = Ring([5, 6])
            iotaL = self.sb(ph, "siota", [128, L + 1], F32)
            b_io = Buf()
            k.dma("sync", lambda e: e.dma_start(out=iotaL[:], in_=self.c_iota[:, :]), writes=[b_io])
            chunks = [(NLAT, T)] + [(i * L, (i + 1) * L) for i in range(NLAT // L)]
            for tb in range(8):
                bt, b_bt = btl.next()
                ct, b_ct = ctl.next()
                k.dma("sync", lambda e, tb=tb, bt=bt: e.dma_start(out=bt[:], in_=self.BTs[:, :, :, tb, :].rearrange("d r p c -> p d r c")), writes=[b_bt])
                k.dma("sync", lambda e, tb=tb, ct=ct: e.dma_start(out=ct[:], in_=self.CTs[:, :, :, tb * 4:(tb + 1) * 4, :].rearrange("d r p g c -> p d r g c")), writes=[b_ct])
                k.dma("sync", lambda e, tb=tb: e.dma_start(out=ust[:], in_=self.pT[0, tb, :, :]), reads=[b_ust], writes=[b_ust])
                k.op("scalar", lambda e: e.copy(out=ubf[:], in_=ust[:]), reads=[b_ust], writes=[b_ubf])
                k.op("vector", lambda e, tb=tb: e.tensor_scalar(out=yacc[:], in0=ust[:], scalar1=dsk[:, tb:tb + 1], scalar2=None, op0=ALU.mult),
                     reads=[b_ust, b_prm], writes=[b_y])
                for d in range(2):
                    for j in range(4):
                        gp = tb * 4 + j
                        thc = th_s[d][:, gp:gp + 1]
                        for which, off in ((1, 0.0), (0, 0.25)):
                            k.op("vector", lambda e, thc=thc, off=off: e.tensor_scalar(out=phs[:], in0=iotaL[:], scalar1=thc, scalar2=off,
                                                                                       op0=ALU.mult, op1=ALU.add), reads=[b_prm, b_io, b_phs], writes=[b_phs])
                            self.sin_turns(tabs[j][which], phs[:], phi[:], pht[:], [b_phs, b_tab[j]])
                        k.op("vector", lambda e, j=j, d=d, gp=gp: e.tensor_scalar(out=rts[j][:], in0=iotaL[:, 0:L], scalar1=0.0,
                                                                                 scalar2=mag_s[d][:, gp:gp + 1], op0=ALU.mult, op1=ALU.add),
                             reads=[b_prm, b_io], writes=[b_tab[j]])
                    order = chunks if d == 0 else [chunks[0]] + chunks[:0:-1]
                    NCI = len(order)
                    Aout, Zout = {}, {}

                    def sv(t2d, ci, d=d, order=order):
                        lo, hi = order[ci]
                        v = t2d[:, lo:hi]
                        return v[:, ::-1] if d == 1 else v

                    def TT(E, out, in0, in1, op, reads, writes):
                        k.op(E, lambda e: e.tensor_tensor(out=out, in0=in0, in1=in1, op=op), reads=reads, writes=writes)

                    def stageA(ci, d=d, tb=tb, bt=bt, b_bt=b_bt, sv=sv, order=order):
                        n = order[ci][1] - order[ci][0]
                        for j in range(4):
                            EA = "gpsimd" if j < 3 else "vector"
                            cs_t = tabs[j][0][:, 0:n]
                            sn_t = tabs[j][1][:, 0:n]
                            p1i, _ = psr.next()
                            p2i, _ = psr.next()
                            p1, p2 = self.ps[p1i], self.ps[p2i]
                            rhs = sv(ubf, ci)
                            k.op("tensor", lambda e, p1=p1, j=j, rhs=rhs, n=n: e.matmul(
                                p1[:, 0:n], lhsT=bt[:, d, 0, j * 128:(j + 1) * 128], rhs=rhs, start=True, stop=True),
                                reads=[b_bt, b_ubf], writes=[self.psb[p1i]])
                            k.op("tensor", lambda e, p2=p2, j=j, rhs=rhs, n=n: e.matmul(
                                p2[:, 0:n], lhsT=bt[:, d, 1, j * 128:(j + 1) * 128], rhs=rhs, start=True, stop=True),
                                reads=[b_bt, b_ubf], writes=[self.psb[p2i]])
                            if EA == "gpsimd":
                                pr, bpr = pre_s.next()
                                k.op("scalar", lambda e, pr=pr, p1=p1, n=n: e.copy(out=pr[:, 0, 0:n], in_=p1[:, 0:n]), reads=[self.psb[p1i]], writes=[bpr])
                                k.op("scalar", lambda e, pr=pr, p2=p2, n=n: e.copy(out=pr[:, 1, 0:n], in_=p2[:, 0:n]), reads=[self.psb[p2i]], writes=[bpr])
                                s_re, s_im = pr[:, 0, 0:n], pr[:, 1, 0:n]
                                rd = [bpr, b_tab[j]]
                                tp, btp = tmpP.next()
                                tpv = tp[:, 0:n]
                            else:
                                s_re, s_im = p1[:, 0:n], p2[:, 0:n]
                                rd = [self.psb[p1i], self.psb[p2i], b_tab[j]]
                                tp, btp = tmpV.next()
                                tpv = tp[:, 0, 0:n]
                            w, bw = wring_.next()
                            TT(EA, w[:, 0, 0:n], s_re, cs_t, ALU.mult, rd, [bw])
                            TT(EA, tpv, s_im, sn_t, ALU.mult, rd, [btp])
                            TT(EA, w[:, 0, 0:n], w[:, 0, 0:n], tpv, ALU.add, [bw, btp], [bw])
                            TT(EA, w[:, 1, 0:n], s_im, cs_t, ALU.mult, rd, [bw])
                            TT(EA, tpv, s_re, sn_t, ALU.mult, rd + [btp], [btp])
                            TT(EA, w[:, 1, 0:n], w[:, 1, 0:n], tpv, ALU.subtract, [bw, btp], [bw])
                            Aout[(ci, j)] = (w, bw, n)

                    def stageB(ci, order=order):
                        zr, bzr = zri.next()
                        if ci > 0:
                            nprev = order[ci - 1][1] - order[ci - 1][0]
                            zp, bzp = Zout["prev"]
                            zend = zp[:, :, :, nprev - 1].rearrange("p j c -> p c j")
                            cLb = tab_all[:, 0, :, nprev].unsqueeze(1).to_broadcast([128, 2, 4])
                            sLb = tab_all[:, 1, :, nprev].unsqueeze(1).to_broadcast([128, 2, 4])
                            rdc = [bzp] + b_tab + [b_init]
                            k.op("vector", lambda e, zend=zend, cLb=cLb: e.tensor_tensor(out=x1_t[:], in0=zend, in1=cLb, op=ALU.mult), reads=rdc, writes=[b_init])
                            k.op("vector", lambda e, zend=zend, sLb=sLb: e.tensor_tensor(out=x2_t[:], in0=zend, in1=sLb, op=ALU.mult), reads=rdc, writes=[b_init])
                            k.op("vector", lambda e: e.tensor_tensor(out=init_t[:, 0, :], in0=x1_t[:, 0, :], in1=x2_t[:, 1, :], op=ALU.subtract), reads=[b_init], writes=[b_init])
                            k.op("vector", lambda e: e.tensor_tensor(out=init_t[:, 1, :], in0=x1_t[:, 1, :], in1=x2_t[:, 0, :], op=ALU.add), reads=[b_init], writes=[b_init])
                        for j in range(4):
                            w, bw, n = Aout.pop((ci, j))
                            for c2 in range(2):
                                ini = 0.0 if ci == 0 else init_t[:, c2, j:j + 1]
                                k.op("vector", lambda e, zr=zr, w=w, c2=c2, ini=ini, j=j, n=n: e.tensor_tensor_scan(
                                    out=zr[:, j, c2, 0:n], data0=rts[j][:, 0:n], data1=w[:, c2, 0:n], initial=ini, op0=ALU.mult, op1=ALU.add),
                                    reads=[bw, b_tab[j], b_init], writes=[bzr])
                            Zout[(ci, j)] = (zr[:, j], bzr, n)
                        Zout["prev"] = (zr, bzr)

                    def stageC(ci, d=d, ct=ct, b_ct=b_ct, sv=sv):
                        pyi, _ = psy.next()
                        py = self.ps[pyi]
                        for j in range(4):
                            zr, bzr, n = Zout.pop((ci, j))
                            E = "vector" if j < 3 else "gpsimd"
                            cs_t = tabs[j][0][:, 0:n]
                            sn_t = tabs[j][1][:, 0:n]
                            xr_, bxr_ = xri.next()
                            tv, btv = tmpV.next() if E == "vector" else tmpP2.next()
                            zre, zim = zr[:, 0, 0:n], zr[:, 1, 0:n]
                            A_, B_ = tv[:, 0, 0:n], tv[:, 1, 0:n]
                            rd2 = [bzr, b_tab[j]]
                            TT(E, A_, zre, cs_t, ALU.mult, rd2, [btv])
                            TT(E, B_, zim, sn_t, ALU.mult, rd2 + [btv], [btv])
                            TT(E, xr_[:, 0, 0:n], A_, B_, ALU.subtract, [btv], [bxr_])
                            TT(E, A_, zim, cs_t, ALU.mult, rd2 + [btv], [btv])
                            TT(E, B_, zre, sn_t, ALU.mult, rd2 + [btv], [btv])
                            TT(E, xr_[:, 1, 0:n], A_, B_, ALU.add, [btv], [bxr_])
                            k.op("tensor", lambda e, py=py, xr_=xr_, j=j, n=n: e.matmul(
                                py[:, 0:n], lhsT=ct[:, d, 0, j, :], rhs=xr_[:, 0, 0:n], start=(j == 0), stop=False),
                                reads=[b_ct, bxr_], writes=[self.psb[pyi]])
                            k.op("tensor", lambda e, py=py, xr_=xr_, j=j, n=n: e.matmul(
                                py[:, 0:n], lhsT=ct[:, d, 1, j, :], rhs=xr_[:, 1, 0:n], start=False, stop=(j == 3)),
                                reads=[b_ct, bxr_], writes=[self.psb[pyi]])
                        yv = sv(yacc, ci)
                        k.op("vector", lambda e, py=py, yv=yv, n=n: e.tensor_tensor(out=yv, in0=py[:, 0:n], in1=yv, op=ALU.add),
                             reads=[self.psb[pyi], b_y], writes=[b_y])

                    stageA(0)
                    for ci in range(NCI):
                        if ci + 1 < NCI:
                            stageA(ci + 1)
                        stageB(ci)
                        stageC(ci)
                k.op("vector", lambda e: e.tensor_tensor(out=ust[:], in0=yacc[:], in1=yacc[:], op=ALU.mult), reads=[b_y, b_ust], writes=[b_ust])
                k.op("vector", lambda e: e.tensor_scalar(out=ust[:], in0=ust[:], scalar1=0.044715, scalar2=1.0, op0=ALU.mult, op1=ALU.add),
                     reads=[b_ust], writes=[b_ust])
                k.op("vector", lambda e: e.tensor_tensor(out=ust[:], in0=ust[:], in1=yacc[:], op=ALU.mult), reads=[b_ust, b_y], writes=[b_ust])
                k.op("scalar", lambda e: e.activation(out=ust[:], in_=ust[:], func=AF.Sigmoid, scale=1.5957691216057308), reads=[b_ust], writes=[b_ust])
                k.op("vector", lambda e: e.tensor_tensor(out=ust[:], in0=ust[:], in1=yacc[:], op=ALU.mult), reads=[b_ust, b_y], writes=[b_ust])
                k.dma("sync", lambda e, tb=tb: e.dma_start(out=self.gS5[tb, :, :], in_=ust[:]), reads=[b_ust])
            k.barrier()

    def glu_phase(self, l, need_ctx):
        k = self.k
        with contextlib.ExitStack() as ph:
            self.wring = self.make_ring(ph, "wr", [128, 4096], BF16, 5)
            gf = self.sb(ph, "ggf", [128, 8, 1024], F32)
            gb = self.sb(ph, "ggb", [128, 8, 1024], BF16)
            bgl = self.sb(ph, "gbgl", [128, 8], F32)
            b_gf, b_gb, b_bg = Buf(), Buf(), Buf()
            sgr = self.make_ring(ph, "gsg", [128, 512], F32, 3)
            outr = self.make_ring(ph, "gout", [128, 512], BF16, 3)
            k.dma("sync", lambda e: e.dma_start(out=bgl[:], in_=self.s5_bglu[l].rearrange("(c p) -> p c", p=128)), writes=[b_bg])
            wv = self.s5_wglu[l].rearrange("(kt p) c -> p kt c", p=128)
            gv = self.gS5.rearrange("c p t -> p c t")
            psr = Ring([1, 2, 3, 4])
            for (t0, ts, s) in self.supertiles(need_ctx):
                nh = max(1, ts // 512)
                n = min(512, ts)
                k.dma("sync", lambda e, t0=t0, ts=ts: e.dma_start(out=gf[:, :, 0:ts], in_=gv[:, :, t0:t0 + ts]), reads=[b_gf], writes=[b_gf])
                k.op("scalar", lambda e, ts=ts: e.copy(out=gb[:, :, 0:ts], in_=gf[:, :, 0:ts]), reads=[b_gf, b_gb], writes=[b_gb])
                stream = Prog.WStream(self, [wv[:, :, bi * 256:(bi + 1) * 256] for bi in range(4)], 3)
                for bi in range(4):
                    wt, wb = stream.get()
                    for sub in range(2):
                        m = bi * 2 + sub
                        for hf in range(nh):
                            tsl = slice(hf * 512, hf * 512 + n)
                            tok0 = t0 + hf * 512
                            pi, _ = psr.next()
                            pp = self.ps[pi]
                            for kt in range(8):
                                k.op("tensor", lambda e, pp=pp, wt=wt, kt=kt, sub=sub, tsl=tsl, n=n: e.matmul(
                                    pp[:, 0:n], lhsT=wt[:, kt, sub * 128:(sub + 1) * 128], rhs=gb[:, kt, tsl],
                                    start=(kt == 0), stop=(kt == 7)), reads=[wb, b_gb], writes=[self.psb[pi]])
                            sg, bsg = sgr.next()
                            k.op("scalar", lambda e, sg=sg, pp=pp, m=m, n=n: e.activation(out=sg[:, 0:n], in_=pp[:, 0:n], func=AF.Sigmoid,
                                                                                       bias=bgl[:, m:m + 1]), reads=[self.psb[pi], b_bg], writes=[bsg])
                            ot, bot = outr.next()
                            k.op("vector", lambda e, ot=ot, sg=sg, m=m, tsl=tsl, n=n: e.tensor_tensor(out=ot[:, 0:n], in0=sg[:, 0:n], in1=gf[:, m, tsl], op=ALU.mult),
                                 reads=[bsg, b_gf], writes=[bot])
                            k.dma("sync", lambda e, ot=ot, m=m, tok0=tok0, n=n: e.dma_start(out=self.mixT[m, :, tok0:tok0 + n], in_=ot[:, 0:n]), reads=[bot])
            k.barrier()

    def outproj_phase(self, l, include_ctx):
        k = self.k
        xTl = self.xT
        with contextlib.ExitStack() as ph:
            self.wring = self.make_ring(ph, "wr", [128, 4096], BF16, 5)
            mx = self.sb(ph, "omx", [128, 16, 1024], BF16)
            b_mx = Buf()
            xrring = self.make_ring(ph, "oxr", [128, 512], F32, 3)
            wv = self.w_out[l].rearrange("(kt p) c -> p kt c", p=128)
            mv = self.mixT.rearrange("c p t -> p c t")
            psr = Ring([1, 2, 3, 4])
            for (t0, ts, s) in self.supertiles(include_ctx):
                nh = max(1, ts // 512)
                n = min(512, ts)
                k.dma("sync", lambda e, t0=t0, ts=ts: e.dma_start(out=mx[:, :, 0:ts], in_=mv[:, :, t0:t0 + ts]), reads=[b_mx], writes=[b_mx])
                stream = Prog.WStream(self, [wv[:, :, bi * 256:(bi + 1) * 256] for bi in range(8)], 3)
                for bi in range(8):
                    wt, wb = stream.get()
                    for sub in range(2):
                        m = bi * 2 + sub
                        for hf in range(nh):
                            tsl = slice(hf * 512, hf * 512 + n)
                            tok0 = t0 + hf * 512
                            xr, bxr = xrring.next()
                            k.dma("sync", lambda e, xr=xr, m=m, tok0=tok0, n=n: e.dma_start(out=xr[:, 0:n], in_=xTl[m, :, tok0:tok0 + n]),
                                  reads=[self.xbuf(m, tok0)], writes=[bxr])
                            pi, _ = psr.next()
                            pp = self.ps[pi]
                            for kt in range(16):
                                k.op("tensor", lambda e, pp=pp, wt=wt, kt=kt, sub=sub, tsl=tsl, n=n: e.matmul(
                                    pp[:, 0:n], lhsT=wt[:, kt, sub * 128:(sub + 1) * 128], rhs=mx[:, kt, tsl],
                                    start=(kt == 0), stop=(kt == 15)), reads=[wb, b_mx], writes=[self.psb[pi]])
                            k.op("vector", lambda e, xr=xr, pp=pp, m=m, s=s, n=n: e.scalar_tensor_tensor(
                                out=xr[:, 0:n], in0=pp[:, 0:n], scalar=self.Gmod[:, s, 1, m:m + 1], in1=xr[:, 0:n],
                                op0=ALU.mult, op1=ALU.add), reads=[self.psb[pi], bxr, self.b_mod], writes=[bxr])
                            k.dma("sync", lambda e, xr=xr, m=m, tok0=tok0, n=n: e.dma_start(out=xTl[m, :, tok0:tok0 + n], in_=xr[:, 0:n]),
                                  reads=[bxr], writes=[self.xbuf(m, tok0)])
            k.barrier()


def _consts():
    ident = np.eye(128, dtype=np.float32)
    s = np.arange(128)[:, None]
    t = np.arange(128)[None, :]
    same = (s // 32) == (t // 32)
    m_f = (same & (s <= t)).astype(np.float32)
    m_b = (same & (s >= t)).astype(np.float32)
    g2 = ((np.arange(128) % 32) // 16)
    m0 = np.repeat((g2 == 0).astype(np.float32)[:, None], 128, 1)
    m1 = np.repeat((g2 == 1).astype(np.float32)[:, None], 128, 1)
    m96 = np.repeat((np.arange(128) >= 96).astype(np.float32)[:, None], 128, 1)
    mg = ((np.arange(128)[:, None] // 16) == np.arange(8)[None, :]).astype(np.float32)
    iota = np.repeat(np.arange(257, dtype=np.float32)[None, :], 128, 0)
    aux = np.concatenate([mg, iota], axis=1).astype(np.float32)
    iota2 = np.repeat(np.arange(513, dtype=np.float32)[None, :], 128, 0)
    return ident, np.stack([m_f, m_b, m0, m1, m96]).astype(np.float32), aux, iota2


W_NAMES = ["w_ada", "b_ada", "norm_w", "ffn_w_gate", "ffn_w_up", "ffn_w_down", "w_in", "w_out",
           "s5_lambda_re", "s5_lambda_im", "s5_log_step", "s5_b_re", "s5_b_im", "s5_c_re", "s5_c_im",
           "s5_d", "s5_w_glu", "s5_b_glu", "hgrn_lower_bounds", "hgrn_norm_w", "final_norm_w"]


def make_in_map(inputs, b, half=0):
    ident, masks, aux, iota2 = _consts()
    m = {"xin": np.ascontiguousarray(np.concatenate([inputs["x"][b], inputs["ctx"][b]], axis=0), dtype=np.float32),
         "cc": np.ascontiguousarray(np.stack([inputs["c"][b], inputs["c_ctx"]], axis=0), dtype=np.float32),
         "c_ident": ident, "c_masks": masks, "c_aux": aux, "c_iota": iota2,
         "sel": np.repeat(np.array([[1.0, 0.0]] if half == 0 else [[0.0, 1.0]], np.float32), 128, 0)}
    for nme in W_NAMES:
        m[nme] = np.ascontiguousarray(inputs[nme], dtype=np.float32)
    return m


def kernel(**inputs):
    nc = Prog().build()
    nb = inputs["x"].shape[0]
    in_maps = [make_in_map(inputs, c % nb, c // nb) for c in range(8)]
    res = run_bass_kernel_spmd(nc, in_maps, core_ids=list(range(8)))
    return np.stack([np.concatenate([np.asarray(res.results[b]["Y"]), np.asarray(res.results[b + nb]["Y"])], axis=0)
                     for b in range(nb)], axis=0).astype(np.float32)
```

```python
import contextlib
import math
from collections import deque

import numpy as np
import concourse.bass as bass
import concourse.mybir as mybir
from concourse.bass_utils import run_bass_kernel_spmd

F32 = mybir.dt.float32
BF16 = mybir.dt.bfloat16
I32 = mybir.dt.int32
ALU = mybir.AluOpType
AF = mybir.ActivationFunctionType

D = 2048
NC_ = 16
FF = 5504
NFF = 43
NLAT = 4096
NCTX = 256
T = NLAT + NCTX
DEPTH = 2
EPS = 1e-6
INC = 6144
GRID = 64

ENGS = ("tensor", "vector", "scalar", "gpsimd", "sync")
SEM_ROLL = 30000
NO_SELF_SYNC = ("tensor",)


class Buf:
    __slots__ = ("name", "w", "r")

    def __init__(self, name=""):
        self.name = name
        self.w = None
        self.r = {}


class K:
    def __init__(self, nc, stack, n_dma_sems=32):
        self.nc = nc
        self.stack = stack
        self.q = {e: [] for e in ENGS}
        self.sem = {}
        self.cnt = {}
        self.waited = {e: {} for e in ENGS}
        self.nsem = 0
        self.sem_owner = {}
        self.no_self_sync = set(NO_SELF_SYNC)
        for e in ("tensor", "vector", "scalar", "gpsimd"):
            self._new_eng_sem(e)
        self.dma_sems = []
        for i in range(n_dma_sems):
            s = stack.enter_context(nc.semaphore(f"dma{i}"))
            self.dma_sems.append([s, 0])
        self.dma_rr = 0
        self.n_instr = 0

    def _new_eng_sem(self, e):
        s = self.stack.enter_context(self.nc.semaphore(f"s_{e}_{self.nsem}"))
        self.nsem += 1
        self.sem[e] = s
        self.cnt[e] = 0
        self.sem_owner[id(s)] = e

    def _collect(self, reads, writes):
        evs = []
        for b in reads:
            if b.w is not None:
                evs.append(b.w)
        for b in writes:
            if b.w is not None:
                evs.append(b.w)
            evs.extend(b.r.values())
        return evs

    def _waits_for(self, eng, evs):
        best = {}
        for (s, v) in evs:
            kk = id(s)
            if eng in self.no_self_sync and self.sem_owner.get(kk) == eng:
                continue
            if kk not in best or best[kk][1] < v:
                best[kk] = (s, v)
        out = []
        wd = self.waited[eng]
        for kk, (s, v) in best.items():
            if wd.get(kk, -1) >= v:
                continue
            wd[kk] = v
            out.append((s, v))
        return out

    def _update(self, ev, reads, writes):
        for b in writes:
            b.w = ev
            b.r = {}
        for b in reads:
            b.r[id(ev[0])] = ev

    def op(self, eng, fn, reads=(), writes=(), extra=()):
        evs = self._collect(reads, writes) + list(extra)
        waits = self._waits_for(eng, evs)
        if self.cnt[eng] >= SEM_ROLL:
            self._new_eng_sem(eng)
        self.cnt[eng] += 1
        ev = (self.sem[eng], self.cnt[eng])
        self.q[eng].append((waits, fn, ev[0], 1))
        self._update(ev, reads, writes)
        self.n_instr += 1
        return ev

    def dma(self, eng, fn, reads=(), writes=(), extra=()):
        evs = self._collect(reads, writes) + list(extra)
        slot = self.dma_sems[self.dma_rr]
        self.dma_rr = (self.dma_rr + 1) % len(self.dma_sems)
        if slot[1] > 0:
            evs.append((slot[0], slot[1]))
        waits = self._waits_for(eng, evs)
        slot[1] += 16
        ev = (slot[0], slot[1])
        self.q[eng].append((waits, fn, ev[0], 16))
        self._update(ev, reads, writes)
        self.n_instr += 1
        return ev

    def all_events(self):
        evs = []
        for e in ("tensor", "vector", "scalar", "gpsimd"):
            if self.cnt[e] > 0:
                evs.append((self.sem[e], self.cnt[e]))
        for s, v in self.dma_sems:
            if v > 0:
                evs.append((s, v))
        return evs

    def barrier(self):
        evs = self.all_events()
        for e in ENGS:
            saved = self.no_self_sync
            self.no_self_sync = set()
            waits = self._waits_for(e, evs)
            self.no_self_sync = saved
            if waits:
                self.q[e].append((waits, None, None, 0))

    def finish(self):
        nc = self.nc
        self.barrier()
        q = self.q

        def run(e, items):
            for (waits, fn, sem, inc) in items:
                for (s, v) in waits:
                    e.wait_ge(s, v)
                if fn is None:
                    continue
                ins = fn(e)
                ins.then_inc(sem, inc)

        with nc.Block() as block:
            @block.sync
            def _(e):
                run(e, q["sync"])

            @block.tensor
            def _(e):
                run(e, q["tensor"])

            @block.vector
            def _(e):
                run(e, q["vector"])

            @block.scalar
            def _(e):
                run(e, q["scalar"])

            @block.gpsimd
            def _(e):
                run(e, q["gpsimd"])


class Ring:
    def __init__(self, tiles):
        self.tiles = tiles
        self.bufs = [Buf() for _ in tiles]
        self.i = 0

    def next(self):
        i = self.i
        self.i = (i + 1) % len(self.tiles)
        return self.tiles[i], self.bufs[i]


def bc_mid(ap2, n):
    return ap2.unsqueeze(1).to_broadcast([ap2.shape[0], n, ap2.shape[1]])


def bc_last(ap2, n):
    return ap2.unsqueeze(2).to_broadcast([ap2.shape[0], ap2.shape[1], n])


class Prog:
    def __init__(self, n_layers=DEPTH, stage="full"):
        self.n_layers = n_layers
        self.stage = stage
        self.nc = bass.Bass("TRN2", target_bir_lowering=False)
        self.st = contextlib.ExitStack()

    def dram_in(self, name, shape):
        return self.nc.dram_tensor(name, list(shape), F32, kind="ExternalInput").ap()

    def sb(self, stack, name, shape, dtype):
        self._uid = getattr(self, "_uid", 0) + 1
        return stack.enter_context(self.nc.sbuf_tensor(f"{name}_{self._uid}", list(shape), dtype))

    def make_ring(self, stack, name, shape, dtype, n):
        return Ring([self.sb(stack, f"{name}{i}", shape, dtype) for i in range(n)])

    def xbuf(self, c, tok):
        return self.xT_bufs[c][tok // 512]

    def wload(self, src):
        t, b = self.wring.next()
        kt, cols = src.shape[1], src.shape[2]
        dst = t[:, 0:kt * cols].rearrange("p (k c) -> p k c", c=cols)
        self.k.dma("gpsimd", lambda e: e.dma_start(out=dst, in_=src), writes=[b])
        return dst, b

    class WStream:
        def __init__(self, prog, srcs, lookahead):
            self.p = prog
            self.srcs = srcs
            self.n = 0
            self.loaded = deque()
            self.la = lookahead

        def get(self):
            while self.n < len(self.srcs) and len(self.loaded) < self.la + 1:
                self.loaded.append(self.p.wload(self.srcs[self.n]))
                self.n += 1
            return self.loaded.popleft()

    def build(self):
        nc = self.nc
        st = self.st
        L = DEPTH
        self.xin = self.dram_in("xin", [T, D])
        self.cc = self.dram_in("cc", [2, D])
        self.w_ada = self.dram_in("w_ada", [L, D, 9 * D])
        self.b_ada = self.dram_in("b_ada", [L, 9 * D])
        self.norm_w = self.dram_in("norm_w", [L, 3, D])
        self.wg = self.dram_in("ffn_w_gate", [L, 2, D, FF])
        self.wu = self.dram_in("ffn_w_up", [L, 2, D, FF])
        self.wd = self.dram_in("ffn_w_down", [L, 2, FF, D])
        self.w_in = self.dram_in("w_in", [L, D, INC])
        self.w_out = self.dram_in("w_out", [L, D, D])
        self.s5_lre = self.dram_in("s5_lambda_re", [L, 2, 64, 64])
        self.s5_lim = self.dram_in("s5_lambda_im", [L, 2, 64, 64])
        self.s5_ls = self.dram_in("s5_log_step", [L, 2, 64])
        self.s5_bre = self.dram_in("s5_b_re", [L, 2, 64, 64, 16])
        self.s5_bim = self.dram_in("s5_b_im", [L, 2, 64, 64, 16])
        self.s5_cre = self.dram_in("s5_c_re", [L, 2, 64, 16, 64])
        self.s5_cim = self.dram_in("s5_c_im", [L, 2, 64, 16, 64])
        self.s5_d = self.dram_in("s5_d", [L, 1024])
        self.s5_wglu = self.dram_in("s5_w_glu", [L, 1024, 1024])
        self.s5_bglu = self.dram_in("s5_b_glu", [L, 1024])
        self.hg_lb = self.dram_in("hgrn_lower_bounds", [L, 2, 1024])
        self.hg_nw = self.dram_in("hgrn_norm_w", [L, 128])
        self.fin_w = self.dram_in("final_norm_w", [D])
        self.c_ident = self.dram_in("c_ident", [128, 128])
        self.c_masks = self.dram_in("c_masks", [5, 128, 128])
        self.c_aux = self.dram_in("c_aux", [128, 8 + 257])
        self.c_iota = self.dram_in("c_iota", [128, 513])
        self.split_tail = (self.stage == "full" and self.n_layers == DEPTH)
        self.cur_nlat = NLAT
        self.sel_in = self.dram_in("sel", [128, 2])
        self.Y = nc.dram_tensor("Y", [NLAT // 2 if self.split_tail else NLAT, D], F32, kind="ExternalOutput").ap()
        self.xsel = nc.dram_tensor("xsel_s", [NC_, 128, NLAT // 2], F32).ap()
        self.xT = nc.dram_tensor("xT_s", [NC_, 128, T], F32).ap()
        self.pT = nc.dram_tensor("pT_s", [6, 8, 128, T], F32).ap()
        self.vtm = nc.dram_tensor("vtm_s", [T, 1024], F32).ap()
        self.gS5 = nc.dram_tensor("gs5_s", [8, 128, T], F32).ap()
        if self.stage in ("hgrn", "s5"):
            self.mixT = nc.dram_tensor("mixT_s", [NC_, 128, T], BF16, kind="ExternalOutput").ap()
        else:
            self.mixT = nc.dram_tensor("mixT_s", [NC_, 128, T], BF16).ap()
        self.xT_bufs = [[Buf() for _ in range(9)] for _ in range(NC_)]
        self.pT_buf = [[Buf() for _ in range(8)] for _ in range(6)]
        self.vtm_buf = Buf()
        self.gS5_buf = [Buf() for _ in range(8)]
        self.mix_buf = [Buf() for _ in range(NC_)]
        self.BTs = nc.dram_tensor("BTs_s", [2, 2, 128, 8, 512], BF16).ap()
        self.CTs = nc.dram_tensor("CTs_s", [2, 2, 128, 32, 128], BF16).ap()

        self.k = K(nc, st)
        k = self.k
        self.ps = [st.enter_context(nc.psum_tensor(f"ps{i}", [128, 512], F32)) for i in range(8)]
        self.psb = [Buf() for _ in range(8)]
        self.ident = self.sb(st, "ident", [128, 128], F32)
        self.onesD = self.sb(st, "onesD", [128, 128], F32)
        self.ones128 = self.sb(st, "ones128", [128, 128], F32)
        self.masks = self.sb(st, "masks", [128, 5, 128], F32)
        self.aux = self.sb(st, "aux", [128, 8 + 257], F32)
        self.modT = self.sb(st, "modT", [128, 2, 144], F32)
        self.Amod = self.sb(st, "Amod", [128, 2, 3, 16], F32)
        self.Gmod = self.sb(st, "Gmod", [128, 2, 3, 16], F32)
        self.wfin = self.sb(st, "wfin", [128, 16], F32)
        self.b_const = Buf()
        self.b_mod = Buf()
        self.ident_bf = self.sb(st, "identbf", [128, 128], BF16)
        self.onesD_bf = self.sb(st, "onesDbf", [128, 128], BF16)

        with nc.allow_non_contiguous_dma("small parameter vectors are laid out feature-on-partition"):
            self.emit()
            k.finish()
        st.close()
        return nc

    def emit(self):
        k = self.k
        k.dma("sync", lambda e: e.dma_start(out=self.ident[:], in_=self.c_ident[:, :]), writes=[self.b_const])
        k.dma("sync", lambda e: e.dma_start(out=self.masks[:], in_=self.c_masks.rearrange("m p f -> p m f")),
              writes=[self.b_const])
        k.dma("sync", lambda e: e.dma_start(out=self.wfin[:], in_=self.fin_w.rearrange("(c p) -> p c", p=128)),
              writes=[self.b_const])
        k.dma("sync", lambda e: e.dma_start(out=self.aux[:], in_=self.c_aux[:, :]), writes=[self.b_const])
        k.op("vector", lambda e: e.memset(self.onesD[:], 1.0 / D), writes=[self.b_const])
        k.op("vector", lambda e: e.memset(self.ones128[:], 1.0 / 128), writes=[self.b_const])
        k.op("vector", lambda e: e.tensor_copy(out=self.ident_bf[:], in_=self.ident[:]), reads=[self.b_const], writes=[self.b_const])
        k.op("vector", lambda e: e.tensor_copy(out=self.onesD_bf[:], in_=self.onesD[:]), reads=[self.b_const], writes=[self.b_const])
        self.input_phase()
        for l in range(self.n_layers):
            last = (l == DEPTH - 1)
            self.mods_phase(l)
            self.ffn_phase(l, 0, 0, include_ctx=True)
            if self.stage == "ffn1":
                break
            self.inproj_phase(l)
            self.hgrn_phase(l, need_ctx=not last)
            if self.stage == "hgrn":
                break
            self.s5_phase(l, need_ctx=not last)
            self.glu_phase(l, need_ctx=not last)
            if self.stage == "s5":
                break
            self.outproj_phase(l, include_ctx=not last)
            if last and self.split_tail:
                self.select_phase()
            self.ffn_phase(l, 1, 2, include_ctx=not last)
        self.output_phase(apply_norm=(self.stage == "full"))

    def supertiles(self, include_ctx, ts=1024):
        out = [(t0, ts, 0) for t0 in range(0, self.cur_nlat, ts)]
        if include_ctx:
            out.append((NLAT, NCTX, 1))
        return out

    def input_phase(self):
        k = self.k
        with contextlib.ExitStack() as ph:
            xtok = self.make_ring(ph, "xtok", [128, D], F32, 2)
            xst = self.make_ring(ph, "xsti", [128, 16, 128], F32, 2)
            xv = self.xT.rearrange("c p t -> p c t")
            ei = 0
            for ti in range(T // 128):
                tok = ti * 128
                xt, bxt = xtok.next()
                k.dma("sync", lambda e, xt=xt, tok=tok: e.dma_start(out=xt[:], in_=self.xin[tok:tok + 128, :]), writes=[bxt])
                xs, bxs = xst.next()
                for g in range(4):
                    pb = 4 + g % 4
                    for j in range(4):
                        c = g * 4 + j
                        k.op("tensor", lambda e, pb=pb, j=j, c=c, xt=xt: e.transpose(
                            out=self.ps[pb][:, j * 128:(j + 1) * 128], in_=xt[:, c * 128:(c + 1) * 128], identity=self.ident[:]),
                            reads=[bxt, self.b_const], writes=[self.psb[pb]])
                    dst = xs[:, g * 4:(g + 1) * 4, :]
                    src = self.ps[pb][:, :].rearrange("p (j t) -> p j t", t=128)
                    if ei % 2 == 0:
                        k.op("scalar", lambda e, dst=dst, src=src: e.copy(out=dst, in_=src), reads=[self.psb[pb]], writes=[bxs])
                    else:
                        k.op("vector", lambda e, dst=dst, src=src: e.tensor_copy(out=dst, in_=src), reads=[self.psb[pb]], writes=[bxs])
                    ei += 1
                k.dma("sync", lambda e, xs=xs, tok=tok: e.dma_start(out=xv[:, :, tok:tok + 128], in_=xs[:]),
                      reads=[bxs], writes=[self.xbuf(c, tok) for c in range(NC_)])
            k.barrier()

    def select_phase(self):
        k = self.k
        H = NLAT // 2
        xfull = self.xT
        with contextlib.ExitStack() as ph:
            selt = self.sb(ph, "selt", [128, 2], F32)
            b_sel = Buf()
            xa = self.make_ring(ph, "sela", [128, H], F32, 2)
            xb = self.make_ring(ph, "selb", [128, H], F32, 2)
            k.dma("sync", lambda e: e.dma_start(out=selt[:], in_=self.sel_in[:, :]), writes=[b_sel])
            for c in range(NC_):
                ta, ba = xa.next()
                tb_, bb = xb.next()
                k.dma("sync", lambda e, ta=ta, c=c: e.dma_start(out=ta[:], in_=xfull[c, :, 0:H]), writes=[ba])
                k.dma("sync", lambda e, tb_=tb_, c=c: e.dma_start(out=tb_[:], in_=xfull[c, :, H:NLAT]), writes=[bb])
                k.op("vector", lambda e, ta=ta: e.tensor_scalar(out=ta[:], in0=ta[:], scalar1=selt[:, 0:1], scalar2=None, op0=ALU.mult),
                     reads=[ba, b_sel], writes=[ba])
                k.op("vector", lambda e, ta=ta, tb_=tb_: e.scalar_tensor_tensor(out=ta[:], in0=tb_[:], scalar=selt[:, 1:2], in1=ta[:],
                                                                               op0=ALU.mult, op1=ALU.add), reads=[ba, bb, b_sel], writes=[ba])
                k.dma("sync", lambda e, ta=ta, c=c: e.dma_start(out=self.xsel[c, :, :], in_=ta[:]), reads=[ba])
            k.barrier()
        self.xT = self.xsel
        self.xT_bufs = [[Buf() for _ in range(9)] for _ in range(NC_)]
        self.cur_nlat = H

    def mods_phase(self, l):
        k = self.k
        with contextlib.ExitStack() as ph:
            self.wring = self.make_ring(ph, "wr", [128, 4096], BF16, 5)
            ccs = self.sb(ph, "ccs", [128, 2, 16], F32)
            scb = self.sb(ph, "scb", [128, 2, 16], BF16)
            bada = self.sb(ph, "bada", [128, 144], F32)
            nwt = self.sb(ph, "nwt", [128, 3, 16], F32)
            b_cc, b_sc, b_ba, b_nw = Buf(), Buf(), Buf(), Buf()
            k.dma("sync", lambda e: e.dma_start(out=ccs[:], in_=self.cc.rearrange("s (kt p) -> p s kt", p=128)), writes=[b_cc])
            k.op("scalar", lambda e: e.activation(out=scb[:], in_=ccs[:], func=AF.Silu), reads=[b_cc], writes=[b_sc])
            k.dma("sync", lambda e: e.dma_start(out=bada[:], in_=self.b_ada[l].rearrange("(j p) -> p j", p=128)), writes=[b_ba])
            k.dma("sync", lambda e: e.dma_start(out=nwt[:], in_=self.norm_w[l].rearrange("i (c p) -> p i c", p=128)), writes=[b_nw])
            wv = self.w_ada[l].rearrange("(kt p) c -> p kt c", p=128)
            stream = Prog.WStream(self, [wv[:, :, jb * 256:(jb + 1) * 256] for jb in range(72)], 3)
            pm = self.ps[7]
            for jb in range(72):
                wt, wb = stream.get()
                for sub in range(2):
                    j = jb * 2 + sub
                    for kt in range(16):
                        k.op("tensor", lambda e, wt=wt, sub=sub, j=j, kt=kt: e.matmul(
                            pm[:, 2 * j:2 * j + 2], lhsT=wt[:, kt, sub * 128:(sub + 1) * 128], rhs=scb[:, :, kt],
                            start=(kt == 0), stop=(kt == 15)), reads=[wb, b_sc], writes=[self.psb[7]])
            for s in range(2):
                k.op("vector", lambda e, s=s: e.tensor_tensor(out=self.modT[:, s, :], in0=pm[:, s:288:2], in1=bada[:], op=ALU.add),
                     reads=[self.psb[7], b_ba], writes=[self.b_mod])
            for s in range(2):
                for i3 in range(3):
                    sc_ = self.modT[:, s, (3 * i3 + 1) * 16:(3 * i3 + 2) * 16]
                    gt_ = self.modT[:, s, (3 * i3 + 2) * 16:(3 * i3 + 3) * 16]
                    k.op("vector", lambda e, s=s, i3=i3, sc_=sc_: e.scalar_tensor_tensor(
                        out=self.Amod[:, s, i3, :], in0=sc_, scalar=1.0, in1=nwt[:, i3, :], op0=ALU.add, op1=ALU.mult),
                        reads=[b_nw, self.b_mod], writes=[self.b_mod])
                    k.op("vector", lambda e, s=s, i3=i3, gt_=gt_: e.tensor_scalar(
                        out=self.Gmod[:, s, i3, :], in0=gt_, scalar1=(1.0 if i3 == 1 else 0.5), scalar2=None, op0=ALU.mult),
                        reads=[self.b_mod], writes=[self.b_mod])
            k.barrier()

    def norm_piece(self, tok, xring, sqring, rsring, dst, bdst, A_ap, B_ap, psn=0):
        k = self.k
        xv = self.xT.rearrange("c p t -> p c t")
        xs, bx = xring.next()
        k.dma("sync", lambda e: e.dma_start(out=xs[:], in_=xv[:, :, tok:tok + 128]),
              reads=[self.xbuf(c, tok) for c in range(NC_)], writes=[bx])
        sq, bs = sqring.next()
        k.op("scalar", lambda e: e.activation(out=sq[:], in_=xs[:], func=AF.Square), reads=[bx], writes=[bs])
        pn = self.ps[psn]
        for c in range(NC_):
            k.op("tensor", lambda e, c=c: e.matmul(pn[:, 0:128], lhsT=self.onesD_bf[:], rhs=sq[:, c, :], start=(c == 0), stop=(c == NC_ - 1)),
                 reads=[bs, self.b_const], writes=[self.psb[psn]])
        rs, brs = rsring.next()
        k.op("vector", lambda e: e.tensor_scalar(out=rs[:], in0=pn[:, 0:128], scalar1=EPS, scalar2=None, op0=ALU.add),
             reads=[self.psb[psn]], writes=[brs])
        k.op("scalar", lambda e: e.activation(out=rs[:], in_=rs[:], func=AF.Sqrt), reads=[brs], writes=[brs])
        k.op("vector", lambda e: e.reciprocal(out=rs[:], in_=rs[:]), reads=[brs], writes=[brs])
        k.op("vector", lambda e: e.tensor_tensor(out=xs[:], in0=xs[:], in1=bc_mid(rs[:], NC_), op=ALU.mult),
             reads=[brs, bx], writes=[bx])
        if B_ap is None:
            k.op("vector", lambda e: e.tensor_tensor(out=dst, in0=xs[:], in1=bc_last(A_ap, 128), op=ALU.mult),
                 reads=[bx, self.b_mod, self.b_const], writes=[bdst])
        else:
            k.op("vector", lambda e: e.tensor_tensor(out=xs[:], in0=xs[:], in1=bc_last(A_ap, 128), op=ALU.mult),
                 reads=[bx, self.b_mod], writes=[bx])
            k.op("vector", lambda e: e.tensor_tensor(out=dst, in0=xs[:], in1=bc_last(B_ap, 128), op=ALU.add),
                 reads=[bx, self.b_mod], writes=[bdst])

    def ffn_phase(self, l, fi, i3, include_ctx):
        k = self.k
        xTl = self.xT
        with contextlib.ExitStack() as ph:
            self.wring = self.make_ring(ph, "wr", [128, 4096], BF16, 5)
            hT = self.sb(ph, "hT", [128, 16, 1024], BF16)
            a = self.sb(ph, "aT", [128, NFF, 1024], BF16)
            b_h = [Buf(), Buf()]
            b_a = [Buf(), Buf()]
            xring = self.make_ring(ph, "fx", [128, 16, 128], F32, 2)
            sqring = self.make_ring(ph, "fsq", [128, 16, 128], BF16, 2)
            rsring = self.make_ring(ph, "frs", [128, 128], F32, 2)
            slring = self.make_ring(ph, "fsl", [128, 512], F32, 2)
            xrring = self.make_ring(ph, "fxr", [128, 512], F32, 3)
            wgv = self.wg[l, fi].rearrange("(kt p) c -> p kt c", p=128)
            wuv = self.wu[l, fi].rearrange("(kt p) c -> p kt c", p=128)
            wdv = self.wd[l, fi].rearrange("(kt p) c -> p kt c", p=128)
            psg = Ring([1, 2]); psu = Ring([3, 4]); psd = Ring([5, 6])
            for (t0, ts, s) in self.supertiles(include_ctx):
                A_ap = self.Amod[:, s, i3, :]
                B_ap = self.modT[:, s, (3 * i3) * 16:(3 * i3 + 1) * 16]
                nh = max(1, ts // 512)
                n = min(512, ts)
                for pc in range(ts // 128):
                    self.norm_piece(t0 + pc * 128, xring, sqring, rsring, hT[:, :, pc * 128:(pc + 1) * 128], b_h[(pc * 128) // 512],
                                    A_ap, B_ap)
                srcs = []
                for jb in range(22):
                    cols = 256 if jb < 21 else 128
                    srcs.append(wgv[:, :, jb * 256:jb * 256 + cols])
                    srcs.append(wuv[:, :, jb * 256:jb * 256 + cols])
                for m in range(16):
                    srcs.append(wdv[:, 0:22, m * 128:(m + 1) * 128])
                    srcs.append(wdv[:, 22:43, m * 128:(m + 1) * 128])
                stream = Prog.WStream(self, srcs, 3)
                for jb in range(22):
                    cols = 256 if jb < 21 else 128
                    gt, gb = stream.get()
                    ut, ub = stream.get()
                    for sub in range(cols // 128):
                        j = jb * 2 + sub
                        for hf in range(nh):
                            tsl = slice(hf * 512, hf * 512 + n)
                            pgi, _ = psg.next(); pui, _ = psu.next()
                            pg, pu = self.ps[pgi], self.ps[pui]
                            for kt in range(16):
                                k.op("tensor", lambda e, pg=pg, gt=gt, kt=kt, sub=sub, tsl=tsl, n=n: e.matmul(
                                    pg[:, 0:n], lhsT=gt[:, kt, sub * 128:(sub + 1) * 128], rhs=hT[:, kt, tsl],
                                    start=(kt == 0), stop=(kt == 15)), reads=[gb, b_h[hf]], writes=[self.psb[pgi]])
                            for kt in range(16):
                                k.op("tensor", lambda e, pu=pu, ut=ut, kt=kt, sub=sub, tsl=tsl, n=n: e.matmul(
                                    pu[:, 0:n], lhsT=ut[:, kt, sub * 128:(sub + 1) * 128], rhs=hT[:, kt, tsl],
                                    start=(kt == 0), stop=(kt == 15)), reads=[ub, b_h[hf]], writes=[self.psb[pui]])
                            sl, bsl = slring.next()
                            k.op("scalar", lambda e, sl=sl, pg=pg, n=n: e.activation(out=sl[:, 0:n], in_=pg[:, 0:n], func=AF.Silu),
                                 reads=[self.psb[pgi]], writes=[bsl])
                            k.op("vector", lambda e, sl=sl, pu=pu, j=j, tsl=tsl, n=n: e.tensor_tensor(
                                out=a[:, j, tsl], in0=sl[:, 0:n], in1=pu[:, 0:n], op=ALU.mult),
                                reads=[bsl, self.psb[pui]], writes=[b_a[hf]])
                for m in range(16):
                    w0, b0 = stream.get()
                    w1, b1 = stream.get()
                    for hf in range(nh):
                        tsl = slice(hf * 512, hf * 512 + n)
                        tok0 = t0 + hf * 512
                        xr, bxr = xrring.next()
                        k.dma("sync", lambda e, xr=xr, m=m, tok0=tok0, n=n: e.dma_start(out=xr[:, 0:n], in_=xTl[m, :, tok0:tok0 + n]),
                              reads=[self.xbuf(m, tok0)], writes=[bxr])
                        pdi, _ = psd.next()
                        pd = self.ps[pdi]
                        for kt in range(NFF):
                            wt = w0[:, kt, :] if kt < 22 else w1[:, kt - 22, :]
                            k.op("tensor", lambda e, pd=pd, wt=wt, kt=kt, tsl=tsl, n=n: e.matmul(
                                pd[:, 0:n], lhsT=wt, rhs=a[:, kt, tsl], start=(kt == 0), stop=(kt == NFF - 1)),
                                reads=[b0, b1, b_a[hf]], writes=[self.psb[pdi]])
                        k.op("vector", lambda e, xr=xr, pd=pd, m=m, s=s, n=n: e.scalar_tensor_tensor(
                            out=xr[:, 0:n], in0=pd[:, 0:n], scalar=self.Gmod[:, s, i3, m:m + 1], in1=xr[:, 0:n],
                            op0=ALU.mult, op1=ALU.add), reads=[self.psb[pdi], bxr, self.b_mod], writes=[bxr])
                        k.dma("sync", lambda e, xr=xr, m=m, tok0=tok0, n=n: e.dma_start(out=xTl[m, :, tok0:tok0 + n], in_=xr[:, 0:n]),
                              reads=[bxr], writes=[self.xbuf(m, tok0)])
            k.barrier()

    def output_phase(self, apply_norm):
        k = self.k
        with contextlib.ExitStack() as ph:
            xring = self.make_ring(ph, "ox", [128, 16, 128], F32, 2)
            sqring = self.make_ring(ph, "osq", [128, 16, 128], BF16, 2)
            rsring = self.make_ring(ph, "ors", [128, 128], F32, 2)
            yst = self.make_ring(ph, "oy", [128, 16, 128], F32, 2)
            ytok = self.make_ring(ph, "oyt", [128, D], F32, 2)
            xv = self.xT.rearrange("c p t -> p c t")
            ei = 0
            for ti in range(self.cur_nlat // 128):
                tok = ti * 128
                ys, bys = yst.next()
                if apply_norm:
                    self.norm_piece(tok, xring, sqring, rsring, ys[:], bys, self.wfin[:], None)
                else:
                    k.dma("sync", lambda e, ys=ys, tok=tok: e.dma_start(out=ys[:], in_=xv[:, :, tok:tok + 128]),
                          reads=[self.xbuf(c, tok) for c in range(NC_)], writes=[bys])
                yt, byt = ytok.next()
                for g in range(4):
                    pb = 4 + g % 4
                    for j in range(4):
                        c = g * 4 + j
                        k.op("tensor", lambda e, pb=pb, j=j, c=c, ys=ys: e.transpose(
                            out=self.ps[pb][:, j * 128:(j + 1) * 128], in_=ys[:, c, :], identity=self.ident[:]),
                            reads=[bys, self.b_const], writes=[self.psb[pb]])
                    dst = yt[:, g * 512:(g + 1) * 512]
                    src = self.ps[pb][:, :]
                    if ei % 2 == 0:
                        k.op("scalar", lambda e, dst=dst, src=src: e.copy(out=dst, in_=src), reads=[self.psb[pb]], writes=[byt])
                    else:
                        k.op("vector", lambda e, dst=dst, src=src: e.tensor_copy(out=dst, in_=src), reads=[self.psb[pb]], writes=[byt])
                    ei += 1
                k.dma("sync", lambda e, yt=yt, tok=tok: e.dma_start(out=self.Y[tok:tok + 128, :], in_=yt[:]), reads=[byt])
            k.barrier()

    def inproj_phase(self, l):
        k = self.k
        with contextlib.ExitStack() as ph:
            self.wring = self.make_ring(ph, "wr", [128, 4096], BF16, 5)
            hT = self.sb(ph, "ihT", [128, 16, 1024], BF16)
            b_h = [Buf(), Buf()]
            xring = self.make_ring(ph, "ix", [128, 16, 128], F32, 2)
            sqring = self.make_ring(ph, "isq", [128, 16, 128], BF16, 2)
            rsring = self.make_ring(ph, "irs", [128, 128], F32, 2)
            evring = self.make_ring(ph, "iev", [128, 512], F32, 4)
            wv = self.w_in[l].rearrange("(kt p) c -> p kt c", p=128)
            psr = Ring([1, 2, 3, 4, 5, 6])
            ei = 0
            for (t0, ts, s) in self.supertiles(True):
                A_ap = self.Amod[:, s, 1, :]
                B_ap = self.modT[:, s, 3 * 16:4 * 16]
                nh = max(1, ts // 512)
                n = min(512, ts)
                for pc in range(ts // 128):
                    self.norm_piece(t0 + pc * 128, xring, sqring, rsring, hT[:, :, pc * 128:(pc + 1) * 128], b_h[(pc * 128) // 512],
                                    A_ap, B_ap)
                stream = Prog.WStream(self, [wv[:, :, bi * 256:(bi + 1) * 256] for bi in range(24)], 3)
                for bi in range(24):
                    wt, wb = stream.get()
                    fam = bi // 4
                    if fam != 4:
                        for sub in range(2):
                            ch = (bi % 4) * 2 + sub
                            for hf in range(nh):
                                tsl = slice(hf * 512, hf * 512 + n)
                                tok0 = t0 + hf * 512
                                pi, _ = psr.next()
                                pp = self.ps[pi]
                                for kt in range(16):
                                    k.op("tensor", lambda e, pp=pp, wt=wt, kt=kt, sub=sub, tsl=tsl, n=n: e.matmul(
                                        pp[:, 0:n], lhsT=wt[:, kt, sub * 128:(sub + 1) * 128], rhs=hT[:, kt, tsl],
                                        start=(kt == 0), stop=(kt == 15)), reads=[wb, b_h[hf]], writes=[self.psb[pi]])
                                ev, bev = evring.next()
                                if fam in (1, 5):
                                    k.op("scalar", lambda e, ev=ev, pp=pp, n=n: e.activation(out=ev[:, 0:n], in_=pp[:, 0:n], func=AF.Silu),
                                         reads=[self.psb[pi]], writes=[bev])
                                elif ei % 2 == 0:
                                    k.op("scalar", lambda e, ev=ev, pp=pp, n=n: e.copy(out=ev[:, 0:n], in_=pp[:, 0:n]),
                                         reads=[self.psb[pi]], writes=[bev])
                                else:
                                    k.op("vector", lambda e, ev=ev, pp=pp, n=n: e.tensor_copy(out=ev[:, 0:n], in_=pp[:, 0:n]),
                                         reads=[self.psb[pi]], writes=[bev])
                                ei += 1
                                k.dma("sync", lambda e, ev=ev, fam=fam, ch=ch, tok0=tok0, n=n: e.dma_start(
                                    out=self.pT[fam, ch, :, tok0:tok0 + n], in_=ev[:, 0:n]), reads=[bev])
                    else:
                        for tt in range(ts // 128):
                            tok0 = t0 + tt * 128
                            pi, _ = psr.next()
                            pp = self.ps[pi]
                            for kt in range(16):
                                k.op("tensor", lambda e, pp=pp, wt=wt, kt=kt, tt=tt: e.matmul(
                                    pp[:, 0:256], lhsT=hT[:, kt, tt * 128:(tt + 1) * 128], rhs=wt[:, kt, :],
                                    start=(kt == 0), stop=(kt == 15)), reads=[wb, b_h[(tt * 128) // 512]], writes=[self.psb[pi]])
                            ev, bev = evring.next()
                            if ei % 2 == 0:
                                k.op("scalar", lambda e, ev=ev, pp=pp: e.copy(out=ev[:, 0:256], in_=pp[:, 0:256]),
                                     reads=[self.psb[pi]], writes=[bev])
                            else:
                                k.op("vector", lambda e, ev=ev, pp=pp: e.tensor_copy(out=ev[:, 0:256], in_=pp[:, 0:256]),
                                     reads=[self.psb[pi]], writes=[bev])
                            ei += 1
                            c0 = (bi % 4) * 256
                            k.dma("sync", lambda e, ev=ev, tok0=tok0, c0=c0: e.dma_start(
                                out=self.vtm[tok0:tok0 + 128, c0:c0 + 256], in_=ev[:, 0:256]), reads=[bev])
            k.barrier()

    def hgrn_phase(self, l, need_ctx):
        k = self.k
        NT = T // 128
        NCH = T // 32
        with contextlib.ExitStack() as ph:
            W = [self.sb(ph, f"hW{i}", [128, T], F32) for i in range(5)]
            bW = [Buf() for _ in range(5)]
            qh = [self.sb(ph, f"hqh{d}", [128, T], BF16) for d in range(2)]
            kt_ = [self.sb(ph, f"hkt{d}", [128, T], BF16) for d in range(2)]
            b_qh = [Buf(), Buf()]
            b_kt = [Buf(), Buf()]
            kh = self.sb(ph, "hkh", [128, T], BF16)
            b_kh = Buf()
            khtm = [self.sb(ph, f"hkhtm{d}", [128, NT, 128], BF16) for d in range(2)]
            b_khtm = [Buf(), Buf()]
            khtm3 = [self.sb(ph, f"hkhtm3{d}", [128, NT, 128], BF16) for d in range(2)]
            V = self.sb(ph, "hV", [128, NT, 128], BF16)
            b_V = Buf()
            dec = [self.sb(ph, f"hdec{d}", [128, NCH], F32) for d in range(2)]
            b_dec = [Buf(), Buf()]
            cm = self.sb(ph, "hcm", [128, T], BF16)
            hb = self.sb(ph, "hhb", [128, 2, 2, 8], F32)
            lbt = self.sb(ph, "hlbt", [128, 2, 8], F32)
            oml = self.sb(ph, "homl", [128, 2, 8], F32)
            nwh = self.sb(ph, "hnwh", [128, 1], F32)
            b_par = Buf()
            S32 = [self.sb(ph, f"hS32{d}", [128, 2, 128], F32) for d in range(2)]
            Sring = [self.sb(ph, f"hSr{d}", [128, 4, 128], BF16) for d in range(2)]
            b_S32 = [[Buf(), Buf()], [Buf(), Buf()]]
            b_Sr = [[Buf() for _ in range(4)] for _ in range(2)]
            psT = self.ps[7][:].bitcast(BF16)

            k.op("vector", lambda e: e.memset(cm[:], 1.0), writes=[b_par])
            k.op("vector", lambda e: e.memset(cm[:, 0:T:32], 0.0), writes=[b_par])
            k.dma("sync", lambda e: e.dma_start(out=hb[:], in_=self.hg_lb.rearrange("l d (h p) -> p l d h", p=128)), writes=[b_par])
            k.dma("sync", lambda e: e.dma_start(out=nwh[:], in_=self.hg_nw[l].rearrange("(p o) -> p o", o=1)), writes=[b_par])
            if l == 0:
                k.op("vector", lambda e: e.memset(lbt[:], 0.0), writes=[b_par])
            else:
                k.op("vector", lambda e: e.tensor_tensor(out=lbt[:], in0=hb[:, 1], in1=hb[:, 0], op=ALU.subtract), reads=[b_par], writes=[b_par])
                k.op("scalar", lambda e: e.activation(out=lbt[:], in_=lbt[:], func=AF.Sigmoid), reads=[b_par], writes=[b_par])
            k.op("vector", lambda e: e.tensor_scalar(out=oml[:], in0=lbt[:], scalar1=-1.0, scalar2=1.0, op0=ALU.mult, op1=ALU.add),
                 reads=[b_par], writes=[b_par])

            def cmv(ap):
                return ap[:, 0:NLAT].rearrange("p (c r) -> p c r", r=GRID)

            def rasv_as_cr(ap):
                return ap[:, 0:NLAT].rearrange("p (r c) -> p c r", c=GRID)

            for hd in range(8):
                vsrc = self.vtm[0:NLAT, hd * 128:(hd + 1) * 128].rearrange("(r ti c2) v -> c2 r ti v", ti=32, c2=2)
                for c2 in range(2):
                    k.dma("gpsimd", lambda e, c2=c2, vsrc=vsrc: e.dma_start(out=V[c2 * 64:(c2 + 1) * 64, 0:32, :], in_=vsrc[c2]), writes=[b_V])
                vsrc2 = self.vtm[NLAT:T, hd * 128:(hd + 1) * 128].rearrange("(ti p) v -> p ti v", p=128)
                k.dma("gpsimd", lambda e, vsrc2=vsrc2: e.dma_start(out=V[:, 32:34, :], in_=vsrc2), writes=[b_V])
                for d in range(2):
                    k.dma("sync", lambda e, hd=hd, d=d: e.dma_start(out=W[4][:], in_=self.pT[2 + d, hd, :, :]), writes=[bW[4]])
                    k.op("scalar", lambda e: e.activation(out=cmv(W[0]), in_=rasv_as_cr(W[4]), func=AF.Sigmoid), reads=[bW[4]], writes=[bW[0]])
                    k.op("scalar", lambda e: e.activation(out=W[0][:, NLAT:T], in_=W[4][:, NLAT:T], func=AF.Sigmoid), reads=[bW[4]], writes=[bW[0]])
                    k.op("vector", lambda e, d=d, hd=hd: e.tensor_scalar(out=W[0][:], in0=W[0][:], scalar1=oml[:, d, hd:hd + 1],
                                                                       scalar2=lbt[:, d, hd:hd + 1], op0=ALU.mult, op1=ALU.add),
                         reads=[b_par, bW[0]], writes=[bW[0]])
                    k.op("gpsimd", lambda e: e.tensor_scalar(out=W[1][:], in0=W[0][:], scalar1=-1.0, scalar2=1.0, op0=ALU.mult, op1=ALU.add),
                         reads=[bW[0]], writes=[bW[1]])
                    k.op("vector", lambda e: e.tensor_scalar(out=W[0][:], in0=W[0][:], scalar1=1e-6, scalar2=None, op0=ALU.max),
                         reads=[bW[0], bW[1]], writes=[bW[0]])
                    k.op("scalar", lambda e: e.activation(out=W[0][:], in_=W[0][:], func=AF.Ln), reads=[bW[0]], writes=[bW[0]])
                    if d == 0:
                        k.op("vector", lambda e: e.tensor_tensor_scan(out=W[2][:], data0=cm[:], data1=W[0][:], initial=0.0,
                                                                      op0=ALU.mult, op1=ALU.add), reads=[bW[0], b_par], writes=[bW[2]])
                    else:
                        k.op("vector", lambda e: e.tensor_tensor_scan(out=W[2][:, ::-1], data0=cm[:], data1=W[0][:, ::-1], initial=0.0,
                                                                      op0=ALU.mult, op1=ALU.add), reads=[bW[0], b_par], writes=[bW[2]])
                    k.op("scalar", lambda e: e.activation(out=W[3][:], in_=W[2][:], func=AF.Exp), reads=[bW[2]], writes=[bW[3]])
                    k.dma("sync", lambda e, hd=hd: e.dma_start(out=W[4][:], in_=self.pT[1, hd, :, :]), reads=[bW[4]], writes=[bW[4]])
                    k.op("vector", lambda e, d=d: e.tensor_tensor(out=cmv(qh[d]), in0=rasv_as_cr(W[4]), in1=cmv(W[3]), op=ALU.mult),
                         reads=[bW[4], bW[3]], writes=[b_qh[d]])
                    k.op("vector", lambda e, d=d: e.tensor_tensor(out=qh[d][:, NLAT:T], in0=W[4][:, NLAT:T], in1=W[3][:, NLAT:T], op=ALU.mult),
                         reads=[bW[4], bW[3]], writes=[b_qh[d]])
                    k.op("gpsimd", lambda e: e.tensor_scalar(out=W[0][:], in0=W[2][:], scalar1=-60.0, scalar2=None, op0=ALU.max),
                         reads=[bW[2], bW[0]], writes=[bW[0]])
                    k.op("scalar", lambda e: e.activation(out=W[0][:], in_=W[0][:], func=AF.Exp, scale=-1.0), reads=[bW[0]], writes=[bW[0]])
                    k.op("vector", lambda e: e.tensor_tensor(out=W[4][:], in0=W[1][:], in1=W[0][:], op=ALU.mult),
                         reads=[bW[1], bW[0], bW[4]], writes=[bW[4]])
                    k.op("scalar", lambda e, d=d: e.copy(out=kt_[d][:], in_=W[4][:]), reads=[bW[4]], writes=[b_kt[d]])
                    glast = W[2][:, (31 if d == 0 else 0):T:32]
                    k.op("scalar", lambda e, d=d, glast=glast: e.activation(out=dec[d][:], in_=glast, func=AF.Exp), reads=[bW[2]], writes=[b_dec[d]])
                    k.op("vector", lambda e, d=d: e.tensor_tensor(out=kh[:].rearrange("p (n j) -> p n j", j=32),
                                                                  in0=W[4][:].rearrange("p (n j) -> p n j", j=32),
                                                                  in1=bc_last(dec[d][:], 32), op=ALU.mult),
                         reads=[bW[4], b_dec[d]], writes=[b_kh])
                    for g0 in range(0, NT, 8):
                        ng = min(8, NT - g0)
                        for j in range(ng):
                            ti = g0 + j
                            k.op("tensor", lambda e, j=j, ti=ti: e.transpose(out=psT[:, j * 128:(j + 1) * 128], in_=kh[:, ti * 128:(ti + 1) * 128],
                                                                             identity=self.ident_bf[:]),
                                 reads=[b_kh, self.b_const], writes=[self.psb[7]])
                        k.op("vector", lambda e, d=d, g0=g0, ng=ng: e.tensor_copy(
                            out=khtm[d][:, g0:g0 + ng, :], in_=psT[:, 0:ng * 128].rearrange("p (n j) -> p n j", j=128)),
                            reads=[self.psb[7]], writes=[b_khtm[d]])
                        k.op("scalar", lambda e, d=d, g0=g0, ng=ng: e.activation(
                            out=khtm3[d][:, g0:g0 + ng, :], in_=psT[:, 0:ng * 128].rearrange("p (n j) -> p n j", j=128),
                            func=AF.Copy, scale=self.masks[:, 4, 0:1]),
                            reads=[self.psb[7], self.b_const], writes=[b_khtm[d]])
                sm_all = [W[1][:].bitcast(BF16)[:, 0:NT * 128].rearrange("p (n j) -> p n j", j=128),
                          W[2][:].bitcast(BF16)[:, 0:NT * 128].rearrange("p (n j) -> p n j", j=128)]
                b_sm = [bW[1], bW[2]]
                for d in range(2):
                    for ti in range(NT):
                        pi = 1 + (ti % 2)
                        k.op("tensor", lambda e, d=d, ti=ti, pi=pi: e.matmul(
                            self.ps[pi][:, 0:128], lhsT=kt_[d][:, ti * 128:(ti + 1) * 128], rhs=qh[d][:, ti * 128:(ti + 1) * 128],
                            start=True, stop=True), reads=[b_kt[d], b_qh[d]], writes=[self.psb[pi]])
                        k.op("vector", lambda e, d=d, ti=ti, pi=pi: e.tensor_tensor(
                            out=sm_all[d][:, ti, :], in0=self.ps[pi][:, 0:128], in1=self.masks[:, d, :], op=ALU.mult),
                            reads=[self.psb[pi], self.b_const], writes=[b_sm[d]])
                order = [[32, 33] + list(range(32)), [33, 32] + list(range(31, -1, -1))]
                seq = [[], []]
                for d in range(2):
                    for ti in order[d]:
                        for cj in ([0, 1, 2, 3] if d == 0 else [3, 2, 1, 0]):
                            seq[d].append((ti, cj))
                NSEQ = len(seq[0])
                NR = 4
                DS_BANK = [[5, 1], [6, 2]]
                psd_b = [[self.psb[DS_BANK[d][par]] for par in range(2)] for d in range(2)]
                for d in range(2):
                    k.op("vector", lambda e, d=d: e.memset(S32[d][:, 0, :], 0.0), reads=[b_S32[d][0]], writes=[b_S32[d][0]])
                    k.op("vector", lambda e, d=d: e.memset(Sring[d][:, 0, :], 0.0), reads=[b_Sr[d][0]], writes=[b_Sr[d][0]])
                oacc = W[0]
                b_o = bW[0]
                visited = set()

                def emit_dS(d, i):
                    ti, cj = seq[d][i]
                    par = i % 2
                    pdv = self.ps[DS_BANK[d][par]][:, 0:128]
                    if cj < 3:
                        k.op("tensor", lambda e: e.matmul(
                            pdv, lhsT=khtm[d][cj * 32:(cj + 1) * 32, ti, :], rhs=V[cj * 32:(cj + 1) * 32, ti, :], start=True, stop=True),
                            reads=[b_khtm[d], b_V], writes=[psd_b[d][par]])
                    else:
                        k.op("tensor", lambda e: e.matmul(
                            pdv, lhsT=khtm3[d][:, ti, :], rhs=V[:, ti, :], start=True, stop=True),
                            reads=[b_khtm[d], b_V], writes=[psd_b[d][par]])

                def emit_update(d, i):
                    ti, cj = seq[d][i]
                    par = i % 2
                    pdv = self.ps[DS_BANK[d][par]][:, 0:128]
                    cidx = ti * 4 + cj
                    slot = (i + 1) % NR
                    so, sn_ = i % 2, (i + 1) % 2
                    k.op("vector", lambda e: e.scalar_tensor_tensor(
                        out=S32[d][:, sn_, :], in0=S32[d][:, so, :], scalar=dec[d][:, cidx:cidx + 1], in1=pdv, op0=ALU.mult, op1=ALU.add),
                        reads=[psd_b[d][par], b_dec[d], b_S32[d][so]], writes=[b_S32[d][sn_]])
                    k.op("scalar", lambda e: e.copy(out=Sring[d][:, slot, :], in_=S32[d][:, sn_, :]),
                         reads=[b_S32[d][sn_], b_Sr[d][slot]], writes=[b_Sr[d][slot]])

                for d in range(2):
                    emit_dS(d, 0)
                for i in range(NSEQ):
                    for d in range(2):
                        ti, cj = seq[d][i]
                        po = self.ps[3 + d]
                        b_po = self.psb[3 + d]
                        first = (i % 4 == 0)
                        lastc = (i % 4 == 3)
                        if first:
                            k.op("tensor", lambda e, d=d, ti=ti, po=po: e.matmul(
                                po[:, 0:128], lhsT=V[:, ti, :], rhs=sm_all[d][:, ti, :], start=True, stop=False),
                                reads=[b_V, b_sm[d]], writes=[b_po])
                        if i + 1 < NSEQ:
                            emit_dS(d, i + 1)
                        c0 = ti * 128 + cj * 32
                        slot = i % NR
                        k.op("tensor", lambda e, d=d, cj=cj, c0=c0, po=po, slot=slot, lastc=lastc: e.matmul(
                            po[:, cj * 32:(cj + 1) * 32], lhsT=Sring[d][:, slot, :], rhs=qh[d][:, c0:c0 + 32], start=False, stop=lastc),
                            reads=[b_Sr[d][slot], b_qh[d]], writes=[b_po])
                        emit_update(d, i)
                        if lastc:
                            osl = oacc[:, ti * 128:(ti + 1) * 128]
                            if ti not in visited:
                                visited.add(ti)
                                k.op("scalar", lambda e, osl=osl, po=po: e.copy(out=osl, in_=po[:, 0:128]), reads=[b_po], writes=[b_o])
                            else:
                                k.op("vector", lambda e, osl=osl, po=po: e.tensor_tensor(out=osl, in0=po[:, 0:128], in1=osl, op=ALU.add),
                                     reads=[b_po, b_o], writes=[b_o])
                k.op("scalar", lambda e: e.activation(out=W[1][:], in_=W[0][:], func=AF.Square), reads=[bW[0], bW[1]], writes=[bW[1]])
                for t0 in range(0, T, 512):
                    n = min(512, T - t0)
                    k.op("tensor", lambda e, t0=t0, n=n: e.matmul(self.ps[0][:, 0:n], lhsT=self.ones128[:], rhs=W[1][:, t0:t0 + n],
                                                                 start=True, stop=True), reads=[bW[1], self.b_const], writes=[self.psb[0]])
                    k.op("vector", lambda e, t0=t0, n=n: e.tensor_scalar(out=W[2][:, t0:t0 + n], in0=self.ps[0][:, 0:n], scalar1=EPS, scalar2=None,
                                                                        op0=ALU.add), reads=[self.psb[0], bW[2]], writes=[bW[2]])
                k.op("scalar", lambda e: e.activation(out=W[2][:], in_=W[2][:], func=AF.Sqrt), reads=[bW[2]], writes=[bW[2]])
                k.op("vector", lambda e: e.reciprocal(out=W[2][:], in_=W[2][:]), reads=[bW[2]], writes=[bW[2]])
                k.op("vector", lambda e: e.tensor_tensor(out=W[3][:], in0=W[0][:], in1=W[2][:], op=ALU.mult), reads=[bW[0], bW[2], bW[3]], writes=[bW[3]])
                k.dma("sync", lambda e, hd=hd: e.dma_start(out=W[4][:], in_=self.pT[5, hd, :, :]), writes=[bW[4]])
                k.op("vector", lambda e: e.scalar_tensor_tensor(
                    out=kh[:, 0:NLAT].rearrange("p (r c) -> p r c", c=GRID), in0=W[3][:, 0:NLAT].rearrange("p (c r) -> p r c", r=GRID),
                    scalar=nwh[:, 0:1], in1=W[4][:, 0:NLAT].rearrange("p (r c) -> p r c", c=GRID), op0=ALU.mult, op1=ALU.mult),
                    reads=[bW[3], bW[4], b_par], writes=[b_kh])
                k.op("vector", lambda e: e.scalar_tensor_tensor(
                    out=kh[:, NLAT:T], in0=W[3][:, NLAT:T], scalar=nwh[:, 0:1], in1=W[4][:, NLAT:T], op0=ALU.mult, op1=ALU.mult),
                    reads=[bW[3], bW[4], b_par], writes=[b_kh])
                k.dma("sync", lambda e, hd=hd: e.dma_start(out=self.mixT[8 + hd, :, :], in_=kh[:]), reads=[b_kh])
            k.barrier()

    def sin_turns(self, out_ap, u, ki, tmp, bufs):
        k = self.k
        TWO_PI = 2 * math.pi * (1.0 - 1e-6)
        k.op("vector", lambda e: e.tensor_copy(out=ki, in_=u), reads=bufs, writes=bufs)
        k.op("vector", lambda e: e.tensor_copy(out=tmp, in_=ki), reads=bufs, writes=bufs)
        k.op("vector", lambda e: e.tensor_tensor(out=u, in0=u, in1=tmp, op=ALU.subtract), reads=bufs, writes=bufs)
        k.op("vector", lambda e: e.tensor_scalar(out=tmp, in0=u, scalar1=0.5, scalar2=None, op0=ALU.is_gt), reads=bufs, writes=bufs)
        k.op("vector", lambda e: e.tensor_tensor(out=u, in0=u, in1=tmp, op=ALU.subtract), reads=bufs, writes=bufs)
        k.op("vector", lambda e: e.tensor_scalar(out=tmp, in0=u, scalar1=-0.5, scalar2=None, op0=ALU.is_lt), reads=bufs, writes=bufs)
        k.op("vector", lambda e: e.tensor_tensor(out=u, in0=u, in1=tmp, op=ALU.add), reads=bufs, writes=bufs)
        k.op("scalar", lambda e: e.activation(out=out_ap, in_=u, func=AF.Sin, scale=TWO_PI), reads=bufs, writes=bufs)

    def s5_phase(self, l, need_ctx):
        k = self.k
        nc = self.nc
        L = 256
        NCHK = T // L
        PI = math.pi
        from concourse.ap import AP as _AP
        with contextlib.ExitStack() as ph:
            mag_s = [self.sb(ph, f"smag{d}", [128, 32], F32) for d in range(2)]
            th_s = [self.sb(ph, f"sth{d}", [128, 32], F32) for d in range(2)]
            dsk = self.sb(ph, "sdsk", [128, 8], F32)
            b_prm = Buf()
            k.dma("sync", lambda e: e.dma_start(out=dsk[:], in_=self.s5_d[l].rearrange("(c p) -> p c", p=128)), writes=[b_prm])
            iota = self.aux[:, 8:8 + 257]
            mg = self.aux[:, 0:8]
            with contextlib.ExitStack() as pre:
                BT = [[self.sb(pre, f"sBT{d}{r}", [128, 8, 4, 128], BF16) for r in range(2)] for d in range(2)]
                CT = [[self.sb(pre, f"sCT{d}{r}", [128, 32, 128], BF16) for r in range(2)] for d in range(2)]

                def nl(nm):
                    return self.sb(pre, nm, [64, 64], F32)
                for d in range(2):
                    for r in range(2):
                        k.op("vector", lambda e, d=d, r=r: e.memset(CT[d][r][:], 0.0), writes=[b_prm])
                lre, lim, ls, aa, mg_, thn, t1, t2, cs, sn, den, nr, cfr, cfi = [nl(f"snl{i}") for i in range(14)]
                nli = self.sb(pre, "snli", [64, 64], I32)
                Bn = [self.sb(pre, f"sBn{r}", [64, 1024], F32) for r in range(2)]
                Bb = [self.sb(pre, f"sBb{r}", [64, 1024], F32) for r in range(2)]
                tmpB = self.sb(pre, "stmpB", [64, 1024], F32)
                Cx = [self.sb(pre, f"sCx{r}", [32, 32, 128], F32) for r in range(2)]
                sl_lre = self.sb(pre, "sslre", [128, 32], F32)
                sl_lim = self.sb(pre, "sslim", [128, 32], F32)
                sl_ls = self.sb(pre, "ssls", [128, 32], F32)
                b_n = Buf()
                b_B = Buf()
                b_C = Buf()
                for d in range(2):
                    k.dma("sync", lambda e, d=d: e.dma_start(out=lre[:], in_=self.s5_lre[l, d].rearrange("g p -> p g")), writes=[b_n])
                    k.dma("sync", lambda e, d=d: e.dma_start(out=lim[:], in_=self.s5_lim[l, d].rearrange("g p -> p g")), writes=[b_n])
                    lsrc = self.s5_ls[l, d]
                    k.dma("sync", lambda e, lsrc=lsrc: e.dma_start(out=ls[:], in_=_AP(lsrc.tensor, lsrc.offset, [[0, 64], [1, 64]])), writes=[b_n])
                    for r, src in ((0, self.s5_bre), (1, self.s5_bim)):
                        k.dma("sync", lambda e, d=d, r=r, src=src: e.dma_start(
                            out=Bn[r][:].rearrange("p (g h) -> p g h", h=16), in_=src[l, d].rearrange("g p h -> p g h")), writes=[b_B])

                    def V_(fn, **kw):
                        k.op("vector", fn, reads=[b_n], writes=[b_n])

                    def A_(fn):
                        k.op("scalar", fn, reads=[b_n], writes=[b_n])
                    V_(lambda e: e.tensor_scalar(out=lre[:], in0=lre[:], scalar1=-1e-4, scalar2=None, op0=ALU.min))
                    A_(lambda e: e.activation(out=ls[:], in_=ls[:], func=AF.Exp))
                    V_(lambda e: e.tensor_tensor(out=aa[:], in0=lre[:], in1=ls[:], op=ALU.mult))
                    A_(lambda e: e.activation(out=mg_[:], in_=aa[:], func=AF.Exp))
                    V_(lambda e: e.tensor_tensor(out=thn[:], in0=lim[:], in1=ls[:], op=ALU.mult))
                    V_(lambda e: e.tensor_scalar(out=t1[:], in0=thn[:], scalar1=1.0 / (2 * PI), scalar2=None, op0=ALU.mult))
                    self.sin_turns(sn[:], t1[:], nli[:], t2[:], [b_n])
                    V_(lambda e: e.tensor_scalar(out=t1[:], in0=thn[:], scalar1=1.0 / (2 * PI), scalar2=0.25, op0=ALU.mult, op1=ALU.add))
                    self.sin_turns(cs[:], t1[:], nli[:], t2[:], [b_n])
                    V_(lambda e: e.tensor_tensor(out=cs[:], in0=cs[:], in1=mg_[:], op=ALU.mult))
                    V_(lambda e: e.tensor_tensor(out=sn[:], in0=sn[:], in1=mg_[:], op=ALU.mult))
                    V_(lambda e: e.tensor_tensor(out=den[:], in0=lre[:], in1=lre[:], op=ALU.mult))
                    V_(lambda e: e.tensor_tensor(out=t1[:], in0=lim[:], in1=lim[:], op=ALU.mult))
                    V_(lambda e: e.tensor_tensor(out=den[:], in0=den[:], in1=t1[:], op=ALU.add))
                    V_(lambda e: e.reciprocal(out=den[:], in_=den[:]))
                    V_(lambda e: e.tensor_scalar(out=nr[:], in0=cs[:], scalar1=-1.0, scalar2=None, op0=ALU.add))
                    V_(lambda e: e.tensor_tensor(out=t1[:], in0=nr[:], in1=lre[:], op=ALU.mult))
                    V_(lambda e: e.tensor_tensor(out=t2[:], in0=sn[:], in1=lim[:], op=ALU.mult))
                    V_(lambda e: e.tensor_tensor(out=cfr[:], in0=t1[:], in1=t2[:], op=ALU.add))
                    V_(lambda e: e.tensor_tensor(out=cfr[:], in0=cfr[:], in1=den[:], op=ALU.mult))
                    V_(lambda e: e.tensor_tensor(out=t1[:], in0=sn[:], in1=lre[:], op=ALU.mult))
                    V_(lambda e: e.tensor_tensor(out=t2[:], in0=nr[:], in1=lim[:], op=ALU.mult))
                    V_(lambda e: e.tensor_tensor(out=cfi[:], in0=t1[:], in1=t2[:], op=ALU.subtract))
                    V_(lambda e: e.tensor_tensor(out=cfi[:], in0=cfi[:], in1=den[:], op=ALU.mult))
                    def v3(t):
                        return t[:].rearrange("p (g h) -> p g h", h=16)
                    k.op("vector", lambda e: e.tensor_tensor(out=v3(Bb[0]), in0=v3(Bn[0]), in1=bc_last(cfr[:], 16), op=ALU.mult), reads=[b_n, b_B], writes=[b_B])
                    k.op("vector", lambda e: e.tensor_tensor(out=v3(tmpB), in0=v3(Bn[1]), in1=bc_last(cfi[:], 16), op=ALU.mult), reads=[b_n, b_B], writes=[b_B])
                    k.op("vector", lambda e: e.tensor_tensor(out=Bb[0][:], in0=Bb[0][:], in1=tmpB[:], op=ALU.subtract), reads=[b_B], writes=[b_B])
                    k.op("vector", lambda e: e.tensor_tensor(out=v3(Bb[1]), in0=v3(Bn[1]), in1=bc_last(cfr[:], 16), op=ALU.mult), reads=[b_n, b_B], writes=[b_B])
                    k.op("vector", lambda e: e.tensor_tensor(out=v3(tmpB), in0=v3(Bn[0]), in1=bc_last(cfi[:], 16), op=ALU.mult), reads=[b_n, b_B], writes=[b_B])
                    k.op("vector", lambda e: e.tensor_tensor(out=Bb[1][:], in0=Bb[1][:], in1=tmpB[:], op=ALU.add), reads=[b_B], writes=[b_B])
                    for r in range(2):
                        for tb in range(8):
                            pb = 1 + (tb % 4)
                            k.op("tensor", lambda e, r=r, tb=tb, pb=pb: e.transpose(
                                out=self.ps[pb][:, 0:64], in_=Bb[r][:, tb * 128:(tb + 1) * 128], identity=self.ident[0:64, 0:64]),
                                reads=[b_B, self.b_const], writes=[self.psb[pb]])
                            for g8 in range(8):
                                dst = BT[d][r][:, tb, g8 // 2, (g8 % 2) * 64:(g8 % 2) * 64 + 64]
                                if g8 % 2 == 0:
                                    k.op("vector", lambda e, dst=dst, pb=pb, g8=g8: e.tensor_scalar(
                                        out=dst, in0=self.ps[pb][:, 0:64], scalar1=mg[:, g8:g8 + 1], scalar2=None, op0=ALU.mult),
                                        reads=[self.psb[pb], self.b_const], writes=[b_prm])
                                else:
                                    k.op("scalar", lambda e, dst=dst, pb=pb, g8=g8: e.activation(
                                        out=dst, in_=self.ps[pb][:, 0:64], func=AF.Copy, scale=mg[:, g8:g8 + 1]),
                                        reads=[self.psb[pb], self.b_const], writes=[b_prm])
                    for g2 in range(2):
                        k.dma("sync", lambda e, d=d, g2=g2: e.dma_start(out=sl_lre[64 * g2:64 * g2 + 64, :],
                                                                        in_=self.s5_lre[l, d, g2::2, :].rearrange("gp p -> p gp")), writes=[b_n])
                        k.dma("sync", lambda e, d=d, g2=g2: e.dma_start(out=sl_lim[64 * g2:64 * g2 + 64, :],
                                                                        in_=self.s5_lim[l, d, g2::2, :].rearrange("gp p -> p gp")), writes=[b_n])
                        lsrc2 = self.s5_ls[l, d, g2::2]
                        k.dma("sync", lambda e, g2=g2, lsrc2=lsrc2: e.dma_start(
                            out=sl_ls[64 * g2:64 * g2 + 64, :], in_=_AP(lsrc2.tensor, lsrc2.offset, [[0, 64], [2, 32]])), writes=[b_n])
                    V_(lambda e: e.tensor_scalar(out=sl_lre[:], in0=sl_lre[:], scalar1=-1e-4, scalar2=None, op0=ALU.min))
                    A_(lambda e: e.activation(out=sl_ls[:], in_=sl_ls[:], func=AF.Exp))
                    V_(lambda e: e.tensor_tensor(out=sl_lre[:], in0=sl_lre[:], in1=sl_ls[:], op=ALU.mult))
                    k.op("scalar", lambda e, d=d: e.activation(out=mag_s[d][:], in_=sl_lre[:], func=AF.Exp), reads=[b_n], writes=[b_prm])
                    k.op("vector", lambda e, d=d: e.scalar_tensor_tensor(out=th_s[d][:], in0=sl_lim[:], scalar=1.0 / (2 * PI), in1=sl_ls[:],
                                                                         op0=ALU.mult, op1=ALU.mult), reads=[b_n], writes=[b_prm])
                    for r, src in ((0, self.s5_cre), (1, self.s5_cim)):
                        k.op("vector", lambda e, r=r: e.memset(Cx[r][:], 0.0), reads=[b_C], writes=[b_C])
                        for g2 in range(2):
                            k.dma("sync", lambda e, d=d, r=r, g2=g2, src=src: e.dma_start(
                                out=Cx[r][16 * g2:16 * g2 + 16, :, 64 * g2:64 * g2 + 64],
                                in_=src[l, d, g2::2].rearrange("gp h p -> h gp p")), reads=[b_C], writes=[b_C])
                        for half in range(2):
                            pb = 5 + half
                            for i in range(16):
                                gp = half * 16 + i
                                k.op("tensor", lambda e, r=r, gp=gp, i=i, pb=pb: e.transpose(
                                    out=self.ps[pb][:, i * 32:(i + 1) * 32], in_=Cx[r][:, gp, :], identity=self.ident[0:32, 0:32]),
                                    reads=[b_C, self.b_const], writes=[self.psb[pb]])
                            pv = self.ps[pb][:, :].rearrange("p (i c) -> p i c", c=32)
                            for j in range(4):
                                dst = CT[d][r][:, half * 16 + j:half * 16 + 16:4, 32 * j:32 * j + 32]
                                k.op("scalar", lambda e, dst=dst, pv=pv, j=j, r=r: e.activation(
                                    out=dst, in_=pv[:, j:16:4, :], func=AF.Copy, scale=(1.0 if r == 0 else -1.0)),
                                    reads=[self.psb[pb]], writes=[b_prm])
                for d in range(2):
                    for r in range(2):
                        k.dma("sync", lambda e, d=d, r=r: e.dma_start(out=self.BTs[d, r], in_=BT[d][r][:].rearrange("p t j c -> p t (j c)")), reads=[b_prm])
                        k.dma("sync", lambda e, d=d, r=r: e.dma_start(out=self.CTs[d, r], in_=CT[d][r][:]), reads=[b_prm])
                k.barrier()
            L = 512
            ust = self.sb(ph, "sust", [128, T], F32)
            ubf = self.sb(ph, "subf", [128, T], BF16)
            yacc = self.sb(ph, "syacc", [128, T], F32)
            b_ust, b_ubf, b_y = Buf(), Buf(), Buf()
            tab_all = self.sb(ph, "stab", [128, 2, 4, L + 1], F32)
            tabs = [[tab_all[:, i, j, :] for i in range(2)] for j in range(4)]
            init_t = self.sb(ph, "sinit", [128, 2, 4], F32)
            x1_t = self.sb(ph, "sx1", [128, 2, 4], F32)
            x2_t = self.sb(ph, "sx2", [128, 2, 4], F32)
            b_init = Buf()
            rts = [self.sb(ph, f"srt{j}", [128, L], F32) for j in range(4)]
            phs = self.sb(ph, "sphs", [128, L + 1], F32)
            pht = self.sb(ph, "spht", [128, L + 1], F32)
            phi = self.sb(ph, "sphi", [128, L + 1], I32)
            b_tab = [Buf() for _ in range(4)]
            b_phs = Buf()
            btl = self.make_ring(ph, "sbtl", [128, 2, 2, 512], BF16, 2)
            ctl = self.make_ring(ph, "sctl", [128, 2, 2, 4, 128], BF16, 2)
            pre_s = self.make_ring(ph, "spre", [128, 2, L], F32, 4)
            wring_ = self.make_ring(ph, "sw", [128, 2, L], F32, 8)
            zri = self.make_ring(ph, "szri", [128, 4, 2, L], F32, 2)
            xri = self.make_ring(ph, "sxri", [128, 2, L], BF16, 4)
            tmpP = self.make_ring(ph, "stp", [128, L], F32, 2)
            tmpV = self.make_ring(ph, "stv", [128, 2, L], F32, 2)
            tmpP2 = self.make_ring(ph, "stp2", [128, 2, L], F32, 1)
            psr = Ring([1, 2, 3, 4])
            psy = Ring([5, 6])
            iotaL = self.sb(ph, "siota", [128, L + 1], F32)
            b_io = Buf()
            k.dma("sync", lambda e: e.dma_start(out=iotaL[:], in_=self.c_iota[:, :]), writes=[b_io])
            chunks = [(NLAT, T)] + [(i * L, (i + 1) * L) for i in range(NLAT // L)]
            for tb in range(8):
                bt, b_bt = btl.next()
                ct, b_ct = ctl.next()
                k.dma("sync", lambda e, tb=tb, bt=bt: e.dma_start(out=bt[:], in_=self.BTs[:, :, :, tb, :].rearrange("d r p c -> p d r c")), writes=[b_bt])
                k.dma("sync", lambda e, tb=tb, ct=ct: e.dma_start(out=ct[:], in_=self.CTs[:, :, :, tb * 4:(tb + 1) * 4, :].rearrange("d r p g c -> p d r g c")), writes=[b_ct])
                k.dma("sync", lambda e, tb=tb: e.dma_start(out=ust[:], in_=self.pT[0, tb, :, :]), reads=[b_ust], writes=[b_ust])
                k.op("scalar", lambda e: e.copy(out=ubf[:], in_=ust[:]), reads=[b_ust], writes=[b_ubf])
                k.op("vector", lambda e, tb=tb: e.tensor_scalar(out=yacc[:], in0=ust[:], scalar1=dsk[:, tb:tb + 1], scalar2=None, op0=ALU.mult),
                     reads=[b_ust, b_prm], writes=[b_y])
                for d in range(2):
                    for j in range(4):
                        gp = tb * 4 + j
                        thc = th_s[d][:, gp:gp + 1]
                        for which, off in ((1, 0.0), (0, 0.25)):
                            k.op("vector", lambda e, thc=thc, off=off: e.tensor_scalar(out=phs[:], in0=iotaL[:], scalar1=thc, scalar2=off,
                                                                                       op0=ALU.mult, op1=ALU.add), reads=[b_prm, b_io, b_phs], writes=[b_phs])
                            self.sin_turns(tabs[j][which], phs[:], phi[:], pht[:], [b_phs, b_tab[j]])
                        k.op("vector", lambda e, j=j, d=d, gp=gp: e.tensor_scalar(out=rts[j][:], in0=iotaL[:, 0:L], scalar1=0.0,
                                                                                 scalar2=mag_s[d][:, gp:gp + 1], op0=ALU.mult, op1=ALU.add),
                             reads=[b_prm, b_io], writes=[b_tab[j]])
                    order = chunks if d == 0 else [chunks[0]] + chunks[:0:-1]
                    NCI = len(order)
                    Aout, Zout = {}, {}

                    def sv(t2d, ci, d=d, order=order):
                        lo, hi = order[ci]
                        v = t2d[:, lo:hi]
                        return v[:, ::-1] if d == 1 else v

                    def TT(E, out, in0, in1, op, reads, writes):
                        k.op(E, lambda e: e.tensor_tensor(out=out, in0=in0, in1=in1, op=op), reads=reads, writes=writes)

                    def stageA(ci, d=d, tb=tb, bt=bt, b_bt=b_bt, sv=sv, order=order):
                        n = order[ci][1] - order[ci][0]
                        for j in range(4):
                            EA = "vector"
                            cs_t = tabs[j][0][:, 0:n]
                            sn_t = tabs[j][1][:, 0:n]
                            p1i, _ = psr.next()
                            p2i, _ = psr.next()
                            p1, p2 = self.ps[p1i], self.ps[p2i]
                            rhs = sv(ubf, ci)
                            k.op("tensor", lambda e, p1=p1, j=j, rhs=rhs, n=n: e.matmul(
                                p1[:, 0:n], lhsT=bt[:, d, 0, j * 128:(j + 1) * 128], rhs=rhs, start=True, stop=True),
                                reads=[b_bt, b_ubf], writes=[self.psb[p1i]])
                            k.op("tensor", lambda e, p2=p2, j=j, rhs=rhs, n=n: e.matmul(
                                p2[:, 0:n], lhsT=bt[:, d, 1, j * 128:(j + 1) * 128], rhs=rhs, start=True, stop=True),
                                reads=[b_bt, b_ubf], writes=[self.psb[p2i]])
                            if EA == "gpsimd":
                                pr, bpr = pre_s.next()
                                k.op("scalar", lambda e, pr=pr, p1=p1, n=n: e.copy(out=pr[:, 0, 0:n], in_=p1[:, 0:n]), reads=[self.psb[p1i]], writes=[bpr])
                                k.op("scalar", lambda e, pr=pr, p2=p2, n=n: e.copy(out=pr[:, 1, 0:n], in_=p2[:, 0:n]), reads=[self.psb[p2i]], writes=[bpr])
                                s_re, s_im = pr[:, 0, 0:n], pr[:, 1, 0:n]
                                rd = [bpr, b_tab[j]]
                                tp, btp = tmpP.next()
                                tpv = tp[:, 0:n]
                            else:
                                s_re, s_im = p1[:, 0:n], p2[:, 0:n]
                                rd = [self.psb[p1i], self.psb[p2i], b_tab[j]]
                                tp, btp = tmpV.next()
                                tpv = tp[:, 0, 0:n]
                            w, bw = wring_.next()
                            TT(EA, w[:, 0, 0:n], s_re, cs_t, ALU.mult, rd, [bw])
                            TT(EA, tpv, s_im, sn_t, ALU.mult, rd, [btp])
                            TT(EA, w[:, 0, 0:n], w[:, 0, 0:n], tpv, ALU.add, [bw, btp], [bw])
                            TT(EA, w[:, 1, 0:n], s_im, cs_t, ALU.mult, rd, [bw])
                            TT(EA, tpv, s_re, sn_t, ALU.mult, rd + [btp], [btp])
                            TT(EA, w[:, 1, 0:n], w[:, 1, 0:n], tpv, ALU.subtract, [bw, btp], [bw])
                            Aout[(ci, j)] = (w, bw, n)

                    def stageB(ci, order=order):
                        zr, bzr = zri.next()
                        if ci > 0:
                            nprev = order[ci - 1][1] - order[ci - 1][0]
                            zp, bzp = Zout["prev"]
                            zend = zp[:, :, :, nprev - 1].rearrange("p j c -> p c j")
                            cLb = tab_all[:, 0, :, nprev].unsqueeze(1).to_broadcast([128, 2, 4])
                            sLb = tab_all[:, 1, :, nprev].unsqueeze(1).to_broadcast([128, 2, 4])
                            rdc = [bzp] + b_tab + [b_init]
                            k.op("vector", lambda e, zend=zend, cLb=cLb: e.tensor_tensor(out=x1_t[:], in0=zend, in1=cLb, op=ALU.mult), reads=rdc, writes=[b_init])
                            k.op("vector", lambda e, zend=zend, sLb=sLb: e.tensor_tensor(out=x2_t[:], in0=zend, in1=sLb, op=ALU.mult), reads=rdc, writes=[b_init])
                            k.op("vector", lambda e: e.tensor_tensor(out=init_t[:, 0, :], in0=x1_t[:, 0, :], in1=x2_t[:, 1, :], op=ALU.subtract), reads=[b_init], writes=[b_init])
                            k.op("vector", lambda e: e.tensor_tensor(out=init_t[:, 1, :], in0=x1_t[:, 1, :], in1=x2_t[:, 0, :], op=ALU.add), reads=[b_init], writes=[b_init])
                        for j in range(4):
                            w, bw, n = Aout.pop((ci, j))
                            for c2 in range(2):
                                ini = 0.0 if ci == 0 else init_t[:, c2, j:j + 1]
                                k.op("vector", lambda e, zr=zr, w=w, c2=c2, ini=ini, j=j, n=n: e.tensor_tensor_scan(
                                    out=zr[:, j, c2, 0:n], data0=rts[j][:, 0:n], data1=w[:, c2, 0:n], initial=ini, op0=ALU.mult, op1=ALU.add),
                                    reads=[bw, b_tab[j], b_init], writes=[bzr])
                            Zout[(ci, j)] = (zr[:, j], bzr, n)
                        Zout["prev"] = (zr, bzr)

                    def stageC(ci, d=d, ct=ct, b_ct=b_ct, sv=sv):
                        pyi, _ = psy.next()
                        py = self.ps[pyi]
                        for j in range(4):
                            zr, bzr, n = Zout.pop((ci, j))
                            E = "vector"
                            cs_t = tabs[j][0][:, 0:n]
                            sn_t = tabs[j][1][:, 0:n]
                            xr_, bxr_ = xri.next()
                            tv, btv = tmpV.next() if E == "vector" else tmpP2.next()
                            zre, zim = zr[:, 0, 0:n], zr[:, 1, 0:n]
                            A_, B_ = tv[:, 0, 0:n], tv[:, 1, 0:n]
                            rd2 = [bzr, b_tab[j]]
                            TT(E, A_, zre, cs_t, ALU.mult, rd2, [btv])
                            TT(E, B_, zim, sn_t, ALU.mult, rd2 + [btv], [btv])
                            TT(E, xr_[:, 0, 0:n], A_, B_, ALU.subtract, [btv], [bxr_])
                            TT(E, A_, zim, cs_t, ALU.mult, rd2 + [btv], [btv])
                            TT(E, B_, zre, sn_t, ALU.mult, rd2 + [btv], [btv])
                            TT(E, xr_[:, 1, 0:n], A_, B_, ALU.add, [btv], [bxr_])
                            k.op("tensor", lambda e, py=py, xr_=xr_, j=j, n=n: e.matmul(
                                py[:, 0:n], lhsT=ct[:, d, 0, j, :], rhs=xr_[:, 0, 0:n], start=(j == 0), stop=False),
                                reads=[b_ct, bxr_], writes=[self.psb[pyi]])
                            k.op("tensor", lambda e, py=py, xr_=xr_, j=j, n=n: e.matmul(
                                py[:, 0:n], lhsT=ct[:, d, 1, j, :], rhs=xr_[:, 1, 0:n], start=False, stop=(j == 3)),
                                reads=[b_ct, bxr_], writes=[self.psb[pyi]])
                        yv = sv(yacc, ci)
                        k.op("vector", lambda e, py=py, yv=yv, n=n: e.tensor_tensor(out=yv, in0=py[:, 0:n], in1=yv, op=ALU.add),
                             reads=[self.psb[pyi], b_y], writes=[b_y])

                    stageA(0)
                    for ci in range(NCI):
                        if ci + 1 < NCI:
                            stageA(ci + 1)
                        stageB(ci)
                        stageC(ci)
                k.op("vector", lambda e: e.tensor_tensor(out=ust[:], in0=yacc[:], in1=yacc[:], op=ALU.mult), reads=[b_y, b_ust], writes=[b_ust])
                k.op("vector", lambda e: e.tensor_scalar(out=ust[:], in0=ust[:], scalar1=0.044715, scalar2=1.0, op0=ALU.mult, op1=ALU.add),
                     reads=[b_ust], writes=[b_ust])
                k.op("vector", lambda e: e.tensor_tensor(out=ust[:], in0=ust[:], in1=yacc[:], op=ALU.mult), reads=[b_ust, b_y], writes=[b_ust])
                k.op("scalar", lambda e: e.activation(out=ust[:], in_=ust[:], func=AF.Sigmoid, scale=1.5957691216057308), reads=[b_ust], writes=[b_ust])
                k.op("vector", lambda e: e.tensor_tensor(out=ust[:], in0=ust[:], in1=yacc[:], op=ALU.mult), reads=[b_ust, b_y], writes=[b_ust])
                k.dma("sync", lambda e, tb=tb: e.dma_start(out=self.gS5[tb, :, :], in_=ust[:]), reads=[b_ust])
            k.barrier()

    def glu_phase(self, l, need_ctx):
        k = self.k
        with contextlib.ExitStack() as ph:
            self.wring = self.make_ring(ph, "wr", [128, 4096], BF16, 5)
            gf = self.sb(ph, "ggf", [128, 8, 1024], F32)
            gb = self.sb(ph, "ggb", [128, 8, 1024], BF16)
            bgl = self.sb(ph, "gbgl", [128, 8], F32)
            b_gf, b_gb, b_bg = Buf(), Buf(), Buf()
            sgr = self.make_ring(ph, "gsg", [128, 512], F32, 3)
            outr = self.make_ring(ph, "gout", [128, 512], BF16, 3)
            k.dma("sync", lambda e: e.dma_start(out=bgl[:], in_=self.s5_bglu[l].rearrange("(c p) -> p c", p=128)), writes=[b_bg])
            wv = self.s5_wglu[l].rearrange("(kt p) c -> p kt c", p=128)
            gv = self.gS5.rearrange("c p t -> p c t")
            psr = Ring([1, 2, 3, 4])
            for (t0, ts, s) in self.supertiles(need_ctx):
                nh = max(1, ts // 512)
                n = min(512, ts)
                k.dma("sync", lambda e, t0=t0, ts=ts: e.dma_start(out=gf[:, :, 0:ts], in_=gv[:, :, t0:t0 + ts]), reads=[b_gf], writes=[b_gf])
                k.op("scalar", lambda e, ts=ts: e.copy(out=gb[:, :, 0:ts], in_=gf[:, :, 0:ts]), reads=[b_gf, b_gb], writes=[b_gb])
                stream = Prog.WStream(self, [wv[:, :, bi * 256:(bi + 1) * 256] for bi in range(4)], 3)
                for bi in range(4):
                    wt, wb = stream.get()
                    for sub in range(2):
                        m = bi * 2 + sub
                        for hf in range(nh):
                            tsl = slice(hf * 512, hf * 512 + n)
                            tok0 = t0 + hf * 512
                            pi, _ = psr.next()
                            pp = self.ps[pi]
                            for kt in range(8):
                                k.op("tensor", lambda e, pp=pp, wt=wt, kt=kt, sub=sub, tsl=tsl, n=n: e.matmul(
                                    pp[:, 0:n], lhsT=wt[:, kt, sub * 128:(sub + 1) * 128], rhs=gb[:, kt, tsl],
                                    start=(kt == 0), stop=(kt == 7)), reads=[wb, b_gb], writes=[self.psb[pi]])
                            sg, bsg = sgr.next()
                            k.op("scalar", lambda e, sg=sg, pp=pp, m=m, n=n: e.activation(out=sg[:, 0:n], in_=pp[:, 0:n], func=AF.Sigmoid,
                                                                                       bias=bgl[:, m:m + 1]), reads=[self.psb[pi], b_bg], writes=[bsg])
                            ot, bot = outr.next()
                            k.op("vector", lambda e, ot=ot, sg=sg, m=m, tsl=tsl, n=n: e.tensor_tensor(out=ot[:, 0:n], in0=sg[:, 0:n], in1=gf[:, m, tsl], op=ALU.mult),
                                 reads=[bsg, b_gf], writes=[bot])
                            k.dma("sync", lambda e, ot=ot, m=m, tok0=tok0, n=n: e.dma_start(out=self.mixT[m, :, tok0:tok0 + n], in_=ot[:, 0:n]), reads=[bot])
            k.barrier()

    def outproj_phase(self, l, include_ctx):
        k = self.k
        xTl = self.xT
        with contextlib.ExitStack() as ph:
            self.wring = self.make_ring(ph, "wr", [128, 4096], BF16, 5)
            mx = self.sb(ph, "omx", [128, 16, 1024], BF16)
            b_mx = Buf()
            xrring = self.make_ring(ph, "oxr", [128, 512], F32, 3)
            wv = self.w_out[l].rearrange("(kt p) c -> p kt c", p=128)
            mv = self.mixT.rearrange("c p t -> p c t")
            psr = Ring([1, 2, 3, 4])
            for (t0, ts, s) in self.supertiles(include_ctx):
                nh = max(1, ts // 512)
                n = min(512, ts)
                k.dma("sync", lambda e, t0=t0, ts=ts: e.dma_start(out=mx[:, :, 0:ts], in_=mv[:, :, t0:t0 + ts]), reads=[b_mx], writes=[b_mx])
                stream = Prog.WStream(self, [wv[:, :, bi * 256:(bi + 1) * 256] for bi in range(8)], 3)
                for bi in range(8):
                    wt, wb = stream.get()
                    for sub in range(2):
                        m = bi * 2 + sub
                        for hf in range(nh):
                            tsl = slice(hf * 512, hf * 512 + n)
                            tok0 = t0 + hf * 512
                            xr, bxr = xrring.next()
                            k.dma("sync", lambda e, xr=xr, m=m, tok0=tok0, n=n: e.dma_start(out=xr[:, 0:n], in_=xTl[m, :, tok0:tok0 + n]),
                                  reads=[self.xbuf(m, tok0)], writes=[bxr])
                            pi, _ = psr.next()
                            pp = self.ps[pi]
                            for kt in range(16):
                                k.op("tensor", lambda e, pp=pp, wt=wt, kt=kt, sub=sub, tsl=tsl, n=n: e.matmul(
                                    pp[:, 0:n], lhsT=wt[:, kt, sub * 128:(sub + 1) * 128], rhs=mx[:, kt, tsl],
                                    start=(kt == 0), stop=(kt == 15)), reads=[wb, b_mx], writes=[self.psb[pi]])
                            k.op("vector", lambda e, xr=xr, pp=pp, m=m, s=s, n=n: e.scalar_tensor_tensor(
                                out=xr[:, 0:n], in0=pp[:, 0:n], scalar=self.Gmod[:, s, 1, m:m + 1], in1=xr[:, 0:n],
                                op0=ALU.mult, op1=ALU.add), reads=[self.psb[pi], bxr, self.b_mod], writes=[bxr])
                            k.dma("sync", lambda e, xr=xr, m=m, tok0=tok0, n=n: e.dma_start(out=xTl[m, :, tok0:tok0 + n], in_=xr[:, 0:n]),
                                  reads=[bxr], writes=[self.xbuf(m, tok0)])
            k.barrier()


def _consts():
    ident = np.eye(128, dtype=np.float32)
    s = np.arange(128)[:, None]
    t = np.arange(128)[None, :]
    same = (s // 32) == (t // 32)
    m_f = (same & (s <= t)).astype(np.float32)
    m_b = (same & (s >= t)).astype(np.float32)
    g2 = ((np.arange(128) % 32) // 16)
    m0 = np.repeat((g2 == 0).astype(np.float32)[:, None], 128, 1)
    m1 = np.repeat((g2 == 1).astype(np.float32)[:, None], 128, 1)
    m96 = np.repeat((np.arange(128) >= 96).astype(np.float32)[:, None], 128, 1)
    mg = ((np.arange(128)[:, None] // 16) == np.arange(8)[None, :]).astype(np.float32)
    iota = np.repeat(np.arange(257, dtype=np.float32)[None, :], 128, 0)
    aux = np.concatenate([mg, iota], axis=1).astype(np.float32)
    iota2 = np.repeat(np.arange(513, dtype=np.float32)[None, :], 128, 0)
    return ident, np.stack([m_f, m_b, m0, m1, m96]).astype(np.float32), aux, iota2


W_NAMES = ["w_ada", "b_ada", "norm_w", "ffn_w_gate", "ffn_w_up", "ffn_w_down", "w_in", "w_out",
           "s5_lambda_re", "s5_lambda_im", "s5_log_step", "s5_b_re", "s5_b_im", "s5_c_re", "s5_c_im",
           "s5_d", "s5_w_glu", "s5_b_glu", "hgrn_lower_bounds", "hgrn_norm_w", "final_norm_w"]


def make_in_map(inputs, b, half=0):
    ident, masks, aux, iota2 = _consts()
    m = {"xin": np.ascontiguousarray(np.concatenate([inputs["x"][b], inputs["ctx"][b]], axis=0), dtype=np.float32),
         "cc": np.ascontiguousarray(np.stack([inputs["c"][b], inputs["c_ctx"]], axis=0), dtype=np.float32),
         "c_ident": ident, "c_masks": masks, "c_aux": aux, "c_iota": iota2,
         "sel": np.repeat(np.array([[1.0, 0.0]] if half == 0 else [[0.0, 1.0]], np.float32), 128, 0)}
    for nme in W_NAMES:
        m[nme] = np.ascontiguousarray(inputs[nme], dtype=np.float32)
    return m


def kernel(**inputs):
    nc = Prog().build()
    nb = inputs["x"].shape[0]
    in_maps = [make_in_map(inputs, c % nb, c // nb) for c in range(8)]
    res = run_bass_kernel_spmd(nc, in_maps, core_ids=list(range(8)))
    return np.stack([np.concatenate([np.asarray(res.results[b]["Y"]), np.asarray(res.results[b + nb]["Y"])], axis=0)
                     for b in range(nb)], axis=0).astype(np.float32)
```
